# Optimizing a Trainium2 kernel written in Bass

```python
import math
import jax, jax.numpy as jnp
from jax import lax
import numpy as np

D_MODEL = 1024
BATCH = 8
SEQ = 4096
DEPTH = 2

D_MIX = D_MODEL
D_DIFF = D_MIX // 2
D_RWKV = D_MIX - D_DIFF
DIFF_HEADS = 4
DIFF_HEAD_DIM = D_DIFF // DIFF_HEADS // 2
DIFF_V_DIM = 2 * DIFF_HEAD_DIM
RWKV_HEAD = 64
RWKV_HEADS = D_RWKV // RWKV_HEAD
LORA_W = 64
LORA_A = 64
LORA_V = 32
LORA_G = 160
D_FF = 4 * D_MODEL
N_BUCKETS = 32
MAX_DISTANCE = 128
Q_BLOCK = 128
LN_EPS = 1e-5
SUBLN_EPS = 1e-5
GN_EPS = 64e-5
ALPHA = (2 * DEPTH) ** 0.25
BETA = (8 * DEPTH) ** -0.25

N_DIFF = 3 * D_DIFF
RW_R = 0
RW_K = D_RWKV
RW_V = 2 * D_RWKV
RW_W = 3 * D_RWKV
RW_A = RW_W + LORA_W
RW_G = RW_A + LORA_A
N_RWKV_BASE = RW_G + LORA_G
N_RWKV_REST = N_RWKV_BASE + LORA_V
N_IN_FIRST = N_DIFF + N_RWKV_BASE
N_IN_REST = N_DIFF + N_RWKV_REST

kernel_name = "hybrid_diffattn_rwkv7_deepnorm"


def layer_norm(x, g, b, eps=LN_EPS):
    xf = x.astype(jnp.float32)
    mu = jnp.mean(xf, -1, keepdims=True)
    var = jnp.mean(jnp.square(xf - mu), -1, keepdims=True)
    return ((xf - mu) * lax.rsqrt(var + eps) * g + b).astype(x.dtype)


def t5_causal_bucket(dist):
    n = jnp.maximum(dist, 0)
    max_exact = N_BUCKETS // 2
    nf = jnp.maximum(n, 1).astype(jnp.float32)
    large = max_exact + (jnp.log(nf / max_exact) / math.log(MAX_DISTANCE / max_exact)
                         * (N_BUCKETS - max_exact)).astype(jnp.int32)
    large = jnp.minimum(large, N_BUCKETS - 1)
    return jnp.where(n < max_exact, n, large)


def diff_attention(q, k, v, rel_bias, lam, subln_g, lam_init):
    B, S = q.shape[0], q.shape[1]
    nb = S // Q_BLOCK
    scale = DIFF_HEAD_DIM ** -0.5
    qb = (q * scale).reshape(B, nb, Q_BLOCK, DIFF_HEADS, 2, DIFF_HEAD_DIM)
    qb = qb.transpose(1, 0, 3, 4, 2, 5)
    kt = k.transpose(0, 2, 3, 1, 4)
    vt = v.transpose(0, 2, 1, 3)
    k_pos = jnp.arange(S, dtype=jnp.int32)

    def block(args):
        q_blk, idx = args
        q_pos = idx * Q_BLOCK + jnp.arange(Q_BLOCK, dtype=jnp.int32)
        dist = q_pos[:, None] - k_pos[None, :]
        bias = jnp.transpose(rel_bias[t5_causal_bucket(dist)], (2, 0, 1)).astype(jnp.float32)
        logits = jnp.einsum('bhmqd,bhmkd->bhmqk', q_blk, kt).astype(jnp.float32)
        logits = logits + bias[None, :, None]
        logits = jnp.where((dist >= 0)[None, None, None], logits, -jnp.inf)
        probs = jax.nn.softmax(logits, axis=-1)
        attn = probs[:, :, 0] - lam * probs[:, :, 1]
        return jnp.einsum('bhqk,bhkd->bhqd', attn.astype(vt.dtype), vt)

    out = lax.map(block, (qb, jnp.arange(nb, dtype=jnp.int32)))
    out = out.transpose(1, 0, 3, 2, 4).reshape(B, S, DIFF_HEADS, DIFF_V_DIM).astype(jnp.float32)
    out = out * lax.rsqrt(jnp.mean(jnp.square(out), -1, keepdims=True) + SUBLN_EPS) * subln_g
    out = out * (1.0 - lam_init)
    return out.reshape(B, S, D_DIFF)


def wkv7_scan(r, w, k, v, a, b):
    B, _, H, N = r.shape

    def step(state, inp):
        r_t, w_t, k_t, v_t, a_t, b_t = inp
        sa = jnp.einsum('bhij,bhj->bhi', state, a_t)
        state = (state * w_t[:, :, None, :] + sa[..., None] * b_t[:, :, None, :]
                 + v_t[..., None] * k_t[:, :, None, :])
        y = jnp.einsum('bhij,bhj->bhi', state, r_t)
        return state, y

    xs = tuple(jnp.moveaxis(t, 1, 0) for t in (r, w, k, v, a, b))
    s0 = jnp.zeros((B, H, N, N), jnp.float32)
    _, ys = lax.scan(step, s0, xs)
    return jnp.moveaxis(ys, 0, 1)


def rwkv7_time_mix(r, k, v, xw, xa, xg, w0, w_up, a0, a_up, g_up, k_k, k_a, r_k, gn_g, gn_b):
    B, S, _ = r.shape
    f32 = jnp.float32
    r, k, v, xw, xa, xg = (t.astype(f32) for t in (r, k, v, xw, xa, xg))
    w = -jax.nn.softplus(-(w0 + jnp.tanh(xw) @ w_up)) - 0.5
    decay = jnp.exp(-jnp.exp(w))
    a = jax.nn.sigmoid(a0 + xa @ a_up)
    g = jax.nn.sigmoid(xg) @ g_up
    heads = lambda t: t.reshape(B, S, RWKV_HEADS, RWKV_HEAD)
    kk = heads(k * k_k)
    kk = kk / jnp.maximum(jnp.sqrt(jnp.sum(jnp.square(kk), -1, keepdims=True)), 1e-12)
    k = k * (1.0 + (a - 1.0) * k_a)
    rh, kh, vh, ah = heads(r), heads(k), heads(v), heads(a)
    y = wkv7_scan(rh, heads(decay), kh, vh, -kk, kk * ah)
    mu = jnp.mean(y, -1, keepdims=True)
    var = jnp.mean(jnp.square(y - mu), -1, keepdims=True)
    y = ((y - mu) * lax.rsqrt(var + GN_EPS)).reshape(B, S, D_RWKV) * gn_g + gn_b
    bonus = jnp.sum(rh * kh * r_k, -1, keepdims=True) * vh
    return (y + bonus.reshape(B, S, D_RWKV)) * g


def setup_inputs(seed: int = 0) -> dict:
    key = jax.random.key(seed)
    ks = iter(jax.random.split(key, 48))
    f32 = jnp.float32
    nrm = lambda shape, s: jax.random.normal(next(ks), shape, f32) * s
    uni = lambda shape, lo, hi: jax.random.uniform(next(ks), shape, f32, lo, hi)

    def in_col_scale(n):
        s = np.ones((n,), np.float32)
        s[2 * D_DIFF:3 * D_DIFF] = BETA
        s[N_DIFF + RW_V:N_DIFF + RW_V + D_RWKV] = BETA
        return jnp.asarray(s)

    D = D_MODEL
    return {
        "x": nrm((BATCH, SEQ, D), 1.0),
        "ln_in_g": 1.0 + nrm((D,), 0.02),
        "ln_in_b": nrm((D,), 0.02),
        "w_in_first": nrm((D, N_IN_FIRST), D ** -0.5) * in_col_scale(N_IN_FIRST),
        "w_in_rest": nrm((DEPTH - 1, D, N_IN_REST), D ** -0.5) * in_col_scale(N_IN_REST),
        "mu_first": uni((N_RWKV_BASE,), 0.1, 0.9),
        "mu_rest": uni((DEPTH - 1, N_RWKV_REST), 0.1, 0.9),
        "rel_bias": nrm((N_BUCKETS, DIFF_HEADS), 0.5),
        "lambda_q1": nrm((DEPTH, DIFF_HEAD_DIM), 0.1),
        "lambda_k1": nrm((DEPTH, DIFF_HEAD_DIM), 0.1),
        "lambda_q2": nrm((DEPTH, DIFF_HEAD_DIM), 0.1),
        "lambda_k2": nrm((DEPTH, DIFF_HEAD_DIM), 0.1),
        "subln_g": 1.0 + nrm((DEPTH, DIFF_V_DIM), 0.02),
        "rw_w0": uni((DEPTH, D_RWKV), -6.0, -0.5),
        "rw_w_up": nrm((DEPTH, LORA_W, D_RWKV), 0.1),
        "rw_a0": nrm((DEPTH, D_RWKV), 0.1),
        "rw_a_up": nrm((DEPTH, LORA_A, D_RWKV), 0.1),
        "rw_g_up": nrm((DEPTH, LORA_G, D_RWKV), LORA_G ** -0.5),
        "rw_v0": uni((DEPTH - 1, D_RWKV), 0.5, 1.5),
        "rw_v_up": nrm((DEPTH - 1, LORA_V, D_RWKV), 0.1),
        "rw_k_k": 0.85 + nrm((DEPTH, D_RWKV), 0.05),
        "rw_k_a": 1.0 + nrm((DEPTH, D_RWKV), 0.05),
        "rw_r_k": nrm((DEPTH, RWKV_HEADS, RWKV_HEAD), 0.1),
        "rw_gn_g": 1.0 + nrm((DEPTH, D_RWKV), 0.02),
        "rw_gn_b": nrm((DEPTH, D_RWKV), 0.02),
        "w_out": nrm((DEPTH, D_MIX, D), D_MIX ** -0.5 * BETA),
        "ln_mix_g": 1.0 + nrm((DEPTH, D), 0.02),
        "ln_mix_b": nrm((DEPTH, D), 0.02),
        "w_up": nrm((DEPTH, D, D_FF), D ** -0.5 * BETA),
        "w_down": nrm((DEPTH, D_FF, D), D_FF ** -0.5 * BETA),
        "ln_ffn_g": 1.0 + nrm((DEPTH, D), 0.02),
        "ln_ffn_b": nrm((DEPTH, D), 0.02),
    }


def reference(x, ln_in_g, ln_in_b, w_in_first, w_in_rest, mu_first, mu_rest, rel_bias,
              lambda_q1, lambda_k1, lambda_q2, lambda_k2, subln_g,
              rw_w0, rw_w_up, rw_a0, rw_a_up, rw_g_up, rw_v0, rw_v_up,
              rw_k_k, rw_k_a, rw_r_k, rw_gn_g, rw_gn_b,
              w_out, ln_mix_g, ln_mix_b, w_up, w_down, ln_ffn_g, ln_ffn_b):
    B, S, _ = x.shape
    f32 = jnp.float32
    x = layer_norm(x, ln_in_g, ln_in_b)
    v_first = None
    for l in range(DEPTH):
        w_in = w_in_first if l == 0 else w_in_rest[l - 1]
        mu = mu_first if l == 0 else mu_rest[l - 1]
        p = jnp.einsum('bsd,dn->bsn', x, w_in)

        q = p[..., 0:D_DIFF].reshape(B, S, DIFF_HEADS, 2, DIFF_HEAD_DIM)
        k = p[..., D_DIFF:2 * D_DIFF].reshape(B, S, DIFF_HEADS, 2, DIFF_HEAD_DIM)
        v = p[..., 2 * D_DIFF:N_DIFF].reshape(B, S, DIFF_HEADS, DIFF_V_DIM)
        lam_init = 0.8 - 0.6 * math.exp(-0.3 * l)
        lam = (jnp.exp(jnp.sum(lambda_q1[l] * lambda_k1[l]).astype(f32))
               - jnp.exp(jnp.sum(lambda_q2[l] * lambda_k2[l]).astype(f32)) + lam_init)
        y_diff = diff_attention(q, k, v, rel_bias, lam, subln_g[l], lam_init)

        pr = p[..., N_DIFF:]
        pr_prev = jnp.pad(pr, ((0, 0), (1, 0), (0, 0)))[:, :-1]
        pr = pr + (pr_prev - pr) * mu
        r_rw = pr[..., RW_R:RW_R + D_RWKV]
        k_rw = pr[..., RW_K:RW_K + D_RWKV]
        v_rw = pr[..., RW_V:RW_V + D_RWKV].astype(f32)
        if l == 0:
            v_first = v_rw
        else:
            xv = pr[..., N_RWKV_BASE:N_RWKV_REST].astype(f32)
            v_rw = v_rw + (v_first - v_rw) * jax.nn.sigmoid(rw_v0[l - 1] + xv @ rw_v_up[l - 1])
        y_rw = rwkv7_time_mix(r_rw, k_rw, v_rw,
                              pr[..., RW_W:RW_A], pr[..., RW_A:RW_G], pr[..., RW_G:N_RWKV_BASE],
                              rw_w0[l], rw_w_up[l], rw_a0[l], rw_a_up[l], rw_g_up[l],
                              rw_k_k[l], rw_k_a[l], rw_r_k[l], rw_gn_g[l], rw_gn_b[l])

        mix = jnp.concatenate([y_diff.astype(x.dtype), y_rw.astype(x.dtype)], axis=-1)
        mix = jnp.einsum('bsm,md->bsd', mix, w_out[l])
        x = layer_norm(ALPHA * x + mix, ln_mix_g[l], ln_mix_b[l])

        h = jnp.square(jax.nn.relu(jnp.einsum('bsd,df->bsf', x, w_up[l])))
        h = jnp.einsum('bsf,fd->bsd', h, w_down[l])
        x = layer_norm(ALPHA * x + h, ln_ffn_g[l], ln_ffn_b[l])
    return x
```

```python
from contextlib import ExitStack
import math
import numpy as np
import concourse.bass as bass
import concourse.mybir as mybir
from concourse.bass_utils import run_bass_kernel_spmd

F32 = mybir.dt.float32
BF16 = mybir.dt.bfloat16
AF = mybir.ActivationFunctionType
ALU = mybir.AluOpType
AX = mybir.AxisListType

ENGS = ("pe", "act", "dve", "pool", "sp")
NPOOL = 14


class T:
    __slots__ = ("ap", "name", "w", "r")

    def __init__(self, ap, name=""):
        self.ap = ap
        self.name = name
        self.w = None
        self.r = []

    def __getitem__(self, k):
        return self.ap[k]


class Op:
    __slots__ = ("eng", "fn", "dma", "deps", "signal", "sem", "val", "prewait")

    def __init__(self, eng, fn, dma):
        self.eng = eng
        self.fn = fn
        self.dma = dma
        self.deps = []
        self.signal = False
        self.sem = None
        self.val = 0
        self.prewait = None


class Ctx:
    def __init__(self, nc, stack):
        self.nc = nc
        self.esem = {e: stack.enter_context(nc.semaphore("s_" + e)) for e in ENGS}
        self.dsem = {q: [stack.enter_context(nc.semaphore("d_%s%d" % (q, i))) for i in range(NPOOL)]
                     for q in ("sp", "pool", "act")}
        self.count = {e: 0 for e in ENGS}
        self.dk = {q: 0 for q in ("sp", "pool", "act")}
        self.seen = {e: {} for e in ENGS}
        self.ninst = 0


class Phase:
    def __init__(self, ctx, name):
        self.ctx = ctx
        self.name = name
        self.ops = []
        self.tiles = []

    def tile(self, ap, name=""):
        t = T(ap, name)
        self.tiles.append(t)
        return t

    def add(self, eng, fn, reads=(), writes=(), dma=False):
        op = Op(eng, fn, dma)
        deps = []
        for t in reads:
            if t.w is not None:
                deps.append((t.w, "raw"))
        for t in writes:
            if t.w is not None:
                deps.append((t.w, "waw"))
            for r in t.r:
                deps.append((r, "war"))
        seen = set()
        for d, kind in deps:
            if d is op or id(d) in seen:
                continue
            if not d.dma and not op.dma and d.eng == eng:
                if eng == "pe":
                    continue
                if kind == "war":
                    continue
            seen.add(id(d))
            op.deps.append(d)
            d.signal = True
        for t in writes:
            t.w = op
            t.r = []
        for t in reads:
            if t.w is not op:
                t.r.append(op)
        self.ops.append(op)
        return op

    def emit(self):
        ctx = self.ctx
        nc = ctx.nc
        fin = Op("sp", None, False)
        lastd = {}
        for o in self.ops:
            if o.dma:
                o.signal = True
        self.ops.append(fin)
        for o in self.ops:
            if o.dma:
                q = o.eng
                k = ctx.dk[q]
                ctx.dk[q] += 1
                o.sem = ctx.dsem[q][k % NPOOL]
                o.val = 16 * (k // NPOOL + 1)
                if o.val > 16:
                    o.prewait = (o.sem, o.val - 16)
            elif o.signal:
                ctx.count[o.eng] += 1
                o.sem = ctx.esem[o.eng]
                o.val = ctx.count[o.eng]
            if o.dma:
                lastd[o.sem.name] = o
        fin.deps = list(lastd.values())
        per = {e: [o for o in self.ops if o.eng == e] for e in ENGS}
        bname = {"pe": "tensor", "act": "scalar", "dve": "vector", "pool": "gpsimd", "sp": "sync"}

        def run(e, engine):
            seen = ctx.seen[e]
            for o in per[e]:
                waits = [(d.sem, d.val) for d in o.deps]
                if o.prewait is not None:
                    waits.append(o.prewait)
                for sem, val in waits:
                    if seen.get(sem.name, 0) < val:
                        engine.wait_ge(sem, val)
                        seen[sem.name] = val
                        ctx.ninst += 1
                if o.fn is None:
                    continue
                inst = o.fn(engine)
                ctx.ninst += 1
                if o.signal:
                    inst.then_inc(o.sem, 16 if o.dma else 1)

        with nc.Block() as block:
            for e in ENGS:
                if per[e]:
                    getattr(block, bname[e])(lambda engine, e=e: run(e, engine))
        for t in self.tiles:
            t.w = None
            t.r = []


D = 1024
LN_EPS = 1e-5


class KB:
    def __init__(self, S):
        self.S = S
        self.NT = S // 128
        self.nc = bass.Bass("TRN2", target_bir_lowering=False)
        self.stack = ExitStack()
        self.ctx = Ctx(self.nc, self.stack)
        self.dram = {}
        self.psum_h = [self.stack.enter_context(self.nc.psum_tensor("ps%d" % i, [128, 512], F32))
                       for i in range(8)]

    def din(self, name, shape, dt=F32):
        t = self.nc.dram_tensor(name, list(shape), dt, kind="ExternalInput")
        self.dram[name] = t
        return t

    def dout(self, name, shape, dt=F32):
        t = self.nc.dram_tensor(name, list(shape), dt, kind="ExternalOutput")
        self.dram[name] = t
        return t

    def dscr(self, name, shape, dt=F32, debug=False):
        t = self.nc.dram_tensor(name, list(shape), dt, kind="ExternalOutput" if debug else "Internal")
        self.dram[name] = t
        return t

    def close(self):
        self.stack.close()


class PH(Phase):
    def __init__(self, kb, name):
        super().__init__(kb.ctx, name)
        self.kb = kb
        self.nc = kb.nc
        self.st = ExitStack()
        self.ps = [self.tile(h, "ps%d" % i) for i, h in enumerate(kb.psum_h)]
        self.nsb = 0

    def sb(self, shape, dt=F32, name=None):
        self.nsb += 1
        h = self.st.enter_context(self.nc.sbuf_tensor("%s_%s%d" % (self.name, name or "t", self.nsb), list(shape), dt))
        return self.tile(h, name or "t")

    def dt_(self, name):
        key = "_dt_" + name
        if not hasattr(self, key):
            setattr(self, key, self.tile(self.kb.dram[name], name))
        return getattr(self, key)

    def finish(self):
        self.emit()
        self.st.close()

    def dma(self, q, out_ap, in_ap, reads, writes):
        return self.add(q, lambda e: e.dma_start(out=out_ap, in_=in_ap), reads, writes, dma=True)

    def mm(self, out_ap, lhsT, rhs, start, stop, reads, writes, skip=False):
        if skip:
            return self.add("pe", lambda e: e.matmul(out_ap, lhsT, rhs, start=start, stop=stop, skip_group_check=True),
                            reads, writes)
        return self.add("pe", lambda e: e.matmul(out_ap, lhsT, rhs, start=start, stop=stop), reads, writes)

    def tr(self, out_ap, in_ap, ident_ap, reads, writes):
        return self.add("pe", lambda e: e.transpose(out_ap, in_ap, ident_ap), reads, writes)

    def act(self, out_ap, in_ap, func, reads, writes, bias=0.0, scale=1.0, accum_out=None, eng="act"):
        kw = {}
        if accum_out is not None:
            kw["accum_out"] = accum_out
        return self.add(eng, lambda e: e.activation(out_ap, in_ap, func, bias=bias, scale=scale, **kw), reads, writes)

    def tt(self, eng, out_ap, in0, in1, op, reads, writes):
        return self.add(eng, lambda e: e.tensor_tensor(out_ap, in0, in1, op), reads, writes)

    def ts(self, eng, out_ap, in0, s1, s2, op0, op1, reads, writes, accum_out=None):
        if s2 is None:
            return self.add(eng, lambda e: e.tensor_scalar(out_ap, in0, s1, None, op0), reads, writes)
        if accum_out is not None:
            return self.add(eng, lambda e: e.tensor_scalar(out_ap, in0, s1, s2, op0, op1, accum_out=accum_out), reads, writes)
        return self.add(eng, lambda e: e.tensor_scalar(out_ap, in0, s1, s2, op0, op1), reads, writes)

    def stt(self, eng, out_ap, in0, scalar, in1, op0, op1, reads, writes):
        return self.add(eng, lambda e: e.scalar_tensor_tensor(out_ap, in0, scalar, in1, op0, op1), reads, writes)

    def cp(self, eng, out_ap, in_ap, reads, writes):
        if eng == "act":
            return self.add(eng, lambda e: e.copy(out_ap, in_ap), reads, writes)
        return self.add(eng, lambda e: e.tensor_copy(out_ap, in_ap), reads, writes)


def bcast_row(dram_ap_1d, n):
    return dram_ap_1d.partition_broadcast(128)


def ln_rows(ph, src, i, gB, bB, ident_bf, xres_name, xT_name, bufs, psb):
    nc = ph.nc
    st, mv, rs, nm, xn, ybf, xTs = (bufs[k] for k in ("st", "mv", "rs", "nm", "xn", "ybf", "xTs"))
    for h in range(2):
        ph.add("dve", lambda e, h=h: e.bn_stats(st[:, h * 6:(h + 1) * 6], src[:, h * 512:(h + 1) * 512]), [src], [st])
    ph.add("dve", lambda e: e.bn_aggr(mv[:, 0:2], st[:, 0:12]), [st], [mv])
    ph.act(rs[:, 0:1], mv[:, 1:2], AF.Ln, [mv], [rs], bias=bufs["eps"][:, 0:1])
    ph.act(rs[:, 0:1], rs[:, 0:1], AF.Exp, [rs], [rs], scale=-0.5)
    ph.ts("dve", nm[:, 0:1], mv[:, 0:1], rs[:, 0:1], -1.0, ALU.mult, ALU.mult, [mv, rs], [nm])
    ph.act(xn[:, :], src[:, :], AF.Identity, [src, rs, nm], [xn], bias=nm[:, 0:1], scale=rs[:, 0:1])
    ph.tt("dve", xn[:, :], xn[:, :], gB[:, :], ALU.mult, [xn, gB], [xn])
    ph.tt("pool", xn[:, :], xn[:, :], bB[:, :], ALU.add, [xn, bB], [xn])
    xres = ph.kb.dram[xres_name]
    ph.dma("sp", xres[i * 128:(i + 1) * 128, :], xn[:, :], [xn], [])
    if xT_name is None:
        return
    ph.cp("act", ybf[:, :], xn[:, :], [xn], [ybf])
    pb = psb.ap[:, :].bitcast(BF16)
    for c in range(8):
        ph.tr(pb[:, c * 128:(c + 1) * 128], ybf[:, c * 128:(c + 1) * 128], ident_bf[:, :], [ybf, ident_bf], [psb])
    ph.cp("dve", xTs[:, :], pb[:, :], [psb], [xTs])
    xT = ph.kb.dram[xT_name]
    dst = xT[:, i * 128:(i + 1) * 128].rearrange("(c p) t -> p c t", p=128)
    ph.dma("pool", dst, xTs[:, :].rearrange("p (c t) -> p c t", c=8), [xTs], [])


def ln_bufs(ph, k):
    eps = ph.sb([128, 1], F32, "eps%d" % k)
    ph.add("pool", lambda e: e.memset(eps[:, :], LN_EPS), [], [eps])
    return dict(eps=eps, st=ph.sb([128, 12], F32, "st%d" % k), mv=ph.sb([128, 2], F32, "mv%d" % k),
                rs=ph.sb([128, 1], F32, "rs%d" % k), nm=ph.sb([128, 1], F32, "nm%d" % k),
                xn=ph.sb([128, 1024], F32, "xn%d" % k), ybf=ph.sb([128, 1024], BF16, "ybf%d" % k),
                xTs=ph.sb([128, 1024], BF16, "xTs%d" % k))


def load_consts(ph):
    c = {}
    c["ident_bf"] = ph.sb([128, 128], BF16, "identbf")
    c["ident_f"] = ph.sb([128, 128], F32, "identf")
    ph.dma("sp", c["ident_f"][:, :], ph.kb.dram["c_ident"][:, :], [], [c["ident_f"]])
    ph.cp("dve", c["ident_bf"][:, :], c["ident_f"][:, :], [c["ident_f"]], [c["ident_bf"]])
    return c


def phase_ln0(kb):
    ph = PH(kb, "ln0")
    S, NT = kb.S, kb.NT
    c = load_consts(ph)
    gB = ph.sb([128, 1024], F32, "gB")
    bB = ph.sb([128, 1024], F32, "bB")
    ph.dma("sp", gB[:, :], kb.dram["ln_in_g"][:].partition_broadcast(128), [], [gB])
    ph.dma("sp", bB[:, :], kb.dram["ln_in_b"][:].partition_broadcast(128), [], [bB])
    NB = 3
    xin = [ph.sb([128, 1024], F32, "xin%d" % k) for k in range(NB)]
    bufs = [ln_bufs(ph, k) for k in range(2)]
    for i in range(NT):
        src = xin[i % NB]
        ph.dma("sp", src[:, :], kb.dram["x"][i * 128:(i + 1) * 128, :], [], [src])
        ln_rows(ph, src, i, gB, bB, c["ident_bf"], "xres", "xT", bufs[i % 2], ph.ps[i % 2])
    ph.finish()


N_DIFF = 1536
NRW = [1824, 1856]


def phase_inproj(kb, l):
    ph = PH(kb, "ip%d" % l)
    S, NT = kb.S, kb.NT
    NG = S // 512
    ncol = N_DIFF + NRW[l]
    nrw_tiles = 15
    w_dram = kb.dram["w_in%d" % l]
    xT_sb = ph.sb([128, 8, S], BF16, "xT")
    xT = kb.dram["xT"]
    for c in range(8):
        ph.dma("sp" if c % 2 == 0 else "pool", xT_sb[:, c, :], xT[c * 128:(c + 1) * 128, :], [], [xT_sb])
    wbf = [ph.sb([128, ncol], BF16, "wbf%d" % c) for c in range(8)]
    hc = ncol // 2
    wst = [ph.sb([128, hc], F32, "wst%d" % k) for k in range(2)]
    for c in range(8):
        for hh in range(2):
            s = wst[hh]
            ph.dma("act", s[:, :], w_dram[c * 128:(c + 1) * 128, hh * hc:(hh + 1) * hc], [], [s])
            ph.cp(("dve", "pool")[hh], wbf[c][:, hh * hc:(hh + 1) * hc], s[:, :], [s], [wbf[c]])
    mu_sb = ph.sb([128, 15], F32, "mu")
    ph.dma("sp", mu_sb[:, :], kb.dram["mu%d" % l][:, :], [], [mu_sb])
    bank = [0]

    def nextbank():
        b = ph.ps[bank[0] % 8]
        bank[0] += 1
        return b

    ob = [ph.sb([128, 512], BF16, "ob%d" % k) for k in range(4)]
    oi = 0
    for which, name, scale in ((0, "qT", 0.125), (1, "kT", 1.0)):
        for h in range(4):
            col0 = which * 512 + h * 128
            for g in range(NG):
                pb = nextbank()
                for c in range(8):
                    ph.mm(pb[:, :], wbf[c][:, col0:col0 + 128], xT_sb[:, c, g * 512:(g + 1) * 512],
                          c == 0, c == 7, [wbf[c], xT_sb], [pb])
                o = ob[oi % 4]
                oi += 1
                if oi % 2 == 0:
                    ph.act(o[:, :], pb[:, :], AF.Copy, [pb], [o], scale=scale)
                else:
                    ph.ts("dve", o[:, :], pb[:, :], scale, None, ALU.mult, None, [pb], [o])
                ph.dma("sp", kb.dram[name][h, :, g * 512:(g + 1) * 512], o[:, :], [o], [])
    for i in range(NT):
        pb = nextbank()
        for c in range(8):
            ph.mm(pb[:, :], xT_sb[:, c, i * 128:(i + 1) * 128], wbf[c][:, 1024:1536], c == 0, c == 7,
                  [wbf[c], xT_sb], [pb])
        o = ob[oi % 4]
        oi += 1
        if oi % 2 == 0:
            ph.cp("act", o[:, :], pb[:, :], [pb], [o])
        else:
            ph.cp("dve", o[:, :], pb[:, :], [pb], [o])
        ph.dma("sp", kb.dram["vd"][i * 128:(i + 1) * 128, :], o[:, :], [o], [])
    pfull = [ph.sb([128, S + 1], F32, "pfull%d" % k) for k in range(2)]
    tmp = [ph.sb([128, S], F32, "tmp%d" % k) for k in range(1)]
    for k in range(2):
        ph.add("pool", lambda e, k=k: e.memset(pfull[k][:, 0:1], 0.0), [], [pfull[k]])
    for t in range(nrw_tiles):
        col0 = N_DIFF + t * 128
        rows = min(128, ncol - col0)
        pf = pfull[t % 2]
        tm = tmp[0]
        for g in range(NG):
            pb = nextbank()
            for c in range(8):
                ph.mm(pb[0:rows, :], wbf[c][:, col0:col0 + rows], xT_sb[:, c, g * 512:(g + 1) * 512],
                      c == 0, c == 7, [wbf[c], xT_sb], [pb])
            ph.cp("act", pf[0:rows, 1 + g * 512:1 + (g + 1) * 512], pb[0:rows, :], [pb], [pf])
        ph.tt("pool", tm[0:rows, :], pf[0:rows, 0:S], pf[0:rows, 1:S + 1], ALU.subtract, [pf], [tm])
        ph.stt("dve", tm[0:rows, :], tm[0:rows, :], mu_sb[0:rows, t:t + 1], pf[0:rows, 1:S + 1], ALU.mult, ALU.add,
               [tm, mu_sb, pf], [tm])
        ph.dma("sp", kb.dram["prT"][t * 128:t * 128 + rows, :], tm[0:rows, :], [tm], [])
        if l == 0 and 8 <= t < 12:
            ph.dma("pool", kb.dram["vfT"][(t - 8) * 128:(t - 7) * 128, :], tm[0:rows, :], [tm], [])
    ph.finish()


LAM_INIT = [0.8 - 0.6 * math.exp(-0.3 * l) for l in range(2)]
SUBLN_EPS = 1e-5


def phase_pre(kb):
    ph = PH(kb, "pre")
    nc = kb.nc
    dr = kb.dram
    lt = ph.sb([1, 8, 64], F32, "lt")
    for i, nm in enumerate(("lambda_q1", "lambda_k1", "lambda_q2", "lambda_k2")):
        ph.dma("sp", lt[0:1, 2 * i:2 * i + 2, :], dr[nm][:, :].rearrange("(o l) d -> o l d", o=1), [], [lt])
    pr = ph.sb([1, 4, 64], F32, "pr")
    ph.tt("dve", pr[0:1, 0:2, :], lt[0:1, 0:2, :], lt[0:1, 2:4, :], ALU.mult, [lt], [pr])
    ph.tt("dve", pr[0:1, 2:4, :], lt[0:1, 4:6, :], lt[0:1, 6:8, :], ALU.mult, [lt], [pr])
    sm = ph.sb([1, 4], F32, "sm")
    ph.add("dve", lambda e: e.reduce_sum(sm[0:1, 0:4], pr[0:1, :, :], AX.X), [pr], [sm])
    ex = ph.sb([1, 4], F32, "ex")
    ph.act(ex[0:1, :], sm[0:1, :], AF.Exp, [sm], [ex])
    lam = ph.sb([1, 2], F32, "lam")
    ph.tt("dve", lam[0:1, :], ex[0:1, 2:4], ex[0:1, 0:2], ALU.subtract, [ex], [lam])
    for l in range(2):
        ph.ts("dve", lam[0:1, l:l + 1], lam[0:1, l:l + 1], -LAM_INIT[l], None, ALU.add, None, [lam], [lam])
    ph.dma("sp", dr["lamneg"][0:1, :], lam[0:1, :], [lam], [])
    rb = ph.sb([32, 4], F32, "rb")
    oh = ph.sb([32, 385], F32, "oh")
    ph.dma("sp", rb[:, :], dr["rel_bias"][:, :], [], [rb])
    ph.dma("sp", oh[:, :], dr["c_oh"][:, :], [], [oh])
    J = ph.sb([128, 128], F32, "J")
    Mc = ph.sb([128, 128], F32, "Mc")
    ph.dma("pool", J[:, :], dr["c_J"][:, :], [], [J])
    ph.dma("pool", Mc[:, :], dr["c_causal"][:, :], [], [Mc])
    pb = ph.ps[0]
    ph.mm(pb[0:4, 0:385], rb[:, :], oh[:, :], True, True, [rb, oh], [pb])
    G = ph.sb([4, 385], F32, "G")
    ph.cp("dve", G[:, :], pb[0:4, 0:385], [pb], [G])
    EG = ph.sb([4, 384], F32, "EG")
    ph.ts("dve", EG[:, :], G[:, 0:384], G[:, 384:385], None, ALU.subtract, None, [G], [EG])
    ph.act(EG[:, :], EG[:, :], AF.Exp, [EG], [EG])
    egd = ph.dt_("EGd")
    ph.dma("sp", dr["EGd"][:, :], EG[:, :], [EG], [egd])
    for h in range(4):
        for ty in range(2):
            tr = ph.sb([128, 128], F32, "trev")
            src = bass.AP(dr["EGd"], h * 384 + 128 * ty, [[1, 128], [1, 128]])
            ph.dma("sp", tr[:, :], src, [egd], [tr])
            p2 = ph.ps[1 + (h * 2 + ty) % 4]
            ph.mm(p2[:, 0:128], J[:, :], tr[:, :], True, True, [J, tr], [p2])
            eb = ph.sb([128, 128], F32, "eb")
            if ty == 0:
                ph.tt("dve", eb[:, :], p2[:, 0:128], Mc[:, :], ALU.mult, [p2, Mc], [eb])
            else:
                ph.cp("dve", eb[:, :], p2[:, 0:128], [p2], [eb])
            ph.dma("sp", dr["Eb"][h, ty, :, :], eb[:, :], [eb], [])
    ph.finish()


def phase_attn(kb, l):
    ph = PH(kb, "at%d" % l)
    S, NT = kb.S, kb.NT
    NG = S // 512
    dr = kb.dram
    c = load_consts(ph)
    lamneg = ph.sb([128, 2], F32, "lamneg")
    ph.dma("sp", lamneg[:, :], bass.AP(dr["lamneg"], 0, [[0, 128], [1, 2]]), [], [lamneg])
    sg = ph.sb([128, 1], F32, "sg")
    ph.dma("sp", sg[:, :], dr["subln_g"][l:l + 1, :].rearrange("o d -> d o"), [], [sg])
    epsS = ph.sb([128, 1], F32, "epsS")
    ph.add("pool", lambda e: e.memset(epsS[:, :], SUBLN_EPS), [], [epsS])
    Eb = [[ph.sb([128, 128], F32, "Eb") for ty in range(2)] for h in range(2)]
    qT = [ph.sb([128, S], BF16, "qT") for _ in range(2)]
    kT = [ph.sb([128, S], BF16, "kT") for _ in range(2)]
    Vx = [ph.sb([128, NT, 129], BF16, "Vx") for _ in range(2)]
    for k in range(2):
        ph.add("pool", lambda e, k=k: e.memset(Vx[k][:, :, 128:129], 1.0), [], [Vx[k]])
    Pt = [ph.sb([128, 512], BF16, "Pt") for _ in range(4)]
    Pf = [ph.sb([128, 256], F32, "Pf") for _ in range(2)]
    Osb = [ph.sb([128, 512], F32, "Osb") for _ in range(4)]
    small = {k: ph.sb([128, 4], F32, k) for k in ("r1", "r2", "ss", "rstd")}
    t1 = [ph.sb([128, 128], F32, "t1") for _ in range(2)]
    ybf = [ph.sb([128, 128], BF16, "ybf") for _ in range(2)]
    outT = [ph.sb([128, 512], BF16, "outT") for _ in range(2)]
    Obank = [ph.ps[0], ph.ps[1], ph.ps[2], ph.ps[3]]
    Sbank = [ph.ps[4], ph.ps[5], ph.ps[6]]
    Tbank = ph.ps[7]
    pi = [0]
    pfi = [0]
    for h in range(4):
        hb = h % 2
        q_sb, k_sb, v_sb, E = qT[hb], kT[hb], Vx[hb], Eb[hb]
        ph.dma("sp", q_sb[:, :], dr["qT"][h, :, :], [], [q_sb])
        ph.dma("pool", k_sb[:, :], dr["kT"][h, :, :], [], [k_sb])
        ph.dma("sp", v_sb[:, :, 0:128], dr["vd"][:, h * 128:(h + 1) * 128].rearrange("(t p) d -> p t d", p=128),
               [], [v_sb])
        for ty in range(2):
            ph.dma("pool", E[ty][:, :], dr["Eb"][h, ty, :, :], [], [E[ty]])
        for g in range(NG):
            steps = [(kbk, m) for kbk in range(4 * g + 4) for m in range(2)]
            sb_of = {}

            def emit_qk(si):
                kbk, m = steps[si]
                jlo = max(0, kbk - 4 * g)
                n = (4 - jlo) * 128
                sbk = Sbank[si % 3]
                sb_of[si] = sbk
                rows = slice(m * 64, (m + 1) * 64)
                ph.mm(sbk[:, 0:n], k_sb[rows, kbk * 128:(kbk + 1) * 128],
                      q_sb[rows, (4 * g + jlo) * 128:(4 * g + 4) * 128], True, True, [k_sb, q_sb], [sbk])

            def emit_rest(si):
                kbk, m = steps[si]
                jlo = max(0, kbk - 4 * g)
                n = (4 - jlo) * 128
                sbk = sb_of[si]
                P = Pt[pi[0] % 4]
                pi[0] += 1
                near = [(j, 4 * g + j - kbk) for j in range(jlo, 4) if 0 <= 4 * g + j - kbk <= 1]
                nn = len(near)
                if nn:
                    j0 = near[0][0]
                    c0 = (j0 - jlo) * 128
                    pf = Pf[pfi[0] % 2]
                    pfi[0] += 1
                    ph.act(pf[:, 0:nn * 128], sbk[:, c0:c0 + nn * 128], AF.Exp, [sbk], [pf])
                    for ii, (j, ty) in enumerate(near):
                        ph.tt("dve", P[:, c0 + ii * 128:c0 + (ii + 1) * 128], pf[:, ii * 128:(ii + 1) * 128],
                              E[ty][:, :], ALU.mult, [pf, E[ty]], [P])
                    far0 = c0 + nn * 128
                    if far0 < n:
                        ph.act(P[:, far0:n], sbk[:, far0:n], AF.Exp, [sbk], [P])
                else:
                    ph.act(P[:, 0:n], sbk[:, 0:n], AF.Exp, [sbk], [P])
                for j in range(jlo, 4):
                    ob = Obank[m * 2 + j // 2]
                    cc = (j % 2) * 256
                    ph.mm(ob[:, cc:cc + 129], P[:, (j - jlo) * 128:(j - jlo + 1) * 128], v_sb[:, kbk, :],
                          kbk == 0 and j % 2 == 0, kbk == 4 * g + j, [P, v_sb], [ob], skip=True)

            ns = len(steps)
            LA = 2
            for si in range(min(LA, ns)):
                emit_qk(si)
            for si in range(ns):
                if si + LA < ns:
                    emit_qk(si + LA)
                emit_rest(si)
            for b4 in range(4):
                if b4 % 2 == 0:
                    ph.cp("act", Osb[b4][:, :], Obank[b4][:, :], [Obank[b4]], [Osb[b4]])
                else:
                    ph.cp("dve", Osb[b4][:, :], Obank[b4][:, :], [Obank[b4]], [Osb[b4]])
            oT = outT[g % 2]
            tb = Tbank.ap[:, :].bitcast(BF16)
            for j in range(4):
                o0 = Osb[0 + j // 2]
                o1 = Osb[2 + j // 2]
                cc = (j % 2) * 256
                r1, r2, ss, rstd = (small[k] for k in ("r1", "r2", "ss", "rstd"))
                ph.add("dve", lambda e, o0=o0, cc=cc, j=j: e.reciprocal(r1[:, j:j + 1], o0[:, cc + 128:cc + 129]), [o0], [r1])
                ph.add("dve", lambda e, o1=o1, cc=cc, j=j: e.reciprocal(r2[:, j:j + 1], o1[:, cc + 128:cc + 129]), [o1], [r2])
                ph.ts("dve", r2[:, j:j + 1], r2[:, j:j + 1], lamneg[:, l:l + 1], None, ALU.mult, None, [r2, lamneg], [r2])
                tt1 = t1[j % 2]
                ph.ts("dve", tt1[:, :], o0[:, cc:cc + 128], r1[:, j:j + 1], None, ALU.mult, None, [o0, r1], [tt1])
                ph.stt("dve", tt1[:, :], o1[:, cc:cc + 128], r2[:, j:j + 1], tt1[:, :], ALU.mult, ALU.add, [o1, r2, tt1], [tt1])
                yb = ybf[j % 2]
                ph.act(yb[:, :], tt1[:, :], AF.Square, [tt1], [yb, ss], accum_out=ss[:, j:j + 1])
                ph.act(rstd[:, j:j + 1], ss[:, j:j + 1], AF.Ln, [ss], [rstd], bias=epsS[:, 0:1], scale=1.0 / 128)
                ph.act(rstd[:, j:j + 1], rstd[:, j:j + 1], AF.Exp, [rstd], [rstd], scale=-0.5)
                ph.act(yb[:, :], tt1[:, :], AF.Copy, [tt1, rstd], [yb], scale=rstd[:, j:j + 1])
                ph.tr(tb[:, j * 128:(j + 1) * 128], yb[:, :], c["ident_bf"][:, :], [yb, c["ident_bf"]], [Tbank])
            ph.ts("dve", oT[:, :], tb[:, 0:512], sg[:, 0:1], 1.0 - LAM_INIT[l], ALU.mult, ALU.mult, [Tbank, sg], [oT])
            ph.dma("sp", dr["mixT"][h * 128:(h + 1) * 128, g * 512:(g + 1) * 512], oT[:, :], [oT], [])
    ph.finish()


def host_consts():
    c = {}
    c["c_ident"] = np.eye(128, dtype=np.float32)
    c["c_J"] = np.ascontiguousarray(np.eye(128, dtype=np.float32)[::-1])
    kk, qq = np.meshgrid(np.arange(128), np.arange(128), indexing="ij")
    c["c_causal"] = (qq >= kk).astype(np.float32)
    n = np.maximum(np.arange(384) - 127, 0)
    nf = np.maximum(n, 1).astype(np.float32)
    large = 16 + (np.log(nf / np.float32(16)) / np.float32(math.log(128 / 16)) * np.float32(16)).astype(np.int32)
    large = np.minimum(large, 31)
    bucket = np.where(n < 16, n, large)
    oh = np.zeros((32, 385), np.float32)
    oh[bucket, np.arange(384)] = 1.0
    oh[31, 384] = 1.0
    c["c_oh"] = oh
    pp, ff = np.meshgrid(np.arange(128), np.arange(128), indexing="ij")
    SU = (ff > pp).astype(np.float32)
    IU = (ff >= pp).astype(np.float32)
    c["c_mask4"] = np.ascontiguousarray(np.concatenate([SU, IU, SU, IU], axis=1))
    c["c_maskSL"] = (ff < pp).astype(np.float32)
    c["c_bones"] = ((pp // 64) == (ff // 64)).astype(np.float32)
    cm = np.ones((128, 512), np.float32)
    cm[:, ::128] = 0.0
    c["c_cm01"] = cm
    return c


def host_rwp(inp, l):
    vs = [inp["rw_w0"][l], inp["rw_a0"][l], inp["rw_k_k"][l], inp["rw_k_a"][l], inp["rw_r_k"][l].reshape(512),
          inp["rw_gn_g"][l], inp["rw_gn_b"][l], inp["rw_v0"][l - 1] if l > 0 else np.zeros(512, np.float32)]
    a = np.stack([np.asarray(v, np.float32).reshape(4, 128).T for v in vs], axis=1)
    return np.ascontiguousarray(a)


GN_EPS = 64e-5
DECAY_C = -math.exp(-0.5)


def phase_rwkv(kb, l):
    ph = PH(kb, "rw%d" % l)
    S, NT = kb.S, kb.NT
    NTB = S // 512
    dr = kb.dram
    cst = load_consts(ph)
    identbf = cst["ident_bf"]
    mask4 = ph.sb([128, 512], F32, "mask4")
    maskSL = ph.sb([128, 128], F32, "maskSL")
    bones = ph.sb([128, 128], F32, "bones")
    cm01 = ph.sb([128, 512], F32, "cm01")
    ph.dma("sp", mask4[:, :], dr["c_mask4"][:, :], [], [mask4])
    ph.dma("sp", maskSL[:, :], dr["c_maskSL"][:, :], [], [maskSL])
    ph.dma("sp", bones[:, :], dr["c_bones"][:, :], [], [bones])
    ph.dma("sp", cm01[:, :], dr["c_cm01"][:, :], [], [cm01])
    prm = ph.sb([128, 8, 4], F32, "prm")
    ph.dma("sp", prm[:, :, :], dr["rwp%d" % l][:, :, :], [], [prm])
    W0, A0, KK_, KA_, RK_, GG_, GB_, V0_ = range(8)
    WA = ph.sb([128, 512], F32, "WA")
    WG1 = ph.sb([128, 512], F32, "WG1")
    WGV = ph.sb([64, 512], F32, "WGV")
    ph.dma("sp", WA[0:64, :], dr["rw_w_up"][l, :, :], [], [WA])
    ph.dma("sp", WA[64:128, :], dr["rw_a_up"][l, :, :], [], [WA])
    ph.dma("sp", WG1[:, :], dr["rw_g_up"][l, 0:128, :], [], [WG1])
    ph.dma("sp", WGV[0:32, :], dr["rw_g_up"][l, 128:160, :], [], [WGV])
    if l > 0:
        ph.dma("sp", WGV[32:64, :], dr["rw_v_up"][l - 1, :, :], [], [WGV])
    omk = ph.sb([128, 4], F32, "omk")
    ph.ts("dve", omk[:, :], prm[:, KA_, :], -1.0, 1.0, ALU.mult, ALU.add, [prm], [omk])
    epsG = ph.sb([128, 1], F32, "epsG")
    ph.add("pool", lambda e: e.memset(epsG[:, :], GN_EPS), [], [epsG])
    H = ph.sb([128, 4, 64], F32, "H")
    Hb = [ph.sb([128, 64], BF16, "Hb%d" % hp) for hp in range(4)]
    Hs = [ph.sb([128, 64], F32, "Hs%d" % hp) for hp in range(4)]
    for hp in range(4):
        ph.add("pool", lambda e, hp=hp: e.memset(Hs[hp][:, :], 0.0), [], [Hs[hp]])
        ph.add("pool", lambda e, hp=hp: e.memset(Hb[hp][:, :], 0.0), [], [Hb[hp]])
    XWA = ph.sb([128, 512], F32, "XWA")
    XG1 = ph.sb([128, 512], F32, "XG1")
    XG2 = ph.sb([64, 512], F32, "XG2")
    TH = ph.sb([64, 512], F32, "TH")
    SG1 = ph.sb([128, 512], F32, "SG1")
    SG2 = ph.sb([32, 512], F32, "SG2")
    def f32t(n):
        return ph.sb([128, 512], F32, n)

    def bft(n):
        return ph.sb([128, 512], BF16, n)
    NSET = 2
    sets = []
    for k in range(NSET):
        sets.append(dict(
            R=f32t("R"), Kt=f32t("Kt"), Vt=f32t("Vt"), VF=f32t("VF") if l > 0 else None,
            cum=f32t("cum"), epos=f32t("epos"), AR=ph.sb([128, 4, 2, 128], BF16, "AR"),
            BT=bft("BT"), KT=bft("KT"), BH=bft("BH"), KH=bft("KH"), VB=bft("VB"),
            tok=[ph.sb([128, 4, 128], BF16, "tok") for _ in range(4)],
            gsb=f32t("gsb"), bon=f32t("bon"), outT=bft("outT")))
    sgw, lw, tmpc, eprev, eneg, edec, a_, kk, kk2, ssm, rn, kkn, tq, kp, bbt, vp, gate, pr_ = (
        f32t(n) for n in ("sgw", "lw", "tmpc", "eprev", "eneg", "edec", "a", "kk", "kk2", "ssm", "rn", "kkn",
                          "tq", "kp", "bbt", "vp", "gate", "pr"))
    M4 = [ph.sb([128, 512], BF16, "M4") for _ in range(4)]
    PQ = [[ph.sb([128, 256], BF16, "PQ") for _ in range(2)] for _ in range(2)]
    P0 = [ph.sb([128, 128], BF16, "P0") for _ in range(2)]
    TT = [[ph.sb([128, 128], BF16, "TT") for _ in range(2)] for _ in range(2)]
    AkV = [ph.sb([128, 64], BF16, "AkV") for _ in range(2)]
    WTsb = [ph.sb([128, 128], BF16, "WTsb") for _ in range(2)]
    Uloc = [ph.sb([128, 64], F32, "Uloc") for _ in range(2)]
    Usb = [ph.sb([128, 64], BF16, "Usb") for _ in range(2)]
    ynorm = [ph.sb([128, 128], BF16, "ynorm") for _ in range(2)]
    gst = [ph.sb([128, 6], F32, "gst") for _ in range(2)]
    gmv = [ph.sb([128, 2], F32, "gmv") for _ in range(2)]
    grs = [ph.sb([128, 1], F32, "grs") for _ in range(2)]
    gnm = [ph.sb([128, 1], F32, "gnm") for _ in range(2)]
    fin = [ph.sb([128, 128], F32, "fin") for _ in range(2)]
    PA = [ph.ps[0], ph.ps[1]]
    TRB = [ph.ps[2], ph.ps[3]]
    kb_h = kb.psum_h

    def subtiles(bi, bounds):
        return [ph.ps[bi] for a, b in bounds]
    W1 = [subtiles(4 + e, [(0, 64), (64, 128), (128, 192), (192, 256), (256, 320), (320, 384), (384, 512)])
          for e in range(2)]
    W2 = [subtiles(6 + e, [(0, 128), (128, 384), (384, 512)]) for e in range(2)]
    pai = [0]

    def nextpa():
        b = PA[pai[0] % 2]
        pai[0] += 1
        return b

    def c3(ap):
        return ap.rearrange("p (c t) -> p c t", c=4)

    ui = [0]
    for tb in range(NTB):
        tcols = slice(tb * 512, (tb + 1) * 512)
        ph.dma("sp", XWA[:, :], dr["prT"][1536:1664, tcols], [], [XWA])
        ph.dma("pool", XG1[:, :], dr["prT"][1664:1792, tcols], [], [XG1])
        ph.dma("pool", XG2[:, :], dr["prT"][1792:1856, tcols], [], [XG2])
        ph.act(TH[:, :], XWA[0:64, :], AF.Tanh, [XWA], [TH])
        ph.act(SG1[:, :], XG1[:, :], AF.Sigmoid, [XG1], [SG1])
        ph.act(SG2[:, :], XG2[0:32, :], AF.Sigmoid, [XG2], [SG2])
        for hp in range(4):
            s = sets[(tb * 4 + hp) % NSET]
            R, Kt, Vt, VF, cum, epos, AR, BT, KT, BH, KH, VB, tok, gsb, bon, outT = (
                s[k] for k in ("R", "Kt", "Vt", "VF", "cum", "epos", "AR", "BT", "KT", "BH", "KH", "VB", "tok",
                               "gsb", "bon", "outT"))
            hc = slice(hp * 128, (hp + 1) * 128)
            P = lambda i: prm[:, i, hp:hp + 1]
            ph.dma("sp", R[:, :], dr["prT"][hp * 128:(hp + 1) * 128, tcols], [], [R])
            ph.dma("sp", Kt[:, :], dr["prT"][512 + hp * 128:512 + (hp + 1) * 128, tcols], [], [Kt])
            ph.dma("pool", Vt[:, :], dr["prT"][1024 + hp * 128:1024 + (hp + 1) * 128, tcols], [], [Vt])
            if l > 0:
                ph.dma("pool", VF[:, :], dr["vfT"][hp * 128:(hp + 1) * 128, tcols], [], [VF])
            pa = nextpa()
            ph.mm(pa[:, :], WA[0:64, hc], TH[:, :], True, True, [WA, TH], [pa])
            ph.act(sgw[:, :], pa[:, :], AF.Sigmoid, [pa, prm], [sgw], bias=P(W0))
            ph.ts("dve", lw[:, :], sgw[:, :], DECAY_C, None, ALU.mult, None, [sgw], [lw])
            ph.add("dve", lambda e, cum=cum: e.tensor_tensor_scan(cum[:, :], cm01[:, :], lw[:, :], 0.0, ALU.mult, ALU.add),
                   [cm01, lw], [cum])
            ph.act(epos[:, :], cum[:, :], AF.Exp, [cum], [epos])
            ph.tt("pool", tmpc[:, :], cum[:, :], lw[:, :], ALU.subtract, [cum, lw], [tmpc])
            ph.act(eprev[:, :], tmpc[:, :], AF.Exp, [tmpc], [eprev])
            ph.act(eneg[:, :], cum[:, :], AF.Exp, [cum], [eneg], scale=-1.0)
            for c in range(4):
                ph.act(edec[:, c * 128:(c + 1) * 128], cum[:, c * 128:(c + 1) * 128], AF.Exp, [cum], [edec],
                       scale=-1.0, bias=cum[:, c * 128 + 127:c * 128 + 128])
            pa = nextpa()
            ph.mm(pa[:, :], WA[64:128, hc], XWA[64:128, :], True, True, [WA, XWA], [pa])
            ph.act(a_[:, :], pa[:, :], AF.Sigmoid, [pa, prm], [a_], bias=P(A0))
            ph.ts("dve", kk[:, :], Kt[:, :], P(KK_), None, ALU.mult, None, [Kt, prm], [kk])
            ph.tt("pool", kk2[:, :], kk[:, :], kk[:, :], ALU.mult, [kk], [kk2])
            pa = nextpa()
            ph.mm(pa[:, :], bones[:, :], kk2[:, :], True, True, [bones, kk2], [pa])
            ph.ts("dve", ssm[:, :], pa[:, :], 1e-24, None, ALU.max, None, [pa], [ssm])
            ph.act(rn[:, :], ssm[:, :], AF.Ln, [ssm], [rn])
            ph.act(rn[:, :], rn[:, :], AF.Exp, [rn], [rn], scale=-0.5)
            ph.tt("dve", kkn[:, :], kk[:, :], rn[:, :], ALU.mult, [kk, rn], [kkn])
            ph.ts("pool", tq[:, :], a_[:, :], P(KA_), omk[:, hp:hp + 1], ALU.mult, ALU.add, [a_, prm, omk], [tq])
            ph.tt("pool", kp[:, :], tq[:, :], Kt[:, :], ALU.mult, [tq, Kt], [kp])
            ph.stt("dve", AR[:, :, 0, :], c3(kkn[:, :]), -1.0, c3(eprev[:, :]), ALU.mult, ALU.mult, [kkn, eprev], [AR])
            ph.tt("pool", AR[:, :, 1, :], c3(R[:, :]), c3(epos[:, :]), ALU.mult, [R, epos], [AR])
            ph.tt("pool", bbt[:, :], kkn[:, :], a_[:, :], ALU.mult, [kkn, a_], [bbt])
            ph.tt("dve", BT[:, :], bbt[:, :], eneg[:, :], ALU.mult, [bbt, eneg], [BT])
            ph.tt("pool", BH[:, :], bbt[:, :], edec[:, :], ALU.mult, [bbt, edec], [BH])
            ph.tt("dve", KT[:, :], kp[:, :], eneg[:, :], ALU.mult, [kp, eneg], [KT])
            ph.tt("pool", KH[:, :], kp[:, :], edec[:, :], ALU.mult, [kp, edec], [KH])
            if l > 0:
                pa = nextpa()
                ph.mm(pa[:, :], WGV[32:64, hc], XG2[32:64, :], True, True, [WGV, XG2], [pa])
                ph.act(gate[:, :], pa[:, :], AF.Sigmoid, [pa, prm], [gate], bias=P(V0_))
                ph.tt("pool", vp[:, :], VF[:, :], Vt[:, :], ALU.subtract, [VF, Vt], [vp])
                ph.tt("dve", vp[:, :], vp[:, :], gate[:, :], ALU.mult, [vp, gate], [vp])
                ph.tt("pool", vp[:, :], vp[:, :], Vt[:, :], ALU.add, [vp, Vt], [vp])
                vsrc = vp
            else:
                vsrc = Vt
            ph.cp("act", VB[:, :], vsrc[:, :], [vsrc], [VB])
            pa = nextpa()
            ph.mm(pa[:, :], WG1[:, hc], SG1[:, :], True, False, [WG1, SG1], [pa])
            ph.mm(pa[:, :], WGV[0:32, hc], SG2[:, :], False, True, [WGV, SG2], [pa])
            ph.cp("act", gsb[:, :], pa[:, :], [pa], [gsb])
            ph.ts("pool", pr_[:, :], R[:, :], P(RK_), None, ALU.mult, None, [R, prm], [pr_])
            ph.tt("pool", pr_[:, :], pr_[:, :], kp[:, :], ALU.mult, [pr_, kp], [pr_])
            pa = nextpa()
            ph.mm(pa[:, :], bones[:, :], pr_[:, :], True, True, [bones, pr_], [pa])
            ph.tt("dve", bon[:, :], pa[:, :], vsrc[:, :], ALU.mult, [pa, vsrc], [bon])
            for c in range(4):
                trb = TRB[c // 2]
                tv = trb.ap[:, :].bitcast(BF16)
                base = (c % 2) * 512
                cs = slice(c * 128, (c + 1) * 128)
                for qi, (src, sap) in enumerate(((AR, AR[:, c, 0, :]), (BH, BH[:, cs]), (KH, KH[:, cs]), (VB, VB[:, cs]))):
                    ph.tr(tv[:, base + qi * 128:base + (qi + 1) * 128], sap, identbf[:, :], [src, identbf], [trb])
                ev = tv[:, base:base + 512].rearrange("p (q t) -> p q t", q=4)
                if c % 2 == 0:
                    ph.cp("dve", tok[c][:, :, :], ev, [trb], [tok[c]])
                else:
                    ph.cp("act", tok[c][:, :, :], ev, [trb], [tok[c]])
            for c in range(4):
                cs = slice(c * 128, (c + 1) * 128)
                gl = c * 128 + 127
                for e in range(2):
                    rows = slice(e * 64, (e + 1) * 64)
                    w1, w2 = W1[e], W2[e]
                    w1h, w2h = kb_h[4 + e], kb_h[6 + e]
                    m4 = M4[ui[0] % 4]
                    ui[0] += 1
                    ph.mm(w1h[:, 0:256], BT[rows, cs], AR[rows, c, :, :].rearrange("p a t -> p (a t)"), True, True,
                          [BT, AR], [w1[0]])
                    ph.mm(w1h[:, 256:512], KT[rows, cs], AR[rows, c, :, :].rearrange("p a t -> p (a t)"), True, True,
                          [KT, AR], [w1[0]])
                    ph.mm(w2h[:, 0:128], AR[rows, c, 0, :], BT[rows, cs], True, True, [AR, BT], [w2[0]])
                    ph.tt("dve", m4[:, :], w1h[:, 0:512], mask4[:, :], ALU.mult, [w1[0], mask4], [m4])
                    ph.tt("dve", P0[e][:, :], w2h[:, 0:128], maskSL[:, :], ALU.mult, [w2[0], maskSL], [P0[e]])
                    ph.tt("pool", TT[e][0][:, :], m4[:, 0:128], identbf[:, :], ALU.add, [m4, identbf], [TT[e][0]])
                for k in range(6):
                    for e in range(2):
                        w2, w2h = W2[e], kb_h[6 + e]
                        m4 = M4[(ui[0] - 2 + e) % 4]
                        if k == 0:
                            Pk, Qk, Pt_, Qt_ = P0[e][:, :], m4[:, 0:128], P0[e], m4
                        else:
                            pq = PQ[e][k % 2]
                            Pk, Qk, Pt_, Qt_ = pq[:, 0:128], pq[:, 128:256], pq, pq
                        pqn = PQ[e][(k + 1) % 2]
                        ph.mm(w2h[:, 128:256], Qk, Pk, True, True, [Pt_, Qt_], [w2[1]])
                        if k < 5:
                            ph.mm(w2h[:, 256:384], Pk, Qk, True, True, [Pt_, Qt_], [w2[1]])
                            ph.cp("act", pqn[:, 0:256], w2h[:, 128:384], [w2[1]], [pqn])
                        else:
                            ph.cp("act", pqn[:, 0:128], w2h[:, 128:256], [w2[1]], [pqn])
                        ph.mm(w2h[:, 384:512], pqn[:, 0:128], TT[e][k % 2][:, :], True, True, [pqn, TT[e][k % 2]], [w2[2]])
                        ph.tt("dve", TT[e][(k + 1) % 2][:, :], TT[e][k % 2][:, :], w2h[:, 384:512], ALU.add,
                              [TT[e][k % 2], w2[2]], [TT[e][(k + 1) % 2]])
                for e in range(2):
                    rows = slice(e * 64, (e + 1) * 64)
                    w1, w1h = W1[e], kb_h[4 + e]
                    m4 = M4[(ui[0] - 2 + e) % 4]
                    TTf = TT[e][0]
                    Vtok = tok[c][:, 3, rows]
                    ph.mm(w1h[:, 0:64], m4[:, 256:384], Vtok, True, True, [m4, tok[c]], [w1[0]])
                    ph.cp("act", AkV[e][:, :], w1h[:, 0:64], [w1[0]], [AkV[e]])
                    ph.mm(w1h[:, 384:512], tok[c][:, 0, :], TTf[:, :], True, True, [tok[c], TTf], [w1[6]])
                    ph.cp("act", WTsb[e][rows, :], w1h[rows, 384:512], [w1[6]], [WTsb[e]])
                    ph.mm(w1h[:, 64:128], TTf[:, :], AkV[e][:, :], True, True, [TTf, AkV[e]], [w1[1]])
                    ph.cp("act", Uloc[e][:, :], w1h[:, 64:128], [w1[1]], [Uloc[e]])
                    ph.mm(w1h[:, 128:192], WTsb[e][rows, :], Hb[hp][rows, :], True, True, [WTsb[e], Hb[hp]], [w1[2]])
                    ph.tt("dve", Usb[e][:, :], w1h[:, 128:192], Uloc[e][:, :], ALU.add, [w1[2], Uloc[e]], [Usb[e]])
                    ph.mm(w1h[:, 256:320], AR[rows, c, 1, :], Hb[hp][rows, :], True, False, [AR, Hb[hp]], [w1[4]])
                    ph.mm(w1h[:, 256:320], m4[:, 128:256], Usb[e][:, :], False, False, [m4, Usb[e]], [w1[4]])
                    ph.mm(w1h[:, 256:320], m4[:, 384:512], Vtok, False, True, [m4, tok[c]], [w1[4]])
                    ph.mm(w1h[:, 192:256], tok[c][:, 1, :], Usb[e][:, :], True, False, [tok[c], Usb[e]], [w1[3]])
                    ph.mm(w1h[:, 192:256], tok[c][:, 2, :], Vtok, False, True, [tok[c]], [w1[3]])
                    ph.stt("dve", Hs[hp][rows, :], Hs[hp][rows, :], epos[rows, gl:gl + 1], w1h[rows, 192:256],
                           ALU.mult, ALU.add, [Hs[hp], epos, w1[3]], [Hs[hp]])
                    ph.cp("act", Hb[hp][rows, :], Hs[hp][rows, :], [Hs[hp]], [Hb[hp]])
                    yn = ynorm[c % 2]
                    ph.add("dve", lambda en, e=e, w1h=w1h: en.bn_stats(gst[e][:, 0:6], w1h[:, 256:320]), [w1[4]], [gst[e]])
                    ph.add("dve", lambda en, e=e: en.bn_aggr(gmv[e][:, 0:2], gst[e][:, 0:6]), [gst[e]], [gmv[e]])
                    ph.act(grs[e][:, 0:1], gmv[e][:, 1:2], AF.Ln, [gmv[e]], [grs[e]], bias=epsG[:, 0:1])
                    ph.act(grs[e][:, 0:1], grs[e][:, 0:1], AF.Exp, [grs[e]], [grs[e]], scale=-0.5)
                    ph.ts("dve", gnm[e][:, 0:1], gmv[e][:, 0:1], grs[e][:, 0:1], -1.0, ALU.mult, ALU.mult,
                          [gmv[e], grs[e]], [gnm[e]])
                    ph.act(yn[:, e * 64:(e + 1) * 64], w1h[:, 256:320], AF.Identity, [w1[4], grs[e], gnm[e]], [yn],
                           bias=gnm[e][:, 0:1], scale=grs[e][:, 0:1])
                yn = ynorm[c % 2]
                trb = TRB[1]
                tv = trb.ap[:, :].bitcast(BF16)
                ph.tr(tv[:, 0:128], yn[:, :], identbf[:, :], [yn, identbf], [trb])
                fn = fin[c % 2]
                ph.ts("dve", fn[:, :], tv[:, 0:128], P(GG_), P(GB_), ALU.mult, ALU.add, [trb, prm], [fn])
                ph.tt("pool", fn[:, :], fn[:, :], bon[:, cs], ALU.add, [fn, bon], [fn])
                ph.tt("pool", outT[:, cs], fn[:, :], gsb[:, cs], ALU.mult, [fn, gsb], [outT])
            ph.dma("sp", dr["mixT"][512 + hp * 128:512 + (hp + 1) * 128, tcols], outT[:, :], [outT], [])
    ph.finish()


ALPHA = (2 * 2) ** 0.25


def load_cast_weight(ph, w_rows_fn, nrows_tiles, ncols, wbf_fn, stage, q="act"):
    pc = stage[0].ap.shape[1]
    k = 0
    for r in range(nrows_tiles):
        for c0 in range(0, ncols, pc):
            n = min(pc, ncols - c0)
            s = stage[k % len(stage)]
            ph.dma(q if k % 2 == 0 else "sp", s[:, 0:n], w_rows_fn(r)[:, c0:c0 + n], [], [s])
            ph.cp(("dve", "pool", "act")[k % 3], wbf_fn(r)[:, c0:c0 + n], s[:, 0:n], [s], [wbf_fn(r)])
            k += 1


def phase_outproj(kb, l):
    ph = PH(kb, "op%d" % l)
    S, NT = kb.S, kb.NT
    dr = kb.dram
    c = load_consts(ph)
    gB = ph.sb([128, 1024], F32, "gB")
    bB = ph.sb([128, 1024], F32, "bB")
    ph.dma("sp", gB[:, :], dr["ln_mix_g"][l, :].partition_broadcast(128), [], [gB])
    ph.dma("sp", bB[:, :], dr["ln_mix_b"][l, :].partition_broadcast(128), [], [bB])
    wbf = [ph.sb([128, 1024], BF16, "wo%d" % r) for r in range(8)]
    stage = [ph.sb([128, 1024], F32, "stg%d" % k) for k in range(2)]
    load_cast_weight(ph, lambda r: dr["w_out"][l, r * 128:(r + 1) * 128, :], 8, 1024, lambda r: wbf[r], stage)
    mx = [ph.sb([128, 8, 512], BF16, "mx%d" % k) for k in range(2)]
    xr = [ph.sb([128, 1024], F32, "xr%d" % k) for k in range(2)]
    srcs = [ph.sb([128, 1024], F32, "src%d" % k) for k in range(2)]
    bufs = [ln_bufs(ph, k) for k in range(2)]
    for g in range(S // 512):
        m = mx[g % 2]
        ph.dma("pool", m[:, :, :], dr["mixT"][:, g * 512:(g + 1) * 512].rearrange("(c p) t -> p c t", p=128), [], [m])
        for j in range(4):
            i = g * 4 + j
            x_ = xr[i % 2]
            ph.dma("sp", x_[:, :], dr["xres"][i * 128:(i + 1) * 128, :], [], [x_])
            src = srcs[i % 2]
            for hh in range(2):
                pb = ph.ps[(i % 2) * 2 + hh]
                for cc in range(8):
                    ph.mm(pb[:, :], m[:, cc, j * 128:(j + 1) * 128], wbf[cc][:, hh * 512:(hh + 1) * 512],
                          cc == 0, cc == 7, [m, wbf[cc]], [pb])
                ph.stt("dve", src[:, hh * 512:(hh + 1) * 512], x_[:, hh * 512:(hh + 1) * 512], ALPHA, pb[:, :],
                       ALU.mult, ALU.add, [x_, pb], [src])
            ln_rows(ph, src, i, gB, bB, c["ident_bf"], "xres", "xT", bufs[i % 2], ph.ps[4 + i % 2])
    ph.finish()


def phase_ffn_up(kb, l):
    ph = PH(kb, "fu%d" % l)
    S, NT = kb.S, kb.NT
    dr = kb.dram
    wbf = [ph.sb([128, 4096], BF16, "wu%d" % r) for r in range(8)]
    stage = [ph.sb([128, 2048], F32, "stg%d" % k) for k in range(2)]
    load_cast_weight(ph, lambda r: dr["w_up"][l, r * 128:(r + 1) * 128, :], 8, 4096, lambda r: wbf[r], stage)
    xg = [ph.sb([128, 8, 512], BF16, "xg%d" % k) for k in range(2)]
    tmp = [ph.sb([128, 512], F32, "tmp%d" % k) for k in range(3)]
    ho = [ph.sb([128, 512], BF16, "ho%d" % k) for k in range(4)]
    k = 0
    for g in range(S // 512):
        x_ = xg[g % 2]
        ph.dma("pool", x_[:, :, :], dr["xT"][:, g * 512:(g + 1) * 512].rearrange("(c p) t -> p c t", p=128), [], [x_])
        for f in range(32):
            pb = ph.ps[k % 6]
            for cc in range(8):
                ph.mm(pb[:, :], wbf[cc][:, f * 128:(f + 1) * 128], x_[:, cc, :], cc == 0, cc == 7, [wbf[cc], x_], [pb])
            t = tmp[k % 3]
            h = ho[k % 4]
            ph.act(t[:, :], pb[:, :], AF.Relu, [pb], [t])
            ph.tt(("pool", "dve")[k % 2], h[:, :], t[:, :], t[:, :], ALU.mult, [t], [h])
            ph.dma("sp", dr["hT"][f * 128:(f + 1) * 128, g * 512:(g + 1) * 512], h[:, :], [h], [])
            k += 1
    ph.finish()


def phase_ffn_down(kb, l, last):
    ph = PH(kb, "fd%d" % l)
    S, NT = kb.S, kb.NT
    dr = kb.dram
    c = load_consts(ph)
    gB = ph.sb([128, 1024], F32, "gB")
    bB = ph.sb([128, 1024], F32, "bB")
    ph.dma("sp", gB[:, :], dr["ln_ffn_g"][l, :].partition_broadcast(128), [], [gB])
    ph.dma("sp", bB[:, :], dr["ln_ffn_b"][l, :].partition_broadcast(128), [], [bB])
    wbf = [ph.sb([128, 1024], BF16, "wd%d" % r) for r in range(32)]
    stage = [ph.sb([128, 1024], F32, "stg%d" % k) for k in range(3)]
    load_cast_weight(ph, lambda r: dr["w_down"][l, r * 128:(r + 1) * 128, :], 32, 1024, lambda r: wbf[r], stage)
    hg = [ph.sb([128, 32, 512], BF16, "hg%d" % k) for k in range(2)]
    xr = [ph.sb([128, 1024], F32, "xr%d" % k) for k in range(2)]
    srcs = [ph.sb([128, 1024], F32, "src%d" % k) for k in range(2)]
    bufs = [ln_bufs(ph, k) for k in range(2)]
    for g in range(S // 512):
        h_ = hg[g % 2]
        for q4 in range(4):
            ph.dma(("pool", "sp")[q4 % 2], h_[:, q4 * 8:(q4 + 1) * 8, :],
                   dr["hT"][q4 * 1024:(q4 + 1) * 1024, g * 512:(g + 1) * 512].rearrange("(f p) t -> p f t", p=128),
                   [], [h_])
        for j in range(4):
            i = g * 4 + j
            x_ = xr[i % 2]
            ph.dma("sp", x_[:, :], dr["xres"][i * 128:(i + 1) * 128, :], [], [x_])
            src = srcs[i % 2]
            for hh in range(2):
                pb = ph.ps[(i % 2) * 2 + hh]
                for f in range(32):
                    ph.mm(pb[:, :], h_[:, f, j * 128:(j + 1) * 128], wbf[f][:, hh * 512:(hh + 1) * 512],
                          f == 0, f == 31, [h_, wbf[f]], [pb])
                ph.stt("dve", src[:, hh * 512:(hh + 1) * 512], x_[:, hh * 512:(hh + 1) * 512], ALPHA, pb[:, :],
                       ALU.mult, ALU.add, [x_, pb], [src])
            if last:
                ln_rows(ph, src, i, gB, bB, c["ident_bf"], "out", None, bufs[i % 2], ph.ps[4 + i % 2])
            else:
                ln_rows(ph, src, i, gB, bB, c["ident_bf"], "xres", "xT", bufs[i % 2], ph.ps[4 + i % 2])
    ph.finish()


CONST_SHAPES = {"c_ident": [128, 128], "c_J": [128, 128], "c_causal": [128, 128], "c_oh": [32, 385],
                "c_mask4": [128, 512], "c_maskSL": [128, 128], "c_bones": [128, 128], "c_cm01": [128, 512]}


def build_program(S, debug=()):
    kb = KB(S)
    kb.din("x", [S, 1024])
    kb.din("ln_in_g", [1024])
    kb.din("ln_in_b", [1024])
    kb.din("w_in0", [1024, 3360])
    kb.din("w_in1", [1024, 3392])
    kb.din("mu0", [128, 15])
    kb.din("mu1", [128, 15])
    kb.din("rel_bias", [32, 4])
    for nm in ("lambda_q1", "lambda_k1", "lambda_q2", "lambda_k2"):
        kb.din(nm, [2, 64])
    kb.din("subln_g", [2, 128])
    kb.din("rwp0", [128, 8, 4])
    kb.din("rwp1", [128, 8, 4])
    kb.din("rw_w_up", [2, 64, 512])
    kb.din("rw_a_up", [2, 64, 512])
    kb.din("rw_g_up", [2, 160, 512])
    kb.din("rw_v_up", [1, 32, 512])
    kb.din("w_out", [2, 1024, 1024])
    kb.din("ln_mix_g", [2, 1024])
    kb.din("ln_mix_b", [2, 1024])
    kb.din("w_up", [2, 1024, 4096])
    kb.din("w_down", [2, 4096, 1024])
    kb.din("ln_ffn_g", [2, 1024])
    kb.din("ln_ffn_b", [2, 1024])
    for nm, shp in CONST_SHAPES.items():
        kb.din(nm, shp)
    dbg = lambda n: n in debug
    kb.dscr("xres", [S, 1024], F32, dbg("xres"))
    kb.dscr("xT", [1024, S], BF16, dbg("xT"))
    kb.dscr("qT", [4, 128, S], BF16, dbg("qT"))
    kb.dscr("kT", [4, 128, S], BF16, dbg("kT"))
    kb.dscr("vd", [S, 512], BF16, dbg("vd"))
    kb.dscr("prT", [1920, S], F32, dbg("prT"))
    kb.dscr("vfT", [512, S], F32, dbg("vfT"))
    kb.dscr("lamneg", [1, 2], F32, dbg("lamneg"))
    kb.dscr("EGd", [4, 384], F32, dbg("EGd"))
    kb.dscr("Eb", [4, 2, 128, 128], F32, dbg("Eb"))
    kb.dscr("mixT", [1024, S], BF16, dbg("mixT"))
    kb.dscr("hT", [4096, S], BF16, dbg("hT"))
    kb.dout("out", [S, 1024], F32)
    phase_pre(kb)
    phase_ln0(kb)
    for l in range(2):
        phase_inproj(kb, l)
        phase_attn(kb, l)
        phase_rwkv(kb, l)
        phase_outproj(kb, l)
        phase_ffn_up(kb, l)
        phase_ffn_down(kb, l, last=(l == 1))
    kb.close()
    return kb


def host_inputs(inp, S):
    f = lambda a: np.ascontiguousarray(np.asarray(a, dtype=np.float32))
    m = {}
    m["ln_in_g"] = f(inp["ln_in_g"])
    m["ln_in_b"] = f(inp["ln_in_b"])
    m["w_in0"] = f(inp["w_in_first"])
    m["w_in1"] = f(inp["w_in_rest"][0])
    for l, mu in enumerate((inp["mu_first"], inp["mu_rest"][0])):
        mp = np.zeros(1920, np.float32)
        mp[:mu.shape[0]] = mu
        m["mu%d" % l] = np.ascontiguousarray(mp.reshape(15, 128).T)
    for nm in ("rel_bias", "lambda_q1", "lambda_k1", "lambda_q2", "lambda_k2", "subln_g", "rw_w_up", "rw_a_up",
               "rw_g_up", "rw_v_up", "w_out", "ln_mix_g", "ln_mix_b", "w_up", "w_down", "ln_ffn_g", "ln_ffn_b"):
        m[nm] = f(inp[nm])
    m["rwp0"] = host_rwp(inp, 0)
    m["rwp1"] = host_rwp(inp, 1)
    m.update(host_consts())
    return m


_PROG = {}


def kernel(**inputs):
    x = np.asarray(inputs["x"], dtype=np.float32)
    B, S, _ = x.shape
    if S not in _PROG:
        _PROG[S] = build_program(S)
    kb = _PROG[S]
    shared = host_inputs(inputs, S)
    in_maps = []
    for b in range(B):
        m = dict(shared)
        m["x"] = np.ascontiguousarray(x[b])
        in_maps.append(m)
    res = run_bass_kernel_spmd(kb.nc, in_maps, core_ids=list(range(B)))
    return np.stack([np.asarray(r["out"], dtype=np.float32) for r in res.results], axis=0)
```

```python
from contextlib import ExitStack
import math
import numpy as np
import concourse.bass as bass
import concourse.mybir as mybir
from concourse.bass_utils import run_bass_kernel_spmd

F32 = mybir.dt.float32
BF16 = mybir.dt.bfloat16
AF = mybir.ActivationFunctionType
ALU = mybir.AluOpType
AX = mybir.AxisListType

ENGS = ("pe", "act", "dve", "pool", "sp")
NPOOL = 14


class T:
    __slots__ = ("ap", "name", "w", "r", "psum")

    def __init__(self, ap, name=""):
        self.ap = ap
        self.name = name
        self.w = None
        self.r = []
        self.psum = False

    def __getitem__(self, k):
        return self.ap[k]


class Op:
    __slots__ = ("eng", "fn", "dma", "deps", "signal", "sem", "val", "prewait")

    def __init__(self, eng, fn, dma):
        self.eng = eng
        self.fn = fn
        self.dma = dma
        self.deps = []
        self.signal = False
        self.sem = None
        self.val = 0
        self.prewait = None


class Ctx:
    def __init__(self, nc, stack):
        self.nc = nc
        self.esem = {e: stack.enter_context(nc.semaphore("s_" + e)) for e in ENGS}
        self.dsem = {q: [stack.enter_context(nc.semaphore("d_%s%d" % (q, i))) for i in range(NPOOL)]
                     for q in ("sp", "pool", "act")}
        self.count = {e: 0 for e in ENGS}
        self.dk = {q: 0 for q in ("sp", "pool", "act")}
        self.seen = {e: {} for e in ENGS}
        self.ninst = 0


class Phase:
    def __init__(self, ctx, name):
        self.ctx = ctx
        self.name = name
        self.ops = []
        self.tiles = []

    def tile(self, ap, name=""):
        t = T(ap, name)
        self.tiles.append(t)
        return t

    def add(self, eng, fn, reads=(), writes=(), dma=False):
        op = Op(eng, fn, dma)
        deps = []
        for t in reads:
            if t.w is not None:
                deps.append((t.w, "raw"))
            if t.psum:
                for r in t.r:
                    if r.eng != eng:
                        deps.append((r, "rar"))
        for t in writes:
            if t.w is not None:
                deps.append((t.w, "waw"))
            for r in t.r:
                deps.append((r, "war"))
        seen = set()
        for d, kind in deps:
            if d is op or id(d) in seen:
                continue
            if not d.dma and not op.dma and d.eng == eng:
                if eng == "pe":
                    continue
                if kind == "war" and eng != "pool":
                    continue
            seen.add(id(d))
            op.deps.append(d)
            d.signal = True
        for t in writes:
            t.w = op
            t.r = []
        for t in reads:
            if t.w is not op:
                t.r.append(op)
        self.ops.append(op)
        return op

    def emit(self):
        ctx = self.ctx
        nc = ctx.nc
        fin = Op("sp", None, False)
        lastd = {}
        for o in self.ops:
            if o.dma:
                o.signal = True
        self.ops.append(fin)
        for o in self.ops:
            if o.dma:
                q = o.eng
                k = ctx.dk[q]
                ctx.dk[q] += 1
                o.sem = ctx.dsem[q][k % NPOOL]
                o.val = 16 * (k // NPOOL + 1)
                if o.val > 16:
                    o.prewait = (o.sem, o.val - 16)
            elif o.signal:
                ctx.count[o.eng] += 1
                o.sem = ctx.esem[o.eng]
                o.val = ctx.count[o.eng]
            if o.dma:
                lastd[o.sem.name] = o
        fin.deps = list(lastd.values())
        per = {e: [o for o in self.ops if o.eng == e] for e in ENGS}
        bname = {"pe": "tensor", "act": "scalar", "dve": "vector", "pool": "gpsimd", "sp": "sync"}

        def run(e, engine):
            seen = ctx.seen[e]
            for o in per[e]:
                waits = [(d.sem, d.val) for d in o.deps]
                if o.prewait is not None:
                    waits.append(o.prewait)
                for sem, val in waits:
                    if seen.get(sem.name, 0) < val:
                        engine.wait_ge(sem, val)
                        seen[sem.name] = val
                        ctx.ninst += 1
                if o.fn is None:
                    continue
                inst = o.fn(engine)
                ctx.ninst += 1
                if o.signal:
                    inst.then_inc(o.sem, 16 if o.dma else 1)

        with nc.Block() as block:
            for e in ENGS:
                if per[e]:
                    getattr(block, bname[e])(lambda engine, e=e: run(e, engine))
        for t in self.tiles:
            t.w = None
            t.r = []


D = 1024
LN_EPS = 1e-5


class KB:
    def __init__(self, S):
        self.S = S
        self.NT = S // 128
        self.nc = bass.Bass("TRN2", target_bir_lowering=False)
        self.stack = ExitStack()
        self.ctx = Ctx(self.nc, self.stack)
        self.dram = {}
        self.psum_h = [self.stack.enter_context(self.nc.psum_tensor("ps%d" % i, [128, 512], F32))
                       for i in range(8)]

    def din(self, name, shape, dt=F32):
        t = self.nc.dram_tensor(name, list(shape), dt, kind="ExternalInput")
        self.dram[name] = t
        return t

    def dout(self, name, shape, dt=F32):
        t = self.nc.dram_tensor(name, list(shape), dt, kind="ExternalOutput")
        self.dram[name] = t
        return t

    def dscr(self, name, shape, dt=F32, debug=False):
        t = self.nc.dram_tensor(name, list(shape), dt, kind="ExternalOutput" if debug else "Internal")
        self.dram[name] = t
        return t

    def close(self):
        self.stack.close()


class PH(Phase):
    def __init__(self, kb, name):
        super().__init__(kb.ctx, name)
        self.kb = kb
        self.nc = kb.nc
        self.st = ExitStack()
        self.ps = [self.tile(h, "ps%d" % i) for i, h in enumerate(kb.psum_h)]
        for t in self.ps:
            t.psum = True
        self.nsb = 0

    def sb(self, shape, dt=F32, name=None):
        self.nsb += 1
        h = self.st.enter_context(self.nc.sbuf_tensor("%s_%s%d" % (self.name, name or "t", self.nsb), list(shape), dt))
        return self.tile(h, name or "t")

    def dt_(self, name):
        key = "_dt_" + name
        if not hasattr(self, key):
            setattr(self, key, self.tile(self.kb.dram[name], name))
        return getattr(self, key)

    def finish(self):
        self.emit()
        self.st.close()

    def dma(self, q, out_ap, in_ap, reads, writes):
        return self.add(q, lambda e: e.dma_start(out=out_ap, in_=in_ap), reads, writes, dma=True)

    def mm(self, out_ap, lhsT, rhs, start, stop, reads, writes, skip=False):
        if skip:
            return self.add("pe", lambda e: e.matmul(out_ap, lhsT, rhs, start=start, stop=stop, skip_group_check=True),
                            reads, writes)
        return self.add("pe", lambda e: e.matmul(out_ap, lhsT, rhs, start=start, stop=stop), reads, writes)

    def tr(self, out_ap, in_ap, ident_ap, reads, writes):
        return self.add("pe", lambda e: e.transpose(out_ap, in_ap, ident_ap), reads, writes)

    def act(self, out_ap, in_ap, func, reads, writes, bias=0.0, scale=1.0, accum_out=None, eng="act"):
        kw = {}
        if accum_out is not None:
            kw["accum_out"] = accum_out
        return self.add(eng, lambda e: e.activation(out_ap, in_ap, func, bias=bias, scale=scale, **kw), reads, writes)

    def tt(self, eng, out_ap, in0, in1, op, reads, writes):
        return self.add(eng, lambda e: e.tensor_tensor(out_ap, in0, in1, op), reads, writes)

    def ts(self, eng, out_ap, in0, s1, s2, op0, op1, reads, writes, accum_out=None):
        if s2 is None:
            return self.add(eng, lambda e: e.tensor_scalar(out_ap, in0, s1, None, op0), reads, writes)
        if accum_out is not None:
            return self.add(eng, lambda e: e.tensor_scalar(out_ap, in0, s1, s2, op0, op1, accum_out=accum_out), reads, writes)
        return self.add(eng, lambda e: e.tensor_scalar(out_ap, in0, s1, s2, op0, op1), reads, writes)

    def stt(self, eng, out_ap, in0, scalar, in1, op0, op1, reads, writes):
        return self.add(eng, lambda e: e.scalar_tensor_tensor(out_ap, in0, scalar, in1, op0, op1), reads, writes)

    def cp(self, eng, out_ap, in_ap, reads, writes):
        if eng == "act":
            return self.add(eng, lambda e: e.copy(out_ap, in_ap), reads, writes)
        return self.add(eng, lambda e: e.tensor_copy(out_ap, in_ap), reads, writes)


def bcast_row(dram_ap_1d, n):
    return dram_ap_1d.partition_broadcast(128)


def ln_rows(ph, src, i, gB, bB, ident_bf, xres_name, xT_name, bufs, psb):
    nc = ph.nc
    st, mv, rs, nm, xn, ybf, xTs = (bufs[k] for k in ("st", "mv", "rs", "nm", "xn", "ybf", "xTs"))
    for h in range(2):
        ph.add("dve", lambda e, h=h: e.bn_stats(st[:, h * 6:(h + 1) * 6], src[:, h * 512:(h + 1) * 512]), [src], [st])
    ph.add("dve", lambda e: e.bn_aggr(mv[:, 0:2], st[:, 0:12]), [st], [mv])
    ph.act(rs[:, 0:1], mv[:, 1:2], AF.Ln, [mv], [rs], bias=bufs["eps"][:, 0:1])
    ph.act(rs[:, 0:1], rs[:, 0:1], AF.Exp, [rs], [rs], scale=-0.5)
    ph.ts("dve", nm[:, 0:1], mv[:, 0:1], rs[:, 0:1], -1.0, ALU.mult, ALU.mult, [mv, rs], [nm])
    ph.act(xn[:, :], src[:, :], AF.Identity, [src, rs, nm], [xn], bias=nm[:, 0:1], scale=rs[:, 0:1])
    ph.tt("dve", xn[:, :], xn[:, :], gB[:, :], ALU.mult, [xn, gB], [xn])
    ph.tt("pool", xn[:, :], xn[:, :], bB[:, :], ALU.add, [xn, bB], [xn])
    xres = ph.kb.dram[xres_name]
    ph.dma("sp", xres[i * 128:(i + 1) * 128, :], xn[:, :], [xn], [])
    if xT_name is None:
        return
    ph.cp("act", ybf[:, :], xn[:, :], [xn], [ybf])
    pb = psb.ap[:, :].bitcast(BF16)
    for c in range(8):
        ph.tr(pb[:, c * 128:(c + 1) * 128], ybf[:, c * 128:(c + 1) * 128], ident_bf[:, :], [ybf, ident_bf], [psb])
    ph.cp("dve", xTs[:, :], pb[:, :], [psb], [xTs])
    xT = ph.kb.dram[xT_name]
    dst = xT[:, i * 128:(i + 1) * 128].rearrange("(c p) t -> p c t", p=128)
    ph.dma("pool", dst, xTs[:, :].rearrange("p (c t) -> p c t", c=8), [xTs], [])


def ln_bufs(ph, k):
    eps = ph.sb([128, 1], F32, "eps%d" % k)
    ph.add("pool", lambda e: e.memset(eps[:, :], LN_EPS), [], [eps])
    return dict(eps=eps, st=ph.sb([128, 12], F32, "st%d" % k), mv=ph.sb([128, 2], F32, "mv%d" % k),
                rs=ph.sb([128, 1], F32, "rs%d" % k), nm=ph.sb([128, 1], F32, "nm%d" % k),
                xn=ph.sb([128, 1024], F32, "xn%d" % k), ybf=ph.sb([128, 1024], BF16, "ybf%d" % k),
                xTs=ph.sb([128, 1024], BF16, "xTs%d" % k))


def load_consts(ph):
    c = {}
    c["ident_bf"] = ph.sb([128, 128], BF16, "identbf")
    c["ident_f"] = ph.sb([128, 128], F32, "identf")
    ph.dma("sp", c["ident_f"][:, :], ph.kb.dram["c_ident"][:, :], [], [c["ident_f"]])
    ph.cp("dve", c["ident_bf"][:, :], c["ident_f"][:, :], [c["ident_f"]], [c["ident_bf"]])
    return c


def phase_ln0(kb):
    ph = PH(kb, "ln0")
    S, NT = kb.S, kb.NT
    c = load_consts(ph)
    gB = ph.sb([128, 1024], F32, "gB")
    bB = ph.sb([128, 1024], F32, "bB")
    ph.dma("sp", gB[:, :], kb.dram["ln_in_g"][:].partition_broadcast(128), [], [gB])
    ph.dma("sp", bB[:, :], kb.dram["ln_in_b"][:].partition_broadcast(128), [], [bB])
    NB = 3
    xin = [ph.sb([128, 1024], F32, "xin%d" % k) for k in range(NB)]
    bufs = [ln_bufs(ph, k) for k in range(2)]
    for i in range(NT):
        src = xin[i % NB]
        ph.dma("sp", src[:, :], kb.dram["x"][i * 128:(i + 1) * 128, :], [], [src])
        ln_rows(ph, src, i, gB, bB, c["ident_bf"], "xres", "xT", bufs[i % 2], ph.ps[i % 2])
    ph.finish()


N_DIFF = 1536
NRW = [1824, 1856]


def phase_inproj(kb, l):
    ph = PH(kb, "ip%d" % l)
    S, NT = kb.S, kb.NT
    NG = S // 512
    ncol = N_DIFF + NRW[l]
    nrw_tiles = 15
    w_dram = kb.dram["w_in%d" % l]
    xT_sb = ph.sb([128, 8, S], BF16, "xT")
    xT = kb.dram["xT"]
    for c in range(8):
        ph.dma("sp" if c % 2 == 0 else "pool", xT_sb[:, c, :], xT[c * 128:(c + 1) * 128, :], [], [xT_sb])
    wbf = [ph.sb([128, ncol], BF16, "wbf%d" % c) for c in range(8)]
    hc = ncol // 2
    wst = [ph.sb([128, hc], F32, "wst%d" % k) for k in range(2)]
    for c in range(8):
        for hh in range(2):
            s = wst[hh]
            ph.dma("act", s[:, :], w_dram[c * 128:(c + 1) * 128, hh * hc:(hh + 1) * hc], [], [s])
            ph.cp(("dve", "pool")[hh], wbf[c][:, hh * hc:(hh + 1) * hc], s[:, :], [s], [wbf[c]])
    mu_sb = ph.sb([128, 15], F32, "mu")
    ph.dma("sp", mu_sb[:, :], kb.dram["mu%d" % l][:, :], [], [mu_sb])
    bank = [0]

    def nextbank():
        b = ph.ps[bank[0] % 8]
        bank[0] += 1
        return b

    ob = [ph.sb([128, 512], BF16, "ob%d" % k) for k in range(4)]
    oi = 0
    for which, name, scale in ((0, "qT", 0.125), (1, "kT", 1.0)):
        for h in range(4):
            col0 = which * 512 + h * 128
            for g in range(NG):
                pb = nextbank()
                for c in range(8):
                    ph.mm(pb[:, :], wbf[c][:, col0:col0 + 128], xT_sb[:, c, g * 512:(g + 1) * 512],
                          c == 0, c == 7, [wbf[c], xT_sb], [pb])
                o = ob[oi % 4]
                oi += 1
                if oi % 2 == 0:
                    ph.act(o[:, :], pb[:, :], AF.Copy, [pb], [o], scale=scale)
                else:
                    ph.ts("dve", o[:, :], pb[:, :], scale, None, ALU.mult, None, [pb], [o])
                ph.dma("sp", kb.dram[name][h, :, g * 512:(g + 1) * 512], o[:, :], [o], [])
    for i in range(NT):
        pb = nextbank()
        for c in range(8):
            ph.mm(pb[:, :], xT_sb[:, c, i * 128:(i + 1) * 128], wbf[c][:, 1024:1536], c == 0, c == 7,
                  [wbf[c], xT_sb], [pb])
        o = ob[oi % 4]
        oi += 1
        if oi % 2 == 0:
            ph.cp("act", o[:, :], pb[:, :], [pb], [o])
        else:
            ph.cp("dve", o[:, :], pb[:, :], [pb], [o])
        ph.dma("sp", kb.dram["vd"][i * 128:(i + 1) * 128, :], o[:, :], [o], [])
    pfull = [ph.sb([128, S + 1], F32, "pfull%d" % k) for k in range(2)]
    tmp = [ph.sb([128, S], F32, "tmp%d" % k) for k in range(1)]
    for k in range(2):
        ph.add("pool", lambda e, k=k: e.memset(pfull[k][:, 0:1], 0.0), [], [pfull[k]])
    for t in range(nrw_tiles):
        col0 = N_DIFF + t * 128
        rows = min(128, ncol - col0)
        pf = pfull[t % 2]
        tm = tmp[0]
        for g in range(NG):
            pb = nextbank()
            for c in range(8):
                ph.mm(pb[0:rows, :], wbf[c][:, col0:col0 + rows], xT_sb[:, c, g * 512:(g + 1) * 512],
                      c == 0, c == 7, [wbf[c], xT_sb], [pb])
            ph.cp("act", pf[0:rows, 1 + g * 512:1 + (g + 1) * 512], pb[0:rows, :], [pb], [pf])
        ph.tt("pool", tm[0:rows, :], pf[0:rows, 0:S], pf[0:rows, 1:S + 1], ALU.subtract, [pf], [tm])
        ph.stt("dve", tm[0:rows, :], tm[0:rows, :], mu_sb[0:rows, t:t + 1], pf[0:rows, 1:S + 1], ALU.mult, ALU.add,
               [tm, mu_sb, pf], [tm])
        ph.dma("sp", kb.dram["prT"][t * 128:t * 128 + rows, :], tm[0:rows, :], [tm], [])
        if l == 0 and 8 <= t < 12:
            ph.dma("pool", kb.dram["vfT"][(t - 8) * 128:(t - 7) * 128, :], tm[0:rows, :], [tm], [])
    ph.finish()


LAM_INIT = [0.8 - 0.6 * math.exp(-0.3 * l) for l in range(2)]
SUBLN_EPS = 1e-5


def phase_pre(kb):
    ph = PH(kb, "pre")
    nc = kb.nc
    dr = kb.dram
    lt = ph.sb([1, 8, 64], F32, "lt")
    for i, nm in enumerate(("lambda_q1", "lambda_k1", "lambda_q2", "lambda_k2")):
        ph.dma("sp", lt[0:1, 2 * i:2 * i + 2, :], dr[nm][:, :].rearrange("(o l) d -> o l d", o=1), [], [lt])
    pr = ph.sb([1, 4, 64], F32, "pr")
    ph.tt("dve", pr[0:1, 0:2, :], lt[0:1, 0:2, :], lt[0:1, 2:4, :], ALU.mult, [lt], [pr])
    ph.tt("dve", pr[0:1, 2:4, :], lt[0:1, 4:6, :], lt[0:1, 6:8, :], ALU.mult, [lt], [pr])
    sm = ph.sb([1, 4], F32, "sm")
    ph.add("dve", lambda e: e.reduce_sum(sm[0:1, 0:4], pr[0:1, :, :], AX.X), [pr], [sm])
    ex = ph.sb([1, 4], F32, "ex")
    ph.act(ex[0:1, :], sm[0:1, :], AF.Exp, [sm], [ex])
    lam = ph.sb([1, 2], F32, "lam")
    ph.tt("dve", lam[0:1, :], ex[0:1, 2:4], ex[0:1, 0:2], ALU.subtract, [ex], [lam])
    for l in range(2):
        ph.ts("dve", lam[0:1, l:l + 1], lam[0:1, l:l + 1], -LAM_INIT[l], None, ALU.add, None, [lam], [lam])
    ph.dma("sp", dr["lamneg"][0:1, :], lam[0:1, :], [lam], [])
    rb = ph.sb([32, 4], F32, "rb")
    oh = ph.sb([32, 385], F32, "oh")
    ph.dma("sp", rb[:, :], dr["rel_bias"][:, :], [], [rb])
    ph.dma("sp", oh[:, :], dr["c_oh"][:, :], [], [oh])
    J = ph.sb([128, 128], F32, "J")
    Mc = ph.sb([128, 128], F32, "Mc")
    ph.dma("pool", J[:, :], dr["c_J"][:, :], [], [J])
    ph.dma("pool", Mc[:, :], dr["c_causal"][:, :], [], [Mc])
    pb = ph.ps[0]
    ph.mm(pb[0:4, 0:385], rb[:, :], oh[:, :], True, True, [rb, oh], [pb])
    G = ph.sb([4, 385], F32, "G")
    ph.cp("dve", G[:, :], pb[0:4, 0:385], [pb], [G])
    EG = ph.sb([4, 384], F32, "EG")
    ph.ts("dve", EG[:, :], G[:, 0:384], G[:, 384:385], None, ALU.subtract, None, [G], [EG])
    ph.act(EG[:, :], EG[:, :], AF.Exp, [EG], [EG])
    egd = ph.dt_("EGd")
    ph.dma("sp", dr["EGd"][:, :], EG[:, :], [EG], [egd])
    for h in range(4):
        for ty in range(2):
            tr = ph.sb([128, 128], F32, "trev")
            src = bass.AP(dr["EGd"], h * 384 + 128 * ty, [[1, 128], [1, 128]])
            ph.dma("sp", tr[:, :], src, [egd], [tr])
            p2 = ph.ps[1 + (h * 2 + ty) % 4]
            ph.mm(p2[:, 0:128], J[:, :], tr[:, :], True, True, [J, tr], [p2])
            eb = ph.sb([128, 128], F32, "eb")
            if ty == 0:
                ph.tt("dve", eb[:, :], p2[:, 0:128], Mc[:, :], ALU.mult, [p2, Mc], [eb])
            else:
                ph.cp("dve", eb[:, :], p2[:, 0:128], [p2], [eb])
            ph.dma("sp", dr["Eb"][h, ty, :, :], eb[:, :], [eb], [])
    ph.finish()


def phase_attn(kb, l):
    ph = PH(kb, "at%d" % l)
    S, NT = kb.S, kb.NT
    NG = S // 512
    dr = kb.dram
    c = load_consts(ph)
    lamneg = ph.sb([128, 2], F32, "lamneg")
    ph.dma("sp", lamneg[:, :], bass.AP(dr["lamneg"], 0, [[0, 128], [1, 2]]), [], [lamneg])
    sg = ph.sb([128, 1], F32, "sg")
    ph.dma("sp", sg[:, :], dr["subln_g"][l:l + 1, :].rearrange("o d -> d o"), [], [sg])
    epsS = ph.sb([128, 1], F32, "epsS")
    ph.add("pool", lambda e: e.memset(epsS[:, :], SUBLN_EPS), [], [epsS])
    Eb = [[ph.sb([128, 128], F32, "Eb") for ty in range(2)] for h in range(2)]
    qT = [ph.sb([128, S], BF16, "qT") for _ in range(2)]
    kT = [ph.sb([128, S], BF16, "kT") for _ in range(2)]
    Vx = [ph.sb([128, NT, 129], BF16, "Vx") for _ in range(2)]
    for k in range(2):
        ph.add("pool", lambda e, k=k: e.memset(Vx[k][:, :, 128:129], 1.0), [], [Vx[k]])
    Pt = [ph.sb([128, 512], BF16, "Pt") for _ in range(4)]
    Pf = [ph.sb([128, 256], F32, "Pf") for _ in range(2)]
    Osb = [ph.sb([128, 512], F32, "Osb") for _ in range(4)]
    small = {k: ph.sb([128, 4], F32, k) for k in ("r1", "r2", "ss", "rstd")}
    t1 = [ph.sb([128, 128], F32, "t1") for _ in range(2)]
    ybf = [ph.sb([128, 128], BF16, "ybf") for _ in range(2)]
    outT = [ph.sb([128, 512], BF16, "outT") for _ in range(2)]
    Obank = [ph.ps[0], ph.ps[1], ph.ps[2], ph.ps[3]]
    Sbank = [ph.ps[4], ph.ps[5], ph.ps[6]]
    Tbank = ph.ps[7]
    pi = [0]
    pfi = [0]
    for h in range(4):
        hb = h % 2
        q_sb, k_sb, v_sb, E = qT[hb], kT[hb], Vx[hb], Eb[hb]
        ph.dma("sp", q_sb[:, :], dr["qT"][h, :, :], [], [q_sb])
        ph.dma("pool", k_sb[:, :], dr["kT"][h, :, :], [], [k_sb])
        ph.dma("sp", v_sb[:, :, 0:128], dr["vd"][:, h * 128:(h + 1) * 128].rearrange("(t p) d -> p t d", p=128),
               [], [v_sb])
        for ty in range(2):
            ph.dma("pool", E[ty][:, :], dr["Eb"][h, ty, :, :], [], [E[ty]])
        for g in range(NG):
            steps = [(kbk, m) for kbk in range(4 * g + 4) for m in range(2)]
            sb_of = {}

            def emit_qk(si):
                kbk, m = steps[si]
                jlo = max(0, kbk - 4 * g)
                n = (4 - jlo) * 128
                sbk = Sbank[si % 3]
                sb_of[si] = sbk
                rows = slice(m * 64, (m + 1) * 64)
                ph.mm(sbk[:, 0:n], k_sb[rows, kbk * 128:(kbk + 1) * 128],
                      q_sb[rows, (4 * g + jlo) * 128:(4 * g + 4) * 128], True, True, [k_sb, q_sb], [sbk])

            def emit_rest(si):
                kbk, m = steps[si]
                jlo = max(0, kbk - 4 * g)
                n = (4 - jlo) * 128
                sbk = sb_of[si]
                P = Pt[pi[0] % 4]
                pi[0] += 1
                near = [(j, 4 * g + j - kbk) for j in range(jlo, 4) if 0 <= 4 * g + j - kbk <= 1]
                nn = len(near)
                if nn:
                    j0 = near[0][0]
                    c0 = (j0 - jlo) * 128
                    pf = Pf[pfi[0] % 2]
                    pfi[0] += 1
                    ph.act(pf[:, 0:nn * 128], sbk[:, c0:c0 + nn * 128], AF.Exp, [sbk], [pf])
                    for ii, (j, ty) in enumerate(near):
                        ph.tt("dve", P[:, c0 + ii * 128:c0 + (ii + 1) * 128], pf[:, ii * 128:(ii + 1) * 128],
                              E[ty][:, :], ALU.mult, [pf, E[ty]], [P])
                    far0 = c0 + nn * 128
                    if far0 < n:
                        ph.act(P[:, far0:n], sbk[:, far0:n], AF.Exp, [sbk], [P])
                else:
                    ph.act(P[:, 0:n], sbk[:, 0:n], AF.Exp, [sbk], [P])
                for j in range(jlo, 4):
                    ob = Obank[m * 2 + j // 2]
                    cc = (j % 2) * 256
                    ph.mm(ob[:, cc:cc + 129], P[:, (j - jlo) * 128:(j - jlo + 1) * 128], v_sb[:, kbk, :],
                          kbk == 0 and j % 2 == 0, kbk == 4 * g + j, [P, v_sb], [ob], skip=True)

            ns = len(steps)
            LA = 2
            for si in range(min(LA, ns)):
                emit_qk(si)
            for si in range(ns):
                if si + LA < ns:
                    emit_qk(si + LA)
                emit_rest(si)
            for b4 in range(4):
                if b4 % 2 == 0:
                    ph.cp("act", Osb[b4][:, :], Obank[b4][:, :], [Obank[b4]], [Osb[b4]])
                else:
                    ph.cp("dve", Osb[b4][:, :], Obank[b4][:, :], [Obank[b4]], [Osb[b4]])
            oT = outT[g % 2]
            tb = Tbank.ap[:, :].bitcast(BF16)
            for j in range(4):
                o0 = Osb[0 + j // 2]
                o1 = Osb[2 + j // 2]
                cc = (j % 2) * 256
                r1, r2, ss, rstd = (small[k] for k in ("r1", "r2", "ss", "rstd"))
                ph.add("dve", lambda e, o0=o0, cc=cc, j=j: e.reciprocal(r1[:, j:j + 1], o0[:, cc + 128:cc + 129]), [o0], [r1])
                ph.add("dve", lambda e, o1=o1, cc=cc, j=j: e.reciprocal(r2[:, j:j + 1], o1[:, cc + 128:cc + 129]), [o1], [r2])
                ph.ts("dve", r2[:, j:j + 1], r2[:, j:j + 1], lamneg[:, l:l + 1], None, ALU.mult, None, [r2, lamneg], [r2])
                tt1 = t1[j % 2]
                ph.ts("dve", tt1[:, :], o0[:, cc:cc + 128], r1[:, j:j + 1], None, ALU.mult, None, [o0, r1], [tt1])
                ph.stt("dve", tt1[:, :], o1[:, cc:cc + 128], r2[:, j:j + 1], tt1[:, :], ALU.mult, ALU.add, [o1, r2, tt1], [tt1])
                yb = ybf[j % 2]
                ph.act(yb[:, :], tt1[:, :], AF.Square, [tt1], [yb, ss], accum_out=ss[:, j:j + 1])
                ph.act(rstd[:, j:j + 1], ss[:, j:j + 1], AF.Ln, [ss], [rstd], bias=epsS[:, 0:1], scale=1.0 / 128)
                ph.act(rstd[:, j:j + 1], rstd[:, j:j + 1], AF.Exp, [rstd], [rstd], scale=-0.5)
                ph.act(yb[:, :], tt1[:, :], AF.Copy, [tt1, rstd], [yb], scale=rstd[:, j:j + 1])
                ph.tr(tb[:, j * 128:(j + 1) * 128], yb[:, :], c["ident_bf"][:, :], [yb, c["ident_bf"]], [Tbank])
            ph.ts("dve", oT[:, :], tb[:, 0:512], sg[:, 0:1], 1.0 - LAM_INIT[l], ALU.mult, ALU.mult, [Tbank, sg], [oT])
            ph.dma("sp", dr["mixT"][h * 128:(h + 1) * 128, g * 512:(g + 1) * 512], oT[:, :], [oT], [])
    ph.finish()


def host_consts():
    c = {}
    c["c_ident"] = np.eye(128, dtype=np.float32)
    c["c_J"] = np.ascontiguousarray(np.eye(128, dtype=np.float32)[::-1])
    kk, qq = np.meshgrid(np.arange(128), np.arange(128), indexing="ij")
    c["c_causal"] = (qq >= kk).astype(np.float32)
    n = np.maximum(np.arange(384) - 127, 0)
    nf = np.maximum(n, 1).astype(np.float32)
    large = 16 + (np.log(nf / np.float32(16)) / np.float32(math.log(128 / 16)) * np.float32(16)).astype(np.int32)
    large = np.minimum(large, 31)
    bucket = np.where(n < 16, n, large)
    oh = np.zeros((32, 385), np.float32)
    oh[bucket, np.arange(384)] = 1.0
    oh[31, 384] = 1.0
    c["c_oh"] = oh
    pp, ff = np.meshgrid(np.arange(128), np.arange(128), indexing="ij")
    SU = (ff > pp).astype(np.float32)
    IU = (ff >= pp).astype(np.float32)
    c["c_mask4"] = np.ascontiguousarray(np.concatenate([SU, IU, SU, IU], axis=1))
    c["c_maskSL"] = (ff < pp).astype(np.float32)
    c["c_bones"] = ((pp // 64) == (ff // 64)).astype(np.float32)
    cm = np.ones((128, 512), np.float32)
    cm[:, ::128] = 0.0
    c["c_cm01"] = cm
    return c


def host_rwp(inp, l):
    vs = [inp["rw_w0"][l], inp["rw_a0"][l], inp["rw_k_k"][l], inp["rw_k_a"][l], inp["rw_r_k"][l].reshape(512),
          inp["rw_gn_g"][l], inp["rw_gn_b"][l], inp["rw_v0"][l - 1] if l > 0 else np.zeros(512, np.float32)]
    a = np.stack([np.asarray(v, np.float32).reshape(4, 128).T for v in vs], axis=1)
    return np.ascontiguousarray(a)


GN_EPS = 64e-5
DECAY_C = -math.exp(-0.5)


def phase_rwkv(kb, l):
    ph = PH(kb, "rw%d" % l)
    S, NT = kb.S, kb.NT
    NTB = S // 512
    dr = kb.dram
    cst = load_consts(ph)
    identbf = cst["ident_bf"]
    mask4 = ph.sb([128, 512], F32, "mask4")
    maskSL = ph.sb([128, 128], F32, "maskSL")
    bones = ph.sb([128, 128], F32, "bones")
    cm01 = ph.sb([128, 512], F32, "cm01")
    ph.dma("sp", mask4[:, :], dr["c_mask4"][:, :], [], [mask4])
    ph.dma("sp", maskSL[:, :], dr["c_maskSL"][:, :], [], [maskSL])
    ph.dma("sp", bones[:, :], dr["c_bones"][:, :], [], [bones])
    ph.dma("sp", cm01[:, :], dr["c_cm01"][:, :], [], [cm01])
    prm = ph.sb([128, 8, 4], F32, "prm")
    ph.dma("sp", prm[:, :, :], dr["rwp%d" % l][:, :, :], [], [prm])
    W0, A0, KK_, KA_, RK_, GG_, GB_, V0_ = range(8)
    WA = ph.sb([128, 512], F32, "WA")
    WG1 = ph.sb([128, 512], F32, "WG1")
    WGV = ph.sb([64, 512], F32, "WGV")
    ph.dma("sp", WA[0:64, :], dr["rw_w_up"][l, :, :], [], [WA])
    ph.dma("sp", WA[64:128, :], dr["rw_a_up"][l, :, :], [], [WA])
    ph.dma("sp", WG1[:, :], dr["rw_g_up"][l, 0:128, :], [], [WG1])
    ph.dma("sp", WGV[0:32, :], dr["rw_g_up"][l, 128:160, :], [], [WGV])
    if l > 0:
        ph.dma("sp", WGV[32:64, :], dr["rw_v_up"][l - 1, :, :], [], [WGV])
    omk = ph.sb([128, 4], F32, "omk")
    ph.ts("dve", omk[:, :], prm[:, KA_, :], -1.0, 1.0, ALU.mult, ALU.add, [prm], [omk])
    epsG = ph.sb([128, 1], F32, "epsG")
    ph.add("pool", lambda e: e.memset(epsG[:, :], GN_EPS), [], [epsG])
    H = ph.sb([128, 4, 64], F32, "H")
    Hb = [ph.sb([128, 64], BF16, "Hb%d" % hp) for hp in range(4)]
    Hs = [ph.sb([128, 64], F32, "Hs%d" % hp) for hp in range(4)]
    for hp in range(4):
        ph.add("pool", lambda e, hp=hp: e.memset(Hs[hp][:, :], 0.0), [], [Hs[hp]])
        ph.add("pool", lambda e, hp=hp: e.memset(Hb[hp][:, :], 0.0), [], [Hb[hp]])
    XWA = ph.sb([128, 512], F32, "XWA")
    XG1 = ph.sb([128, 512], F32, "XG1")
    XG2 = ph.sb([64, 512], F32, "XG2")
    TH = ph.sb([64, 512], F32, "TH")
    SG1 = ph.sb([128, 512], F32, "SG1")
    SG2 = ph.sb([32, 512], F32, "SG2")
    def f32t(n):
        return ph.sb([128, 512], F32, n)

    def bft(n):
        return ph.sb([128, 512], BF16, n)
    NSET = 2
    sets = []
    for k in range(NSET):
        sets.append(dict(
            R=f32t("R"), Kt=f32t("Kt"), Vt=f32t("Vt"), VF=f32t("VF") if l > 0 else None,
            cum=f32t("cum"), epos=f32t("epos"), AR=ph.sb([128, 4, 2, 128], BF16, "AR"),
            BT=bft("BT"), KT=bft("KT"), BH=bft("BH"), KH=bft("KH"), VB=bft("VB"),
            tok=[ph.sb([128, 4, 128], BF16, "tok") for _ in range(4)],
            gsb=f32t("gsb"), bon=f32t("bon"), outT=bft("outT")))
    sgw, lw, tmpc, eprev, eneg, edec, a_, kk, kk2, ssm, rn, kkn, tq, kp, bbt, vp, gate, pr_ = (
        f32t(n) for n in ("sgw", "lw", "tmpc", "eprev", "eneg", "edec", "a", "kk", "kk2", "ssm", "rn", "kkn",
                          "tq", "kp", "bbt", "vp", "gate", "pr"))
    M4 = [ph.sb([128, 512], BF16, "M4") for _ in range(4)]
    PQ = [[ph.sb([128, 256], BF16, "PQ") for _ in range(2)] for _ in range(2)]
    P0 = [ph.sb([128, 128], BF16, "P0") for _ in range(2)]
    TT = [[ph.sb([128, 128], BF16, "TT") for _ in range(2)] for _ in range(2)]
    AkV = [ph.sb([128, 64], BF16, "AkV") for _ in range(2)]
    WTsb = [ph.sb([128, 128], BF16, "WTsb") for _ in range(2)]
    Uloc = [ph.sb([128, 64], F32, "Uloc") for _ in range(2)]
    Usb = [ph.sb([128, 64], BF16, "Usb") for _ in range(2)]
    ynorm = [ph.sb([128, 128], BF16, "ynorm") for _ in range(2)]
    gst = [ph.sb([128, 6], F32, "gst") for _ in range(2)]
    gmv = [ph.sb([128, 2], F32, "gmv") for _ in range(2)]
    grs = [ph.sb([128, 1], F32, "grs") for _ in range(2)]
    gnm = [ph.sb([128, 1], F32, "gnm") for _ in range(2)]
    fin = [ph.sb([128, 128], F32, "fin") for _ in range(2)]
    PA = [ph.ps[0], ph.ps[1]]
    TRB = [ph.ps[2], ph.ps[3]]
    kb_h = kb.psum_h

    def subtiles(bi, bounds):
        return [ph.ps[bi] for a, b in bounds]
    W1 = [subtiles(4 + e, [(0, 64), (64, 128), (128, 192), (192, 256), (256, 320), (320, 384), (384, 512)])
          for e in range(2)]
    W2 = [subtiles(6 + e, [(0, 128), (128, 384), (384, 512)]) for e in range(2)]
    pai = [0]

    def nextpa():
        b = PA[pai[0] % 2]
        pai[0] += 1
        return b

    def c3(ap):
        return ap.rearrange("p (c t) -> p c t", c=4)

    ui = [0]
    for tb in range(NTB):
        tcols = slice(tb * 512, (tb + 1) * 512)
        ph.dma("sp", XWA[:, :], dr["prT"][1536:1664, tcols], [], [XWA])
        ph.dma("pool", XG1[:, :], dr["prT"][1664:1792, tcols], [], [XG1])
        ph.dma("pool", XG2[:, :], dr["prT"][1792:1856, tcols], [], [XG2])
        ph.act(TH[:, :], XWA[0:64, :], AF.Tanh, [XWA], [TH])
        ph.act(SG1[:, :], XG1[:, :], AF.Sigmoid, [XG1], [SG1])
        ph.act(SG2[:, :], XG2[0:32, :], AF.Sigmoid, [XG2], [SG2])
        for hp in range(4):
            s = sets[(tb * 4 + hp) % NSET]
            R, Kt, Vt, VF, cum, epos, AR, BT, KT, BH, KH, VB, tok, gsb, bon, outT = (
                s[k] for k in ("R", "Kt", "Vt", "VF", "cum", "epos", "AR", "BT", "KT", "BH", "KH", "VB", "tok",
                               "gsb", "bon", "outT"))
            hc = slice(hp * 128, (hp + 1) * 128)
            P = lambda i: prm[:, i, hp:hp + 1]
            ph.dma("sp", R[:, :], dr["prT"][hp * 128:(hp + 1) * 128, tcols], [], [R])
            ph.dma("sp", Kt[:, :], dr["prT"][512 + hp * 128:512 + (hp + 1) * 128, tcols], [], [Kt])
            ph.dma("pool", Vt[:, :], dr["prT"][1024 + hp * 128:1024 + (hp + 1) * 128, tcols], [], [Vt])
            if l > 0:
                ph.dma("pool", VF[:, :], dr["vfT"][hp * 128:(hp + 1) * 128, tcols], [], [VF])
            pa = nextpa()
            ph.mm(pa[:, :], WA[0:64, hc], TH[:, :], True, True, [WA, TH], [pa])
            ph.act(sgw[:, :], pa[:, :], AF.Sigmoid, [pa, prm], [sgw], bias=P(W0))
            ph.ts("dve", lw[:, :], sgw[:, :], DECAY_C, None, ALU.mult, None, [sgw], [lw])
            ph.add("dve", lambda e, cum=cum: e.tensor_tensor_scan(cum[:, :], cm01[:, :], lw[:, :], 0.0, ALU.mult, ALU.add),
                   [cm01, lw], [cum])
            ph.act(epos[:, :], cum[:, :], AF.Exp, [cum], [epos])
            ph.tt("pool", tmpc[:, :], cum[:, :], lw[:, :], ALU.subtract, [cum, lw], [tmpc])
            ph.act(eprev[:, :], tmpc[:, :], AF.Exp, [tmpc], [eprev])
            ph.act(eneg[:, :], cum[:, :], AF.Exp, [cum], [eneg], scale=-1.0)
            for c in range(4):
                ph.act(edec[:, c * 128:(c + 1) * 128], cum[:, c * 128:(c + 1) * 128], AF.Exp, [cum], [edec],
                       scale=-1.0, bias=cum[:, c * 128 + 127:c * 128 + 128])
            pa = nextpa()
            ph.mm(pa[:, :], WA[64:128, hc], XWA[64:128, :], True, True, [WA, XWA], [pa])
            ph.act(a_[:, :], pa[:, :], AF.Sigmoid, [pa, prm], [a_], bias=P(A0))
            ph.ts("dve", kk[:, :], Kt[:, :], P(KK_), None, ALU.mult, None, [Kt, prm], [kk])
            ph.tt("pool", kk2[:, :], kk[:, :], kk[:, :], ALU.mult, [kk], [kk2])
            pa = nextpa()
            ph.mm(pa[:, :], bones[:, :], kk2[:, :], True, True, [bones, kk2], [pa])
            ph.ts("dve", ssm[:, :], pa[:, :], 1e-24, None, ALU.max, None, [pa], [ssm])
            ph.act(rn[:, :], ssm[:, :], AF.Ln, [ssm], [rn])
            ph.act(rn[:, :], rn[:, :], AF.Exp, [rn], [rn], scale=-0.5)
            ph.tt("dve", kkn[:, :], kk[:, :], rn[:, :], ALU.mult, [kk, rn], [kkn])
            ph.ts("pool", tq[:, :], a_[:, :], P(KA_), omk[:, hp:hp + 1], ALU.mult, ALU.add, [a_, prm, omk], [tq])
            ph.tt("pool", kp[:, :], tq[:, :], Kt[:, :], ALU.mult, [tq, Kt], [kp])
            ph.stt("dve", AR[:, :, 0, :], c3(kkn[:, :]), -1.0, c3(eprev[:, :]), ALU.mult, ALU.mult, [kkn, eprev], [AR])
            ph.tt("pool", AR[:, :, 1, :], c3(R[:, :]), c3(epos[:, :]), ALU.mult, [R, epos], [AR])
            ph.tt("pool", bbt[:, :], kkn[:, :], a_[:, :], ALU.mult, [kkn, a_], [bbt])
            ph.tt("dve", BT[:, :], bbt[:, :], eneg[:, :], ALU.mult, [bbt, eneg], [BT])
            ph.tt("pool", BH[:, :], bbt[:, :], edec[:, :], ALU.mult, [bbt, edec], [BH])
            ph.tt("dve", KT[:, :], kp[:, :], eneg[:, :], ALU.mult, [kp, eneg], [KT])
            ph.tt("pool", KH[:, :], kp[:, :], edec[:, :], ALU.mult, [kp, edec], [KH])
            if l > 0:
                pa = nextpa()
                ph.mm(pa[:, :], WGV[32:64, hc], XG2[32:64, :], True, True, [WGV, XG2], [pa])
                ph.act(gate[:, :], pa[:, :], AF.Sigmoid, [pa, prm], [gate], bias=P(V0_))
                ph.tt("pool", vp[:, :], VF[:, :], Vt[:, :], ALU.subtract, [VF, Vt], [vp])
                ph.tt("dve", vp[:, :], vp[:, :], gate[:, :], ALU.mult, [vp, gate], [vp])
                ph.tt("pool", vp[:, :], vp[:, :], Vt[:, :], ALU.add, [vp, Vt], [vp])
                vsrc = vp
            else:
                vsrc = Vt
            ph.cp("act", VB[:, :], vsrc[:, :], [vsrc], [VB])
            pa = nextpa()
            ph.mm(pa[:, :], WG1[:, hc], SG1[:, :], True, False, [WG1, SG1], [pa])
            ph.mm(pa[:, :], WGV[0:32, hc], SG2[:, :], False, True, [WGV, SG2], [pa])
            ph.cp("act", gsb[:, :], pa[:, :], [pa], [gsb])
            ph.ts("pool", pr_[:, :], R[:, :], P(RK_), None, ALU.mult, None, [R, prm], [pr_])
            ph.tt("pool", pr_[:, :], pr_[:, :], kp[:, :], ALU.mult, [pr_, kp], [pr_])
            pa = nextpa()
            ph.mm(pa[:, :], bones[:, :], pr_[:, :], True, True, [bones, pr_], [pa])
            ph.tt("dve", bon[:, :], pa[:, :], vsrc[:, :], ALU.mult, [pa, vsrc], [bon])
            for c in range(4):
                trb = TRB[c // 2]
                tv = trb.ap[:, :].bitcast(BF16)
                base = (c % 2) * 512
                cs = slice(c * 128, (c + 1) * 128)
                for qi, (src, sap) in enumerate(((AR, AR[:, c, 0, :]), (BH, BH[:, cs]), (KH, KH[:, cs]), (VB, VB[:, cs]))):
                    ph.tr(tv[:, base + qi * 128:base + (qi + 1) * 128], sap, identbf[:, :], [src, identbf], [trb])
                ev = tv[:, base:base + 512].rearrange("p (q t) -> p q t", q=4)
                if c % 2 == 0:
                    ph.cp("dve", tok[c][:, :, :], ev, [trb], [tok[c]])
                else:
                    ph.cp("act", tok[c][:, :, :], ev, [trb], [tok[c]])
            for c in range(4):
                cs = slice(c * 128, (c + 1) * 128)
                gl = c * 128 + 127
                for e in range(2):
                    rows = slice(e * 64, (e + 1) * 64)
                    w1, w2 = W1[e], W2[e]
                    w1h, w2h = kb_h[4 + e], kb_h[6 + e]
                    m4 = M4[ui[0] % 4]
                    ui[0] += 1
                    ph.mm(w1h[:, 0:256], BT[rows, cs], AR[rows, c, :, :].rearrange("p a t -> p (a t)"), True, True,
                          [BT, AR], [w1[0]])
                    ph.mm(w1h[:, 256:512], KT[rows, cs], AR[rows, c, :, :].rearrange("p a t -> p (a t)"), True, True,
                          [KT, AR], [w1[0]])
                    ph.mm(w2h[:, 0:128], AR[rows, c, 0, :], BT[rows, cs], True, True, [AR, BT], [w2[0]])
                    ph.tt("dve", m4[:, :], w1h[:, 0:512], mask4[:, :], ALU.mult, [w1[0], mask4], [m4])
                    ph.tt("dve", P0[e][:, :], w2h[:, 0:128], maskSL[:, :], ALU.mult, [w2[0], maskSL], [P0[e]])
                    ph.tt("pool", TT[e][0][:, :], m4[:, 0:128], identbf[:, :], ALU.add, [m4, identbf], [TT[e][0]])
                for k in range(6):
                    for e in range(2):
                        w2, w2h = W2[e], kb_h[6 + e]
                        m4 = M4[(ui[0] - 2 + e) % 4]
                        if k == 0:
                            Pk, Qk, Pt_, Qt_ = P0[e][:, :], m4[:, 0:128], P0[e], m4
                        else:
                            pq = PQ[e][k % 2]
                            Pk, Qk, Pt_, Qt_ = pq[:, 0:128], pq[:, 128:256], pq, pq
                        pqn = PQ[e][(k + 1) % 2]
                        ph.mm(w2h[:, 128:256], Qk, Pk, True, True, [Pt_, Qt_], [w2[1]])
                        if k < 5:
                            ph.mm(w2h[:, 256:384], Pk, Qk, True, True, [Pt_, Qt_], [w2[1]])
                            ph.cp("act", pqn[:, 0:256], w2h[:, 128:384], [w2[1]], [pqn])
                        else:
                            ph.cp("act", pqn[:, 0:128], w2h[:, 128:256], [w2[1]], [pqn])
                        ph.mm(w2h[:, 384:512], pqn[:, 0:128], TT[e][k % 2][:, :], True, True, [pqn, TT[e][k % 2]], [w2[2]])
                        ph.tt("dve", TT[e][(k + 1) % 2][:, :], TT[e][k % 2][:, :], w2h[:, 384:512], ALU.add,
                              [TT[e][k % 2], w2[2]], [TT[e][(k + 1) % 2]])
                for e in range(2):
                    rows = slice(e * 64, (e + 1) * 64)
                    w1, w1h = W1[e], kb_h[4 + e]
                    m4 = M4[(ui[0] - 2 + e) % 4]
                    TTf = TT[e][0]
                    Vtok = tok[c][:, 3, rows]
                    ph.mm(w1h[:, 0:64], m4[:, 256:384], Vtok, True, True, [m4, tok[c]], [w1[0]])
                    ph.cp("act", AkV[e][:, :], w1h[:, 0:64], [w1[0]], [AkV[e]])
                    ph.mm(w1h[:, 384:512], tok[c][:, 0, :], TTf[:, :], True, True, [tok[c], TTf], [w1[6]])
                    ph.cp("act", WTsb[e][rows, :], w1h[rows, 384:512], [w1[6]], [WTsb[e]])
                    ph.mm(w1h[:, 64:128], TTf[:, :], AkV[e][:, :], True, True, [TTf, AkV[e]], [w1[1]])
                    ph.cp("act", Uloc[e][:, :], w1h[:, 64:128], [w1[1]], [Uloc[e]])
                    ph.mm(w1h[:, 128:192], WTsb[e][rows, :], Hb[hp][rows, :], True, True, [WTsb[e], Hb[hp]], [w1[2]])
                    ph.tt("dve", Usb[e][:, :], w1h[:, 128:192], Uloc[e][:, :], ALU.add, [w1[2], Uloc[e]], [Usb[e]])
                    ph.mm(w1h[:, 256:320], AR[rows, c, 1, :], Hb[hp][rows, :], True, False, [AR, Hb[hp]], [w1[4]])
                    ph.mm(w1h[:, 256:320], m4[:, 128:256], Usb[e][:, :], False, False, [m4, Usb[e]], [w1[4]])
                    ph.mm(w1h[:, 256:320], m4[:, 384:512], Vtok, False, True, [m4, tok[c]], [w1[4]])
                    ph.mm(w1h[:, 192:256], tok[c][:, 1, :], Usb[e][:, :], True, False, [tok[c], Usb[e]], [w1[3]])
                    ph.mm(w1h[:, 192:256], tok[c][:, 2, :], Vtok, False, True, [tok[c]], [w1[3]])
                    ph.stt("dve", Hs[hp][rows, :], Hs[hp][rows, :], epos[rows, gl:gl + 1], w1h[rows, 192:256],
                           ALU.mult, ALU.add, [Hs[hp], epos, w1[3]], [Hs[hp]])
                    ph.cp("act", Hb[hp][rows, :], Hs[hp][rows, :], [Hs[hp]], [Hb[hp]])
                    yn = ynorm[c % 2]
                    ph.add("dve", lambda en, e=e, w1h=w1h: en.bn_stats(gst[e][:, 0:6], w1h[:, 256:320]), [w1[4]], [gst[e]])
                    ph.add("dve", lambda en, e=e: en.bn_aggr(gmv[e][:, 0:2], gst[e][:, 0:6]), [gst[e]], [gmv[e]])
                    ph.act(grs[e][:, 0:1], gmv[e][:, 1:2], AF.Ln, [gmv[e]], [grs[e]], bias=epsG[:, 0:1])
                    ph.act(grs[e][:, 0:1], grs[e][:, 0:1], AF.Exp, [grs[e]], [grs[e]], scale=-0.5)
                    ph.ts("dve", gnm[e][:, 0:1], gmv[e][:, 0:1], grs[e][:, 0:1], -1.0, ALU.mult, ALU.mult,
                          [gmv[e], grs[e]], [gnm[e]])
                    ph.act(yn[:, e * 64:(e + 1) * 64], w1h[:, 256:320], AF.Identity, [w1[4], grs[e], gnm[e]], [yn],
                           bias=gnm[e][:, 0:1], scale=grs[e][:, 0:1])
                yn = ynorm[c % 2]
                trb = TRB[1]
                tv = trb.ap[:, :].bitcast(BF16)
                ph.tr(tv[:, 0:128], yn[:, :], identbf[:, :], [yn, identbf], [trb])
                fn = fin[c % 2]
                ph.ts("dve", fn[:, :], tv[:, 0:128], P(GG_), P(GB_), ALU.mult, ALU.add, [trb, prm], [fn])
                ph.tt("pool", fn[:, :], fn[:, :], bon[:, cs], ALU.add, [fn, bon], [fn])
                ph.tt("pool", outT[:, cs], fn[:, :], gsb[:, cs], ALU.mult, [fn, gsb], [outT])
            ph.dma("sp", dr["mixT"][512 + hp * 128:512 + (hp + 1) * 128, tcols], outT[:, :], [outT], [])
    ph.finish()


ALPHA = (2 * 2) ** 0.25


def load_cast_weight(ph, w_rows_fn, nrows_tiles, ncols, wbf_fn, stage, q="act"):
    pc = stage[0].ap.shape[1]
    k = 0
    for r in range(nrows_tiles):
        for c0 in range(0, ncols, pc):
            n = min(pc, ncols - c0)
            s = stage[k % len(stage)]
            ph.dma(q if k % 2 == 0 else "sp", s[:, 0:n], w_rows_fn(r)[:, c0:c0 + n], [], [s])
            ph.cp(("dve", "pool", "act")[k % 3], wbf_fn(r)[:, c0:c0 + n], s[:, 0:n], [s], [wbf_fn(r)])
            k += 1


def phase_outproj(kb, l):
    ph = PH(kb, "op%d" % l)
    S, NT = kb.S, kb.NT
    dr = kb.dram
    c = load_consts(ph)
    gB = ph.sb([128, 1024], F32, "gB")
    bB = ph.sb([128, 1024], F32, "bB")
    ph.dma("sp", gB[:, :], dr["ln_mix_g"][l, :].partition_broadcast(128), [], [gB])
    ph.dma("sp", bB[:, :], dr["ln_mix_b"][l, :].partition_broadcast(128), [], [bB])
    wbf = [ph.sb([128, 1024], BF16, "wo%d" % r) for r in range(8)]
    stage = [ph.sb([128, 1024], F32, "stg%d" % k) for k in range(2)]
    load_cast_weight(ph, lambda r: dr["w_out"][l, r * 128:(r + 1) * 128, :], 8, 1024, lambda r: wbf[r], stage)
    mx = [ph.sb([128, 8, 512], BF16, "mx%d" % k) for k in range(2)]
    xr = [ph.sb([128, 1024], F32, "xr%d" % k) for k in range(2)]
    srcs = [ph.sb([128, 1024], F32, "src%d" % k) for k in range(2)]
    bufs = [ln_bufs(ph, k) for k in range(2)]
    for g in range(S // 512):
        m = mx[g % 2]
        ph.dma("pool", m[:, :, :], dr["mixT"][:, g * 512:(g + 1) * 512].rearrange("(c p) t -> p c t", p=128), [], [m])
        for j in range(4):
            i = g * 4 + j
            x_ = xr[i % 2]
            ph.dma("sp", x_[:, :], dr["xres"][i * 128:(i + 1) * 128, :], [], [x_])
            src = srcs[i % 2]
            for hh in range(2):
                pb = ph.ps[(i % 2) * 2 + hh]
                for cc in range(8):
                    ph.mm(pb[:, :], m[:, cc, j * 128:(j + 1) * 128], wbf[cc][:, hh * 512:(hh + 1) * 512],
                          cc == 0, cc == 7, [m, wbf[cc]], [pb])
                ph.stt("dve", src[:, hh * 512:(hh + 1) * 512], x_[:, hh * 512:(hh + 1) * 512], ALPHA, pb[:, :],
                       ALU.mult, ALU.add, [x_, pb], [src])
            ln_rows(ph, src, i, gB, bB, c["ident_bf"], "xres", "xT", bufs[i % 2], ph.ps[4 + i % 2])
    ph.finish()


def phase_ffn_up(kb, l):
    ph = PH(kb, "fu%d" % l)
    S, NT = kb.S, kb.NT
    dr = kb.dram
    wbf = [ph.sb([128, 4096], BF16, "wu%d" % r) for r in range(8)]
    stage = [ph.sb([128, 2048], F32, "stg%d" % k) for k in range(2)]
    load_cast_weight(ph, lambda r: dr["w_up"][l, r * 128:(r + 1) * 128, :], 8, 4096, lambda r: wbf[r], stage)
    xg = [ph.sb([128, 8, 512], BF16, "xg%d" % k) for k in range(2)]
    tmp = [ph.sb([128, 512], F32, "tmp%d" % k) for k in range(3)]
    ho = [ph.sb([128, 512], BF16, "ho%d" % k) for k in range(4)]
    k = 0
    for g in range(S // 512):
        x_ = xg[g % 2]
        ph.dma("pool", x_[:, :, :], dr["xT"][:, g * 512:(g + 1) * 512].rearrange("(c p) t -> p c t", p=128), [], [x_])
        for f in range(32):
            pb = ph.ps[k % 6]
            for cc in range(8):
                ph.mm(pb[:, :], wbf[cc][:, f * 128:(f + 1) * 128], x_[:, cc, :], cc == 0, cc == 7, [wbf[cc], x_], [pb])
            t = tmp[k % 3]
            h = ho[k % 4]
            ph.act(t[:, :], pb[:, :], AF.Relu, [pb], [t])
            ph.tt(("pool", "dve")[k % 2], h[:, :], t[:, :], t[:, :], ALU.mult, [t], [h])
            ph.dma("sp", dr["hT"][f * 128:(f + 1) * 128, g * 512:(g + 1) * 512], h[:, :], [h], [])
            k += 1
    ph.finish()


def phase_ffn_down(kb, l, last):
    ph = PH(kb, "fd%d" % l)
    S, NT = kb.S, kb.NT
    dr = kb.dram
    c = load_consts(ph)
    gB = ph.sb([128, 1024], F32, "gB")
    bB = ph.sb([128, 1024], F32, "bB")
    ph.dma("sp", gB[:, :], dr["ln_ffn_g"][l, :].partition_broadcast(128), [], [gB])
    ph.dma("sp", bB[:, :], dr["ln_ffn_b"][l, :].partition_broadcast(128), [], [bB])
    wbf = [ph.sb([128, 1024], BF16, "wd%d" % r) for r in range(32)]
    stage = [ph.sb([128, 1024], F32, "stg%d" % k) for k in range(3)]
    load_cast_weight(ph, lambda r: dr["w_down"][l, r * 128:(r + 1) * 128, :], 32, 1024, lambda r: wbf[r], stage)
    hg = [ph.sb([128, 32, 512], BF16, "hg%d" % k) for k in range(2)]
    xr = [ph.sb([128, 1024], F32, "xr%d" % k) for k in range(2)]
    srcs = [ph.sb([128, 1024], F32, "src%d" % k) for k in range(2)]
    bufs = [ln_bufs(ph, k) for k in range(2)]
    for g in range(S // 512):
        h_ = hg[g % 2]
        for q4 in range(4):
            ph.dma(("pool", "sp")[q4 % 2], h_[:, q4 * 8:(q4 + 1) * 8, :],
                   dr["hT"][q4 * 1024:(q4 + 1) * 1024, g * 512:(g + 1) * 512].rearrange("(f p) t -> p f t", p=128),
                   [], [h_])
        for j in range(4):
            i = g * 4 + j
            x_ = xr[i % 2]
            ph.dma("sp", x_[:, :], dr["xres"][i * 128:(i + 1) * 128, :], [], [x_])
            src = srcs[i % 2]
            for hh in range(2):
                pb = ph.ps[(i % 2) * 2 + hh]
                for f in range(32):
                    ph.mm(pb[:, :], h_[:, f, j * 128:(j + 1) * 128], wbf[f][:, hh * 512:(hh + 1) * 512],
                          f == 0, f == 31, [h_, wbf[f]], [pb])
                ph.stt("dve", src[:, hh * 512:(hh + 1) * 512], x_[:, hh * 512:(hh + 1) * 512], ALPHA, pb[:, :],
                       ALU.mult, ALU.add, [x_, pb], [src])
            if last:
                ln_rows(ph, src, i, gB, bB, c["ident_bf"], "out", None, bufs[i % 2], ph.ps[4 + i % 2])
            else:
                ln_rows(ph, src, i, gB, bB, c["ident_bf"], "xres", "xT", bufs[i % 2], ph.ps[4 + i % 2])
    ph.finish()


CONST_SHAPES = {"c_ident": [128, 128], "c_J": [128, 128], "c_causal": [128, 128], "c_oh": [32, 385],
                "c_mask4": [128, 512], "c_maskSL": [128, 128], "c_bones": [128, 128], "c_cm01": [128, 512]}


def build_program(S, debug=(), only=None):
    kb = KB(S)
    kb.din("x", [S, 1024])
    kb.din("ln_in_g", [1024])
    kb.din("ln_in_b", [1024])
    kb.din("w_in0", [1024, 3360])
    kb.din("w_in1", [1024, 3392])
    kb.din("mu0", [128, 15])
    kb.din("mu1", [128, 15])
    kb.din("rel_bias", [32, 4])
    for nm in ("lambda_q1", "lambda_k1", "lambda_q2", "lambda_k2"):
        kb.din(nm, [2, 64])
    kb.din("subln_g", [2, 128])
    kb.din("rwp0", [128, 8, 4])
    kb.din("rwp1", [128, 8, 4])
    kb.din("rw_w_up", [2, 64, 512])
    kb.din("rw_a_up", [2, 64, 512])
    kb.din("rw_g_up", [2, 160, 512])
    kb.din("rw_v_up", [1, 32, 512])
    kb.din("w_out", [2, 1024, 1024])
    kb.din("ln_mix_g", [2, 1024])
    kb.din("ln_mix_b", [2, 1024])
    kb.din("w_up", [2, 1024, 4096])
    kb.din("w_down", [2, 4096, 1024])
    kb.din("ln_ffn_g", [2, 1024])
    kb.din("ln_ffn_b", [2, 1024])
    for nm, shp in CONST_SHAPES.items():
        kb.din(nm, shp)
    dbg = lambda n: n in debug
    kb.dscr("xres", [S, 1024], F32, dbg("xres"))
    kb.dscr("xT", [1024, S], BF16, dbg("xT"))
    kb.dscr("qT", [4, 128, S], BF16, dbg("qT"))
    kb.dscr("kT", [4, 128, S], BF16, dbg("kT"))
    kb.dscr("vd", [S, 512], BF16, dbg("vd"))
    kb.dscr("prT", [1920, S], F32, dbg("prT"))
    kb.dscr("vfT", [512, S], F32, dbg("vfT"))
    kb.dscr("lamneg", [1, 2], F32, dbg("lamneg"))
    kb.dscr("EGd", [4, 384], F32, dbg("EGd"))
    kb.dscr("Eb", [4, 2, 128, 128], F32, dbg("Eb"))
    kb.dscr("mixT", [1024, S], BF16, dbg("mixT"))
    kb.dscr("hT", [4096, S], BF16, dbg("hT"))
    kb.dout("out", [S, 1024], F32)
    on = lambda n: only is None or n in only
    if on("pre"):
        phase_pre(kb)
    if on("ln0"):
        phase_ln0(kb)
    for l in range(2):
        if on("ip%d" % l):
            phase_inproj(kb, l)
        if on("at%d" % l):
            phase_attn(kb, l)
        if on("rw%d" % l):
            phase_rwkv2(kb, l)
        if on("op%d" % l):
            phase_outproj(kb, l)
        if on("fu%d" % l):
            phase_ffn_up(kb, l)
        if on("fd%d" % l):
            phase_ffn_down(kb, l, last=(l == 1))
    kb.close()
    return kb


def host_inputs(inp, S):
    f = lambda a: np.ascontiguousarray(np.asarray(a, dtype=np.float32))
    m = {}
    m["ln_in_g"] = f(inp["ln_in_g"])
    m["ln_in_b"] = f(inp["ln_in_b"])
    m["w_in0"] = f(inp["w_in_first"])
    m["w_in1"] = f(inp["w_in_rest"][0])
    for l, mu in enumerate((inp["mu_first"], inp["mu_rest"][0])):
        mp = np.zeros(1920, np.float32)
        mp[:mu.shape[0]] = mu
        m["mu%d" % l] = np.ascontiguousarray(mp.reshape(15, 128).T)
    for nm in ("rel_bias", "lambda_q1", "lambda_k1", "lambda_q2", "lambda_k2", "subln_g", "rw_w_up", "rw_a_up",
               "rw_g_up", "rw_v_up", "w_out", "ln_mix_g", "ln_mix_b", "w_up", "w_down", "ln_ffn_g", "ln_ffn_b"):
        m[nm] = f(inp[nm])
    m["rwp0"] = host_rwp(inp, 0)
    m["rwp1"] = host_rwp(inp, 1)
    m.update(host_consts())
    return m


_PROG = {}


def kernel(**inputs):
    x = np.asarray(inputs["x"], dtype=np.float32)
    B, S, _ = x.shape
    if S not in _PROG:
        _PROG[S] = build_program(S)
    kb = _PROG[S]
    shared = host_inputs(inputs, S)
    in_maps = []
    for b in range(B):
        m = dict(shared)
        m["x"] = np.ascontiguousarray(x[b])
        in_maps.append(m)
    res = run_bass_kernel_spmd(kb.nc, in_maps, core_ids=list(range(B)))
    return np.stack([np.asarray(r["out"], dtype=np.float32) for r in res.results], axis=0)


def phase_rwkv2(kb, l):
    ph = PH(kb, "rw%d" % l)
    S, NT = kb.S, kb.NT
    NTB = S // 512
    dr = kb.dram
    cst = load_consts(ph)
    identbf = cst["ident_bf"]
    mask4 = ph.sb([128, 512], F32, "mask4")
    maskSL = ph.sb([128, 128], F32, "maskSL")
    bones = ph.sb([128, 128], F32, "bones")
    cm01 = ph.sb([128, 512], F32, "cm01")
    ph.dma("sp", mask4[:, :], dr["c_mask4"][:, :], [], [mask4])
    ph.dma("sp", maskSL[:, :], dr["c_maskSL"][:, :], [], [maskSL])
    ph.dma("sp", bones[:, :], dr["c_bones"][:, :], [], [bones])
    ph.dma("sp", cm01[:, :], dr["c_cm01"][:, :], [], [cm01])
    prm = ph.sb([128, 8, 4], F32, "prm")
    ph.dma("sp", prm[:, :, :], dr["rwp%d" % l][:, :, :], [], [prm])
    W0, A0, KK_, KA_, RK_, GG_, GB_, V0_ = range(8)
    WA = ph.sb([128, 512], F32, "WA")
    WG1 = ph.sb([128, 512], F32, "WG1")
    WGV = ph.sb([64, 512], F32, "WGV")
    ph.dma("sp", WA[0:64, :], dr["rw_w_up"][l, :, :], [], [WA])
    ph.dma("sp", WA[64:128, :], dr["rw_a_up"][l, :, :], [], [WA])
    ph.dma("sp", WG1[:, :], dr["rw_g_up"][l, 0:128, :], [], [WG1])
    ph.dma("sp", WGV[0:32, :], dr["rw_g_up"][l, 128:160, :], [], [WGV])
    if l > 0:
        ph.dma("sp", WGV[32:64, :], dr["rw_v_up"][l - 1, :, :], [], [WGV])
    omk = ph.sb([128, 4], F32, "omk")
    ph.ts("dve", omk[:, :], prm[:, KA_, :], -1.0, 1.0, ALU.mult, ALU.add, [prm], [omk])
    epsG = ph.sb([128, 1], F32, "epsG")
    ph.add("pool", lambda e: e.memset(epsG[:, :], GN_EPS), [], [epsG])
    Hb = [ph.sb([128, 64], BF16, "Hb%d" % hp) for hp in range(4)]
    Hs = [ph.sb([128, 64], F32, "Hs%d" % hp) for hp in range(4)]
    for hp in range(4):
        ph.add("pool", lambda e, hp=hp: e.memset(Hs[hp][:, :], 0.0), [], [Hs[hp]])
        ph.add("pool", lambda e, hp=hp: e.memset(Hb[hp][:, :], 0.0), [], [Hb[hp]])
    XWA = ph.sb([128, 512], F32, "XWA")
    XG1 = ph.sb([128, 512], F32, "XG1")
    XG2 = ph.sb([64, 512], F32, "XG2")
    TH = ph.sb([64, 512], F32, "TH")
    SG1 = ph.sb([128, 512], F32, "SG1")
    SG2 = ph.sb([32, 512], F32, "SG2")

    def f32t(n):
        return ph.sb([128, 512], F32, n)

    def bft(n):
        return ph.sb([128, 512], BF16, n)
    pers = []
    for hp in range(4):
        pers.append(dict(gsb=f32t("gsb"), bon=f32t("bon"), AR=ph.sb([128, 4, 2, 128], BF16, "AR"),
                         BT=bft("BT"), KT=bft("KT"), tok=[ph.sb([128, 4, 128], BF16, "tok") for _ in range(4)],
                         outT=bft("outT"), gam=ph.sb([128, 4], F32, "gam"),
                         keepM=[ph.sb([128, 256], BF16, "keepM") for _ in range(8)],
                         Uloc=[ph.sb([128, 64], F32, "Uloc") for _ in range(8)],
                         WTsb=[ph.sb([128, 128], BF16, "WTsb") for _ in range(4)]))
    RIN = [dict(R=f32t("R"), Kt=f32t("Kt"), Vt=f32t("Vt"), VF=f32t("VF") if l > 0 else None) for _ in range(2)]
    tA, tB, tC, tD, tE, tF, tG, tH, tI, tJ, tK = (f32t("tmp%d" % i) for i in range(11))
    BH, KH, VB = bft("BH"), bft("KH"), bft("VB")
    M4 = [ph.sb([128, 512], BF16, "M4") for _ in range(8)]
    P0 = [ph.sb([128, 128], BF16, "P0") for _ in range(8)]
    PQ = [[ph.sb([128, 256], BF16, "PQ") for _ in range(2)] for _ in range(8)]
    SS = [[ph.sb([128, 128], BF16, "SS") for _ in range(2)] for _ in range(8)]
    AkV = [ph.sb([128, 64], BF16, "AkV") for _ in range(8)]
    Usb = [ph.sb([128, 64], BF16, "Usb") for _ in range(8)]
    ynorm = [ph.sb([128, 128], BF16, "ynorm") for _ in range(4)]
    gst = [ph.sb([128, 6], F32, "gst") for _ in range(8)]
    gmv = [ph.sb([128, 2], F32, "gmv") for _ in range(8)]
    grs = [ph.sb([128, 1], F32, "grs") for _ in range(8)]
    gnm = [ph.sb([128, 1], F32, "gnm") for _ in range(8)]
    fin = [ph.sb([128, 128], F32, "fin") for _ in range(4)]
    kb_h = kb.psum_h
    PA = [ph.ps[0], ph.ps[1]]
    pai = [0]

    def nextpa():
        b = PA[pai[0] % 2]
        pai[0] += 1
        return b

    def c3(ap):
        return ap.rearrange("p (c t) -> p c t", c=4)
    evi = [0]

    import os
    EV = os.environ.get("RW2_EV", "")

    def evac_copy(out_ap, in_ap, reads, writes):
        evi[0] += 1
        eng = ("act", "dve")[evi[0] % 2]
        if EV:
            eng = EV
        ph.cp(eng, out_ap, in_ap, reads, writes)

    for tb in range(NTB):
        tcols = slice(tb * 512, (tb + 1) * 512)
        ph.dma("sp", XWA[:, :], dr["prT"][1536:1664, tcols], [], [XWA])
        ph.dma("pool", XG1[:, :], dr["prT"][1664:1792, tcols], [], [XG1])
        ph.dma("pool", XG2[:, :], dr["prT"][1792:1856, tcols], [], [XG2])
        ph.act(TH[:, :], XWA[0:64, :], AF.Tanh, [XWA], [TH])
        ph.act(SG1[:, :], XG1[:, :], AF.Sigmoid, [XG1], [SG1])
        ph.act(SG2[:, :], XG2[0:32, :], AF.Sigmoid, [XG2], [SG2])
        for hp in range(4):
            s = pers[hp]
            rin = RIN[hp % 2]
            R, Kt, Vt, VF = rin["R"], rin["Kt"], rin["Vt"], rin["VF"]
            gsb, bon, AR, BT, KT, tok, gam = (s[k] for k in ("gsb", "bon", "AR", "BT", "KT", "tok", "gam"))
            hc = slice(hp * 128, (hp + 1) * 128)
            P = lambda i: prm[:, i, hp:hp + 1]
            ph.dma("sp", R[:, :], dr["prT"][hp * 128:(hp + 1) * 128, tcols], [], [R])
            ph.dma("sp", Kt[:, :], dr["prT"][512 + hp * 128:512 + (hp + 1) * 128, tcols], [], [Kt])
            ph.dma("pool", Vt[:, :], dr["prT"][1024 + hp * 128:1024 + (hp + 1) * 128, tcols], [], [Vt])
            if l > 0:
                ph.dma("pool", VF[:, :], dr["vfT"][hp * 128:(hp + 1) * 128, tcols], [], [VF])
            sgw, cum, epos, eprev, eneg, edec, a_, kk, kkn, kp, bbt = tA, tB, tC, tD, tE, tF, tG, tH, tI, tJ, tK
            pa = nextpa()
            ph.mm(pa[:, :], WA[0:64, hc], TH[:, :], True, True, [WA, TH], [pa])
            ph.act(sgw[:, :], pa[:, :], AF.Sigmoid, [pa, prm], [sgw], bias=P(W0))
            ph.ts("dve", sgw[:, :], sgw[:, :], DECAY_C, None, ALU.mult, None, [sgw], [sgw])
            ph.add("dve", lambda e, cum=cum, sgw=sgw: e.tensor_tensor_scan(cum[:, :], cm01[:, :], sgw[:, :], 0.0,
                                                                          ALU.mult, ALU.add), [cm01, sgw], [cum])
            ph.act(epos[:, :], cum[:, :], AF.Exp, [cum], [epos])
            ph.tt("pool", eprev[:, :], cum[:, :], sgw[:, :], ALU.subtract, [cum, sgw], [eprev])
            ph.act(eprev[:, :], eprev[:, :], AF.Exp, [eprev], [eprev])
            ph.act(eneg[:, :], cum[:, :], AF.Exp, [cum], [eneg], scale=-1.0)
            for c in range(4):
                ph.act(edec[:, c * 128:(c + 1) * 128], cum[:, c * 128:(c + 1) * 128], AF.Exp, [cum], [edec],
                       scale=-1.0, bias=cum[:, c * 128 + 127:c * 128 + 128])
            ph.cp("pool", gam[:, :], c3(epos[:, :])[:, :, 127], [epos], [gam])
            pa = nextpa()
            ph.mm(pa[:, :], WA[64:128, hc], XWA[64:128, :], True, True, [WA, XWA], [pa])
            ph.act(a_[:, :], pa[:, :], AF.Sigmoid, [pa, prm], [a_], bias=P(A0))
            ph.ts("dve", kk[:, :], Kt[:, :], P(KK_), None, ALU.mult, None, [Kt, prm], [kk])
            ph.tt("pool", sgw[:, :], kk[:, :], kk[:, :], ALU.mult, [kk], [sgw])
            pa = nextpa()
            ph.mm(pa[:, :], bones[:, :], sgw[:, :], True, True, [bones, sgw], [pa])
            ph.ts("dve", sgw[:, :], pa[:, :], 1e-24, None, ALU.max, None, [pa], [sgw])
            ph.act(sgw[:, :], sgw[:, :], AF.Ln, [sgw], [sgw])
            ph.act(sgw[:, :], sgw[:, :], AF.Exp, [sgw], [sgw], scale=-0.5)
            ph.tt("dve", kkn[:, :], kk[:, :], sgw[:, :], ALU.mult, [kk, sgw], [kkn])
            ph.ts("pool", kp[:, :], a_[:, :], P(KA_), omk[:, hp:hp + 1], ALU.mult, ALU.add, [a_, prm, omk], [kp])
            ph.tt("pool", kp[:, :], kp[:, :], Kt[:, :], ALU.mult, [kp, Kt], [kp])
            ph.stt("dve", AR[:, :, 0, :], c3(kkn[:, :]), -1.0, c3(eprev[:, :]), ALU.mult, ALU.mult, [kkn, eprev], [AR])
            ph.tt("pool", AR[:, :, 1, :], c3(R[:, :]), c3(epos[:, :]), ALU.mult, [R, epos], [AR])
            ph.tt("pool", bbt[:, :], kkn[:, :], a_[:, :], ALU.mult, [kkn, a_], [bbt])
            ph.tt("dve", BT[:, :], bbt[:, :], eneg[:, :], ALU.mult, [bbt, eneg], [BT])
            ph.tt("pool", BH[:, :], bbt[:, :], edec[:, :], ALU.mult, [bbt, edec], [BH])
            ph.tt("dve", KT[:, :], kp[:, :], eneg[:, :], ALU.mult, [kp, eneg], [KT])
            ph.tt("pool", KH[:, :], kp[:, :], edec[:, :], ALU.mult, [kp, edec], [KH])
            if l > 0:
                pa = nextpa()
                ph.mm(pa[:, :], WGV[32:64, hc], XG2[32:64, :], True, True, [WGV, XG2], [pa])
                ph.act(a_[:, :], pa[:, :], AF.Sigmoid, [pa, prm], [a_], bias=P(V0_))
                ph.tt("pool", kk[:, :], VF[:, :], Vt[:, :], ALU.subtract, [VF, Vt], [kk])
                ph.tt("dve", kk[:, :], kk[:, :], a_[:, :], ALU.mult, [kk, a_], [kk])
                ph.tt("pool", kk[:, :], kk[:, :], Vt[:, :], ALU.add, [kk, Vt], [kk])
                vsrc = kk
            else:
                vsrc = Vt
            ph.cp("act", VB[:, :], vsrc[:, :], [vsrc], [VB])
            pa = nextpa()
            ph.mm(pa[:, :], WG1[:, hc], SG1[:, :], True, False, [WG1, SG1], [pa])
            ph.mm(pa[:, :], WGV[0:32, hc], SG2[:, :], False, True, [WGV, SG2], [pa])
            ph.cp("act", gsb[:, :], pa[:, :], [pa], [gsb])
            ph.ts("pool", bbt[:, :], R[:, :], P(RK_), None, ALU.mult, None, [R, prm], [bbt])
            ph.tt("pool", bbt[:, :], bbt[:, :], kp[:, :], ALU.mult, [bbt, kp], [bbt])
            pa = nextpa()
            ph.mm(pa[:, :], bones[:, :], bbt[:, :], True, True, [bones, bbt], [pa])
            ph.tt("dve", bon[:, :], pa[:, :], vsrc[:, :], ALU.mult, [pa, vsrc], [bon])
            for c in range(4):
                trb = ph.ps[6 + c // 2]
                tv = trb.ap[:, :].bitcast(BF16)
                base = (c % 2) * 512
                cs = slice(c * 128, (c + 1) * 128)
                for qi, (src, sap) in enumerate(((AR, AR[:, c, 0, :]), (BH, BH[:, cs]), (KH, KH[:, cs]), (VB, VB[:, cs]))):
                    ph.tr(tv[:, base + qi * 128:base + (qi + 1) * 128], sap, identbf[:, :], [src, identbf], [trb])
                ev = tv[:, base:base + 512].rearrange("p (q t) -> p q t", q=4)
                evac_copy(tok[c][:, :, :], ev, [trb], [tok[c]])
        import os
        STOP = int(os.environ.get("RW2_STOP", "99"))
        if STOP <= 1:
            continue
        for hp in range(4):
            s = pers[hp]
            AR, BT, KT, tok, keepM, Uloc, WTsb = (s[k] for k in ("AR", "BT", "KT", "tok", "keepM", "Uloc", "WTsb"))
            units = [(c, e) for c in range(4) for e in range(2)]
            for sw in range(4):
                for ui2 in range(2):
                    u = sw * 2 + ui2
                    c, e = units[u]
                    rows = slice(e * 64, (e + 1) * 64)
                    cs = slice(c * 128, (c + 1) * 128)
                    gb = ph.ps[4 + ui2]
                    g3 = ph.ps[2 + ui2]
                    arr = AR[rows, c, :, :].rearrange("p a t -> p (a t)")
                    ph.mm(gb[:, 0:256], BT[rows, cs], arr, True, True, [BT, AR], [gb])
                    ph.mm(gb[:, 256:512], KT[rows, cs], arr, True, True, [KT, AR], [gb])
                    ph.mm(g3[:, 0:128], AR[rows, c, 0, :], BT[rows, cs], True, True, [AR, BT], [g3])
                for ui2 in range(2):
                    u = sw * 2 + ui2
                    gb = ph.ps[4 + ui2]
                    g3 = ph.ps[2 + ui2]
                    ph.tt("dve", M4[u][:, :], gb[:, 0:512], mask4[:, :], ALU.mult, [gb, mask4], [M4[u]])
                    ph.tt("dve", P0[u][:, :], g3[:, 0:128], maskSL[:, :], ALU.mult, [g3, maskSL], [P0[u]])
                    ph.tt("pool", SS[u][1][:, :], M4[u][:, 0:128], identbf[:, :], ALU.add, [M4[u], identbf], [SS[u][1]])
                    ph.cp("pool", keepM[u][:, 0:128], M4[u][:, 128:256], [M4[u]], [keepM[u]])
                    ph.cp("pool", keepM[u][:, 128:256], M4[u][:, 384:512], [M4[u]], [keepM[u]])
            if STOP <= 2:
                continue
            for k in range(7):
                for u in range(8):
                    pqb = ph.ps[2 + u // 2]
                    po = (u % 2) * 256
                    ssb = ph.ps[u // 4]
                    so = (u % 4) * 128
                    if k == 0:
                        Pk, Qk, Pt_, Qt_ = P0[u][:, :], M4[u][:, 0:128], P0[u], M4[u]
                    else:
                        pq = PQ[u][k % 2]
                        Pk, Qk, Pt_, Qt_ = pq[:, 0:128], pq[:, 128:256], pq, pq
                    if k < 6:
                        ph.mm(pqb[:, po:po + 128], Qk, Pk, True, True, [Pt_, Qt_], [pqb])
                        if k < 5:
                            ph.mm(pqb[:, po + 128:po + 256], Pk, Qk, True, True, [Pt_, Qt_], [pqb])
                    if k >= 1:
                        ssk = SS[u][k % 2]
                        ph.mm(ssb[:, so:so + 128], Pk, ssk[:, :], True, False, [Pt_, ssk], [ssb])
                        ph.mm(ssb[:, so:so + 128], identbf[:, :], ssk[:, :], False, True, [identbf, ssk], [ssb])
                for u in range(8):
                    pqb = ph.ps[2 + u // 2]
                    po = (u % 2) * 256
                    ssb = ph.ps[u // 4]
                    so = (u % 4) * 128
                    if k < 6:
                        pqn = PQ[u][(k + 1) % 2]
                        n = 256 if k < 5 else 128
                        evac_copy(pqn[:, 0:n], pqb[:, po:po + n], [pqb], [pqn])
                    if k >= 1:
                        evac_copy(SS[u][(k + 1) % 2][:, :], ssb[:, so:so + 128], [ssb], [SS[u][(k + 1) % 2]])
            if STOP <= 3:
                continue
            for u in range(8):
                c, e = units[u]
                rows = slice(e * 64, (e + 1) * 64)
                TTf = SS[u][1]
                Vtok = tok[c][:, 3, rows]
                ph.mm(ph.ps[2][:, u * 64:(u + 1) * 64], M4[u][:, 256:384], Vtok, True, True, [M4[u], tok[c]], [ph.ps[2]])
                wb = ph.ps[3 + u // 4]
                ph.mm(wb[:, (u % 4) * 128:(u % 4 + 1) * 128], tok[c][:, 0, :], TTf[:, :], True, True, [tok[c], TTf], [wb])
            for u in range(8):
                c, e = units[u]
                rows = slice(e * 64, (e + 1) * 64)
                evac_copy(AkV[u][:, :], ph.ps[2][:, u * 64:(u + 1) * 64], [ph.ps[2]], [AkV[u]])
                wb = ph.ps[3 + u // 4]
                evac_copy(WTsb[c][rows, :], wb[rows, (u % 4) * 128:(u % 4 + 1) * 128], [wb], [WTsb[c]])
            for u in range(8):
                ph.mm(ph.ps[5][:, u * 64:(u + 1) * 64], SS[u][1][:, :], AkV[u][:, :], True, True, [SS[u][1], AkV[u]], [ph.ps[5]])
            for u in range(8):
                evac_copy(Uloc[u][:, :], ph.ps[5][:, u * 64:(u + 1) * 64], [ph.ps[5]], [Uloc[u]])
        for c in range(4 if STOP > 4 else 0):
            cs = slice(c * 128, (c + 1) * 128)
            HE = [(hp, e) for hp in range(4) for e in range(2)]

            def reg(hp, e, j):
                return ph.ps[2 + e * 2 + (hp % 2)], (hp // 2) * 192 + j * 64
            for hp, e in HE:
                s = pers[hp]
                rows = slice(e * 64, (e + 1) * 64)
                b, o = reg(hp, e, 0)
                ph.mm(b[:, o:o + 64], s["WTsb"][c][rows, :], Hb[hp][rows, :], True, True, [s["WTsb"][c], Hb[hp]], [b])
            for hp, e in HE:
                s = pers[hp]
                b, o = reg(hp, e, 0)
                us = Usb[hp * 2 + e]
                ph.tt("dve", us[:, :], b[:, o:o + 64], s["Uloc"][c * 2 + e][:, :], ALU.add, [b, s["Uloc"][c * 2 + e]], [us])
            if STOP <= 5:
                continue
            for hp, e in HE:
                s = pers[hp]
                rows = slice(e * 64, (e + 1) * 64)
                us = Usb[hp * 2 + e]
                km = s["keepM"][c * 2 + e]
                tokc = s["tok"][c]
                Vtok = tokc[:, 3, rows]
                b, o = reg(hp, e, 1)
                ph.mm(b[:, o:o + 64], s["AR"][rows, c, 1, :], Hb[hp][rows, :], True, False, [s["AR"], Hb[hp]], [b])
                ph.mm(b[:, o:o + 64], km[:, 0:128], us[:, :], False, False, [km, us], [b])
                ph.mm(b[:, o:o + 64], km[:, 128:256], Vtok, False, True, [km, tokc], [b])
                b, o = reg(hp, e, 2)
                ph.mm(b[:, o:o + 64], tokc[:, 1, :], us[:, :], True, False, [tokc, us], [b])
                ph.mm(b[:, o:o + 64], tokc[:, 2, :], Vtok, False, True, [tokc], [b])
            if STOP <= 6:
                continue
            for hp, e in HE:
                s = pers[hp]
                rows = slice(e * 64, (e + 1) * 64)
                b, o = reg(hp, e, 2)
                ph.stt("dve", Hs[hp][rows, :], Hs[hp][rows, :], s["gam"][rows, c:c + 1], b[rows, o:o + 64],
                       ALU.mult, ALU.add, [Hs[hp], s["gam"], b], [Hs[hp]])
                ph.cp("act", Hb[hp][rows, :], Hs[hp][rows, :], [Hs[hp]], [Hb[hp]])
            if STOP <= 7:
                continue
            for hp, e in HE:
                i8 = hp * 2 + e
                b, o = reg(hp, e, 1)
                ph.add("dve", lambda en, i8=i8, b=b, o=o: en.bn_stats(gst[i8][:, 0:6], b[:, o:o + 64]), [b], [gst[i8]])
                ph.add("dve", lambda en, i8=i8: en.bn_aggr(gmv[i8][:, 0:2], gst[i8][:, 0:6]), [gst[i8]], [gmv[i8]])
            for hp, e in HE:
                i8 = hp * 2 + e
                ph.act(grs[i8][:, 0:1], gmv[i8][:, 1:2], AF.Ln, [gmv[i8]], [grs[i8]], bias=epsG[:, 0:1])
            for hp, e in HE:
                i8 = hp * 2 + e
                ph.act(grs[i8][:, 0:1], grs[i8][:, 0:1], AF.Exp, [grs[i8]], [grs[i8]], scale=-0.5)
            for hp, e in HE:
                i8 = hp * 2 + e
                ph.ts("dve", gnm[i8][:, 0:1], gmv[i8][:, 0:1], grs[i8][:, 0:1], -1.0, ALU.mult, ALU.mult,
                      [gmv[i8], grs[i8]], [gnm[i8]])
            for hp, e in HE:
                i8 = hp * 2 + e
                b, o = reg(hp, e, 1)
                ph.act(ynorm[hp][:, e * 64:(e + 1) * 64], b[:, o:o + 64], AF.Identity, [b, grs[i8], gnm[i8]], [ynorm[hp]],
                       bias=gnm[i8][:, 0:1], scale=grs[i8][:, 0:1])
            if STOP <= 8:
                continue
            tv = ph.ps[6].ap[:, :].bitcast(BF16)
            for hp in range(4):
                ph.tr(tv[:, hp * 128:(hp + 1) * 128], ynorm[hp][:, :], identbf[:, :], [ynorm[hp], identbf], [ph.ps[6]])
            for hp in range(4):
                s = pers[hp]
                P = lambda i: prm[:, i, hp:hp + 1]
                ph.ts("dve", fin[hp][:, :], tv[:, hp * 128:(hp + 1) * 128], P(GG_), P(GB_), ALU.mult, ALU.add,
                      [ph.ps[6], prm], [fin[hp]])
                ph.tt("pool", fin[hp][:, :], fin[hp][:, :], s["bon"][:, cs], ALU.add, [fin[hp], s["bon"]], [fin[hp]])
                ph.tt("pool", s["outT"][:, cs], fin[hp][:, :], s["gsb"][:, cs], ALU.mult, [fin[hp], s["gsb"]], [s["outT"]])
        for hp in range(4):
            ph.dma("sp", dr["mixT"][512 + hp * 128:512 + (hp + 1) * 128, tcols], pers[hp]["outT"][:, :], [pers[hp]["outT"]], [])
    ph.finish()
```

```python
from contextlib import ExitStack
import math
import numpy as np
import concourse.bass as bass
import concourse.mybir as mybir
from concourse.bass_utils import run_bass_kernel_spmd

F32 = mybir.dt.float32
BF16 = mybir.dt.bfloat16
AF = mybir.ActivationFunctionType
ALU = mybir.AluOpType
AX = mybir.AxisListType

ENGS = ("pe", "act", "dve", "pool", "sp")
NPOOL = 14


class T:
    __slots__ = ("ap", "name", "w", "r", "psum")

    def __init__(self, ap, name=""):
        self.ap = ap
        self.name = name
        self.w = None
        self.r = []
        self.psum = False

    def __getitem__(self, k):
        return self.ap[k]


class Op:
    __slots__ = ("eng", "fn", "dma", "deps", "signal", "sem", "val", "prewait")

    def __init__(self, eng, fn, dma):
        self.eng = eng
        self.fn = fn
        self.dma = dma
        self.deps = []
        self.signal = False
        self.sem = None
        self.val = 0
        self.prewait = None


class Ctx:
    def __init__(self, nc, stack):
        self.nc = nc
        self.esem = {e: stack.enter_context(nc.semaphore("s_" + e)) for e in ENGS}
        self.dsem = {q: [stack.enter_context(nc.semaphore("d_%s%d" % (q, i))) for i in range(NPOOL)]
                     for q in ("sp", "pool", "act")}
        self.count = {e: 0 for e in ENGS}
        self.dk = {q: 0 for q in ("sp", "pool", "act")}
        self.seen = {e: {} for e in ENGS}
        self.ninst = 0


class Phase:
    def __init__(self, ctx, name):
        self.ctx = ctx
        self.name = name
        self.ops = []
        self.tiles = []

    def tile(self, ap, name=""):
        t = T(ap, name)
        self.tiles.append(t)
        return t

    def add(self, eng, fn, reads=(), writes=(), dma=False):
        op = Op(eng, fn, dma)
        deps = []
        for t in reads:
            if t.w is not None:
                deps.append((t.w, "raw"))
            if t.psum:
                for r in t.r:
                    if r.eng != eng:
                        deps.append((r, "rar"))
        for t in writes:
            if t.w is not None:
                deps.append((t.w, "waw"))
            for r in t.r:
                deps.append((r, "war"))
        seen = set()
        for d, kind in deps:
            if d is op or id(d) in seen:
                continue
            if not d.dma and not op.dma and d.eng == eng:
                if eng == "pe":
                    continue
                if kind == "war" and eng != "pool":
                    continue
            seen.add(id(d))
            op.deps.append(d)
            d.signal = True
        for t in writes:
            t.w = op
            t.r = []
        for t in reads:
            if t.w is not op:
                t.r.append(op)
        self.ops.append(op)
        return op

    def emit(self):
        ctx = self.ctx
        nc = ctx.nc
        fin = Op("sp", None, False)
        lastd = {}
        for o in self.ops:
            if o.dma:
                o.signal = True
        self.ops.append(fin)
        for o in self.ops:
            if o.dma:
                q = o.eng
                k = ctx.dk[q]
                ctx.dk[q] += 1
                o.sem = ctx.dsem[q][k % NPOOL]
                o.val = 16 * (k // NPOOL + 1)
                if o.val > 16:
                    o.prewait = (o.sem, o.val - 16)
            elif o.signal:
                ctx.count[o.eng] += 1
                o.sem = ctx.esem[o.eng]
                o.val = ctx.count[o.eng]
            if o.dma:
                lastd[o.sem.name] = o
        fin.deps = list(lastd.values())
        per = {e: [o for o in self.ops if o.eng == e] for e in ENGS}
        bname = {"pe": "tensor", "act": "scalar", "dve": "vector", "pool": "gpsimd", "sp": "sync"}

        def run(e, engine):
            seen = ctx.seen[e]
            for o in per[e]:
                waits = [(d.sem, d.val) for d in o.deps]
                if o.prewait is not None:
                    waits.append(o.prewait)
                for sem, val in waits:
                    if seen.get(sem.name, 0) < val:
                        engine.wait_ge(sem, val)
                        seen[sem.name] = val
                        ctx.ninst += 1
                if o.fn is None:
                    continue
                inst = o.fn(engine)
                ctx.ninst += 1
                if o.signal:
                    inst.then_inc(o.sem, 16 if o.dma else 1)

        with nc.Block() as block:
            for e in ENGS:
                if per[e]:
                    getattr(block, bname[e])(lambda engine, e=e: run(e, engine))
        for t in self.tiles:
            t.w = None
            t.r = []


D = 1024
LN_EPS = 1e-5


class KB:
    def __init__(self, S):
        self.S = S
        self.NT = S // 128
        self.nc = bass.Bass("TRN2", target_bir_lowering=False)
        self.stack = ExitStack()
        self.ctx = Ctx(self.nc, self.stack)
        self.dram = {}
        self.psum_h = [self.stack.enter_context(self.nc.psum_tensor("ps%d" % i, [128, 512], F32))
                       for i in range(8)]

    def din(self, name, shape, dt=F32):
        t = self.nc.dram_tensor(name, list(shape), dt, kind="ExternalInput")
        self.dram[name] = t
        return t

    def dout(self, name, shape, dt=F32):
        t = self.nc.dram_tensor(name, list(shape), dt, kind="ExternalOutput")
        self.dram[name] = t
        return t

    def dscr(self, name, shape, dt=F32, debug=False):
        t = self.nc.dram_tensor(name, list(shape), dt, kind="ExternalOutput" if debug else "Internal")
        self.dram[name] = t
        return t

    def close(self):
        self.stack.close()


class PH(Phase):
    def __init__(self, kb, name):
        super().__init__(kb.ctx, name)
        self.kb = kb
        self.nc = kb.nc
        self.st = ExitStack()
        self.ps = [self.tile(h, "ps%d" % i) for i, h in enumerate(kb.psum_h)]
        for t in self.ps:
            t.psum = True
        self.nsb = 0

    def sb(self, shape, dt=F32, name=None):
        self.nsb += 1
        h = self.st.enter_context(self.nc.sbuf_tensor("%s_%s%d" % (self.name, name or "t", self.nsb), list(shape), dt))
        return self.tile(h, name or "t")

    def dt_(self, name):
        key = "_dt_" + name
        if not hasattr(self, key):
            setattr(self, key, self.tile(self.kb.dram[name], name))
        return getattr(self, key)

    def finish(self):
        self.emit()
        self.st.close()

    def dma(self, q, out_ap, in_ap, reads, writes):
        return self.add(q, lambda e: e.dma_start(out=out_ap, in_=in_ap), reads, writes, dma=True)

    def mm(self, out_ap, lhsT, rhs, start, stop, reads, writes, skip=False):
        if skip:
            return self.add("pe", lambda e: e.matmul(out_ap, lhsT, rhs, start=start, stop=stop, skip_group_check=True),
                            reads, writes)
        return self.add("pe", lambda e: e.matmul(out_ap, lhsT, rhs, start=start, stop=stop), reads, writes)

    def tr(self, out_ap, in_ap, ident_ap, reads, writes):
        return self.add("pe", lambda e: e.transpose(out_ap, in_ap, ident_ap), reads, writes)

    def act(self, out_ap, in_ap, func, reads, writes, bias=0.0, scale=1.0, accum_out=None, eng="act"):
        kw = {}
        if accum_out is not None:
            kw["accum_out"] = accum_out
        return self.add(eng, lambda e: e.activation(out_ap, in_ap, func, bias=bias, scale=scale, **kw), reads, writes)

    def tt(self, eng, out_ap, in0, in1, op, reads, writes):
        return self.add(eng, lambda e: e.tensor_tensor(out_ap, in0, in1, op), reads, writes)

    def ts(self, eng, out_ap, in0, s1, s2, op0, op1, reads, writes, accum_out=None):
        if s2 is None:
            return self.add(eng, lambda e: e.tensor_scalar(out_ap, in0, s1, None, op0), reads, writes)
        if accum_out is not None:
            return self.add(eng, lambda e: e.tensor_scalar(out_ap, in0, s1, s2, op0, op1, accum_out=accum_out), reads, writes)
        return self.add(eng, lambda e: e.tensor_scalar(out_ap, in0, s1, s2, op0, op1), reads, writes)

    def stt(self, eng, out_ap, in0, scalar, in1, op0, op1, reads, writes):
        return self.add(eng, lambda e: e.scalar_tensor_tensor(out_ap, in0, scalar, in1, op0, op1), reads, writes)

    def cp(self, eng, out_ap, in_ap, reads, writes):
        if eng == "act":
            return self.add(eng, lambda e: e.copy(out_ap, in_ap), reads, writes)
        return self.add(eng, lambda e: e.tensor_copy(out_ap, in_ap), reads, writes)


def bcast_row(dram_ap_1d, n):
    return dram_ap_1d.partition_broadcast(128)


def ln_rows(ph, src, i, gB, bB, ident_bf, xres_name, xT_name, bufs, psb):
    nc = ph.nc
    st, mv, rs, nm, xn, ybf, xTs = (bufs[k] for k in ("st", "mv", "rs", "nm", "xn", "ybf", "xTs"))
    for h in range(2):
        ph.add("dve", lambda e, h=h: e.bn_stats(st[:, h * 6:(h + 1) * 6], src[:, h * 512:(h + 1) * 512]), [src], [st])
    ph.add("dve", lambda e: e.bn_aggr(mv[:, 0:2], st[:, 0:12]), [st], [mv])
    ph.act(rs[:, 0:1], mv[:, 1:2], AF.Ln, [mv], [rs], bias=bufs["eps"][:, 0:1])
    ph.act(rs[:, 0:1], rs[:, 0:1], AF.Exp, [rs], [rs], scale=-0.5)
    ph.ts("dve", nm[:, 0:1], mv[:, 0:1], rs[:, 0:1], -1.0, ALU.mult, ALU.mult, [mv, rs], [nm])
    ph.act(xn[:, :], src[:, :], AF.Identity, [src, rs, nm], [xn], bias=nm[:, 0:1], scale=rs[:, 0:1])
    ph.tt("dve", xn[:, :], xn[:, :], gB[:, :], ALU.mult, [xn, gB], [xn])
    ph.tt("pool", xn[:, :], xn[:, :], bB[:, :], ALU.add, [xn, bB], [xn])
    xres = ph.kb.dram[xres_name]
    ph.dma("sp", xres[i * 128:(i + 1) * 128, :], xn[:, :], [xn], [])
    if xT_name is None:
        return
    ph.cp("act", ybf[:, :], xn[:, :], [xn], [ybf])
    pb = psb.ap[:, :].bitcast(BF16)
    for c in range(8):
        ph.tr(pb[:, c * 128:(c + 1) * 128], ybf[:, c * 128:(c + 1) * 128], ident_bf[:, :], [ybf, ident_bf], [psb])
    ph.cp("dve", xTs[:, :], pb[:, :], [psb], [xTs])
    xT = ph.kb.dram[xT_name]
    dst = xT[:, i * 128:(i + 1) * 128].rearrange("(c p) t -> p c t", p=128)
    ph.dma("pool", dst, xTs[:, :].rearrange("p (c t) -> p c t", c=8), [xTs], [])


def ln_bufs(ph, k):
    eps = ph.sb([128, 1], F32, "eps%d" % k)
    ph.add("pool", lambda e: e.memset(eps[:, :], LN_EPS), [], [eps])
    return dict(eps=eps, st=ph.sb([128, 12], F32, "st%d" % k), mv=ph.sb([128, 2], F32, "mv%d" % k),
                rs=ph.sb([128, 1], F32, "rs%d" % k), nm=ph.sb([128, 1], F32, "nm%d" % k),
                xn=ph.sb([128, 1024], F32, "xn%d" % k), ybf=ph.sb([128, 1024], BF16, "ybf%d" % k),
                xTs=ph.sb([128, 1024], BF16, "xTs%d" % k))


def load_consts(ph):
    c = {}
    c["ident_bf"] = ph.sb([128, 128], BF16, "identbf")
    c["ident_f"] = ph.sb([128, 128], F32, "identf")
    ph.dma("sp", c["ident_f"][:, :], ph.kb.dram["c_ident"][:, :], [], [c["ident_f"]])
    ph.cp("dve", c["ident_bf"][:, :], c["ident_f"][:, :], [c["ident_f"]], [c["ident_bf"]])
    return c


def phase_ln0(kb):
    ph = PH(kb, "ln0")
    S, NT = kb.S, kb.NT
    c = load_consts(ph)
    gB = ph.sb([128, 1024], F32, "gB")
    bB = ph.sb([128, 1024], F32, "bB")
    ph.dma("sp", gB[:, :], kb.dram["ln_in_g"][:].partition_broadcast(128), [], [gB])
    ph.dma("sp", bB[:, :], kb.dram["ln_in_b"][:].partition_broadcast(128), [], [bB])
    NB = 3
    xin = [ph.sb([128, 1024], F32, "xin%d" % k) for k in range(NB)]
    bufs = [ln_bufs(ph, k) for k in range(2)]
    for i in range(NT):
        src = xin[i % NB]
        ph.dma("sp", src[:, :], kb.dram["x"][i * 128:(i + 1) * 128, :], [], [src])
        ln_rows(ph, src, i, gB, bB, c["ident_bf"], "xres", "xT", bufs[i % 2], ph.ps[i % 2])
    ph.finish()


N_DIFF = 1536
NRW = [1824, 1856]


def phase_inproj(kb, l):
    ph = PH(kb, "ip%d" % l)
    S, NT = kb.S, kb.NT
    NG = S // 512
    ncol = N_DIFF + NRW[l]
    nrw_tiles = 15
    w_dram = kb.dram["w_in%d" % l]
    xT_sb = ph.sb([128, 8, S], BF16, "xT")
    xT = kb.dram["xT"]
    for c in range(8):
        ph.dma("sp" if c % 2 == 0 else "pool", xT_sb[:, c, :], xT[c * 128:(c + 1) * 128, :], [], [xT_sb])
    wbf = [ph.sb([128, ncol], BF16, "wbf%d" % c) for c in range(8)]
    hc = ncol // 2
    wst = [ph.sb([128, hc], F32, "wst%d" % k) for k in range(2)]
    for c in range(8):
        for hh in range(2):
            s = wst[hh]
            ph.dma("act", s[:, :], w_dram[c * 128:(c + 1) * 128, hh * hc:(hh + 1) * hc], [], [s])
            ph.cp(("dve", "pool")[hh], wbf[c][:, hh * hc:(hh + 1) * hc], s[:, :], [s], [wbf[c]])
    mu_sb = ph.sb([128, 15], F32, "mu")
    ph.dma("sp", mu_sb[:, :], kb.dram["mu%d" % l][:, :], [], [mu_sb])
    bank = [0]

    def nextbank():
        b = ph.ps[bank[0] % 8]
        bank[0] += 1
        return b

    ob = [ph.sb([128, 512], BF16, "ob%d" % k) for k in range(4)]
    oi = 0
    for which, name, scale in ((0, "qT", 0.125), (1, "kT", 1.0)):
        for h in range(4):
            col0 = which * 512 + h * 128
            for g in range(NG):
                pb = nextbank()
                for c in range(8):
                    ph.mm(pb[:, :], wbf[c][:, col0:col0 + 128], xT_sb[:, c, g * 512:(g + 1) * 512],
                          c == 0, c == 7, [wbf[c], xT_sb], [pb])
                o = ob[oi % 4]
                oi += 1
                if oi % 2 == 0:
                    ph.act(o[:, :], pb[:, :], AF.Copy, [pb], [o], scale=scale)
                else:
                    ph.ts("dve", o[:, :], pb[:, :], scale, None, ALU.mult, None, [pb], [o])
                ph.dma("sp", kb.dram[name][h, :, g * 512:(g + 1) * 512], o[:, :], [o], [])
    for i in range(NT):
        pb = nextbank()
        for c in range(8):
            ph.mm(pb[:, :], xT_sb[:, c, i * 128:(i + 1) * 128], wbf[c][:, 1024:1536], c == 0, c == 7,
                  [wbf[c], xT_sb], [pb])
        o = ob[oi % 4]
        oi += 1
        if oi % 2 == 0:
            ph.cp("act", o[:, :], pb[:, :], [pb], [o])
        else:
            ph.cp("dve", o[:, :], pb[:, :], [pb], [o])
        ph.dma("sp", kb.dram["vd"][i * 128:(i + 1) * 128, :], o[:, :], [o], [])
    pfull = [ph.sb([128, S + 1], F32, "pfull%d" % k) for k in range(2)]
    tmp = [ph.sb([128, S], F32, "tmp%d" % k) for k in range(1)]
    for k in range(2):
        ph.add("pool", lambda e, k=k: e.memset(pfull[k][:, 0:1], 0.0), [], [pfull[k]])
    for t in range(nrw_tiles):
        col0 = N_DIFF + t * 128
        rows = min(128, ncol - col0)
        pf = pfull[t % 2]
        tm = tmp[0]
        for g in range(NG):
            pb = nextbank()
            for c in range(8):
                ph.mm(pb[0:rows, :], wbf[c][:, col0:col0 + rows], xT_sb[:, c, g * 512:(g + 1) * 512],
                      c == 0, c == 7, [wbf[c], xT_sb], [pb])
            ph.cp("act", pf[0:rows, 1 + g * 512:1 + (g + 1) * 512], pb[0:rows, :], [pb], [pf])
        ph.tt("pool", tm[0:rows, :], pf[0:rows, 0:S], pf[0:rows, 1:S + 1], ALU.subtract, [pf], [tm])
        ph.stt("dve", tm[0:rows, :], tm[0:rows, :], mu_sb[0:rows, t:t + 1], pf[0:rows, 1:S + 1], ALU.mult, ALU.add,
               [tm, mu_sb, pf], [tm])
        ph.dma("sp", kb.dram["prT"][t * 128:t * 128 + rows, :], tm[0:rows, :], [tm], [])
        if l == 0 and 8 <= t < 12:
            ph.dma("pool", kb.dram["vfT"][(t - 8) * 128:(t - 7) * 128, :], tm[0:rows, :], [tm], [])
    ph.finish()


LAM_INIT = [0.8 - 0.6 * math.exp(-0.3 * l) for l in range(2)]
SUBLN_EPS = 1e-5


def phase_pre(kb):
    ph = PH(kb, "pre")
    nc = kb.nc
    dr = kb.dram
    lt = ph.sb([1, 8, 64], F32, "lt")
    for i, nm in enumerate(("lambda_q1", "lambda_k1", "lambda_q2", "lambda_k2")):
        ph.dma("sp", lt[0:1, 2 * i:2 * i + 2, :], dr[nm][:, :].rearrange("(o l) d -> o l d", o=1), [], [lt])
    pr = ph.sb([1, 4, 64], F32, "pr")
    ph.tt("dve", pr[0:1, 0:2, :], lt[0:1, 0:2, :], lt[0:1, 2:4, :], ALU.mult, [lt], [pr])
    ph.tt("dve", pr[0:1, 2:4, :], lt[0:1, 4:6, :], lt[0:1, 6:8, :], ALU.mult, [lt], [pr])
    sm = ph.sb([1, 4], F32, "sm")
    ph.add("dve", lambda e: e.reduce_sum(sm[0:1, 0:4], pr[0:1, :, :], AX.X), [pr], [sm])
    ex = ph.sb([1, 4], F32, "ex")
    ph.act(ex[0:1, :], sm[0:1, :], AF.Exp, [sm], [ex])
    lam = ph.sb([1, 2], F32, "lam")
    ph.tt("dve", lam[0:1, :], ex[0:1, 2:4], ex[0:1, 0:2], ALU.subtract, [ex], [lam])
    for l in range(2):
        ph.ts("dve", lam[0:1, l:l + 1], lam[0:1, l:l + 1], -LAM_INIT[l], None, ALU.add, None, [lam], [lam])
    ph.dma("sp", dr["lamneg"][0:1, :], lam[0:1, :], [lam], [])
    rb = ph.sb([32, 4], F32, "rb")
    oh = ph.sb([32, 385], F32, "oh")
    ph.dma("sp", rb[:, :], dr["rel_bias"][:, :], [], [rb])
    ph.dma("sp", oh[:, :], dr["c_oh"][:, :], [], [oh])
    J = ph.sb([128, 128], F32, "J")
    Mc = ph.sb([128, 128], F32, "Mc")
    ph.dma("pool", J[:, :], dr["c_J"][:, :], [], [J])
    ph.dma("pool", Mc[:, :], dr["c_causal"][:, :], [], [Mc])
    pb = ph.ps[0]
    ph.mm(pb[0:4, 0:385], rb[:, :], oh[:, :], True, True, [rb, oh], [pb])
    G = ph.sb([4, 385], F32, "G")
    ph.cp("dve", G[:, :], pb[0:4, 0:385], [pb], [G])
    EG = ph.sb([4, 384], F32, "EG")
    ph.ts("dve", EG[:, :], G[:, 0:384], G[:, 384:385], None, ALU.subtract, None, [G], [EG])
    ph.act(EG[:, :], EG[:, :], AF.Exp, [EG], [EG])
    egd = ph.dt_("EGd")
    ph.dma("sp", dr["EGd"][:, :], EG[:, :], [EG], [egd])
    for h in range(4):
        for ty in range(2):
            tr = ph.sb([128, 128], F32, "trev")
            src = bass.AP(dr["EGd"], h * 384 + 128 * ty, [[1, 128], [1, 128]])
            ph.dma("sp", tr[:, :], src, [egd], [tr])
            p2 = ph.ps[1 + (h * 2 + ty) % 4]
            ph.mm(p2[:, 0:128], J[:, :], tr[:, :], True, True, [J, tr], [p2])
            eb = ph.sb([128, 128], F32, "eb")
            if ty == 0:
                ph.tt("dve", eb[:, :], p2[:, 0:128], Mc[:, :], ALU.mult, [p2, Mc], [eb])
            else:
                ph.cp("dve", eb[:, :], p2[:, 0:128], [p2], [eb])
            ph.dma("sp", dr["Eb"][h, ty, :, :], eb[:, :], [eb], [])
    ph.finish()


def phase_attn(kb, l):
    ph = PH(kb, "at%d" % l)
    S, NT = kb.S, kb.NT
    NG = S // 512
    dr = kb.dram
    c = load_consts(ph)
    lamneg = ph.sb([128, 2], F32, "lamneg")
    ph.dma("sp", lamneg[:, :], bass.AP(dr["lamneg"], 0, [[0, 128], [1, 2]]), [], [lamneg])
    sg = ph.sb([128, 1], F32, "sg")
    ph.dma("sp", sg[:, :], dr["subln_g"][l:l + 1, :].rearrange("o d -> d o"), [], [sg])
    epsS = ph.sb([128, 1], F32, "epsS")
    ph.add("pool", lambda e: e.memset(epsS[:, :], SUBLN_EPS), [], [epsS])
    Eb = [[ph.sb([128, 128], F32, "Eb") for ty in range(2)] for h in range(2)]
    qT = [ph.sb([128, S], BF16, "qT") for _ in range(2)]
    kT = [ph.sb([128, S], BF16, "kT") for _ in range(2)]
    Vx = [ph.sb([128, NT, 129], BF16, "Vx") for _ in range(2)]
    for k in range(2):
        ph.add("pool", lambda e, k=k: e.memset(Vx[k][:, :, 128:129], 1.0), [], [Vx[k]])
    Pt = [ph.sb([128, 512], BF16, "Pt") for _ in range(4)]
    Pf = [ph.sb([128, 256], F32, "Pf") for _ in range(2)]
    Osb = [ph.sb([128, 512], F32, "Osb") for _ in range(4)]
    small = {k: ph.sb([128, 4], F32, k) for k in ("r1", "r2", "ss", "rstd")}
    t1 = [ph.sb([128, 128], F32, "t1") for _ in range(2)]
    ybf = [ph.sb([128, 128], BF16, "ybf") for _ in range(2)]
    outT = [ph.sb([128, 512], BF16, "outT") for _ in range(2)]
    Obank = [ph.ps[0], ph.ps[1], ph.ps[2], ph.ps[3]]
    Sbank = [ph.ps[4], ph.ps[5], ph.ps[6]]
    Tbank = ph.ps[7]
    pi = [0]
    pfi = [0]
    for h in range(4):
        hb = h % 2
        q_sb, k_sb, v_sb, E = qT[hb], kT[hb], Vx[hb], Eb[hb]
        ph.dma("sp", q_sb[:, :], dr["qT"][h, :, :], [], [q_sb])
        ph.dma("pool", k_sb[:, :], dr["kT"][h, :, :], [], [k_sb])
        ph.dma("sp", v_sb[:, :, 0:128], dr["vd"][:, h * 128:(h + 1) * 128].rearrange("(t p) d -> p t d", p=128),
               [], [v_sb])
        for ty in range(2):
            ph.dma("pool", E[ty][:, :], dr["Eb"][h, ty, :, :], [], [E[ty]])
        for g in range(NG):
            steps = [(kbk, m) for kbk in range(4 * g + 4) for m in range(2)]
            sb_of = {}

            def emit_qk(si):
                kbk, m = steps[si]
                jlo = max(0, kbk - 4 * g)
                n = (4 - jlo) * 128
                sbk = Sbank[si % 3]
                sb_of[si] = sbk
                rows = slice(m * 64, (m + 1) * 64)
                ph.mm(sbk[:, 0:n], k_sb[rows, kbk * 128:(kbk + 1) * 128],
                      q_sb[rows, (4 * g + jlo) * 128:(4 * g + 4) * 128], True, True, [k_sb, q_sb], [sbk])

            def emit_rest(si):
                kbk, m = steps[si]
                jlo = max(0, kbk - 4 * g)
                n = (4 - jlo) * 128
                sbk = sb_of[si]
                P = Pt[pi[0] % 4]
                pi[0] += 1
                near = [(j, 4 * g + j - kbk) for j in range(jlo, 4) if 0 <= 4 * g + j - kbk <= 1]
                nn = len(near)
                if nn:
                    j0 = near[0][0]
                    c0 = (j0 - jlo) * 128
                    pf = Pf[pfi[0] % 2]
                    pfi[0] += 1
                    ph.act(pf[:, 0:nn * 128], sbk[:, c0:c0 + nn * 128], AF.Exp, [sbk], [pf])
                    for ii, (j, ty) in enumerate(near):
                        ph.tt("dve", P[:, c0 + ii * 128:c0 + (ii + 1) * 128], pf[:, ii * 128:(ii + 1) * 128],
                              E[ty][:, :], ALU.mult, [pf, E[ty]], [P])
                    far0 = c0 + nn * 128
                    if far0 < n:
                        ph.act(P[:, far0:n], sbk[:, far0:n], AF.Exp, [sbk], [P])
                else:
                    ph.act(P[:, 0:n], sbk[:, 0:n], AF.Exp, [sbk], [P])
                for j in range(jlo, 4):
                    ob = Obank[m * 2 + j // 2]
                    cc = (j % 2) * 256
                    ph.mm(ob[:, cc:cc + 129], P[:, (j - jlo) * 128:(j - jlo + 1) * 128], v_sb[:, kbk, :],
                          kbk == 0 and j % 2 == 0, kbk == 4 * g + j, [P, v_sb], [ob], skip=True)

            ns = len(steps)
            LA = 2
            for si in range(min(LA, ns)):
                emit_qk(si)
            for si in range(ns):
                if si + LA < ns:
                    emit_qk(si + LA)
                emit_rest(si)
            for b4 in range(4):
                if b4 % 2 == 0:
                    ph.cp("act", Osb[b4][:, :], Obank[b4][:, :], [Obank[b4]], [Osb[b4]])
                else:
                    ph.cp("dve", Osb[b4][:, :], Obank[b4][:, :], [Obank[b4]], [Osb[b4]])
            oT = outT[g % 2]
            tb = Tbank.ap[:, :].bitcast(BF16)
            for j in range(4):
                o0 = Osb[0 + j // 2]
                o1 = Osb[2 + j // 2]
                cc = (j % 2) * 256
                r1, r2, ss, rstd = (small[k] for k in ("r1", "r2", "ss", "rstd"))
                ph.add("dve", lambda e, o0=o0, cc=cc, j=j: e.reciprocal(r1[:, j:j + 1], o0[:, cc + 128:cc + 129]), [o0], [r1])
                ph.add("dve", lambda e, o1=o1, cc=cc, j=j: e.reciprocal(r2[:, j:j + 1], o1[:, cc + 128:cc + 129]), [o1], [r2])
                ph.ts("dve", r2[:, j:j + 1], r2[:, j:j + 1], lamneg[:, l:l + 1], None, ALU.mult, None, [r2, lamneg], [r2])
                tt1 = t1[j % 2]
                ph.ts("dve", tt1[:, :], o0[:, cc:cc + 128], r1[:, j:j + 1], None, ALU.mult, None, [o0, r1], [tt1])
                ph.stt("dve", tt1[:, :], o1[:, cc:cc + 128], r2[:, j:j + 1], tt1[:, :], ALU.mult, ALU.add, [o1, r2, tt1], [tt1])
                yb = ybf[j % 2]
                ph.act(yb[:, :], tt1[:, :], AF.Square, [tt1], [yb, ss], accum_out=ss[:, j:j + 1])
                ph.act(rstd[:, j:j + 1], ss[:, j:j + 1], AF.Ln, [ss], [rstd], bias=epsS[:, 0:1], scale=1.0 / 128)
                ph.act(rstd[:, j:j + 1], rstd[:, j:j + 1], AF.Exp, [rstd], [rstd], scale=-0.5)
                ph.act(yb[:, :], tt1[:, :], AF.Copy, [tt1, rstd], [yb], scale=rstd[:, j:j + 1])
                ph.tr(tb[:, j * 128:(j + 1) * 128], yb[:, :], c["ident_bf"][:, :], [yb, c["ident_bf"]], [Tbank])
            ph.ts("dve", oT[:, :], tb[:, 0:512], sg[:, 0:1], 1.0 - LAM_INIT[l], ALU.mult, ALU.mult, [Tbank, sg], [oT])
            ph.dma("sp", dr["mixT"][h * 128:(h + 1) * 128, g * 512:(g + 1) * 512], oT[:, :], [oT], [])
    ph.finish()


def host_consts():
    c = {}
    c["c_ident"] = np.eye(128, dtype=np.float32)
    c["c_J"] = np.ascontiguousarray(np.eye(128, dtype=np.float32)[::-1])
    kk, qq = np.meshgrid(np.arange(128), np.arange(128), indexing="ij")
    c["c_causal"] = (qq >= kk).astype(np.float32)
    n = np.maximum(np.arange(384) - 127, 0)
    nf = np.maximum(n, 1).astype(np.float32)
    large = 16 + (np.log(nf / np.float32(16)) / np.float32(math.log(128 / 16)) * np.float32(16)).astype(np.int32)
    large = np.minimum(large, 31)
    bucket = np.where(n < 16, n, large)
    oh = np.zeros((32, 385), np.float32)
    oh[bucket, np.arange(384)] = 1.0
    oh[31, 384] = 1.0
    c["c_oh"] = oh
    pp, ff = np.meshgrid(np.arange(128), np.arange(128), indexing="ij")
    SU = (ff > pp).astype(np.float32)
    IU = (ff >= pp).astype(np.float32)
    c["c_mask4"] = np.ascontiguousarray(np.concatenate([SU, IU, SU, IU], axis=1))
    c["c_maskSL"] = (ff < pp).astype(np.float32)
    c["c_bones"] = ((pp // 64) == (ff // 64)).astype(np.float32)
    cm = np.ones((128, 512), np.float32)
    cm[:, ::128] = 0.0
    c["c_cm01"] = cm
    return c


def host_rwp(inp, l):
    vs = [inp["rw_w0"][l], inp["rw_a0"][l], inp["rw_k_k"][l], inp["rw_k_a"][l], inp["rw_r_k"][l].reshape(512),
          inp["rw_gn_g"][l], inp["rw_gn_b"][l], inp["rw_v0"][l - 1] if l > 0 else np.zeros(512, np.float32)]
    a = np.stack([np.asarray(v, np.float32).reshape(4, 128).T for v in vs], axis=1)
    return np.ascontiguousarray(a)


GN_EPS = 64e-5
DECAY_C = -math.exp(-0.5)


def phase_rwkv(kb, l):
    ph = PH(kb, "rw%d" % l)
    S, NT = kb.S, kb.NT
    NTB = S // 512
    dr = kb.dram
    cst = load_consts(ph)
    identbf = cst["ident_bf"]
    mask4 = ph.sb([128, 512], F32, "mask4")
    maskSL = ph.sb([128, 128], F32, "maskSL")
    bones = ph.sb([128, 128], F32, "bones")
    cm01 = ph.sb([128, 512], F32, "cm01")
    ph.dma("sp", mask4[:, :], dr["c_mask4"][:, :], [], [mask4])
    ph.dma("sp", maskSL[:, :], dr["c_maskSL"][:, :], [], [maskSL])
    ph.dma("sp", bones[:, :], dr["c_bones"][:, :], [], [bones])
    ph.dma("sp", cm01[:, :], dr["c_cm01"][:, :], [], [cm01])
    prm = ph.sb([128, 8, 4], F32, "prm")
    ph.dma("sp", prm[:, :, :], dr["rwp%d" % l][:, :, :], [], [prm])
    W0, A0, KK_, KA_, RK_, GG_, GB_, V0_ = range(8)
    WA = ph.sb([128, 512], F32, "WA")
    WG1 = ph.sb([128, 512], F32, "WG1")
    WGV = ph.sb([64, 512], F32, "WGV")
    ph.dma("sp", WA[0:64, :], dr["rw_w_up"][l, :, :], [], [WA])
    ph.dma("sp", WA[64:128, :], dr["rw_a_up"][l, :, :], [], [WA])
    ph.dma("sp", WG1[:, :], dr["rw_g_up"][l, 0:128, :], [], [WG1])
    ph.dma("sp", WGV[0:32, :], dr["rw_g_up"][l, 128:160, :], [], [WGV])
    if l > 0:
        ph.dma("sp", WGV[32:64, :], dr["rw_v_up"][l - 1, :, :], [], [WGV])
    omk = ph.sb([128, 4], F32, "omk")
    ph.ts("dve", omk[:, :], prm[:, KA_, :], -1.0, 1.0, ALU.mult, ALU.add, [prm], [omk])
    epsG = ph.sb([128, 1], F32, "epsG")
    ph.add("pool", lambda e: e.memset(epsG[:, :], GN_EPS), [], [epsG])
    H = ph.sb([128, 4, 64], F32, "H")
    Hb = [ph.sb([128, 64], BF16, "Hb%d" % hp) for hp in range(4)]
    Hs = [ph.sb([128, 64], F32, "Hs%d" % hp) for hp in range(4)]
    for hp in range(4):
        ph.add("pool", lambda e, hp=hp: e.memset(Hs[hp][:, :], 0.0), [], [Hs[hp]])
        ph.add("pool", lambda e, hp=hp: e.memset(Hb[hp][:, :], 0.0), [], [Hb[hp]])
    XWA = ph.sb([128, 512], F32, "XWA")
    XG1 = ph.sb([128, 512], F32, "XG1")
    XG2 = ph.sb([64, 512], F32, "XG2")
    TH = ph.sb([64, 512], F32, "TH")
    SG1 = ph.sb([128, 512], F32, "SG1")
    SG2 = ph.sb([32, 512], F32, "SG2")
    def f32t(n):
        return ph.sb([128, 512], F32, n)

    def bft(n):
        return ph.sb([128, 512], BF16, n)
    NSET = 2
    sets = []
    for k in range(NSET):
        sets.append(dict(
            R=f32t("R"), Kt=f32t("Kt"), Vt=f32t("Vt"), VF=f32t("VF") if l > 0 else None,
            cum=f32t("cum"), epos=f32t("epos"), AR=ph.sb([128, 4, 2, 128], BF16, "AR"),
            BT=bft("BT"), KT=bft("KT"), BH=bft("BH"), KH=bft("KH"), VB=bft("VB"),
            tok=[ph.sb([128, 4, 128], BF16, "tok") for _ in range(4)],
            gsb=f32t("gsb"), bon=f32t("bon"), outT=bft("outT")))
    sgw, lw, tmpc, eprev, eneg, edec, a_, kk, kk2, ssm, rn, kkn, tq, kp, bbt, vp, gate, pr_ = (
        f32t(n) for n in ("sgw", "lw", "tmpc", "eprev", "eneg", "edec", "a", "kk", "kk2", "ssm", "rn", "kkn",
                          "tq", "kp", "bbt", "vp", "gate", "pr"))
    M4 = [ph.sb([128, 512], BF16, "M4") for _ in range(4)]
    PQ = [[ph.sb([128, 256], BF16, "PQ") for _ in range(2)] for _ in range(2)]
    P0 = [ph.sb([128, 128], BF16, "P0") for _ in range(2)]
    TT = [[ph.sb([128, 128], BF16, "TT") for _ in range(2)] for _ in range(2)]
    AkV = [ph.sb([128, 64], BF16, "AkV") for _ in range(2)]
    WTsb = [ph.sb([128, 128], BF16, "WTsb") for _ in range(2)]
    Uloc = [ph.sb([128, 64], F32, "Uloc") for _ in range(2)]
    Usb = [ph.sb([128, 64], BF16, "Usb") for _ in range(2)]
    ynorm = [ph.sb([128, 128], BF16, "ynorm") for _ in range(2)]
    gst = [ph.sb([128, 6], F32, "gst") for _ in range(2)]
    gmv = [ph.sb([128, 2], F32, "gmv") for _ in range(2)]
    grs = [ph.sb([128, 1], F32, "grs") for _ in range(2)]
    gnm = [ph.sb([128, 1], F32, "gnm") for _ in range(2)]
    fin = [ph.sb([128, 128], F32, "fin") for _ in range(2)]
    PA = [ph.ps[0], ph.ps[1]]
    TRB = [ph.ps[2], ph.ps[3]]
    kb_h = kb.psum_h

    def subtiles(bi, bounds):
        return [ph.ps[bi] for a, b in bounds]
    W1 = [subtiles(4 + e, [(0, 64), (64, 128), (128, 192), (192, 256), (256, 320), (320, 384), (384, 512)])
          for e in range(2)]
    W2 = [subtiles(6 + e, [(0, 128), (128, 384), (384, 512)]) for e in range(2)]
    pai = [0]

    def nextpa():
        b = PA[pai[0] % 2]
        pai[0] += 1
        return b

    def c3(ap):
        return ap.rearrange("p (c t) -> p c t", c=4)

    ui = [0]
    for tb in range(NTB):
        tcols = slice(tb * 512, (tb + 1) * 512)
        ph.dma("sp", XWA[:, :], dr["prT"][1536:1664, tcols], [], [XWA])
        ph.dma("pool", XG1[:, :], dr["prT"][1664:1792, tcols], [], [XG1])
        ph.dma("pool", XG2[:, :], dr["prT"][1792:1856, tcols], [], [XG2])
        ph.act(TH[:, :], XWA[0:64, :], AF.Tanh, [XWA], [TH])
        ph.act(SG1[:, :], XG1[:, :], AF.Sigmoid, [XG1], [SG1])
        ph.act(SG2[:, :], XG2[0:32, :], AF.Sigmoid, [XG2], [SG2])
        for hp in range(4):
            s = sets[(tb * 4 + hp) % NSET]
            R, Kt, Vt, VF, cum, epos, AR, BT, KT, BH, KH, VB, tok, gsb, bon, outT = (
                s[k] for k in ("R", "Kt", "Vt", "VF", "cum", "epos", "AR", "BT", "KT", "BH", "KH", "VB", "tok",
                               "gsb", "bon", "outT"))
            hc = slice(hp * 128, (hp + 1) * 128)
            P = lambda i: prm[:, i, hp:hp + 1]
            ph.dma("sp", R[:, :], dr["prT"][hp * 128:(hp + 1) * 128, tcols], [], [R])
            ph.dma("sp", Kt[:, :], dr["prT"][512 + hp * 128:512 + (hp + 1) * 128, tcols], [], [Kt])
            ph.dma("pool", Vt[:, :], dr["prT"][1024 + hp * 128:1024 + (hp + 1) * 128, tcols], [], [Vt])
            if l > 0:
                ph.dma("pool", VF[:, :], dr["vfT"][hp * 128:(hp + 1) * 128, tcols], [], [VF])
            pa = nextpa()
            ph.mm(pa[:, :], WA[0:64, hc], TH[:, :], True, True, [WA, TH], [pa])
            ph.act(sgw[:, :], pa[:, :], AF.Sigmoid, [pa, prm], [sgw], bias=P(W0))
            ph.ts("dve", lw[:, :], sgw[:, :], DECAY_C, None, ALU.mult, None, [sgw], [lw])
            ph.add("dve", lambda e, cum=cum: e.tensor_tensor_scan(cum[:, :], cm01[:, :], lw[:, :], 0.0, ALU.mult, ALU.add),
                   [cm01, lw], [cum])
            ph.act(epos[:, :], cum[:, :], AF.Exp, [cum], [epos])
            ph.tt("pool", tmpc[:, :], cum[:, :], lw[:, :], ALU.subtract, [cum, lw], [tmpc])
            ph.act(eprev[:, :], tmpc[:, :], AF.Exp, [tmpc], [eprev])
            ph.act(eneg[:, :], cum[:, :], AF.Exp, [cum], [eneg], scale=-1.0)
            for c in range(4):
                ph.act(edec[:, c * 128:(c + 1) * 128], cum[:, c * 128:(c + 1) * 128], AF.Exp, [cum], [edec],
                       scale=-1.0, bias=cum[:, c * 128 + 127:c * 128 + 128])
            pa = nextpa()
            ph.mm(pa[:, :], WA[64:128, hc], XWA[64:128, :], True, True, [WA, XWA], [pa])
            ph.act(a_[:, :], pa[:, :], AF.Sigmoid, [pa, prm], [a_], bias=P(A0))
            ph.ts("dve", kk[:, :], Kt[:, :], P(KK_), None, ALU.mult, None, [Kt, prm], [kk])
            ph.tt("pool", kk2[:, :], kk[:, :], kk[:, :], ALU.mult, [kk], [kk2])
            pa = nextpa()
            ph.mm(pa[:, :], bones[:, :], kk2[:, :], True, True, [bones, kk2], [pa])
            ph.ts("dve", ssm[:, :], pa[:, :], 1e-24, None, ALU.max, None, [pa], [ssm])
            ph.act(rn[:, :], ssm[:, :], AF.Ln, [ssm], [rn])
            ph.act(rn[:, :], rn[:, :], AF.Exp, [rn], [rn], scale=-0.5)
            ph.tt("dve", kkn[:, :], kk[:, :], rn[:, :], ALU.mult, [kk, rn], [kkn])
            ph.ts("pool", tq[:, :], a_[:, :], P(KA_), omk[:, hp:hp + 1], ALU.mult, ALU.add, [a_, prm, omk], [tq])
            ph.tt("pool", kp[:, :], tq[:, :], Kt[:, :], ALU.mult, [tq, Kt], [kp])
            ph.stt("dve", AR[:, :, 0, :], c3(kkn[:, :]), -1.0, c3(eprev[:, :]), ALU.mult, ALU.mult, [kkn, eprev], [AR])
            ph.tt("pool", AR[:, :, 1, :], c3(R[:, :]), c3(epos[:, :]), ALU.mult, [R, epos], [AR])
            ph.tt("pool", bbt[:, :], kkn[:, :], a_[:, :], ALU.mult, [kkn, a_], [bbt])
            ph.tt("dve", BT[:, :], bbt[:, :], eneg[:, :], ALU.mult, [bbt, eneg], [BT])
            ph.tt("pool", BH[:, :], bbt[:, :], edec[:, :], ALU.mult, [bbt, edec], [BH])
            ph.tt("dve", KT[:, :], kp[:, :], eneg[:, :], ALU.mult, [kp, eneg], [KT])
            ph.tt("pool", KH[:, :], kp[:, :], edec[:, :], ALU.mult, [kp, edec], [KH])
            if l > 0:
                pa = nextpa()
                ph.mm(pa[:, :], WGV[32:64, hc], XG2[32:64, :], True, True, [WGV, XG2], [pa])
                ph.act(gate[:, :], pa[:, :], AF.Sigmoid, [pa, prm], [gate], bias=P(V0_))
                ph.tt("pool", vp[:, :], VF[:, :], Vt[:, :], ALU.subtract, [VF, Vt], [vp])
                ph.tt("dve", vp[:, :], vp[:, :], gate[:, :], ALU.mult, [vp, gate], [vp])
                ph.tt("pool", vp[:, :], vp[:, :], Vt[:, :], ALU.add, [vp, Vt], [vp])
                vsrc = vp
            else:
                vsrc = Vt
            ph.cp("act", VB[:, :], vsrc[:, :], [vsrc], [VB])
            pa = nextpa()
            ph.mm(pa[:, :], WG1[:, hc], SG1[:, :], True, False, [WG1, SG1], [pa])
            ph.mm(pa[:, :], WGV[0:32, hc], SG2[:, :], False, True, [WGV, SG2], [pa])
            ph.cp("act", gsb[:, :], pa[:, :], [pa], [gsb])
            ph.ts("pool", pr_[:, :], R[:, :], P(RK_), None, ALU.mult, None, [R, prm], [pr_])
            ph.tt("pool", pr_[:, :], pr_[:, :], kp[:, :], ALU.mult, [pr_, kp], [pr_])
            pa = nextpa()
            ph.mm(pa[:, :], bones[:, :], pr_[:, :], True, True, [bones, pr_], [pa])
            ph.tt("dve", bon[:, :], pa[:, :], vsrc[:, :], ALU.mult, [pa, vsrc], [bon])
            for c in range(4):
                trb = TRB[c // 2]
                tv = trb.ap[:, :].bitcast(BF16)
                base = (c % 2) * 512
                cs = slice(c * 128, (c + 1) * 128)
                for qi, (src, sap) in enumerate(((AR, AR[:, c, 0, :]), (BH, BH[:, cs]), (KH, KH[:, cs]), (VB, VB[:, cs]))):
                    ph.tr(tv[:, base + qi * 128:base + (qi + 1) * 128], sap, identbf[:, :], [src, identbf], [trb])
                ev = tv[:, base:base + 512].rearrange("p (q t) -> p q t", q=4)
                if c % 2 == 0:
                    ph.cp("dve", tok[c][:, :, :], ev, [trb], [tok[c]])
                else:
                    ph.cp("act", tok[c][:, :, :], ev, [trb], [tok[c]])
            for c in range(4):
                cs = slice(c * 128, (c + 1) * 128)
                gl = c * 128 + 127
                for e in range(2):
                    rows = slice(e * 64, (e + 1) * 64)
                    w1, w2 = W1[e], W2[e]
                    w1h, w2h = kb_h[4 + e], kb_h[6 + e]
                    m4 = M4[ui[0] % 4]
                    ui[0] += 1
                    ph.mm(w1h[:, 0:256], BT[rows, cs], AR[rows, c, :, :].rearrange("p a t -> p (a t)"), True, True,
                          [BT, AR], [w1[0]])
                    ph.mm(w1h[:, 256:512], KT[rows, cs], AR[rows, c, :, :].rearrange("p a t -> p (a t)"), True, True,
                          [KT, AR], [w1[0]])
                    ph.mm(w2h[:, 0:128], AR[rows, c, 0, :], BT[rows, cs], True, True, [AR, BT], [w2[0]])
                    ph.tt("dve", m4[:, :], w1h[:, 0:512], mask4[:, :], ALU.mult, [w1[0], mask4], [m4])
                    ph.tt("dve", P0[e][:, :], w2h[:, 0:128], maskSL[:, :], ALU.mult, [w2[0], maskSL], [P0[e]])
                    ph.tt("pool", TT[e][0][:, :], m4[:, 0:128], identbf[:, :], ALU.add, [m4, identbf], [TT[e][0]])
                for k in range(6):
                    for e in range(2):
                        w2, w2h = W2[e], kb_h[6 + e]
                        m4 = M4[(ui[0] - 2 + e) % 4]
                        if k == 0:
                            Pk, Qk, Pt_, Qt_ = P0[e][:, :], m4[:, 0:128], P0[e], m4
                        else:
                            pq = PQ[e][k % 2]
                            Pk, Qk, Pt_, Qt_ = pq[:, 0:128], pq[:, 128:256], pq, pq
                        pqn = PQ[e][(k + 1) % 2]
                        ph.mm(w2h[:, 128:256], Qk, Pk, True, True, [Pt_, Qt_], [w2[1]])
                        if k < 5:
                            ph.mm(w2h[:, 256:384], Pk, Qk, True, True, [Pt_, Qt_], [w2[1]])
                            ph.cp("act", pqn[:, 0:256], w2h[:, 128:384], [w2[1]], [pqn])
                        else:
                            ph.cp("act", pqn[:, 0:128], w2h[:, 128:256], [w2[1]], [pqn])
                        ph.mm(w2h[:, 384:512], pqn[:, 0:128], TT[e][k % 2][:, :], True, True, [pqn, TT[e][k % 2]], [w2[2]])
                        ph.tt("dve", TT[e][(k + 1) % 2][:, :], TT[e][k % 2][:, :], w2h[:, 384:512], ALU.add,
                              [TT[e][k % 2], w2[2]], [TT[e][(k + 1) % 2]])
                for e in range(2):
                    rows = slice(e * 64, (e + 1) * 64)
                    w1, w1h = W1[e], kb_h[4 + e]
                    m4 = M4[(ui[0] - 2 + e) % 4]
                    TTf = TT[e][0]
                    Vtok = tok[c][:, 3, rows]
                    ph.mm(w1h[:, 0:64], m4[:, 256:384], Vtok, True, True, [m4, tok[c]], [w1[0]])
                    ph.cp("act", AkV[e][:, :], w1h[:, 0:64], [w1[0]], [AkV[e]])
                    ph.mm(w1h[:, 384:512], tok[c][:, 0, :], TTf[:, :], True, True, [tok[c], TTf], [w1[6]])
                    ph.cp("act", WTsb[e][rows, :], w1h[rows, 384:512], [w1[6]], [WTsb[e]])
                    ph.mm(w1h[:, 64:128], TTf[:, :], AkV[e][:, :], True, True, [TTf, AkV[e]], [w1[1]])
                    ph.cp("act", Uloc[e][:, :], w1h[:, 64:128], [w1[1]], [Uloc[e]])
                    ph.mm(w1h[:, 128:192], WTsb[e][rows, :], Hb[hp][rows, :], True, True, [WTsb[e], Hb[hp]], [w1[2]])
                    ph.tt("dve", Usb[e][:, :], w1h[:, 128:192], Uloc[e][:, :], ALU.add, [w1[2], Uloc[e]], [Usb[e]])
                    ph.mm(w1h[:, 256:320], AR[rows, c, 1, :], Hb[hp][rows, :], True, False, [AR, Hb[hp]], [w1[4]])
                    ph.mm(w1h[:, 256:320], m4[:, 128:256], Usb[e][:, :], False, False, [m4, Usb[e]], [w1[4]])
                    ph.mm(w1h[:, 256:320], m4[:, 384:512], Vtok, False, True, [m4, tok[c]], [w1[4]])
                    ph.mm(w1h[:, 192:256], tok[c][:, 1, :], Usb[e][:, :], True, False, [tok[c], Usb[e]], [w1[3]])
                    ph.mm(w1h[:, 192:256], tok[c][:, 2, :], Vtok, False, True, [tok[c]], [w1[3]])
                    ph.stt("dve", Hs[hp][rows, :], Hs[hp][rows, :], epos[rows, gl:gl + 1], w1h[rows, 192:256],
                           ALU.mult, ALU.add, [Hs[hp], epos, w1[3]], [Hs[hp]])
                    ph.cp("act", Hb[hp][rows, :], Hs[hp][rows, :], [Hs[hp]], [Hb[hp]])
                    yn = ynorm[c % 2]
                    ph.add("dve", lambda en, e=e, w1h=w1h: en.bn_stats(gst[e][:, 0:6], w1h[:, 256:320]), [w1[4]], [gst[e]])
                    ph.add("dve", lambda en, e=e: en.bn_aggr(gmv[e][:, 0:2], gst[e][:, 0:6]), [gst[e]], [gmv[e]])
                    ph.act(grs[e][:, 0:1], gmv[e][:, 1:2], AF.Ln, [gmv[e]], [grs[e]], bias=epsG[:, 0:1])
                    ph.act(grs[e][:, 0:1], grs[e][:, 0:1], AF.Exp, [grs[e]], [grs[e]], scale=-0.5)
                    ph.ts("dve", gnm[e][:, 0:1], gmv[e][:, 0:1], grs[e][:, 0:1], -1.0, ALU.mult, ALU.mult,
                          [gmv[e], grs[e]], [gnm[e]])
                    ph.act(yn[:, e * 64:(e + 1) * 64], w1h[:, 256:320], AF.Identity, [w1[4], grs[e], gnm[e]], [yn],
                           bias=gnm[e][:, 0:1], scale=grs[e][:, 0:1])
                yn = ynorm[c % 2]
                trb = TRB[1]
                tv = trb.ap[:, :].bitcast(BF16)
                ph.tr(tv[:, 0:128], yn[:, :], identbf[:, :], [yn, identbf], [trb])
                fn = fin[c % 2]
                ph.ts("dve", fn[:, :], tv[:, 0:128], P(GG_), P(GB_), ALU.mult, ALU.add, [trb, prm], [fn])
                ph.tt("pool", fn[:, :], fn[:, :], bon[:, cs], ALU.add, [fn, bon], [fn])
                ph.tt("pool", outT[:, cs], fn[:, :], gsb[:, cs], ALU.mult, [fn, gsb], [outT])
            ph.dma("sp", dr["mixT"][512 + hp * 128:512 + (hp + 1) * 128, tcols], outT[:, :], [outT], [])
    ph.finish()


ALPHA = (2 * 2) ** 0.25


def load_cast_weight(ph, w_rows_fn, nrows_tiles, ncols, wbf_fn, stage, q="act"):
    pc = stage[0].ap.shape[1]
    k = 0
    for r in range(nrows_tiles):
        for c0 in range(0, ncols, pc):
            n = min(pc, ncols - c0)
            s = stage[k % len(stage)]
            ph.dma(q if k % 2 == 0 else "sp", s[:, 0:n], w_rows_fn(r)[:, c0:c0 + n], [], [s])
            ph.cp(("dve", "pool", "act")[k % 3], wbf_fn(r)[:, c0:c0 + n], s[:, 0:n], [s], [wbf_fn(r)])
            k += 1


def phase_outproj(kb, l):
    ph = PH(kb, "op%d" % l)
    S, NT = kb.S, kb.NT
    dr = kb.dram
    c = load_consts(ph)
    gB = ph.sb([128, 1024], F32, "gB")
    bB = ph.sb([128, 1024], F32, "bB")
    ph.dma("sp", gB[:, :], dr["ln_mix_g"][l, :].partition_broadcast(128), [], [gB])
    ph.dma("sp", bB[:, :], dr["ln_mix_b"][l, :].partition_broadcast(128), [], [bB])
    wbf = [ph.sb([128, 1024], BF16, "wo%d" % r) for r in range(8)]
    stage = [ph.sb([128, 1024], F32, "stg%d" % k) for k in range(2)]
    load_cast_weight(ph, lambda r: dr["w_out"][l, r * 128:(r + 1) * 128, :], 8, 1024, lambda r: wbf[r], stage)
    mx = [ph.sb([128, 8, 512], BF16, "mx%d" % k) for k in range(2)]
    xr = [ph.sb([128, 1024], F32, "xr%d" % k) for k in range(2)]
    srcs = [ph.sb([128, 1024], F32, "src%d" % k) for k in range(2)]
    bufs = [ln_bufs(ph, k) for k in range(2)]
    for g in range(S // 512):
        m = mx[g % 2]
        ph.dma("pool", m[:, :, :], dr["mixT"][:, g * 512:(g + 1) * 512].rearrange("(c p) t -> p c t", p=128), [], [m])
        for j in range(4):
            i = g * 4 + j
            x_ = xr[i % 2]
            ph.dma("sp", x_[:, :], dr["xres"][i * 128:(i + 1) * 128, :], [], [x_])
            src = srcs[i % 2]
            for hh in range(2):
                pb = ph.ps[(i % 2) * 2 + hh]
                for cc in range(8):
                    ph.mm(pb[:, :], m[:, cc, j * 128:(j + 1) * 128], wbf[cc][:, hh * 512:(hh + 1) * 512],
                          cc == 0, cc == 7, [m, wbf[cc]], [pb])
                ph.stt("dve", src[:, hh * 512:(hh + 1) * 512], x_[:, hh * 512:(hh + 1) * 512], ALPHA, pb[:, :],
                       ALU.mult, ALU.add, [x_, pb], [src])
            ln_rows(ph, src, i, gB, bB, c["ident_bf"], "xres", "xT", bufs[i % 2], ph.ps[4 + i % 2])
    ph.finish()


def phase_ffn_up(kb, l):
    ph = PH(kb, "fu%d" % l)
    S, NT = kb.S, kb.NT
    dr = kb.dram
    wbf = [ph.sb([128, 4096], BF16, "wu%d" % r) for r in range(8)]
    stage = [ph.sb([128, 2048], F32, "stg%d" % k) for k in range(2)]
    load_cast_weight(ph, lambda r: dr["w_up"][l, r * 128:(r + 1) * 128, :], 8, 4096, lambda r: wbf[r], stage)
    xg = [ph.sb([128, 8, 512], BF16, "xg%d" % k) for k in range(2)]
    tmp = [ph.sb([128, 512], F32, "tmp%d" % k) for k in range(3)]
    ho = [ph.sb([128, 512], BF16, "ho%d" % k) for k in range(4)]
    k = 0
    for g in range(S // 512):
        x_ = xg[g % 2]
        ph.dma("pool", x_[:, :, :], dr["xT"][:, g * 512:(g + 1) * 512].rearrange("(c p) t -> p c t", p=128), [], [x_])
        for f in range(32):
            pb = ph.ps[k % 6]
            for cc in range(8):
                ph.mm(pb[:, :], wbf[cc][:, f * 128:(f + 1) * 128], x_[:, cc, :], cc == 0, cc == 7, [wbf[cc], x_], [pb])
            t = tmp[k % 3]
            h = ho[k % 4]
            ph.act(t[:, :], pb[:, :], AF.Relu, [pb], [t])
            ph.tt(("pool", "dve")[k % 2], h[:, :], t[:, :], t[:, :], ALU.mult, [t], [h])
            ph.dma("sp", dr["hT"][f * 128:(f + 1) * 128, g * 512:(g + 1) * 512], h[:, :], [h], [])
            k += 1
    ph.finish()


def phase_ffn_down(kb, l, last):
    ph = PH(kb, "fd%d" % l)
    S, NT = kb.S, kb.NT
    dr = kb.dram
    c = load_consts(ph)
    gB = ph.sb([128, 1024], F32, "gB")
    bB = ph.sb([128, 1024], F32, "bB")
    ph.dma("sp", gB[:, :], dr["ln_ffn_g"][l, :].partition_broadcast(128), [], [gB])
    ph.dma("sp", bB[:, :], dr["ln_ffn_b"][l, :].partition_broadcast(128), [], [bB])
    wbf = [ph.sb([128, 1024], BF16, "wd%d" % r) for r in range(32)]
    stage = [ph.sb([128, 1024], F32, "stg%d" % k) for k in range(3)]
    load_cast_weight(ph, lambda r: dr["w_down"][l, r * 128:(r + 1) * 128, :], 32, 1024, lambda r: wbf[r], stage)
    hg = [ph.sb([128, 32, 512], BF16, "hg%d" % k) for k in range(2)]
    xr = [ph.sb([128, 1024], F32, "xr%d" % k) for k in range(2)]
    srcs = [ph.sb([128, 1024], F32, "src%d" % k) for k in range(2)]
    bufs = [ln_bufs(ph, k) for k in range(2)]
    for g in range(S // 512):
        h_ = hg[g % 2]
        for q4 in range(4):
            ph.dma(("pool", "sp")[q4 % 2], h_[:, q4 * 8:(q4 + 1) * 8, :],
                   dr["hT"][q4 * 1024:(q4 + 1) * 1024, g * 512:(g + 1) * 512].rearrange("(f p) t -> p f t", p=128),
                   [], [h_])
        for j in range(4):
            i = g * 4 + j
            x_ = xr[i % 2]
            ph.dma("sp", x_[:, :], dr["xres"][i * 128:(i + 1) * 128, :], [], [x_])
            src = srcs[i % 2]
            for hh in range(2):
                pb = ph.ps[(i % 2) * 2 + hh]
                for f in range(32):
                    ph.mm(pb[:, :], h_[:, f, j * 128:(j + 1) * 128], wbf[f][:, hh * 512:(hh + 1) * 512],
                          f == 0, f == 31, [h_, wbf[f]], [pb])
                ph.stt("dve", src[:, hh * 512:(hh + 1) * 512], x_[:, hh * 512:(hh + 1) * 512], ALPHA, pb[:, :],
                       ALU.mult, ALU.add, [x_, pb], [src])
            if last:
                ln_rows(ph, src, i, gB, bB, c["ident_bf"], "out", None, bufs[i % 2], ph.ps[4 + i % 2])
            else:
                ln_rows(ph, src, i, gB, bB, c["ident_bf"], "xres", "xT", bufs[i % 2], ph.ps[4 + i % 2])
    ph.finish()


CONST_SHAPES = {"c_ident": [128, 128], "c_J": [128, 128], "c_causal": [128, 128], "c_oh": [32, 385],
                "c_mask4": [128, 512], "c_maskSL": [128, 128], "c_bones": [128, 128], "c_cm01": [128, 512]}


def build_program(S, debug=(), only=None):
    kb = KB(S)
    kb.din("x", [S, 1024])
    kb.din("ln_in_g", [1024])
    kb.din("ln_in_b", [1024])
    kb.din("w_in0", [1024, 3360])
    kb.din("w_in1", [1024, 3392])
    kb.din("mu0", [128, 15])
    kb.din("mu1", [128, 15])
    kb.din("rel_bias", [32, 4])
    for nm in ("lambda_q1", "lambda_k1", "lambda_q2", "lambda_k2"):
        kb.din(nm, [2, 64])
    kb.din("subln_g", [2, 128])
    kb.din("rwp0", [128, 8, 4])
    kb.din("rwp1", [128, 8, 4])
    kb.din("rw_w_up", [2, 64, 512])
    kb.din("rw_a_up", [2, 64, 512])
    kb.din("rw_g_up", [2, 160, 512])
    kb.din("rw_v_up", [1, 32, 512])
    kb.din("w_out", [2, 1024, 1024])
    kb.din("ln_mix_g", [2, 1024])
    kb.din("ln_mix_b", [2, 1024])
    kb.din("w_up", [2, 1024, 4096])
    kb.din("w_down", [2, 4096, 1024])
    kb.din("ln_ffn_g", [2, 1024])
    kb.din("ln_ffn_b", [2, 1024])
    for nm, shp in CONST_SHAPES.items():
        kb.din(nm, shp)
    dbg = lambda n: n in debug
    kb.dscr("xres", [S, 1024], F32, dbg("xres"))
    kb.dscr("xT", [1024, S], BF16, dbg("xT"))
    kb.dscr("qT", [4, 128, S], BF16, dbg("qT"))
    kb.dscr("kT", [4, 128, S], BF16, dbg("kT"))
    kb.dscr("vd", [S, 512], BF16, dbg("vd"))
    kb.dscr("prT", [1920, S], F32, dbg("prT"))
    kb.dscr("vfT", [512, S], F32, dbg("vfT"))
    kb.dscr("lamneg", [1, 2], F32, dbg("lamneg"))
    kb.dscr("EGd", [4, 384], F32, dbg("EGd"))
    kb.dscr("Eb", [4, 2, 128, 128], F32, dbg("Eb"))
    kb.dscr("mixT", [1024, S], BF16, dbg("mixT"))
    kb.dscr("hT", [4096, S], BF16, dbg("hT"))
    kb.dout("out", [S, 1024], F32)
    on = lambda n: only is None or n in only
    if on("pre"):
        phase_pre(kb)
    if on("ln0"):
        phase_ln0(kb)
    for l in range(2):
        if on("ip%d" % l):
            phase_inproj(kb, l)
        if on("at%d" % l):
            phase_attn(kb, l)
        if on("rw%d" % l):
            phase_rwkv2(kb, l)
        if on("op%d" % l):
            phase_outproj(kb, l)
        if on("fu%d" % l):
            phase_ffn_up(kb, l)
        if on("fd%d" % l):
            phase_ffn_down(kb, l, last=(l == 1))
    kb.close()
    return kb


def host_inputs(inp, S):
    f = lambda a: np.ascontiguousarray(np.asarray(a, dtype=np.float32))
    m = {}
    m["ln_in_g"] = f(inp["ln_in_g"])
    m["ln_in_b"] = f(inp["ln_in_b"])
    m["w_in0"] = f(inp["w_in_first"])
    m["w_in1"] = f(inp["w_in_rest"][0])
    for l, mu in enumerate((inp["mu_first"], inp["mu_rest"][0])):
        mp = np.zeros(1920, np.float32)
        mp[:mu.shape[0]] = mu
        m["mu%d" % l] = np.ascontiguousarray(mp.reshape(15, 128).T)
    for nm in ("rel_bias", "lambda_q1", "lambda_k1", "lambda_q2", "lambda_k2", "subln_g", "rw_w_up", "rw_a_up",
               "rw_g_up", "rw_v_up", "w_out", "ln_mix_g", "ln_mix_b", "w_up", "w_down", "ln_ffn_g", "ln_ffn_b"):
        m[nm] = f(inp[nm])
    m["rwp0"] = host_rwp(inp, 0)
    m["rwp1"] = host_rwp(inp, 1)
    m.update(host_consts())
    return m


_PROG = {}


def kernel(**inputs):
    x = np.asarray(inputs["x"], dtype=np.float32)
    B, S, _ = x.shape
    if S not in _PROG:
        _PROG[S] = build_program(S)
    kb = _PROG[S]
    shared = host_inputs(inputs, S)
    in_maps = []
    for b in range(B):
        m = dict(shared)
        m["x"] = np.ascontiguousarray(x[b])
        in_maps.append(m)
    res = run_bass_kernel_spmd(kb.nc, in_maps, core_ids=list(range(B)))
    return np.stack([np.asarray(r["out"], dtype=np.float32) for r in res.results], axis=0)


def phase_rwkv2(kb, l):
    ph = PH(kb, "rw%d" % l)
    S, NT = kb.S, kb.NT
    NTB = S // 512
    dr = kb.dram
    cst = load_consts(ph)
    identbf = cst["ident_bf"]
    mask4 = ph.sb([128, 512], F32, "mask4")
    maskSL = ph.sb([128, 128], F32, "maskSL")
    bones = ph.sb([128, 128], F32, "bones")
    cm01 = ph.sb([128, 512], F32, "cm01")
    ph.dma("sp", mask4[:, :], dr["c_mask4"][:, :], [], [mask4])
    ph.dma("sp", maskSL[:, :], dr["c_maskSL"][:, :], [], [maskSL])
    ph.dma("sp", bones[:, :], dr["c_bones"][:, :], [], [bones])
    ph.dma("sp", cm01[:, :], dr["c_cm01"][:, :], [], [cm01])
    prm = ph.sb([128, 8, 4], F32, "prm")
    ph.dma("sp", prm[:, :, :], dr["rwp%d" % l][:, :, :], [], [prm])
    W0, A0, KK_, KA_, RK_, GG_, GB_, V0_ = range(8)
    WA = ph.sb([128, 512], F32, "WA")
    WG1 = ph.sb([128, 512], F32, "WG1")
    WGV = ph.sb([64, 512], F32, "WGV")
    ph.dma("sp", WA[0:64, :], dr["rw_w_up"][l, :, :], [], [WA])
    ph.dma("sp", WA[64:128, :], dr["rw_a_up"][l, :, :], [], [WA])
    ph.dma("sp", WG1[:, :], dr["rw_g_up"][l, 0:128, :], [], [WG1])
    ph.dma("sp", WGV[0:32, :], dr["rw_g_up"][l, 128:160, :], [], [WGV])
    if l > 0:
        ph.dma("sp", WGV[32:64, :], dr["rw_v_up"][l - 1, :, :], [], [WGV])
    omk = ph.sb([128, 4], F32, "omk")
    ph.ts("dve", omk[:, :], prm[:, KA_, :], -1.0, 1.0, ALU.mult, ALU.add, [prm], [omk])
    epsG = ph.sb([128, 1], F32, "epsG")
    ph.add("pool", lambda e: e.memset(epsG[:, :], GN_EPS), [], [epsG])
    Hb = [ph.sb([128, 64], BF16, "Hb%d" % hp) for hp in range(4)]
    Hs = [ph.sb([128, 64], F32, "Hs%d" % hp) for hp in range(4)]
    for hp in range(4):
        ph.add("pool", lambda e, hp=hp: e.memset(Hs[hp][:, :], 0.0), [], [Hs[hp]])
        ph.add("pool", lambda e, hp=hp: e.memset(Hb[hp][:, :], 0.0), [], [Hb[hp]])
    XWA = ph.sb([128, 512], F32, "XWA")
    XG1 = ph.sb([128, 512], F32, "XG1")
    XG2 = ph.sb([64, 512], F32, "XG2")
    TH = ph.sb([64, 512], F32, "TH")
    SG1 = ph.sb([128, 512], F32, "SG1")
    SG2 = ph.sb([32, 512], F32, "SG2")

    def f32t(n):
        return ph.sb([128, 512], F32, n)

    def bft(n):
        return ph.sb([128, 512], BF16, n)
    pers = []
    for hp in range(4):
        pers.append(dict(gsb=f32t("gsb"), bon=f32t("bon"), AR=ph.sb([128, 4, 2, 128], BF16, "AR"),
                         BT=bft("BT"), KT=bft("KT"), tok=[ph.sb([128, 4, 128], BF16, "tok") for _ in range(4)],
                         outT=bft("outT"), gam=ph.sb([128, 4], F32, "gam"),
                         keepM=[ph.sb([128, 256], BF16, "keepM") for _ in range(8)],
                         Uloc=[ph.sb([128, 64], F32, "Uloc") for _ in range(8)],
                         WTsb=[ph.sb([128, 128], BF16, "WTsb") for _ in range(4)]))
    RIN = [dict(R=f32t("R"), Kt=f32t("Kt"), Vt=f32t("Vt"), VF=f32t("VF") if l > 0 else None) for _ in range(2)]
    TMP = [[f32t("tmp%d" % i) for i in range(10)] for _ in range(2)]
    TB16 = [(bft("BH"), bft("KH"), bft("VB")) for _ in range(2)]
    M4 = [ph.sb([128, 512], BF16, "M4") for _ in range(8)]
    P0 = [ph.sb([128, 128], BF16, "P0") for _ in range(8)]
    XK = [[ph.sb([128, 384], BF16, "XK") for _ in range(2)] for _ in range(8)]
    AkV = [ph.sb([128, 64], BF16, "AkV") for _ in range(8)]
    Usb = [ph.sb([128, 64], BF16, "Usb") for _ in range(8)]
    ynorm = [ph.sb([128, 128], BF16, "ynorm") for _ in range(4)]
    gst = [ph.sb([128, 6], F32, "gst") for _ in range(8)]
    gmv = [ph.sb([128, 2], F32, "gmv") for _ in range(8)]
    grs = [ph.sb([128, 1], F32, "grs") for _ in range(8)]
    gnm = [ph.sb([128, 1], F32, "gnm") for _ in range(8)]
    fin = [ph.sb([128, 128], F32, "fin") for _ in range(4)]
    kb_h = kb.psum_h
    PA = [ph.ps[0], ph.ps[1], ph.ps[2], ph.ps[3]]
    pai = [0]

    def nextpa():
        b = PA[pai[0] % 4]
        pai[0] += 1
        return b

    def c3(ap):
        return ap.rearrange("p (c t) -> p c t", c=4)
    evi = [0]

    import os
    EV = os.environ.get("RW2_EV", "")

    def evac_copy(out_ap, in_ap, reads, writes):
        evi[0] += 1
        eng = ("act", "dve")[evi[0] % 2]
        if EV:
            eng = EV
        ph.cp(eng, out_ap, in_ap, reads, writes)

    for tb in range(NTB):
        tcols = slice(tb * 512, (tb + 1) * 512)
        ph.dma("sp", XWA[:, :], dr["prT"][1536:1664, tcols], [], [XWA])
        ph.dma("pool", XG1[:, :], dr["prT"][1664:1792, tcols], [], [XG1])
        ph.dma("pool", XG2[:, :], dr["prT"][1792:1856, tcols], [], [XG2])
        ph.act(TH[:, :], XWA[0:64, :], AF.Tanh, [XWA], [TH])
        ph.act(SG1[:, :], XG1[:, :], AF.Sigmoid, [XG1], [SG1])
        ph.act(SG2[:, :], XG2[0:32, :], AF.Sigmoid, [XG2], [SG2])
        def prep_gen(hp):
                s = pers[hp]
                rin = RIN[hp % 2]
                R, Kt, Vt, VF = rin["R"], rin["Kt"], rin["Vt"], rin["VF"]
                gsb, bon, AR, BT, KT, tok, gam = (s[k] for k in ("gsb", "bon", "AR", "BT", "KT", "tok", "gam"))
                hc = slice(hp * 128, (hp + 1) * 128)
                P = lambda i: prm[:, i, hp:hp + 1]
                ph.dma("sp", R[:, :], dr["prT"][hp * 128:(hp + 1) * 128, tcols], [], [R])
                yield
                ph.dma("sp", Kt[:, :], dr["prT"][512 + hp * 128:512 + (hp + 1) * 128, tcols], [], [Kt])
                yield
                ph.dma("pool", Vt[:, :], dr["prT"][1024 + hp * 128:1024 + (hp + 1) * 128, tcols], [], [Vt])
                yield
                if l > 0:
                    ph.dma("pool", VF[:, :], dr["vfT"][hp * 128:(hp + 1) * 128, tcols], [], [VF])
                    yield
                sgw, cum, epos, eneg, edec, a_, kk, kkn, kp, bbt = TMP[hp % 2]
                BH, KH, VB = TB16[hp % 2]
                pa = nextpa()
                ph.mm(pa[:, :], WA[0:64, hc], TH[:, :], True, True, [WA, TH], [pa])
                yield
                ph.act(sgw[:, :], pa[:, :], AF.Sigmoid, [pa, prm], [sgw], bias=P(W0))
                yield
                ph.ts("dve", sgw[:, :], sgw[:, :], DECAY_C, None, ALU.mult, None, [sgw], [sgw])
                yield
                ph.add("dve", lambda e, cum=cum, sgw=sgw: e.tensor_tensor_scan(cum[:, :], cm01[:, :], sgw[:, :], 0.0,
                                                                              ALU.mult, ALU.add), [cm01, sgw], [cum])
                yield
                ph.act(epos[:, :], cum[:, :], AF.Exp, [cum], [epos])
                yield
                ph.act(eneg[:, :], cum[:, :], AF.Exp, [cum], [eneg], scale=-1.0)
                yield
                for c in range(4):
                    ph.act(edec[:, c * 128:(c + 1) * 128], cum[:, c * 128:(c + 1) * 128], AF.Exp, [cum], [edec],
                           scale=-1.0, bias=cum[:, c * 128 + 127:c * 128 + 128])
                    yield
                ph.cp("pool", gam[:, :], c3(epos[:, :])[:, :, 127], [epos], [gam])
                yield
                pa = nextpa()
                ph.mm(pa[:, :], WA[64:128, hc], XWA[64:128, :], True, True, [WA, XWA], [pa])
                yield
                ph.act(a_[:, :], pa[:, :], AF.Sigmoid, [pa, prm], [a_], bias=P(A0))
                yield
                ph.act(kk[:, :], Kt[:, :], AF.Square, [Kt, prm], [kk], scale=P(KK_))
                yield
                pa = nextpa()
                ph.mm(pa[:, :], bones[:, :], kk[:, :], True, True, [bones, kk], [pa])
                yield
                ph.ts("dve", kk[:, :], pa[:, :], 1e-24, None, ALU.max, None, [pa], [kk])
                yield
                ph.act(kk[:, :], kk[:, :], AF.Ln, [kk], [kk])
                yield
                ph.act(kk[:, :], kk[:, :], AF.Exp, [kk], [kk], scale=-0.5)
                yield
                ph.stt("dve", kkn[:, :], Kt[:, :], P(KK_), kk[:, :], ALU.mult, ALU.mult, [Kt, prm, kk], [kkn])
                yield
                ph.act(kp[:, :], a_[:, :], AF.Identity, [a_, prm, omk], [kp], scale=P(KA_), bias=omk[:, hp:hp + 1])
                yield
                ph.tt("pool", kp[:, :], kp[:, :], Kt[:, :], ALU.mult, [kp, Kt], [kp])
                yield
                ph.stt("dve", AR[:, :, 0, 1:128], c3(kkn[:, :])[:, :, 1:128], -1.0, c3(epos[:, :])[:, :, 0:127],
                       ALU.mult, ALU.mult, [kkn, epos], [AR])
                yield
                ph.ts("dve", AR[:, :, 0, 0:1], c3(kkn[:, :])[:, :, 0:1], -1.0, None, ALU.mult, None, [kkn], [AR])
                yield
                ph.tt("pool", AR[:, :, 1, :], c3(R[:, :]), c3(epos[:, :]), ALU.mult, [R, epos], [AR])
                yield
                ph.tt("pool", bbt[:, :], kkn[:, :], a_[:, :], ALU.mult, [kkn, a_], [bbt])
                yield
                ph.tt("dve", BT[:, :], bbt[:, :], eneg[:, :], ALU.mult, [bbt, eneg], [BT])
                yield
                ph.tt("pool", BH[:, :], bbt[:, :], edec[:, :], ALU.mult, [bbt, edec], [BH])
                yield
                ph.tt("dve", KT[:, :], kp[:, :], eneg[:, :], ALU.mult, [kp, eneg], [KT])
                yield
                ph.tt("pool", KH[:, :], kp[:, :], edec[:, :], ALU.mult, [kp, edec], [KH])
                yield
                if l > 0:
                    pa = nextpa()
                    ph.mm(pa[:, :], WGV[32:64, hc], XG2[32:64, :], True, True, [WGV, XG2], [pa])
                    yield
                    ph.act(a_[:, :], pa[:, :], AF.Sigmoid, [pa, prm], [a_], bias=P(V0_))
                    yield
                    ph.tt("pool", sgw[:, :], VF[:, :], Vt[:, :], ALU.subtract, [VF, Vt], [sgw])
                    yield
                    ph.tt("dve", sgw[:, :], sgw[:, :], a_[:, :], ALU.mult, [sgw, a_], [sgw])
                    yield
                    ph.tt("pool", sgw[:, :], sgw[:, :], Vt[:, :], ALU.add, [sgw, Vt], [sgw])
                    yield
                    vsrc = sgw
                else:
                    vsrc = Vt
                ph.cp("act", VB[:, :], vsrc[:, :], [vsrc], [VB])
                yield
                pa = nextpa()
                ph.mm(pa[:, :], WG1[:, hc], SG1[:, :], True, False, [WG1, SG1], [pa])
                yield
                ph.mm(pa[:, :], WGV[0:32, hc], SG2[:, :], False, True, [WGV, SG2], [pa])
                yield
                ph.cp("act", gsb[:, :], pa[:, :], [pa], [gsb])
                yield
                ph.stt("dve", bbt[:, :], R[:, :], P(RK_), kp[:, :], ALU.mult, ALU.mult, [R, prm, kp], [bbt])
                yield
                pa = nextpa()
                ph.mm(pa[:, :], bones[:, :], bbt[:, :], True, True, [bones, bbt], [pa])
                yield
                ph.tt("dve", bon[:, :], pa[:, :], vsrc[:, :], ALU.mult, [pa, vsrc], [bon])
                yield
                for c in range(4):
                    trb = ph.ps[6 + c // 2]
                    tv = trb.ap[:, :].bitcast(BF16)
                    base = (c % 2) * 512
                    cs = slice(c * 128, (c + 1) * 128)
                    for qi, (src, sap) in enumerate(((AR, AR[:, c, 0, :]), (BH, BH[:, cs]), (KH, KH[:, cs]), (VB, VB[:, cs]))):
                        ph.tr(tv[:, base + qi * 128:base + (qi + 1) * 128], sap, identbf[:, :], [src, identbf], [trb])
                    ev = tv[:, base:base + 512].rearrange("p (q t) -> p q t", q=4)
                    evac_copy(tok[c][:, :, :], ev, [trb], [tok[c]])
                    yield
        for pair in ((0, 1), (2, 3)):
            gens = [prep_gen(hp) for hp in pair]
            while gens:
                for g_ in list(gens):
                    try:
                        next(g_)
                    except StopIteration:
                        gens.remove(g_)
        import os
        STOP = int(os.environ.get("RW2_STOP", "99"))
        if STOP <= 1:
            continue
        for hp in range(4):
            s = pers[hp]
            AR, BT, KT, tok, keepM, Uloc, WTsb = (s[k] for k in ("AR", "BT", "KT", "tok", "keepM", "Uloc", "WTsb"))
            units = [(c, e) for c in range(4) for e in range(2)]
            for sw in range(4):
                for ui2 in range(2):
                    u = sw * 2 + ui2
                    c, e = units[u]
                    rows = slice(e * 64, (e + 1) * 64)
                    cs = slice(c * 128, (c + 1) * 128)
                    gb = ph.ps[4 + ui2]
                    g3 = ph.ps[2 + ui2]
                    arr = AR[rows, c, :, :].rearrange("p a t -> p (a t)")
                    ph.mm(gb[:, 0:256], BT[rows, cs], arr, True, True, [BT, AR], [gb])
                    ph.mm(gb[:, 256:512], KT[rows, cs], arr, True, True, [KT, AR], [gb])
                    ph.mm(g3[:, 0:128], AR[rows, c, 0, :], BT[rows, cs], True, True, [AR, BT], [g3])
                for ui2 in range(2):
                    u = sw * 2 + ui2
                    gb = ph.ps[4 + ui2]
                    g3 = ph.ps[2 + ui2]
                    ph.tt("dve", M4[u][:, :], gb[:, 0:512], mask4[:, :], ALU.mult, [gb, mask4], [M4[u]])
                    ph.tt("dve", P0[u][:, :], g3[:, 0:128], maskSL[:, :], ALU.mult, [g3, maskSL], [P0[u]])
                    ph.tt("pool", XK[u][1][:, 256:384], M4[u][:, 0:128], identbf[:, :], ALU.add, [M4[u], identbf], [XK[u][1]])
                    ph.cp("pool", keepM[u][:, 0:128], M4[u][:, 128:256], [M4[u]], [keepM[u]])
                    ph.cp("pool", keepM[u][:, 128:256], M4[u][:, 384:512], [M4[u]], [keepM[u]])
            if STOP <= 2:
                continue
            for k in range(7):
                for rnd in range(1):
                    us_ = range(8)
                    for u in us_:
                        cb = ph.ps[u]
                        if k == 0:
                            Pk, Qk, Pt_, Qt_ = P0[u][:, :], M4[u][:, 0:128], P0[u], M4[u]
                        else:
                            xk = XK[u][k % 2]
                            Pk, Qk, Pt_, Qt_ = xk[:, 0:128], xk[:, 128:256], xk, xk
                        if k < 6:
                            ph.mm(cb[:, 0:128], Qk, Pk, True, True, [Pt_, Qt_], [cb])
                        if k == 0:
                            ph.mm(cb[:, 128:256], Pk, Qk, True, True, [Pt_, Qt_], [cb])
                        elif k < 5:
                            ph.mm(cb[:, 128:384], Pk, xk[:, 128:384], True, True, [xk], [cb])
                        else:
                            ph.mm(cb[:, 256:384], Pk, xk[:, 256:384], True, True, [xk], [cb])
                    for u in us_:
                        cb = ph.ps[u]
                        xn = XK[u][(k + 1) % 2]
                        if k < 5:
                            ph.cp("act", xn[:, 0:256], cb[:, 0:256], [cb], [xn])
                        elif k == 5:
                            ph.cp("act", xn[:, 0:128], cb[:, 0:128], [cb], [xn])
                        if k >= 1:
                            xk = XK[u][k % 2]
                            ph.tt("dve", xn[:, 256:384], cb[:, 256:384], xk[:, 256:384], ALU.add, [cb, xk], [xn])
            if STOP <= 3:
                continue
            for u in range(8):
                c, e = units[u]
                rows = slice(e * 64, (e + 1) * 64)
                TTf = XK[u][1]
                Vtok = tok[c][:, 3, rows]
                ph.mm(ph.ps[2][:, u * 64:(u + 1) * 64], M4[u][:, 256:384], Vtok, True, True, [M4[u], tok[c]], [ph.ps[2]])
                wb = ph.ps[3 + u // 4]
                ph.mm(wb[:, (u % 4) * 128:(u % 4 + 1) * 128], tok[c][:, 0, :], TTf[:, 256:384], True, True, [tok[c], TTf], [wb])
            for u in range(8):
                c, e = units[u]
                rows = slice(e * 64, (e + 1) * 64)
                evac_copy(AkV[u][:, :], ph.ps[2][:, u * 64:(u + 1) * 64], [ph.ps[2]], [AkV[u]])
                wb = ph.ps[3 + u // 4]
                evac_copy(WTsb[c][rows, :], wb[rows, (u % 4) * 128:(u % 4 + 1) * 128], [wb], [WTsb[c]])
            for u in range(8):
                ph.mm(ph.ps[5][:, u * 64:(u + 1) * 64], XK[u][1][:, 256:384], AkV[u][:, :], True, True, [XK[u][1], AkV[u]], [ph.ps[5]])
            for u in range(8):
                evac_copy(Uloc[u][:, :], ph.ps[5][:, u * 64:(u + 1) * 64], [ph.ps[5]], [Uloc[u]])
        for c in range(4 if STOP > 4 else 0):
            cs = slice(c * 128, (c + 1) * 128)
            HE = [(hp, e) for hp in range(4) for e in range(2)]

            def reg(hp, e, j):
                return ph.ps[2 + e * 2 + (hp % 2)], (hp // 2) * 192 + j * 64
            for hp, e in HE:
                s = pers[hp]
                rows = slice(e * 64, (e + 1) * 64)
                b, o = reg(hp, e, 0)
                ph.mm(b[:, o:o + 64], s["WTsb"][c][rows, :], Hb[hp][rows, :], True, True, [s["WTsb"][c], Hb[hp]], [b])
            for hp, e in HE:
                s = pers[hp]
                b, o = reg(hp, e, 0)
                us = Usb[hp * 2 + e]
                ph.tt("dve", us[:, :], b[:, o:o + 64], s["Uloc"][c * 2 + e][:, :], ALU.add, [b, s["Uloc"][c * 2 + e]], [us])
            if STOP <= 5:
                continue
            for hp, e in HE:
                s = pers[hp]
                rows = slice(e * 64, (e + 1) * 64)
                us = Usb[hp * 2 + e]
                km = s["keepM"][c * 2 + e]
                tokc = s["tok"][c]
                Vtok = tokc[:, 3, rows]
                b, o = reg(hp, e, 1)
                ph.mm(b[:, o:o + 64], s["AR"][rows, c, 1, :], Hb[hp][rows, :], True, False, [s["AR"], Hb[hp]], [b])
                ph.mm(b[:, o:o + 64], km[:, 0:128], us[:, :], False, False, [km, us], [b])
                ph.mm(b[:, o:o + 64], km[:, 128:256], Vtok, False, True, [km, tokc], [b])
                b, o = reg(hp, e, 2)
                ph.mm(b[:, o:o + 64], tokc[:, 1, :], us[:, :], True, False, [tokc, us], [b])
                ph.mm(b[:, o:o + 64], tokc[:, 2, :], Vtok, False, True, [tokc], [b])
            if STOP <= 6:
                continue
            for hp, e in HE:
                s = pers[hp]
                rows = slice(e * 64, (e + 1) * 64)
                b, o = reg(hp, e, 2)
                ph.stt("dve", Hs[hp][rows, :], Hs[hp][rows, :], s["gam"][rows, c:c + 1], b[rows, o:o + 64],
                       ALU.mult, ALU.add, [Hs[hp], s["gam"], b], [Hs[hp]])
                ph.cp("act", Hb[hp][rows, :], Hs[hp][rows, :], [Hs[hp]], [Hb[hp]])
            if STOP <= 7:
                continue
            for hp, e in HE:
                i8 = hp * 2 + e
                b, o = reg(hp, e, 1)
                ph.add("dve", lambda en, i8=i8, b=b, o=o: en.bn_stats(gst[i8][:, 0:6], b[:, o:o + 64]), [b], [gst[i8]])
                ph.add("dve", lambda en, i8=i8: en.bn_aggr(gmv[i8][:, 0:2], gst[i8][:, 0:6]), [gst[i8]], [gmv[i8]])
            for hp, e in HE:
                i8 = hp * 2 + e
                ph.act(grs[i8][:, 0:1], gmv[i8][:, 1:2], AF.Ln, [gmv[i8]], [grs[i8]], bias=epsG[:, 0:1])
            for hp, e in HE:
                i8 = hp * 2 + e
                ph.act(grs[i8][:, 0:1], grs[i8][:, 0:1], AF.Exp, [grs[i8]], [grs[i8]], scale=-0.5)
            for hp, e in HE:
                i8 = hp * 2 + e
                ph.ts("dve", gnm[i8][:, 0:1], gmv[i8][:, 0:1], grs[i8][:, 0:1], -1.0, ALU.mult, ALU.mult,
                      [gmv[i8], grs[i8]], [gnm[i8]])
            for hp, e in HE:
                i8 = hp * 2 + e
                b, o = reg(hp, e, 1)
                ph.act(ynorm[hp][:, e * 64:(e + 1) * 64], b[:, o:o + 64], AF.Identity, [b, grs[i8], gnm[i8]], [ynorm[hp]],
                       bias=gnm[i8][:, 0:1], scale=grs[i8][:, 0:1])
            if STOP <= 8:
                continue
            tv = ph.ps[6].ap[:, :].bitcast(BF16)
            for hp in range(4):
                ph.tr(tv[:, hp * 128:(hp + 1) * 128], ynorm[hp][:, :], identbf[:, :], [ynorm[hp], identbf], [ph.ps[6]])
            for hp in range(4):
                s = pers[hp]
                P = lambda i: prm[:, i, hp:hp + 1]
                ph.ts("dve", fin[hp][:, :], tv[:, hp * 128:(hp + 1) * 128], P(GG_), P(GB_), ALU.mult, ALU.add,
                      [ph.ps[6], prm], [fin[hp]])
                ph.tt("pool", fin[hp][:, :], fin[hp][:, :], s["bon"][:, cs], ALU.add, [fin[hp], s["bon"]], [fin[hp]])
                ph.tt("pool", s["outT"][:, cs], fin[hp][:, :], s["gsb"][:, cs], ALU.mult, [fin[hp], s["gsb"]], [s["outT"]])
        for hp in range(4):
            ph.dma("sp", dr["mixT"][512 + hp * 128:512 + (hp + 1) * 128, tcols], pers[hp]["outT"][:, :], [pers[hp]["outT"]], [])
    ph.finish()
```

```python
from contextlib import ExitStack
import math
import numpy as np
import concourse.bass as bass
import concourse.mybir as mybir
from concourse.bass_utils import run_bass_kernel_spmd

F32 = mybir.dt.float32
BF16 = mybir.dt.bfloat16
AF = mybir.ActivationFunctionType
ALU = mybir.AluOpType
AX = mybir.AxisListType

ENGS = ("pe", "act", "dve", "pool", "sp")
NPOOL = 14


class T:
    __slots__ = ("ap", "name", "w", "r", "psum")

    def __init__(self, ap, name=""):
        self.ap = ap
        self.name = name
        self.w = None
        self.r = []
        self.psum = False

    def __getitem__(self, k):
        return self.ap[k]


class Op:
    __slots__ = ("eng", "fn", "dma", "deps", "signal", "sem", "val", "prewait")

    def __init__(self, eng, fn, dma):
        self.eng = eng
        self.fn = fn
        self.dma = dma
        self.deps = []
        self.signal = False
        self.sem = None
        self.val = 0
        self.prewait = None


class Ctx:
    def __init__(self, nc, stack):
        self.nc = nc
        self.esem = {e: stack.enter_context(nc.semaphore("s_" + e)) for e in ENGS}
        self.dsem = {q: [stack.enter_context(nc.semaphore("d_%s%d" % (q, i))) for i in range(NPOOL)]
                     for q in ("sp", "pool", "act")}
        self.count = {e: 0 for e in ENGS}
        self.dk = {q: 0 for q in ("sp", "pool", "act")}
        self.seen = {e: {} for e in ENGS}
        self.ninst = 0


class Phase:
    def __init__(self, ctx, name):
        self.ctx = ctx
        self.name = name
        self.ops = []
        self.tiles = []

    def tile(self, ap, name=""):
        t = T(ap, name)
        self.tiles.append(t)
        return t

    def add(self, eng, fn, reads=(), writes=(), dma=False):
        op = Op(eng, fn, dma)
        deps = []
        for t in reads:
            if t.w is not None:
                deps.append((t.w, "raw"))
            if t.psum:
                for r in t.r:
                    if r.eng != eng:
                        deps.append((r, "rar"))
        for t in writes:
            if t.w is not None:
                deps.append((t.w, "waw"))
            for r in t.r:
                deps.append((r, "war"))
        seen = set()
        for d, kind in deps:
            if d is op or id(d) in seen:
                continue
            if not d.dma and not op.dma and d.eng == eng:
                if eng == "pe":
                    continue
                if kind == "war" and eng != "pool":
                    continue
            seen.add(id(d))
            op.deps.append(d)
            d.signal = True
        for t in writes:
            t.w = op
            t.r = []
        for t in reads:
            if t.w is not op:
                t.r.append(op)
        self.ops.append(op)
        return op

    def emit(self):
        ctx = self.ctx
        nc = ctx.nc
        fin = Op("sp", None, False)
        lastd = {}
        for o in self.ops:
            if o.dma:
                o.signal = True
        self.ops.append(fin)
        for o in self.ops:
            if o.dma:
                q = o.eng
                k = ctx.dk[q]
                ctx.dk[q] += 1
                o.sem = ctx.dsem[q][k % NPOOL]
                o.val = 16 * (k // NPOOL + 1)
                if o.val > 16:
                    o.prewait = (o.sem, o.val - 16)
            elif o.signal:
                ctx.count[o.eng] += 1
                o.sem = ctx.esem[o.eng]
                o.val = ctx.count[o.eng]
            if o.dma:
                lastd[o.sem.name] = o
        fin.deps = list(lastd.values())
        per = {e: [o for o in self.ops if o.eng == e] for e in ENGS}
        bname = {"pe": "tensor", "act": "scalar", "dve": "vector", "pool": "gpsimd", "sp": "sync"}

        def run(e, engine):
            seen = ctx.seen[e]
            for o in per[e]:
                waits = [(d.sem, d.val) for d in o.deps]
                if o.prewait is not None:
                    waits.append(o.prewait)
                for sem, val in waits:
                    if seen.get(sem.name, 0) < val:
                        engine.wait_ge(sem, val)
                        seen[sem.name] = val
                        ctx.ninst += 1
                if o.fn is None:
                    continue
                inst = o.fn(engine)
                ctx.ninst += 1
                if o.signal:
                    inst.then_inc(o.sem, 16 if o.dma else 1)

        with nc.Block() as block:
            for e in ENGS:
                if per[e]:
                    getattr(block, bname[e])(lambda engine, e=e: run(e, engine))
        for t in self.tiles:
            t.w = None
            t.r = []


D = 1024
LN_EPS = 1e-5


class KB:
    def __init__(self, S):
        self.S = S
        self.NT = S // 128
        self.nc = bass.Bass("TRN2", target_bir_lowering=False)
        self.stack = ExitStack()
        self.ctx = Ctx(self.nc, self.stack)
        self.dram = {}
        self.psum_h = [self.stack.enter_context(self.nc.psum_tensor("ps%d" % i, [128, 512], F32))
                       for i in range(8)]

    def din(self, name, shape, dt=F32):
        t = self.nc.dram_tensor(name, list(shape), dt, kind="ExternalInput")
        self.dram[name] = t
        return t

    def dout(self, name, shape, dt=F32):
        t = self.nc.dram_tensor(name, list(shape), dt, kind="ExternalOutput")
        self.dram[name] = t
        return t

    def dscr(self, name, shape, dt=F32, debug=False):
        t = self.nc.dram_tensor(name, list(shape), dt, kind="ExternalOutput" if debug else "Internal")
        self.dram[name] = t
        return t

    def close(self):
        self.stack.close()


class PH(Phase):
    def __init__(self, kb, name):
        super().__init__(kb.ctx, name)
        self.kb = kb
        self.nc = kb.nc
        self.st = ExitStack()
        self.ps = [self.tile(h, "ps%d" % i) for i, h in enumerate(kb.psum_h)]
        for t in self.ps:
            t.psum = True
        self.nsb = 0

    def sb(self, shape, dt=F32, name=None):
        self.nsb += 1
        h = self.st.enter_context(self.nc.sbuf_tensor("%s_%s%d" % (self.name, name or "t", self.nsb), list(shape), dt))
        return self.tile(h, name or "t")

    def dt_(self, name):
        key = "_dt_" + name
        if not hasattr(self, key):
            setattr(self, key, self.tile(self.kb.dram[name], name))
        return getattr(self, key)

    def finish(self):
        self.emit()
        self.st.close()

    def dma(self, q, out_ap, in_ap, reads, writes):
        return self.add(q, lambda e: e.dma_start(out=out_ap, in_=in_ap), reads, writes, dma=True)

    def mm(self, out_ap, lhsT, rhs, start, stop, reads, writes, skip=False):
        if skip:
            return self.add("pe", lambda e: e.matmul(out_ap, lhsT, rhs, start=start, stop=stop, skip_group_check=True),
                            reads, writes)
        return self.add("pe", lambda e: e.matmul(out_ap, lhsT, rhs, start=start, stop=stop), reads, writes)

    def tr(self, out_ap, in_ap, ident_ap, reads, writes):
        return self.add("pe", lambda e: e.transpose(out_ap, in_ap, ident_ap), reads, writes)

    def act(self, out_ap, in_ap, func, reads, writes, bias=0.0, scale=1.0, accum_out=None, eng="act"):
        kw = {}
        if accum_out is not None:
            kw["accum_out"] = accum_out
        return self.add(eng, lambda e: e.activation(out_ap, in_ap, func, bias=bias, scale=scale, **kw), reads, writes)

    def tt(self, eng, out_ap, in0, in1, op, reads, writes):
        return self.add(eng, lambda e: e.tensor_tensor(out_ap, in0, in1, op), reads, writes)

    def ts(self, eng, out_ap, in0, s1, s2, op0, op1, reads, writes, accum_out=None):
        if s2 is None:
            return self.add(eng, lambda e: e.tensor_scalar(out_ap, in0, s1, None, op0), reads, writes)
        if accum_out is not None:
            return self.add(eng, lambda e: e.tensor_scalar(out_ap, in0, s1, s2, op0, op1, accum_out=accum_out), reads, writes)
        return self.add(eng, lambda e: e.tensor_scalar(out_ap, in0, s1, s2, op0, op1), reads, writes)

    def stt(self, eng, out_ap, in0, scalar, in1, op0, op1, reads, writes):
        return self.add(eng, lambda e: e.scalar_tensor_tensor(out_ap, in0, scalar, in1, op0, op1), reads, writes)

    def cp(self, eng, out_ap, in_ap, reads, writes):
        if eng == "act":
            return self.add(eng, lambda e: e.copy(out_ap, in_ap), reads, writes)
        return self.add(eng, lambda e: e.tensor_copy(out_ap, in_ap), reads, writes)


def bcast_row(dram_ap_1d, n):
    return dram_ap_1d.partition_broadcast(128)


def ln_rows(ph, src, i, gB, bB, ident_bf, xres_name, xT_name, bufs, psb):
    nc = ph.nc
    st, mv, rs, nm, xn, ybf, xTs = (bufs[k] for k in ("st", "mv", "rs", "nm", "xn", "ybf", "xTs"))
    xh = bufs["xh"]
    for h in range(2):
        ph.add("dve", lambda e, h=h: e.bn_stats(st[:, h * 6:(h + 1) * 6], src[:, h * 512:(h + 1) * 512]), [src], [st])
    ph.add("dve", lambda e: e.bn_aggr(mv[:, 0:2], st[:, 0:12]), [st], [mv])
    ph.act(rs[:, 0:1], mv[:, 1:2], AF.Ln, [mv], [rs], bias=bufs["eps"][:, 0:1])
    ph.act(rs[:, 0:1], rs[:, 0:1], AF.Exp, [rs], [rs], scale=-0.5)
    ph.ts("dve", nm[:, 0:1], mv[:, 0:1], rs[:, 0:1], -1.0, ALU.mult, ALU.mult, [mv, rs], [nm])
    xres = ph.kb.dram[xres_name]
    for h in range(2):
        hs = slice(h * 512, (h + 1) * 512)
        ph.act(xn[:, hs], src[:, hs], AF.Identity, [src, rs, nm], [xh[h]], bias=nm[:, 0:1], scale=rs[:, 0:1])
        ph.tt("dve", xn[:, hs], xn[:, hs], gB[:, hs], ALU.mult, [xh[h], gB], [xh[h]])
        ph.tt(("dve", "pool")[h], xn[:, hs], xn[:, hs], bB[:, hs], ALU.add, [xh[h], bB], [xh[h]])
        ph.dma("sp", xres[i * 128:(i + 1) * 128, hs], xn[:, hs], [xh[h]], [])
    if xT_name is None:
        return
    pb = psb.ap[:, :].bitcast(BF16)
    for h in range(2):
        hs = slice(h * 512, (h + 1) * 512)
        ph.cp("act", ybf[:, hs], xn[:, hs], [xh[h]], [ybf])
    for c in range(8):
        ph.tr(pb[:, c * 128:(c + 1) * 128], ybf[:, c * 128:(c + 1) * 128], ident_bf[:, :], [ybf, ident_bf], [psb])
    ph.cp("dve", xTs[:, :], pb[:, :], [psb], [xTs])
    xT = ph.kb.dram[xT_name]
    dst = xT[:, i * 128:(i + 1) * 128].rearrange("(c p) t -> p c t", p=128)
    ph.dma("pool", dst, xTs[:, :].rearrange("p (c t) -> p c t", c=8), [xTs], [])


def ln_bufs(ph, k):
    eps = ph.sb([128, 1], F32, "eps%d" % k)
    ph.add("pool", lambda e: e.memset(eps[:, :], LN_EPS), [], [eps])
    xn = ph.sb([128, 1024], F32, "xn%d" % k)
    xh = [ph.tile(xn.ap[:, 0:512], "xnA"), ph.tile(xn.ap[:, 512:1024], "xnB")]
    return dict(eps=eps, st=ph.sb([128, 12], F32, "st%d" % k), mv=ph.sb([128, 2], F32, "mv%d" % k),
                rs=ph.sb([128, 1], F32, "rs%d" % k), nm=ph.sb([128, 1], F32, "nm%d" % k),
                xn=xn, xh=xh, ybf=ph.sb([128, 1024], BF16, "ybf%d" % k),
                xTs=ph.sb([128, 1024], BF16, "xTs%d" % k))


def load_consts(ph):
    c = {}
    c["ident_bf"] = ph.sb([128, 128], BF16, "identbf")
    c["ident_f"] = ph.sb([128, 128], F32, "identf")
    ph.dma("sp", c["ident_f"][:, :], ph.kb.dram["c_ident"][:, :], [], [c["ident_f"]])
    ph.cp("dve", c["ident_bf"][:, :], c["ident_f"][:, :], [c["ident_f"]], [c["ident_bf"]])
    return c


def phase_ln0(kb):
    ph = PH(kb, "ln0")
    S, NT = kb.S, kb.NT
    c = load_consts(ph)
    gB = ph.sb([128, 1024], F32, "gB")
    bB = ph.sb([128, 1024], F32, "bB")
    ph.dma("sp", gB[:, :], kb.dram["ln_in_g"][:].partition_broadcast(128), [], [gB])
    ph.dma("sp", bB[:, :], kb.dram["ln_in_b"][:].partition_broadcast(128), [], [bB])
    NB = 3
    xin = [ph.sb([128, 1024], F32, "xin%d" % k) for k in range(NB)]
    bufs = [ln_bufs(ph, k) for k in range(4)]
    for i in range(NT):
        src = xin[i % NB]
        ph.dma("sp", src[:, :], kb.dram["x"][i * 128:(i + 1) * 128, :], [], [src])
        ln_rows(ph, src, i, gB, bB, c["ident_bf"], "xres", "xT", bufs[i % 4], ph.ps[i % 4])
    ph.finish()


N_DIFF = 1536
NRW = [1824, 1856]


def phase_inproj(kb, l):
    ph = PH(kb, "ip%d" % l)
    S, NT = kb.S, kb.NT
    NG = S // 512
    ncol = N_DIFF + NRW[l]
    nrw_tiles = 15
    w_dram = kb.dram["w_in%d" % l]
    xT_sb = ph.sb([128, 8, S], BF16, "xT")
    xT = kb.dram["xT"]
    for c in range(8):
        ph.dma("sp" if c % 2 == 0 else "pool", xT_sb[:, c, :], xT[c * 128:(c + 1) * 128, :], [], [xT_sb])
    wbf = [ph.sb([128, ncol], BF16, "wbf%d" % c) for c in range(8)]
    hc = ncol // 2
    wst = [ph.sb([128, hc], F32, "wst%d" % k) for k in range(2)]
    for c in range(8):
        for hh in range(2):
            s = wst[hh]
            ph.dma("act", s[:, :], w_dram[c * 128:(c + 1) * 128, hh * hc:(hh + 1) * hc], [], [s])
            ph.cp(("dve", "pool")[hh], wbf[c][:, hh * hc:(hh + 1) * hc], s[:, :], [s], [wbf[c]])
    mu_sb = ph.sb([128, 15], F32, "mu")
    ph.dma("sp", mu_sb[:, :], kb.dram["mu%d" % l][:, :], [], [mu_sb])
    bank = [0]

    def nextbank():
        b = ph.ps[bank[0] % 8]
        bank[0] += 1
        return b

    ob = [ph.sb([128, 512], BF16, "ob%d" % k) for k in range(4)]
    oi = 0
    for which, name, scale in ((0, "qT", 0.125), (1, "kT", 1.0)):
        for h in range(4):
            col0 = which * 512 + h * 128
            for g in range(NG):
                pb = nextbank()
                for c in range(8):
                    ph.mm(pb[:, :], wbf[c][:, col0:col0 + 128], xT_sb[:, c, g * 512:(g + 1) * 512],
                          c == 0, c == 7, [wbf[c], xT_sb], [pb])
                o = ob[oi % 4]
                oi += 1
                if oi % 2 == 0:
                    ph.act(o[:, :], pb[:, :], AF.Copy, [pb], [o], scale=scale)
                else:
                    ph.ts("dve", o[:, :], pb[:, :], scale, None, ALU.mult, None, [pb], [o])
                ph.dma("sp", kb.dram[name][h, :, g * 512:(g + 1) * 512], o[:, :], [o], [])
    for i in range(NT):
        pb = nextbank()
        for c in range(8):
            ph.mm(pb[:, :], xT_sb[:, c, i * 128:(i + 1) * 128], wbf[c][:, 1024:1536], c == 0, c == 7,
                  [wbf[c], xT_sb], [pb])
        o = ob[oi % 4]
        oi += 1
        if oi % 2 == 0:
            ph.cp("act", o[:, :], pb[:, :], [pb], [o])
        else:
            ph.cp("dve", o[:, :], pb[:, :], [pb], [o])
        ph.dma("sp", kb.dram["vd"][i * 128:(i + 1) * 128, :], o[:, :], [o], [])
    pfull = [ph.sb([128, S + 1], F32, "pfull%d" % k) for k in range(2)]
    tmp = [ph.sb([128, S], F32, "tmp%d" % k) for k in range(1)]
    for k in range(2):
        ph.add("pool", lambda e, k=k: e.memset(pfull[k][:, 0:1], 0.0), [], [pfull[k]])
    for t in range(nrw_tiles):
        col0 = N_DIFF + t * 128
        rows = min(128, ncol - col0)
        pf = pfull[t % 2]
        tm = tmp[0]
        for g in range(NG):
            pb = nextbank()
            for c in range(8):
                ph.mm(pb[0:rows, :], wbf[c][:, col0:col0 + rows], xT_sb[:, c, g * 512:(g + 1) * 512],
                      c == 0, c == 7, [wbf[c], xT_sb], [pb])
            ph.cp("act", pf[0:rows, 1 + g * 512:1 + (g + 1) * 512], pb[0:rows, :], [pb], [pf])
        ph.tt("pool", tm[0:rows, :], pf[0:rows, 0:S], pf[0:rows, 1:S + 1], ALU.subtract, [pf], [tm])
        ph.stt("dve", tm[0:rows, :], tm[0:rows, :], mu_sb[0:rows, t:t + 1], pf[0:rows, 1:S + 1], ALU.mult, ALU.add,
               [tm, mu_sb, pf], [tm])
        ph.dma("sp", kb.dram["prT"][t * 128:t * 128 + rows, :], tm[0:rows, :], [tm], [])
        if l == 0 and 8 <= t < 12:
            ph.dma("pool", kb.dram["vfT"][(t - 8) * 128:(t - 7) * 128, :], tm[0:rows, :], [tm], [])
    ph.finish()


LAM_INIT = [0.8 - 0.6 * math.exp(-0.3 * l) for l in range(2)]
SUBLN_EPS = 1e-5


def phase_pre(kb):
    ph = PH(kb, "pre")
    nc = kb.nc
    dr = kb.dram
    lt = ph.sb([1, 8, 64], F32, "lt")
    for i, nm in enumerate(("lambda_q1", "lambda_k1", "lambda_q2", "lambda_k2")):
        ph.dma("sp", lt[0:1, 2 * i:2 * i + 2, :], dr[nm][:, :].rearrange("(o l) d -> o l d", o=1), [], [lt])
    pr = ph.sb([1, 4, 64], F32, "pr")
    ph.tt("dve", pr[0:1, 0:2, :], lt[0:1, 0:2, :], lt[0:1, 2:4, :], ALU.mult, [lt], [pr])
    ph.tt("dve", pr[0:1, 2:4, :], lt[0:1, 4:6, :], lt[0:1, 6:8, :], ALU.mult, [lt], [pr])
    sm = ph.sb([1, 4], F32, "sm")
    ph.add("dve", lambda e: e.reduce_sum(sm[0:1, 0:4], pr[0:1, :, :], AX.X), [pr], [sm])
    ex = ph.sb([1, 4], F32, "ex")
    ph.act(ex[0:1, :], sm[0:1, :], AF.Exp, [sm], [ex])
    lam = ph.sb([1, 2], F32, "lam")
    ph.tt("dve", lam[0:1, :], ex[0:1, 2:4], ex[0:1, 0:2], ALU.subtract, [ex], [lam])
    for l in range(2):
        ph.ts("dve", lam[0:1, l:l + 1], lam[0:1, l:l + 1], -LAM_INIT[l], None, ALU.add, None, [lam], [lam])
    ph.dma("sp", dr["lamneg"][0:1, :], lam[0:1, :], [lam], [])
    rb = ph.sb([32, 4], F32, "rb")
    oh = ph.sb([32, 385], F32, "oh")
    ph.dma("sp", rb[:, :], dr["rel_bias"][:, :], [], [rb])
    ph.dma("sp", oh[:, :], dr["c_oh"][:, :], [], [oh])
    J = ph.sb([128, 128], F32, "J")
    Mc = ph.sb([128, 128], F32, "Mc")
    ph.dma("pool", J[:, :], dr["c_J"][:, :], [], [J])
    ph.dma("pool", Mc[:, :], dr["c_causal"][:, :], [], [Mc])
    pb = ph.ps[0]
    ph.mm(pb[0:4, 0:385], rb[:, :], oh[:, :], True, True, [rb, oh], [pb])
    G = ph.sb([4, 385], F32, "G")
    ph.cp("dve", G[:, :], pb[0:4, 0:385], [pb], [G])
    EG = ph.sb([4, 384], F32, "EG")
    ph.ts("dve", EG[:, :], G[:, 0:384], G[:, 384:385], None, ALU.subtract, None, [G], [EG])
    ph.act(EG[:, :], EG[:, :], AF.Exp, [EG], [EG])
    egd = ph.dt_("EGd")
    ph.dma("sp", dr["EGd"][:, :], EG[:, :], [EG], [egd])
    for h in range(4):
        for ty in range(2):
            tr = ph.sb([128, 128], F32, "trev")
            src = bass.AP(dr["EGd"], h * 384 + 128 * ty, [[1, 128], [1, 128]])
            ph.dma("sp", tr[:, :], src, [egd], [tr])
            p2 = ph.ps[1 + (h * 2 + ty) % 4]
            ph.mm(p2[:, 0:128], J[:, :], tr[:, :], True, True, [J, tr], [p2])
            eb = ph.sb([128, 128], F32, "eb")
            if ty == 0:
                ph.tt("dve", eb[:, :], p2[:, 0:128], Mc[:, :], ALU.mult, [p2, Mc], [eb])
            else:
                ph.cp("dve", eb[:, :], p2[:, 0:128], [p2], [eb])
            ph.dma("sp", dr["Eb"][h, ty, :, :], eb[:, :], [eb], [])
    ph.finish()


def phase_attn(kb, l):
    ph = PH(kb, "at%d" % l)
    S, NT = kb.S, kb.NT
    NG = S // 512
    dr = kb.dram
    c = load_consts(ph)
    lamneg = ph.sb([128, 2], F32, "lamneg")
    ph.dma("sp", lamneg[:, :], bass.AP(dr["lamneg"], 0, [[0, 128], [1, 2]]), [], [lamneg])
    sg = ph.sb([128, 1], F32, "sg")
    ph.dma("sp", sg[:, :], dr["subln_g"][l:l + 1, :].rearrange("o d -> d o"), [], [sg])
    epsS = ph.sb([128, 1], F32, "epsS")
    ph.add("pool", lambda e: e.memset(epsS[:, :], SUBLN_EPS), [], [epsS])
    Eb = [[ph.sb([128, 128], F32, "Eb") for ty in range(2)] for h in range(2)]
    qT = [ph.sb([128, S], BF16, "qT") for _ in range(2)]
    kT = [ph.sb([128, S], BF16, "kT") for _ in range(2)]
    Vx = [ph.sb([128, NT, 129], BF16, "Vx") for _ in range(2)]
    for k in range(2):
        ph.add("pool", lambda e, k=k: e.memset(Vx[k][:, :, 128:129], 1.0), [], [Vx[k]])
    Pt = [ph.sb([128, 512], BF16, "Pt") for _ in range(4)]
    Pf = [ph.sb([128, 256], F32, "Pf") for _ in range(2)]
    Osb = [ph.sb([128, 480], F32, "Osb") for _ in range(3)]
    small = {k: ph.sb([128, 4], F32, k) for k in ("r1", "r2", "ss", "rstd")}
    t1 = [ph.sb([128, 128], F32, "t1") for _ in range(2)]
    ybf = [ph.sb([128, 128], BF16, "ybf") for _ in range(2)]
    outT = [ph.sb([128, 512], BF16, "outT") for _ in range(2)]
    Obank = [ph.ps[0], ph.ps[1], ph.ps[2]]
    Sbank = [ph.ps[3], ph.ps[4], ph.ps[5], ph.ps[6]]
    Tbank = ph.ps[7]
    pi = [0]
    pfi = [0]
    for h in range(4):
        hb = h % 2
        q_sb, k_sb, v_sb, E = qT[hb], kT[hb], Vx[hb], Eb[hb]
        ph.dma("sp", q_sb[:, :], dr["qT"][h, :, :], [], [q_sb])
        ph.dma("pool", k_sb[:, :], dr["kT"][h, :, :], [], [k_sb])
        ph.dma("sp", v_sb[:, :, 0:128], dr["vd"][:, h * 128:(h + 1) * 128].rearrange("(t p) d -> p t d", p=128),
               [], [v_sb])
        for ty in range(2):
            ph.dma("pool", E[ty][:, :], dr["Eb"][h, ty, :, :], [], [E[ty]])
        for g in range(NG):
            steps = [(kbk, m) for kbk in range(4 * g + 4) for m in range(2)]
            sb_of = {}

            def emit_qk(si):
                kbk, m = steps[si]
                jlo = max(0, kbk - 4 * g)
                n = (4 - jlo) * 128
                sbk = Sbank[(kbk % 2) * 2 + m]
                sb_of[si] = sbk
                rows = slice(m * 64, (m + 1) * 64)
                ph.mm(sbk[:, 0:n], k_sb[rows, kbk * 128:(kbk + 1) * 128],
                      q_sb[rows, (4 * g + jlo) * 128:(4 * g + 4) * 128], True, True, [k_sb, q_sb], [sbk])

            def emit_rest(si):
                kbk, m = steps[si]
                jlo = max(0, kbk - 4 * g)
                n = (4 - jlo) * 128
                sbk = sb_of[si]
                P = Pt[pi[0] % 4]
                pi[0] += 1
                near = [(j, 4 * g + j - kbk) for j in range(jlo, 4) if 0 <= 4 * g + j - kbk <= 1]
                nn = len(near)
                if nn:
                    j0 = near[0][0]
                    c0 = (j0 - jlo) * 128
                    pf = Pf[pfi[0] % 2]
                    pfi[0] += 1
                    ph.act(pf[:, 0:nn * 128], sbk[:, c0:c0 + nn * 128], AF.Exp, [sbk], [pf])
                    for ii, (j, ty) in enumerate(near):
                        ph.tt("dve", P[:, c0 + ii * 128:c0 + (ii + 1) * 128], pf[:, ii * 128:(ii + 1) * 128],
                              E[ty][:, :], ALU.mult, [pf, E[ty]], [P])
                    far0 = c0 + nn * 128
                    if far0 < n:
                        ph.act(P[:, far0:n], sbk[:, far0:n], AF.Exp, [sbk], [P])
                else:
                    ph.act(P[:, 0:n], sbk[:, 0:n], AF.Exp, [sbk], [P])
                for j in range(jlo, 4):
                    idx = m * 4 + j
                    ob = Obank[idx // 3]
                    cc = (idx % 3) * 160
                    ph.mm(ob[:, cc:cc + 129], P[:, (j - jlo) * 128:(j - jlo + 1) * 128], v_sb[:, kbk, :],
                          kbk == 0 and idx % 3 == 0, kbk == 4 * g + j, [P, v_sb], [ob], skip=True)

            ns = len(steps)
            emit_qk(0)
            emit_qk(1)
            for si in range(0, ns, 2):
                if si + 2 < ns:
                    emit_qk(si + 2)
                    emit_qk(si + 3)
                emit_rest(si)
                emit_rest(si + 1)
            for b4 in range(3):
                if b4 % 2 == 0:
                    ph.cp("act", Osb[b4][:, :], Obank[b4][:, 0:480], [Obank[b4]], [Osb[b4]])
                else:
                    ph.cp("dve", Osb[b4][:, :], Obank[b4][:, 0:480], [Obank[b4]], [Osb[b4]])
            oT = outT[g % 2]
            tb = Tbank.ap[:, :].bitcast(BF16)
            for j in range(4):
                o0 = Osb[j // 3]
                o1 = Osb[(4 + j) // 3]
                cc = (j % 3) * 160
                cc1 = ((4 + j) % 3) * 160
                r1, r2, ss, rstd = (small[k] for k in ("r1", "r2", "ss", "rstd"))
                ph.add("dve", lambda e, o0=o0, cc=cc, j=j: e.reciprocal(r1[:, j:j + 1], o0[:, cc + 128:cc + 129]), [o0], [r1])
                ph.add("dve", lambda e, o1=o1, cc1=cc1, j=j: e.reciprocal(r2[:, j:j + 1], o1[:, cc1 + 128:cc1 + 129]), [o1], [r2])
                ph.ts("dve", r2[:, j:j + 1], r2[:, j:j + 1], lamneg[:, l:l + 1], None, ALU.mult, None, [r2, lamneg], [r2])
                tt1 = t1[j % 2]
                ph.ts("dve", tt1[:, :], o0[:, cc:cc + 128], r1[:, j:j + 1], None, ALU.mult, None, [o0, r1], [tt1])
                ph.stt("dve", tt1[:, :], o1[:, cc1:cc1 + 128], r2[:, j:j + 1], tt1[:, :], ALU.mult, ALU.add, [o1, r2, tt1], [tt1])
                yb = ybf[j % 2]
                ph.act(yb[:, :], tt1[:, :], AF.Square, [tt1], [yb, ss], accum_out=ss[:, j:j + 1])
                ph.act(rstd[:, j:j + 1], ss[:, j:j + 1], AF.Ln, [ss], [rstd], bias=epsS[:, 0:1], scale=1.0 / 128)
                ph.act(rstd[:, j:j + 1], rstd[:, j:j + 1], AF.Exp, [rstd], [rstd], scale=-0.5)
                ph.act(yb[:, :], tt1[:, :], AF.Copy, [tt1, rstd], [yb], scale=rstd[:, j:j + 1])
                ph.tr(tb[:, j * 128:(j + 1) * 128], yb[:, :], c["ident_bf"][:, :], [yb, c["ident_bf"]], [Tbank])
            ph.ts("dve", oT[:, :], tb[:, 0:512], sg[:, 0:1], 1.0 - LAM_INIT[l], ALU.mult, ALU.mult, [Tbank, sg], [oT])
            ph.dma("sp", dr["mixT"][h * 128:(h + 1) * 128, g * 512:(g + 1) * 512], oT[:, :], [oT], [])
    ph.finish()


def host_consts():
    c = {}
    c["c_ident"] = np.eye(128, dtype=np.float32)
    c["c_J"] = np.ascontiguousarray(np.eye(128, dtype=np.float32)[::-1])
    kk, qq = np.meshgrid(np.arange(128), np.arange(128), indexing="ij")
    c["c_causal"] = (qq >= kk).astype(np.float32)
    n = np.maximum(np.arange(384) - 127, 0)
    nf = np.maximum(n, 1).astype(np.float32)
    large = 16 + (np.log(nf / np.float32(16)) / np.float32(math.log(128 / 16)) * np.float32(16)).astype(np.int32)
    large = np.minimum(large, 31)
    bucket = np.where(n < 16, n, large)
    oh = np.zeros((32, 385), np.float32)
    oh[bucket, np.arange(384)] = 1.0
    oh[31, 384] = 1.0
    c["c_oh"] = oh
    pp, ff = np.meshgrid(np.arange(128), np.arange(128), indexing="ij")
    SU = (ff > pp).astype(np.float32)
    IU = (ff >= pp).astype(np.float32)
    c["c_mask4"] = np.ascontiguousarray(np.concatenate([SU, IU, SU, IU], axis=1))
    c["c_maskSL"] = (ff < pp).astype(np.float32)
    c["c_bones"] = ((pp // 64) == (ff // 64)).astype(np.float32)
    cm = np.ones((128, 512), np.float32)
    cm[:, ::128] = 0.0
    c["c_cm01"] = cm
    return c


def host_rwp(inp, l):
    vs = [inp["rw_w0"][l], inp["rw_a0"][l], inp["rw_k_k"][l], inp["rw_k_a"][l], inp["rw_r_k"][l].reshape(512),
          inp["rw_gn_g"][l], inp["rw_gn_b"][l], inp["rw_v0"][l - 1] if l > 0 else np.zeros(512, np.float32)]
    a = np.stack([np.asarray(v, np.float32).reshape(4, 128).T for v in vs], axis=1)
    return np.ascontiguousarray(a)


GN_EPS = 64e-5
DECAY_C = -math.exp(-0.5)


def phase_rwkv(kb, l):
    ph = PH(kb, "rw%d" % l)
    S, NT = kb.S, kb.NT
    NTB = S // 512
    dr = kb.dram
    cst = load_consts(ph)
    identbf = cst["ident_bf"]
    mask4 = ph.sb([128, 512], F32, "mask4")
    maskSL = ph.sb([128, 128], F32, "maskSL")
    bones = ph.sb([128, 128], F32, "bones")
    cm01 = ph.sb([128, 512], F32, "cm01")
    ph.dma("sp", mask4[:, :], dr["c_mask4"][:, :], [], [mask4])
    ph.dma("sp", maskSL[:, :], dr["c_maskSL"][:, :], [], [maskSL])
    ph.dma("sp", bones[:, :], dr["c_bones"][:, :], [], [bones])
    ph.dma("sp", cm01[:, :], dr["c_cm01"][:, :], [], [cm01])
    prm = ph.sb([128, 8, 4], F32, "prm")
    ph.dma("sp", prm[:, :, :], dr["rwp%d" % l][:, :, :], [], [prm])
    W0, A0, KK_, KA_, RK_, GG_, GB_, V0_ = range(8)
    WA = ph.sb([128, 512], F32, "WA")
    WG1 = ph.sb([128, 512], F32, "WG1")
    WGV = ph.sb([64, 512], F32, "WGV")
    ph.dma("sp", WA[0:64, :], dr["rw_w_up"][l, :, :], [], [WA])
    ph.dma("sp", WA[64:128, :], dr["rw_a_up"][l, :, :], [], [WA])
    ph.dma("sp", WG1[:, :], dr["rw_g_up"][l, 0:128, :], [], [WG1])
    ph.dma("sp", WGV[0:32, :], dr["rw_g_up"][l, 128:160, :], [], [WGV])
    if l > 0:
        ph.dma("sp", WGV[32:64, :], dr["rw_v_up"][l - 1, :, :], [], [WGV])
    omk = ph.sb([128, 4], F32, "omk")
    ph.ts("dve", omk[:, :], prm[:, KA_, :], -1.0, 1.0, ALU.mult, ALU.add, [prm], [omk])
    epsG = ph.sb([128, 1], F32, "epsG")
    ph.add("pool", lambda e: e.memset(epsG[:, :], GN_EPS), [], [epsG])
    H = ph.sb([128, 4, 64], F32, "H")
    Hb = [ph.sb([128, 64], BF16, "Hb%d" % hp) for hp in range(4)]
    Hs = [ph.sb([128, 64], F32, "Hs%d" % hp) for hp in range(4)]
    for hp in range(4):
        ph.add("pool", lambda e, hp=hp: e.memset(Hs[hp][:, :], 0.0), [], [Hs[hp]])
        ph.add("pool", lambda e, hp=hp: e.memset(Hb[hp][:, :], 0.0), [], [Hb[hp]])
    XWA = ph.sb([128, 512], F32, "XWA")
    XG1 = ph.sb([128, 512], F32, "XG1")
    XG2 = ph.sb([64, 512], F32, "XG2")
    TH = ph.sb([64, 512], F32, "TH")
    SG1 = ph.sb([128, 512], F32, "SG1")
    SG2 = ph.sb([32, 512], F32, "SG2")
    def f32t(n):
        return ph.sb([128, 512], F32, n)

    def bft(n):
        return ph.sb([128, 512], BF16, n)
    NSET = 2
    sets = []
    for k in range(NSET):
        sets.append(dict(
            R=f32t("R"), Kt=f32t("Kt"), Vt=f32t("Vt"), VF=f32t("VF") if l > 0 else None,
            cum=f32t("cum"), epos=f32t("epos"), AR=ph.sb([128, 4, 2, 128], BF16, "AR"),
            BT=bft("BT"), KT=bft("KT"), BH=bft("BH"), KH=bft("KH"), VB=bft("VB"),
            tok=[ph.sb([128, 4, 128], BF16, "tok") for _ in range(4)],
            gsb=f32t("gsb"), bon=f32t("bon"), outT=bft("outT")))
    sgw, lw, tmpc, eprev, eneg, edec, a_, kk, kk2, ssm, rn, kkn, tq, kp, bbt, vp, gate, pr_ = (
        f32t(n) for n in ("sgw", "lw", "tmpc", "eprev", "eneg", "edec", "a", "kk", "kk2", "ssm", "rn", "kkn",
                          "tq", "kp", "bbt", "vp", "gate", "pr"))
    M4 = [ph.sb([128, 512], BF16, "M4") for _ in range(4)]
    PQ = [[ph.sb([128, 256], BF16, "PQ") for _ in range(2)] for _ in range(2)]
    P0 = [ph.sb([128, 128], BF16, "P0") for _ in range(2)]
    TT = [[ph.sb([128, 128], BF16, "TT") for _ in range(2)] for _ in range(2)]
    AkV = [ph.sb([128, 64], BF16, "AkV") for _ in range(2)]
    WTsb = [ph.sb([128, 128], BF16, "WTsb") for _ in range(2)]
    Uloc = [ph.sb([128, 64], F32, "Uloc") for _ in range(2)]
    Usb = [ph.sb([128, 64], BF16, "Usb") for _ in range(2)]
    ynorm = [ph.sb([128, 128], BF16, "ynorm") for _ in range(2)]
    gst = [ph.sb([128, 6], F32, "gst") for _ in range(2)]
    gmv = [ph.sb([128, 2], F32, "gmv") for _ in range(2)]
    grs = [ph.sb([128, 1], F32, "grs") for _ in range(2)]
    gnm = [ph.sb([128, 1], F32, "gnm") for _ in range(2)]
    fin = [ph.sb([128, 128], F32, "fin") for _ in range(2)]
    PA = [ph.ps[0], ph.ps[1]]
    TRB = [ph.ps[2], ph.ps[3]]
    kb_h = kb.psum_h

    def subtiles(bi, bounds):
        return [ph.ps[bi] for a, b in bounds]
    W1 = [subtiles(4 + e, [(0, 64), (64, 128), (128, 192), (192, 256), (256, 320), (320, 384), (384, 512)])
          for e in range(2)]
    W2 = [subtiles(6 + e, [(0, 128), (128, 384), (384, 512)]) for e in range(2)]
    pai = [0]

    def nextpa():
        b = PA[pai[0] % 2]
        pai[0] += 1
        return b

    def c3(ap):
        return ap.rearrange("p (c t) -> p c t", c=4)

    ui = [0]
    for tb in range(NTB):
        tcols = slice(tb * 512, (tb + 1) * 512)
        ph.dma("sp", XWA[:, :], dr["prT"][1536:1664, tcols], [], [XWA])
        ph.dma("pool", XG1[:, :], dr["prT"][1664:1792, tcols], [], [XG1])
        ph.dma("pool", XG2[:, :], dr["prT"][1792:1856, tcols], [], [XG2])
        ph.act(TH[:, :], XWA[0:64, :], AF.Tanh, [XWA], [TH])
        ph.act(SG1[:, :], XG1[:, :], AF.Sigmoid, [XG1], [SG1])
        ph.act(SG2[:, :], XG2[0:32, :], AF.Sigmoid, [XG2], [SG2])
        for hp in range(4):
            s = sets[(tb * 4 + hp) % NSET]
            R, Kt, Vt, VF, cum, epos, AR, BT, KT, BH, KH, VB, tok, gsb, bon, outT = (
                s[k] for k in ("R", "Kt", "Vt", "VF", "cum", "epos", "AR", "BT", "KT", "BH", "KH", "VB", "tok",
                               "gsb", "bon", "outT"))
            hc = slice(hp * 128, (hp + 1) * 128)
            P = lambda i: prm[:, i, hp:hp + 1]
            ph.dma("sp", R[:, :], dr["prT"][hp * 128:(hp + 1) * 128, tcols], [], [R])
            ph.dma("sp", Kt[:, :], dr["prT"][512 + hp * 128:512 + (hp + 1) * 128, tcols], [], [Kt])
            ph.dma("pool", Vt[:, :], dr["prT"][1024 + hp * 128:1024 + (hp + 1) * 128, tcols], [], [Vt])
            if l > 0:
                ph.dma("pool", VF[:, :], dr["vfT"][hp * 128:(hp + 1) * 128, tcols], [], [VF])
            pa = nextpa()
            ph.mm(pa[:, :], WA[0:64, hc], TH[:, :], True, True, [WA, TH], [pa])
            ph.act(sgw[:, :], pa[:, :], AF.Sigmoid, [pa, prm], [sgw], bias=P(W0))
            ph.ts("dve", lw[:, :], sgw[:, :], DECAY_C, None, ALU.mult, None, [sgw], [lw])
            ph.add("dve", lambda e, cum=cum: e.tensor_tensor_scan(cum[:, :], cm01[:, :], lw[:, :], 0.0, ALU.mult, ALU.add),
                   [cm01, lw], [cum])
            ph.act(epos[:, :], cum[:, :], AF.Exp, [cum], [epos])
            ph.tt("pool", tmpc[:, :], cum[:, :], lw[:, :], ALU.subtract, [cum, lw], [tmpc])
            ph.act(eprev[:, :], tmpc[:, :], AF.Exp, [tmpc], [eprev])
            ph.act(eneg[:, :], cum[:, :], AF.Exp, [cum], [eneg], scale=-1.0)
            for c in range(4):
                ph.act(edec[:, c * 128:(c + 1) * 128], cum[:, c * 128:(c + 1) * 128], AF.Exp, [cum], [edec],
                       scale=-1.0, bias=cum[:, c * 128 + 127:c * 128 + 128])
            pa = nextpa()
            ph.mm(pa[:, :], WA[64:128, hc], XWA[64:128, :], True, True, [WA, XWA], [pa])
            ph.act(a_[:, :], pa[:, :], AF.Sigmoid, [pa, prm], [a_], bias=P(A0))
            ph.ts("dve", kk[:, :], Kt[:, :], P(KK_), None, ALU.mult, None, [Kt, prm], [kk])
            ph.tt("pool", kk2[:, :], kk[:, :], kk[:, :], ALU.mult, [kk], [kk2])
            pa = nextpa()
            ph.mm(pa[:, :], bones[:, :], kk2[:, :], True, True, [bones, kk2], [pa])
            ph.ts("dve", ssm[:, :], pa[:, :], 1e-24, None, ALU.max, None, [pa], [ssm])
            ph.act(rn[:, :], ssm[:, :], AF.Ln, [ssm], [rn])
            ph.act(rn[:, :], rn[:, :], AF.Exp, [rn], [rn], scale=-0.5)
            ph.tt("dve", kkn[:, :], kk[:, :], rn[:, :], ALU.mult, [kk, rn], [kkn])
            ph.ts("pool", tq[:, :], a_[:, :], P(KA_), omk[:, hp:hp + 1], ALU.mult, ALU.add, [a_, prm, omk], [tq])
            ph.tt("pool", kp[:, :], tq[:, :], Kt[:, :], ALU.mult, [tq, Kt], [kp])
            ph.stt("dve", AR[:, :, 0, :], c3(kkn[:, :]), -1.0, c3(eprev[:, :]), ALU.mult, ALU.mult, [kkn, eprev], [AR])
            ph.tt("pool", AR[:, :, 1, :], c3(R[:, :]), c3(epos[:, :]), ALU.mult, [R, epos], [AR])
            ph.tt("pool", bbt[:, :], kkn[:, :], a_[:, :], ALU.mult, [kkn, a_], [bbt])
            ph.tt("dve", BT[:, :], bbt[:, :], eneg[:, :], ALU.mult, [bbt, eneg], [BT])
            ph.tt("pool", BH[:, :], bbt[:, :], edec[:, :], ALU.mult, [bbt, edec], [BH])
            ph.tt("dve", KT[:, :], kp[:, :], eneg[:, :], ALU.mult, [kp, eneg], [KT])
            ph.tt("pool", KH[:, :], kp[:, :], edec[:, :], ALU.mult, [kp, edec], [KH])
            if l > 0:
                pa = nextpa()
                ph.mm(pa[:, :], WGV[32:64, hc], XG2[32:64, :], True, True, [WGV, XG2], [pa])
                ph.act(gate[:, :], pa[:, :], AF.Sigmoid, [pa, prm], [gate], bias=P(V0_))
                ph.tt("pool", vp[:, :], VF[:, :], Vt[:, :], ALU.subtract, [VF, Vt], [vp])
                ph.tt("dve", vp[:, :], vp[:, :], gate[:, :], ALU.mult, [vp, gate], [vp])
                ph.tt("pool", vp[:, :], vp[:, :], Vt[:, :], ALU.add, [vp, Vt], [vp])
                vsrc = vp
            else:
                vsrc = Vt
            ph.cp("act", VB[:, :], vsrc[:, :], [vsrc], [VB])
            pa = nextpa()
            ph.mm(pa[:, :], WG1[:, hc], SG1[:, :], True, False, [WG1, SG1], [pa])
            ph.mm(pa[:, :], WGV[0:32, hc], SG2[:, :], False, True, [WGV, SG2], [pa])
            ph.cp("act", gsb[:, :], pa[:, :], [pa], [gsb])
            ph.ts("pool", pr_[:, :], R[:, :], P(RK_), None, ALU.mult, None, [R, prm], [pr_])
            ph.tt("pool", pr_[:, :], pr_[:, :], kp[:, :], ALU.mult, [pr_, kp], [pr_])
            pa = nextpa()
            ph.mm(pa[:, :], bones[:, :], pr_[:, :], True, True, [bones, pr_], [pa])
            ph.tt("dve", bon[:, :], pa[:, :], vsrc[:, :], ALU.mult, [pa, vsrc], [bon])
            for c in range(4):
                trb = TRB[c // 2]
                tv = trb.ap[:, :].bitcast(BF16)
                base = (c % 2) * 512
                cs = slice(c * 128, (c + 1) * 128)
                for qi, (src, sap) in enumerate(((AR, AR[:, c, 0, :]), (BH, BH[:, cs]), (KH, KH[:, cs]), (VB, VB[:, cs]))):
                    ph.tr(tv[:, base + qi * 128:base + (qi + 1) * 128], sap, identbf[:, :], [src, identbf], [trb])
                ev = tv[:, base:base + 512].rearrange("p (q t) -> p q t", q=4)
                if c % 2 == 0:
                    ph.cp("dve", tok[c][:, :, :], ev, [trb], [tok[c]])
                else:
                    ph.cp("act", tok[c][:, :, :], ev, [trb], [tok[c]])
            for c in range(4):
                cs = slice(c * 128, (c + 1) * 128)
                gl = c * 128 + 127
                for e in range(2):
                    rows = slice(e * 64, (e + 1) * 64)
                    w1, w2 = W1[e], W2[e]
                    w1h, w2h = kb_h[4 + e], kb_h[6 + e]
                    m4 = M4[ui[0] % 4]
                    ui[0] += 1
                    ph.mm(w1h[:, 0:256], BT[rows, cs], AR[rows, c, :, :].rearrange("p a t -> p (a t)"), True, True,
                          [BT, AR], [w1[0]])
                    ph.mm(w1h[:, 256:512], KT[rows, cs], AR[rows, c, :, :].rearrange("p a t -> p (a t)"), True, True,
                          [KT, AR], [w1[0]])
                    ph.mm(w2h[:, 0:128], AR[rows, c, 0, :], BT[rows, cs], True, True, [AR, BT], [w2[0]])
                    ph.tt("dve", m4[:, :], w1h[:, 0:512], mask4[:, :], ALU.mult, [w1[0], mask4], [m4])
                    ph.tt("dve", P0[e][:, :], w2h[:, 0:128], maskSL[:, :], ALU.mult, [w2[0], maskSL], [P0[e]])
                    ph.tt("pool", TT[e][0][:, :], m4[:, 0:128], identbf[:, :], ALU.add, [m4, identbf], [TT[e][0]])
                for k in range(6):
                    for e in range(2):
                        w2, w2h = W2[e], kb_h[6 + e]
                        m4 = M4[(ui[0] - 2 + e) % 4]
                        if k == 0:
                            Pk, Qk, Pt_, Qt_ = P0[e][:, :], m4[:, 0:128], P0[e], m4
                        else:
                            pq = PQ[e][k % 2]
                            Pk, Qk, Pt_, Qt_ = pq[:, 0:128], pq[:, 128:256], pq, pq
                        pqn = PQ[e][(k + 1) % 2]
                        ph.mm(w2h[:, 128:256], Qk, Pk, True, True, [Pt_, Qt_], [w2[1]])
                        if k < 5:
                            ph.mm(w2h[:, 256:384], Pk, Qk, True, True, [Pt_, Qt_], [w2[1]])
                            ph.cp("act", pqn[:, 0:256], w2h[:, 128:384], [w2[1]], [pqn])
                        else:
                            ph.cp("act", pqn[:, 0:128], w2h[:, 128:256], [w2[1]], [pqn])
                        ph.mm(w2h[:, 384:512], pqn[:, 0:128], TT[e][k % 2][:, :], True, True, [pqn, TT[e][k % 2]], [w2[2]])
                        ph.tt("dve", TT[e][(k + 1) % 2][:, :], TT[e][k % 2][:, :], w2h[:, 384:512], ALU.add,
                              [TT[e][k % 2], w2[2]], [TT[e][(k + 1) % 2]])
                for e in range(2):
                    rows = slice(e * 64, (e + 1) * 64)
                    w1, w1h = W1[e], kb_h[4 + e]
                    m4 = M4[(ui[0] - 2 + e) % 4]
                    TTf = TT[e][0]
                    Vtok = tok[c][:, 3, rows]
                    ph.mm(w1h[:, 0:64], m4[:, 256:384], Vtok, True, True, [m4, tok[c]], [w1[0]])
                    ph.cp("act", AkV[e][:, :], w1h[:, 0:64], [w1[0]], [AkV[e]])
                    ph.mm(w1h[:, 384:512], tok[c][:, 0, :], TTf[:, :], True, True, [tok[c], TTf], [w1[6]])
                    ph.cp("act", WTsb[e][rows, :], w1h[rows, 384:512], [w1[6]], [WTsb[e]])
                    ph.mm(w1h[:, 64:128], TTf[:, :], AkV[e][:, :], True, True, [TTf, AkV[e]], [w1[1]])
                    ph.cp("act", Uloc[e][:, :], w1h[:, 64:128], [w1[1]], [Uloc[e]])
                    ph.mm(w1h[:, 128:192], WTsb[e][rows, :], Hb[hp][rows, :], True, True, [WTsb[e], Hb[hp]], [w1[2]])
                    ph.tt("dve", Usb[e][:, :], w1h[:, 128:192], Uloc[e][:, :], ALU.add, [w1[2], Uloc[e]], [Usb[e]])
                    ph.mm(w1h[:, 256:320], AR[rows, c, 1, :], Hb[hp][rows, :], True, False, [AR, Hb[hp]], [w1[4]])
                    ph.mm(w1h[:, 256:320], m4[:, 128:256], Usb[e][:, :], False, False, [m4, Usb[e]], [w1[4]])
                    ph.mm(w1h[:, 256:320], m4[:, 384:512], Vtok, False, True, [m4, tok[c]], [w1[4]])
                    ph.mm(w1h[:, 192:256], tok[c][:, 1, :], Usb[e][:, :], True, False, [tok[c], Usb[e]], [w1[3]])
                    ph.mm(w1h[:, 192:256], tok[c][:, 2, :], Vtok, False, True, [tok[c]], [w1[3]])
                    ph.stt("dve", Hs[hp][rows, :], Hs[hp][rows, :], epos[rows, gl:gl + 1], w1h[rows, 192:256],
                           ALU.mult, ALU.add, [Hs[hp], epos, w1[3]], [Hs[hp]])
                    ph.cp("act", Hb[hp][rows, :], Hs[hp][rows, :], [Hs[hp]], [Hb[hp]])
                    yn = ynorm[c % 2]
                    ph.add("dve", lambda en, e=e, w1h=w1h: en.bn_stats(gst[e][:, 0:6], w1h[:, 256:320]), [w1[4]], [gst[e]])
                    ph.add("dve", lambda en, e=e: en.bn_aggr(gmv[e][:, 0:2], gst[e][:, 0:6]), [gst[e]], [gmv[e]])
                    ph.act(grs[e][:, 0:1], gmv[e][:, 1:2], AF.Ln, [gmv[e]], [grs[e]], bias=epsG[:, 0:1])
                    ph.act(grs[e][:, 0:1], grs[e][:, 0:1], AF.Exp, [grs[e]], [grs[e]], scale=-0.5)
                    ph.ts("dve", gnm[e][:, 0:1], gmv[e][:, 0:1], grs[e][:, 0:1], -1.0, ALU.mult, ALU.mult,
                          [gmv[e], grs[e]], [gnm[e]])
                    ph.act(yn[:, e * 64:(e + 1) * 64], w1h[:, 256:320], AF.Identity, [w1[4], grs[e], gnm[e]], [yn],
                           bias=gnm[e][:, 0:1], scale=grs[e][:, 0:1])
                yn = ynorm[c % 2]
                trb = TRB[1]
                tv = trb.ap[:, :].bitcast(BF16)
                ph.tr(tv[:, 0:128], yn[:, :], identbf[:, :], [yn, identbf], [trb])
                fn = fin[c % 2]
                ph.ts("dve", fn[:, :], tv[:, 0:128], P(GG_), P(GB_), ALU.mult, ALU.add, [trb, prm], [fn])
                ph.tt("pool", fn[:, :], fn[:, :], bon[:, cs], ALU.add, [fn, bon], [fn])
                ph.tt("pool", outT[:, cs], fn[:, :], gsb[:, cs], ALU.mult, [fn, gsb], [outT])
            ph.dma("sp", dr["mixT"][512 + hp * 128:512 + (hp + 1) * 128, tcols], outT[:, :], [outT], [])
    ph.finish()


ALPHA = (2 * 2) ** 0.25


def load_cast_weight(ph, w_rows_fn, nrows_tiles, ncols, wbf_fn, stage, q="act"):
    pc = stage[0].ap.shape[1]
    k = 0
    for r in range(nrows_tiles):
        for c0 in range(0, ncols, pc):
            n = min(pc, ncols - c0)
            s = stage[k % len(stage)]
            ph.dma(q if k % 2 == 0 else "sp", s[:, 0:n], w_rows_fn(r)[:, c0:c0 + n], [], [s])
            ph.cp(("dve", "pool", "act")[k % 3], wbf_fn(r)[:, c0:c0 + n], s[:, 0:n], [s], [wbf_fn(r)])
            k += 1


def phase_outproj(kb, l):
    ph = PH(kb, "op%d" % l)
    S, NT = kb.S, kb.NT
    dr = kb.dram
    c = load_consts(ph)
    gB = ph.sb([128, 1024], F32, "gB")
    bB = ph.sb([128, 1024], F32, "bB")
    ph.dma("sp", gB[:, :], dr["ln_mix_g"][l, :].partition_broadcast(128), [], [gB])
    ph.dma("sp", bB[:, :], dr["ln_mix_b"][l, :].partition_broadcast(128), [], [bB])
    wbf = [ph.sb([128, 1024], BF16, "wo%d" % r) for r in range(8)]
    stage = [ph.sb([128, 1024], F32, "stg%d" % k) for k in range(2)]
    load_cast_weight(ph, lambda r: dr["w_out"][l, r * 128:(r + 1) * 128, :], 8, 1024, lambda r: wbf[r], stage)
    mx = [ph.sb([128, 8, 512], BF16, "mx%d" % k) for k in range(2)]
    xr = [ph.sb([128, 1024], F32, "xr%d" % k) for k in range(3)]
    srcs = [ph.sb([128, 1024], F32, "src%d" % k) for k in range(3)]
    bufs = [ln_bufs(ph, k) for k in range(4)]
    for g in range(S // 512):
        m = mx[g % 2]
        ph.dma("pool", m[:, :, :], dr["mixT"][:, g * 512:(g + 1) * 512].rearrange("(c p) t -> p c t", p=128), [], [m])
        for j in range(4):
            i = g * 4 + j
            x_ = xr[i % 3]
            ph.dma("sp", x_[:, :], dr["xres"][i * 128:(i + 1) * 128, :], [], [x_])
            src = srcs[i % 3]
            for hh in range(2):
                pb = ph.ps[(i % 2) * 2 + hh]
                for cc in range(8):
                    ph.mm(pb[:, :], m[:, cc, j * 128:(j + 1) * 128], wbf[cc][:, hh * 512:(hh + 1) * 512],
                          cc == 0, cc == 7, [m, wbf[cc]], [pb])
                ph.stt("dve", src[:, hh * 512:(hh + 1) * 512], x_[:, hh * 512:(hh + 1) * 512], ALPHA, pb[:, :],
                       ALU.mult, ALU.add, [x_, pb], [src])
            ln_rows(ph, src, i, gB, bB, c["ident_bf"], "xres", "xT", bufs[i % 4], ph.ps[4 + i % 4])
    ph.finish()


def phase_ffn_up(kb, l):
    ph = PH(kb, "fu%d" % l)
    S, NT = kb.S, kb.NT
    dr = kb.dram
    wbf = [ph.sb([128, 4096], BF16, "wu%d" % r) for r in range(8)]
    stage = [ph.sb([128, 2048], F32, "stg%d" % k) for k in range(2)]
    load_cast_weight(ph, lambda r: dr["w_up"][l, r * 128:(r + 1) * 128, :], 8, 4096, lambda r: wbf[r], stage)
    xg = [ph.sb([128, 8, 512], BF16, "xg%d" % k) for k in range(2)]
    tmp = [ph.sb([128, 512], F32, "tmp%d" % k) for k in range(3)]
    ho = [ph.sb([128, 512], BF16, "ho%d" % k) for k in range(4)]
    k = 0
    for g in range(S // 512):
        x_ = xg[g % 2]
        ph.dma("pool", x_[:, :, :], dr["xT"][:, g * 512:(g + 1) * 512].rearrange("(c p) t -> p c t", p=128), [], [x_])
        for f in range(32):
            pb = ph.ps[k % 6]
            for cc in range(8):
                ph.mm(pb[:, :], wbf[cc][:, f * 128:(f + 1) * 128], x_[:, cc, :], cc == 0, cc == 7, [wbf[cc], x_], [pb])
            t = tmp[k % 3]
            h = ho[k % 4]
            ph.act(t[:, :], pb[:, :], AF.Relu, [pb], [t])
            ph.tt(("pool", "dve")[k % 2], h[:, :], t[:, :], t[:, :], ALU.mult, [t], [h])
            ph.dma("sp", dr["hT"][f * 128:(f + 1) * 128, g * 512:(g + 1) * 512], h[:, :], [h], [])
            k += 1
    ph.finish()


def phase_ffn_down(kb, l, last):
    ph = PH(kb, "fd%d" % l)
    S, NT = kb.S, kb.NT
    dr = kb.dram
    c = load_consts(ph)
    gB = ph.sb([128, 1024], F32, "gB")
    bB = ph.sb([128, 1024], F32, "bB")
    ph.dma("sp", gB[:, :], dr["ln_ffn_g"][l, :].partition_broadcast(128), [], [gB])
    ph.dma("sp", bB[:, :], dr["ln_ffn_b"][l, :].partition_broadcast(128), [], [bB])
    wbf = [ph.sb([128, 1024], BF16, "wd%d" % r) for r in range(32)]
    stage = [ph.sb([128, 1024], F32, "stg%d" % k) for k in range(3)]
    load_cast_weight(ph, lambda r: dr["w_down"][l, r * 128:(r + 1) * 128, :], 32, 1024, lambda r: wbf[r], stage)
    hg = [ph.sb([128, 32, 512], BF16, "hg%d" % k) for k in range(2)]
    xr = [ph.sb([128, 1024], F32, "xr%d" % k) for k in range(3)]
    srcs = [ph.sb([128, 1024], F32, "src%d" % k) for k in range(3)]
    bufs = [ln_bufs(ph, k) for k in range(4)]
    for g in range(S // 512):
        h_ = hg[g % 2]
        for q4 in range(4):
            ph.dma(("pool", "sp")[q4 % 2], h_[:, q4 * 8:(q4 + 1) * 8, :],
                   dr["hT"][q4 * 1024:(q4 + 1) * 1024, g * 512:(g + 1) * 512].rearrange("(f p) t -> p f t", p=128),
                   [], [h_])
        for j in range(4):
            i = g * 4 + j
            x_ = xr[i % 3]
            ph.dma("sp", x_[:, :], dr["xres"][i * 128:(i + 1) * 128, :], [], [x_])
            src = srcs[i % 3]
            for hh in range(2):
                pb = ph.ps[(i % 2) * 2 + hh]
                for f in range(32):
                    ph.mm(pb[:, :], h_[:, f, j * 128:(j + 1) * 128], wbf[f][:, hh * 512:(hh + 1) * 512],
                          f == 0, f == 31, [h_, wbf[f]], [pb])
                ph.stt("dve", src[:, hh * 512:(hh + 1) * 512], x_[:, hh * 512:(hh + 1) * 512], ALPHA, pb[:, :],
                       ALU.mult, ALU.add, [x_, pb], [src])
            if last:
                ln_rows(ph, src, i, gB, bB, c["ident_bf"], "out", None, bufs[i % 4], ph.ps[4 + i % 4])
            else:
                ln_rows(ph, src, i, gB, bB, c["ident_bf"], "xres", "xT", bufs[i % 4], ph.ps[4 + i % 4])
    ph.finish()


CONST_SHAPES = {"c_ident": [128, 128], "c_J": [128, 128], "c_causal": [128, 128], "c_oh": [32, 385],
                "c_mask4": [128, 512], "c_maskSL": [128, 128], "c_bones": [128, 128], "c_cm01": [128, 512]}


def build_program(S, debug=(), only=None):
    kb = KB(S)
    kb.din("x", [S, 1024])
    kb.din("ln_in_g", [1024])
    kb.din("ln_in_b", [1024])
    kb.din("w_in0", [1024, 3360])
    kb.din("w_in1", [1024, 3392])
    kb.din("mu0", [128, 15])
    kb.din("mu1", [128, 15])
    kb.din("rel_bias", [32, 4])
    for nm in ("lambda_q1", "lambda_k1", "lambda_q2", "lambda_k2"):
        kb.din(nm, [2, 64])
    kb.din("subln_g", [2, 128])
    kb.din("rwp0", [128, 8, 4])
    kb.din("rwp1", [128, 8, 4])
    kb.din("rw_w_up", [2, 64, 512])
    kb.din("rw_a_up", [2, 64, 512])
    kb.din("rw_g_up", [2, 160, 512])
    kb.din("rw_v_up", [1, 32, 512])
    kb.din("w_out", [2, 1024, 1024])
    kb.din("ln_mix_g", [2, 1024])
    kb.din("ln_mix_b", [2, 1024])
    kb.din("w_up", [2, 1024, 4096])
    kb.din("w_down", [2, 4096, 1024])
    kb.din("ln_ffn_g", [2, 1024])
    kb.din("ln_ffn_b", [2, 1024])
    for nm, shp in CONST_SHAPES.items():
        kb.din(nm, shp)
    dbg = lambda n: n in debug
    kb.dscr("xres", [S, 1024], F32, dbg("xres"))
    kb.dscr("xT", [1024, S], BF16, dbg("xT"))
    kb.dscr("qT", [4, 128, S], BF16, dbg("qT"))
    kb.dscr("kT", [4, 128, S], BF16, dbg("kT"))
    kb.dscr("vd", [S, 512], BF16, dbg("vd"))
    kb.dscr("prT", [1920, S], F32, dbg("prT"))
    kb.dscr("vfT", [512, S], F32, dbg("vfT"))
    kb.dscr("lamneg", [1, 2], F32, dbg("lamneg"))
    kb.dscr("EGd", [4, 384], F32, dbg("EGd"))
    kb.dscr("Eb", [4, 2, 128, 128], F32, dbg("Eb"))
    kb.dscr("mixT", [1024, S], BF16, dbg("mixT"))
    kb.dscr("hT", [4096, S], BF16, dbg("hT"))
    kb.dout("out", [S, 1024], F32)
    on = lambda n: only is None or n in only
    if on("pre"):
        phase_pre(kb)
    if on("ln0"):
        phase_ln0(kb)
    for l in range(2):
        if on("ip%d" % l):
            phase_inproj(kb, l)
        if on("at%d" % l):
            phase_attn(kb, l)
        if on("rw%d" % l):
            phase_rwkv2(kb, l)
        if on("op%d" % l):
            phase_outproj(kb, l)
        if on("fu%d" % l):
            phase_ffn_up(kb, l)
        if on("fd%d" % l):
            phase_ffn_down(kb, l, last=(l == 1))
    kb.close()
    return kb


def host_inputs(inp, S):
    f = lambda a: np.ascontiguousarray(np.asarray(a, dtype=np.float32))
    m = {}
    m["ln_in_g"] = f(inp["ln_in_g"])
    m["ln_in_b"] = f(inp["ln_in_b"])
    m["w_in0"] = f(inp["w_in_first"])
    m["w_in1"] = f(inp["w_in_rest"][0])
    for l, mu in enumerate((inp["mu_first"], inp["mu_rest"][0])):
        mp = np.zeros(1920, np.float32)
        mp[:mu.shape[0]] = mu
        m["mu%d" % l] = np.ascontiguousarray(mp.reshape(15, 128).T)
    for nm in ("rel_bias", "lambda_q1", "lambda_k1", "lambda_q2", "lambda_k2", "subln_g", "rw_w_up", "rw_a_up",
               "rw_g_up", "rw_v_up", "w_out", "ln_mix_g", "ln_mix_b", "w_up", "w_down", "ln_ffn_g", "ln_ffn_b"):
        m[nm] = f(inp[nm])
    m["rwp0"] = host_rwp(inp, 0)
    m["rwp1"] = host_rwp(inp, 1)
    m.update(host_consts())
    return m


_PROG = {}


def kernel(**inputs):
    x = np.asarray(inputs["x"], dtype=np.float32)
    B, S, _ = x.shape
    if S not in _PROG:
        _PROG[S] = build_program(S)
    kb = _PROG[S]
    shared = host_inputs(inputs, S)
    in_maps = []
    for b in range(B):
        m = dict(shared)
        m["x"] = np.ascontiguousarray(x[b])
        in_maps.append(m)
    res = run_bass_kernel_spmd(kb.nc, in_maps, core_ids=list(range(B)))
    return np.stack([np.asarray(r["out"], dtype=np.float32) for r in res.results], axis=0)


def phase_rwkv2(kb, l):
    ph = PH(kb, "rw%d" % l)
    S, NT = kb.S, kb.NT
    NTB = S // 512
    dr = kb.dram
    cst = load_consts(ph)
    identbf = cst["ident_bf"]
    mask4 = ph.sb([128, 512], F32, "mask4")
    maskSL = ph.sb([128, 128], F32, "maskSL")
    bones = ph.sb([128, 128], F32, "bones")
    cm01 = ph.sb([128, 512], F32, "cm01")
    ph.dma("sp", mask4[:, :], dr["c_mask4"][:, :], [], [mask4])
    ph.dma("sp", maskSL[:, :], dr["c_maskSL"][:, :], [], [maskSL])
    ph.dma("sp", bones[:, :], dr["c_bones"][:, :], [], [bones])
    ph.dma("sp", cm01[:, :], dr["c_cm01"][:, :], [], [cm01])
    prm = ph.sb([128, 8, 4], F32, "prm")
    ph.dma("sp", prm[:, :, :], dr["rwp%d" % l][:, :, :], [], [prm])
    W0, A0, KK_, KA_, RK_, GG_, GB_, V0_ = range(8)
    WA = ph.sb([128, 512], F32, "WA")
    WG1 = ph.sb([128, 512], F32, "WG1")
    WGV = ph.sb([64, 512], F32, "WGV")
    ph.dma("sp", WA[0:64, :], dr["rw_w_up"][l, :, :], [], [WA])
    ph.dma("sp", WA[64:128, :], dr["rw_a_up"][l, :, :], [], [WA])
    ph.dma("sp", WG1[:, :], dr["rw_g_up"][l, 0:128, :], [], [WG1])
    ph.dma("sp", WGV[0:32, :], dr["rw_g_up"][l, 128:160, :], [], [WGV])
    if l > 0:
        ph.dma("sp", WGV[32:64, :], dr["rw_v_up"][l - 1, :, :], [], [WGV])
    omk = ph.sb([128, 4], F32, "omk")
    ph.ts("dve", omk[:, :], prm[:, KA_, :], -1.0, 1.0, ALU.mult, ALU.add, [prm], [omk])
    epsG = ph.sb([128, 1], F32, "epsG")
    ph.add("pool", lambda e: e.memset(epsG[:, :], GN_EPS), [], [epsG])
    Hb = [ph.sb([128, 64], BF16, "Hb%d" % hp) for hp in range(4)]
    Hs = [ph.sb([128, 64], F32, "Hs%d" % hp) for hp in range(4)]
    for hp in range(4):
        ph.add("pool", lambda e, hp=hp: e.memset(Hs[hp][:, :], 0.0), [], [Hs[hp]])
        ph.add("pool", lambda e, hp=hp: e.memset(Hb[hp][:, :], 0.0), [], [Hb[hp]])
    XWA = ph.sb([128, 512], F32, "XWA")
    XG1 = ph.sb([128, 512], F32, "XG1")
    XG2 = ph.sb([64, 512], F32, "XG2")
    TH = ph.sb([64, 512], F32, "TH")
    SG1 = ph.sb([128, 512], F32, "SG1")
    SG2 = ph.sb([32, 512], F32, "SG2")

    def f32t(n):
        return ph.sb([128, 512], F32, n)

    def bft(n):
        return ph.sb([128, 512], BF16, n)
    pers = []
    for hp in range(4):
        pers.append(dict(gsb=f32t("gsb"), bon=f32t("bon"), AR=ph.sb([128, 4, 2, 128], BF16, "AR"),
                         BT=bft("BT"), KT=bft("KT"), tok=[ph.sb([128, 4, 128], BF16, "tok") for _ in range(4)],
                         outT=bft("outT"), gam=ph.sb([128, 4], F32, "gam"),
                         keepM=[ph.sb([128, 256], BF16, "keepM") for _ in range(8)],
                         Uloc=[ph.sb([128, 64], F32, "Uloc") for _ in range(8)],
                         WTsb=[ph.sb([128, 128], BF16, "WTsb") for _ in range(4)]))
    RIN = [dict(R=f32t("R"), Kt=f32t("Kt"), Vt=f32t("Vt"), VF=f32t("VF") if l > 0 else None) for _ in range(2)]
    TMP = [[f32t("tmp%d" % i) for i in range(10)] for _ in range(2)]
    TB16 = [(bft("BH"), bft("KH"), bft("VB")) for _ in range(2)]
    M4 = [ph.sb([128, 512], BF16, "M4") for _ in range(8)]
    P0 = [ph.sb([128, 128], BF16, "P0") for _ in range(8)]
    XK = [[ph.sb([128, 384], BF16, "XK") for _ in range(2)] for _ in range(8)]
    AkV = [ph.sb([128, 64], BF16, "AkV") for _ in range(8)]
    Usb = [ph.sb([128, 64], BF16, "Usb") for _ in range(8)]
    ynorm = [ph.sb([128, 128], BF16, "ynorm") for _ in range(4)]
    gst = [ph.sb([128, 6], F32, "gst") for _ in range(8)]
    gmv = [ph.sb([128, 2], F32, "gmv") for _ in range(8)]
    grs = [ph.sb([128, 1], F32, "grs") for _ in range(8)]
    gnm = [ph.sb([128, 1], F32, "gnm") for _ in range(8)]
    fin = [ph.sb([128, 128], F32, "fin") for _ in range(4)]
    kb_h = kb.psum_h
    PA = [ph.ps[0], ph.ps[1], ph.ps[2], ph.ps[3]]
    pai = [0]

    def nextpa():
        b = PA[pai[0] % 4]
        pai[0] += 1
        return b

    def c3(ap):
        return ap.rearrange("p (c t) -> p c t", c=4)
    evi = [0]

    import os
    EV = os.environ.get("RW2_EV", "")

    def evac_copy(out_ap, in_ap, reads, writes):
        evi[0] += 1
        eng = ("act", "dve")[evi[0] % 2]
        if EV:
            eng = EV
        ph.cp(eng, out_ap, in_ap, reads, writes)

    for tb in range(NTB):
        tcols = slice(tb * 512, (tb + 1) * 512)
        ph.dma("sp", XWA[:, :], dr["prT"][1536:1664, tcols], [], [XWA])
        ph.dma("pool", XG1[:, :], dr["prT"][1664:1792, tcols], [], [XG1])
        ph.dma("pool", XG2[:, :], dr["prT"][1792:1856, tcols], [], [XG2])
        ph.act(TH[:, :], XWA[0:64, :], AF.Tanh, [XWA], [TH])
        ph.act(SG1[:, :], XG1[:, :], AF.Sigmoid, [XG1], [SG1])
        ph.act(SG2[:, :], XG2[0:32, :], AF.Sigmoid, [XG2], [SG2])
        def prep_gen(hp):
                s = pers[hp]
                rin = RIN[hp % 2]
                R, Kt, Vt, VF = rin["R"], rin["Kt"], rin["Vt"], rin["VF"]
                gsb, bon, AR, BT, KT, tok, gam = (s[k] for k in ("gsb", "bon", "AR", "BT", "KT", "tok", "gam"))
                hc = slice(hp * 128, (hp + 1) * 128)
                P = lambda i: prm[:, i, hp:hp + 1]
                ph.dma("sp", R[:, :], dr["prT"][hp * 128:(hp + 1) * 128, tcols], [], [R])
                yield
                ph.dma("sp", Kt[:, :], dr["prT"][512 + hp * 128:512 + (hp + 1) * 128, tcols], [], [Kt])
                yield
                ph.dma("pool", Vt[:, :], dr["prT"][1024 + hp * 128:1024 + (hp + 1) * 128, tcols], [], [Vt])
                yield
                if l > 0:
                    ph.dma("pool", VF[:, :], dr["vfT"][hp * 128:(hp + 1) * 128, tcols], [], [VF])
                    yield
                sgw, cum, epos, eneg, edec, a_, kk, kkn, kp, bbt = TMP[hp % 2]
                BH, KH, VB = TB16[hp % 2]
                pa = nextpa()
                ph.mm(pa[:, :], WA[0:64, hc], TH[:, :], True, True, [WA, TH], [pa])
                yield
                ph.act(sgw[:, :], pa[:, :], AF.Sigmoid, [pa, prm], [sgw], bias=P(W0))
                yield
                ph.ts("dve", sgw[:, :], sgw[:, :], DECAY_C, None, ALU.mult, None, [sgw], [sgw])
                yield
                ph.add("dve", lambda e, cum=cum, sgw=sgw: e.tensor_tensor_scan(cum[:, :], cm01[:, :], sgw[:, :], 0.0,
                                                                              ALU.mult, ALU.add), [cm01, sgw], [cum])
                yield
                ph.act(epos[:, :], cum[:, :], AF.Exp, [cum], [epos])
                yield
                ph.act(eneg[:, :], cum[:, :], AF.Exp, [cum], [eneg], scale=-1.0)
                yield
                for c in range(4):
                    ph.act(edec[:, c * 128:(c + 1) * 128], cum[:, c * 128:(c + 1) * 128], AF.Exp, [cum], [edec],
                           scale=-1.0, bias=cum[:, c * 128 + 127:c * 128 + 128])
                    yield
                ph.cp("pool", gam[:, :], c3(epos[:, :])[:, :, 127], [epos], [gam])
                yield
                pa = nextpa()
                ph.mm(pa[:, :], WA[64:128, hc], XWA[64:128, :], True, True, [WA, XWA], [pa])
                yield
                ph.act(a_[:, :], pa[:, :], AF.Sigmoid, [pa, prm], [a_], bias=P(A0))
                yield
                ph.act(kk[:, :], Kt[:, :], AF.Square, [Kt, prm], [kk], scale=P(KK_))
                yield
                pa = nextpa()
                ph.mm(pa[:, :], bones[:, :], kk[:, :], True, True, [bones, kk], [pa])
                yield
                ph.ts("dve", kk[:, :], pa[:, :], 1e-24, None, ALU.max, None, [pa], [kk])
                yield
                ph.act(kk[:, :], kk[:, :], AF.Ln, [kk], [kk])
                yield
                ph.act(kk[:, :], kk[:, :], AF.Exp, [kk], [kk], scale=-0.5)
                yield
                ph.stt("dve", kkn[:, :], Kt[:, :], P(KK_), kk[:, :], ALU.mult, ALU.mult, [Kt, prm, kk], [kkn])
                yield
                ph.act(kp[:, :], a_[:, :], AF.Identity, [a_, prm, omk], [kp], scale=P(KA_), bias=omk[:, hp:hp + 1])
                yield
                ph.tt("pool", kp[:, :], kp[:, :], Kt[:, :], ALU.mult, [kp, Kt], [kp])
                yield
                ph.stt("dve", AR[:, :, 0, 1:128], c3(kkn[:, :])[:, :, 1:128], -1.0, c3(epos[:, :])[:, :, 0:127],
                       ALU.mult, ALU.mult, [kkn, epos], [AR])
                yield
                ph.ts("dve", AR[:, :, 0, 0:1], c3(kkn[:, :])[:, :, 0:1], -1.0, None, ALU.mult, None, [kkn], [AR])
                yield
                ph.tt("pool", AR[:, :, 1, :], c3(R[:, :]), c3(epos[:, :]), ALU.mult, [R, epos], [AR])
                yield
                ph.tt("pool", bbt[:, :], kkn[:, :], a_[:, :], ALU.mult, [kkn, a_], [bbt])
                yield
                ph.tt("dve", BT[:, :], bbt[:, :], eneg[:, :], ALU.mult, [bbt, eneg], [BT])
                yield
                ph.tt("pool", BH[:, :], bbt[:, :], edec[:, :], ALU.mult, [bbt, edec], [BH])
                yield
                ph.tt("dve", KT[:, :], kp[:, :], eneg[:, :], ALU.mult, [kp, eneg], [KT])
                yield
                ph.tt("pool", KH[:, :], kp[:, :], edec[:, :], ALU.mult, [kp, edec], [KH])
                yield
                if l > 0:
                    pa = nextpa()
                    ph.mm(pa[:, :], WGV[32:64, hc], XG2[32:64, :], True, True, [WGV, XG2], [pa])
                    yield
                    ph.act(a_[:, :], pa[:, :], AF.Sigmoid, [pa, prm], [a_], bias=P(V0_))
                    yield
                    ph.tt("pool", sgw[:, :], VF[:, :], Vt[:, :], ALU.subtract, [VF, Vt], [sgw])
                    yield
                    ph.tt("dve", sgw[:, :], sgw[:, :], a_[:, :], ALU.mult, [sgw, a_], [sgw])
                    yield
                    ph.tt("pool", sgw[:, :], sgw[:, :], Vt[:, :], ALU.add, [sgw, Vt], [sgw])
                    yield
                    vsrc = sgw
                else:
                    vsrc = Vt
                ph.cp("act", VB[:, :], vsrc[:, :], [vsrc], [VB])
                yield
                pa = nextpa()
                ph.mm(pa[:, :], WG1[:, hc], SG1[:, :], True, False, [WG1, SG1], [pa])
                yield
                ph.mm(pa[:, :], WGV[0:32, hc], SG2[:, :], False, True, [WGV, SG2], [pa])
                yield
                ph.cp("act", gsb[:, :], pa[:, :], [pa], [gsb])
                yield
                ph.stt("dve", bbt[:, :], R[:, :], P(RK_), kp[:, :], ALU.mult, ALU.mult, [R, prm, kp], [bbt])
                yield
                pa = nextpa()
                ph.mm(pa[:, :], bones[:, :], bbt[:, :], True, True, [bones, bbt], [pa])
                yield
                ph.tt("dve", bon[:, :], pa[:, :], vsrc[:, :], ALU.mult, [pa, vsrc], [bon])
                yield
                for c in range(4):
                    trb = ph.ps[6 + c // 2]
                    tv = trb.ap[:, :].bitcast(BF16)
                    base = (c % 2) * 512
                    cs = slice(c * 128, (c + 1) * 128)
                    for qi, (src, sap) in enumerate(((AR, AR[:, c, 0, :]), (BH, BH[:, cs]), (KH, KH[:, cs]), (VB, VB[:, cs]))):
                        ph.tr(tv[:, base + qi * 128:base + (qi + 1) * 128], sap, identbf[:, :], [src, identbf], [trb])
                    ev = tv[:, base:base + 512].rearrange("p (q t) -> p q t", q=4)
                    evac_copy(tok[c][:, :, :], ev, [trb], [tok[c]])
                    yield
        for pair in ((0, 1), (2, 3)):
            gens = [prep_gen(hp) for hp in pair]
            while gens:
                for g_ in list(gens):
                    try:
                        next(g_)
                    except StopIteration:
                        gens.remove(g_)
        import os
        STOP = int(os.environ.get("RW2_STOP", "99"))
        if STOP <= 1:
            continue
        for hp in range(4):
            s = pers[hp]
            AR, BT, KT, tok, keepM, Uloc, WTsb = (s[k] for k in ("AR", "BT", "KT", "tok", "keepM", "Uloc", "WTsb"))
            units = [(c, e) for c in range(4) for e in range(2)]
            for sw in range(4):
                for ui2 in range(2):
                    u = sw * 2 + ui2
                    c, e = units[u]
                    rows = slice(e * 64, (e + 1) * 64)
                    cs = slice(c * 128, (c + 1) * 128)
                    gb = ph.ps[4 + ui2]
                    g3 = ph.ps[2 + ui2]
                    arr = AR[rows, c, :, :].rearrange("p a t -> p (a t)")
                    ph.mm(gb[:, 0:256], BT[rows, cs], arr, True, True, [BT, AR], [gb])
                    ph.mm(gb[:, 256:512], KT[rows, cs], arr, True, True, [KT, AR], [gb])
                    ph.mm(g3[:, 0:128], AR[rows, c, 0, :], BT[rows, cs], True, True, [AR, BT], [g3])
                for ui2 in range(2):
                    u = sw * 2 + ui2
                    gb = ph.ps[4 + ui2]
                    g3 = ph.ps[2 + ui2]
                    ph.tt("dve", M4[u][:, :], gb[:, 0:512], mask4[:, :], ALU.mult, [gb, mask4], [M4[u]])
                    ph.tt("dve", P0[u][:, :], g3[:, 0:128], maskSL[:, :], ALU.mult, [g3, maskSL], [P0[u]])
                    ph.tt("pool", XK[u][1][:, 256:384], M4[u][:, 0:128], identbf[:, :], ALU.add, [M4[u], identbf], [XK[u][1]])
                    ph.cp("pool", keepM[u][:, 0:128], M4[u][:, 128:256], [M4[u]], [keepM[u]])
                    ph.cp("pool", keepM[u][:, 128:256], M4[u][:, 384:512], [M4[u]], [keepM[u]])
            if STOP <= 2:
                continue
            for k in range(7):
                for rnd in range(1):
                    us_ = range(8)
                    for u in us_:
                        cb = ph.ps[u]
                        if k == 0:
                            Pk, Qk, Pt_, Qt_ = P0[u][:, :], M4[u][:, 0:128], P0[u], M4[u]
                        else:
                            xk = XK[u][k % 2]
                            Pk, Qk, Pt_, Qt_ = xk[:, 0:128], xk[:, 128:256], xk, xk
                        if k < 6:
                            ph.mm(cb[:, 0:128], Qk, Pk, True, True, [Pt_, Qt_], [cb])
                        if k == 0:
                            ph.mm(cb[:, 128:256], Pk, Qk, True, True, [Pt_, Qt_], [cb])
                        elif k < 5:
                            ph.mm(cb[:, 128:384], Pk, xk[:, 128:384], True, True, [xk], [cb])
                        else:
                            ph.mm(cb[:, 256:384], Pk, xk[:, 256:384], True, True, [xk], [cb])
                    for u in us_:
                        cb = ph.ps[u]
                        xn = XK[u][(k + 1) % 2]
                        if k < 5:
                            ph.cp("act", xn[:, 0:256], cb[:, 0:256], [cb], [xn])
                        elif k == 5:
                            ph.cp("act", xn[:, 0:128], cb[:, 0:128], [cb], [xn])
                        if k >= 1:
                            xk = XK[u][k % 2]
                            ph.tt("dve", xn[:, 256:384], cb[:, 256:384], xk[:, 256:384], ALU.add, [cb, xk], [xn])
            if STOP <= 3:
                continue
            for u in range(8):
                c, e = units[u]
                rows = slice(e * 64, (e + 1) * 64)
                TTf = XK[u][1]
                Vtok = tok[c][:, 3, rows]
                ph.mm(ph.ps[2][:, u * 64:(u + 1) * 64], M4[u][:, 256:384], Vtok, True, True, [M4[u], tok[c]], [ph.ps[2]])
                wb = ph.ps[3 + u // 4]
                ph.mm(wb[:, (u % 4) * 128:(u % 4 + 1) * 128], tok[c][:, 0, :], TTf[:, 256:384], True, True, [tok[c], TTf], [wb])
            for u in range(8):
                c, e = units[u]
                rows = slice(e * 64, (e + 1) * 64)
                evac_copy(AkV[u][:, :], ph.ps[2][:, u * 64:(u + 1) * 64], [ph.ps[2]], [AkV[u]])
                wb = ph.ps[3 + u // 4]
                evac_copy(WTsb[c][rows, :], wb[rows, (u % 4) * 128:(u % 4 + 1) * 128], [wb], [WTsb[c]])
            for u in range(8):
                ph.mm(ph.ps[5][:, u * 64:(u + 1) * 64], XK[u][1][:, 256:384], AkV[u][:, :], True, True, [XK[u][1], AkV[u]], [ph.ps[5]])
            for u in range(8):
                evac_copy(Uloc[u][:, :], ph.ps[5][:, u * 64:(u + 1) * 64], [ph.ps[5]], [Uloc[u]])
        for c in range(4 if STOP > 4 else 0):
            cs = slice(c * 128, (c + 1) * 128)
            HE = [(hp, e) for hp in range(4) for e in range(2)]

            def reg(hp, e, j):
                return ph.ps[2 + e * 2 + (hp % 2)], (hp // 2) * 192 + j * 64
            for hp, e in HE:
                s = pers[hp]
                rows = slice(e * 64, (e + 1) * 64)
                b, o = reg(hp, e, 0)
                ph.mm(b[:, o:o + 64], s["WTsb"][c][rows, :], Hb[hp][rows, :], True, True, [s["WTsb"][c], Hb[hp]], [b])
            for hp, e in HE:
                s = pers[hp]
                b, o = reg(hp, e, 0)
                us = Usb[hp * 2 + e]
                ph.tt("dve", us[:, :], b[:, o:o + 64], s["Uloc"][c * 2 + e][:, :], ALU.add, [b, s["Uloc"][c * 2 + e]], [us])
            if STOP <= 5:
                continue
            for hp, e in HE:
                s = pers[hp]
                rows = slice(e * 64, (e + 1) * 64)
                us = Usb[hp * 2 + e]
                km = s["keepM"][c * 2 + e]
                tokc = s["tok"][c]
                Vtok = tokc[:, 3, rows]
                b, o = reg(hp, e, 1)
                ph.mm(b[:, o:o + 64], s["AR"][rows, c, 1, :], Hb[hp][rows, :], True, False, [s["AR"], Hb[hp]], [b])
                ph.mm(b[:, o:o + 64], km[:, 0:128], us[:, :], False, False, [km, us], [b])
                ph.mm(b[:, o:o + 64], km[:, 128:256], Vtok, False, True, [km, tokc], [b])
                b, o = reg(hp, e, 2)
                ph.mm(b[:, o:o + 64], tokc[:, 1, :], us[:, :], True, False, [tokc, us], [b])
                ph.mm(b[:, o:o + 64], tokc[:, 2, :], Vtok, False, True, [tokc], [b])
            if STOP <= 6:
                continue
            for hp, e in HE:
                s = pers[hp]
                rows = slice(e * 64, (e + 1) * 64)
                b, o = reg(hp, e, 2)
                ph.stt("dve", Hs[hp][rows, :], Hs[hp][rows, :], s["gam"][rows, c:c + 1], b[rows, o:o + 64],
                       ALU.mult, ALU.add, [Hs[hp], s["gam"], b], [Hs[hp]])
                ph.cp("act", Hb[hp][rows, :], Hs[hp][rows, :], [Hs[hp]], [Hb[hp]])
            if STOP <= 7:
                continue
            for hp, e in HE:
                i8 = hp * 2 + e
                b, o = reg(hp, e, 1)
                ph.add("dve", lambda en, i8=i8, b=b, o=o: en.bn_stats(gst[i8][:, 0:6], b[:, o:o + 64]), [b], [gst[i8]])
                ph.add("dve", lambda en, i8=i8: en.bn_aggr(gmv[i8][:, 0:2], gst[i8][:, 0:6]), [gst[i8]], [gmv[i8]])
            for hp, e in HE:
                i8 = hp * 2 + e
                ph.act(grs[i8][:, 0:1], gmv[i8][:, 1:2], AF.Ln, [gmv[i8]], [grs[i8]], bias=epsG[:, 0:1])
            for hp, e in HE:
                i8 = hp * 2 + e
                ph.act(grs[i8][:, 0:1], grs[i8][:, 0:1], AF.Exp, [grs[i8]], [grs[i8]], scale=-0.5)
            for hp, e in HE:
                i8 = hp * 2 + e
                ph.ts("dve", gnm[i8][:, 0:1], gmv[i8][:, 0:1], grs[i8][:, 0:1], -1.0, ALU.mult, ALU.mult,
                      [gmv[i8], grs[i8]], [gnm[i8]])
            for hp, e in HE:
                i8 = hp * 2 + e
                b, o = reg(hp, e, 1)
                ph.act(ynorm[hp][:, e * 64:(e + 1) * 64], b[:, o:o + 64], AF.Identity, [b, grs[i8], gnm[i8]], [ynorm[hp]],
                       bias=gnm[i8][:, 0:1], scale=grs[i8][:, 0:1])
            if STOP <= 8:
                continue
            tv = ph.ps[6].ap[:, :].bitcast(BF16)
            for hp in range(4):
                ph.tr(tv[:, hp * 128:(hp + 1) * 128], ynorm[hp][:, :], identbf[:, :], [ynorm[hp], identbf], [ph.ps[6]])
            for hp in range(4):
                s = pers[hp]
                P = lambda i: prm[:, i, hp:hp + 1]
                ph.ts("dve", fin[hp][:, :], tv[:, hp * 128:(hp + 1) * 128], P(GG_), P(GB_), ALU.mult, ALU.add,
                      [ph.ps[6], prm], [fin[hp]])
                ph.tt("pool", fin[hp][:, :], fin[hp][:, :], s["bon"][:, cs], ALU.add, [fin[hp], s["bon"]], [fin[hp]])
                ph.tt("pool", s["outT"][:, cs], fin[hp][:, :], s["gsb"][:, cs], ALU.mult, [fin[hp], s["gsb"]], [s["outT"]])
        for hp in range(4):
            ph.dma("sp", dr["mixT"][512 + hp * 128:512 + (hp + 1) * 128, tcols], pers[hp]["outT"][:, :], [pers[hp]["outT"]], [])
    ph.finish()
```

```python
from contextlib import ExitStack
import math
import numpy as np
import concourse.bass as bass
import concourse.mybir as mybir
from concourse.bass_utils import run_bass_kernel_spmd

F32 = mybir.dt.float32
BF16 = mybir.dt.bfloat16
AF = mybir.ActivationFunctionType
ALU = mybir.AluOpType
AX = mybir.AxisListType

ENGS = ("pe", "act", "dve", "pool", "sp")
NPOOL = 14


class T:
    __slots__ = ("ap", "name", "w", "r", "psum")

    def __init__(self, ap, name=""):
        self.ap = ap
        self.name = name
        self.w = None
        self.r = []
        self.psum = False

    def __getitem__(self, k):
        return self.ap[k]


class Op:
    __slots__ = ("eng", "fn", "dma", "deps", "signal", "sem", "val", "prewait")

    def __init__(self, eng, fn, dma):
        self.eng = eng
        self.fn = fn
        self.dma = dma
        self.deps = []
        self.signal = False
        self.sem = None
        self.val = 0
        self.prewait = None


class Ctx:
    def __init__(self, nc, stack):
        self.nc = nc
        self.esem = {e: stack.enter_context(nc.semaphore("s_" + e)) for e in ENGS}
        self.dsem = {q: [stack.enter_context(nc.semaphore("d_%s%d" % (q, i))) for i in range(NPOOL)]
                     for q in ("sp", "pool", "act")}
        self.count = {e: 0 for e in ENGS}
        self.dk = {q: 0 for q in ("sp", "pool", "act")}
        self.seen = {e: {} for e in ENGS}
        self.ninst = 0


class Phase:
    def __init__(self, ctx, name):
        self.ctx = ctx
        self.name = name
        self.ops = []
        self.tiles = []

    def tile(self, ap, name=""):
        t = T(ap, name)
        self.tiles.append(t)
        return t

    def add(self, eng, fn, reads=(), writes=(), dma=False):
        op = Op(eng, fn, dma)
        deps = []
        for t in reads:
            if t.w is not None:
                deps.append((t.w, "raw"))
            if t.psum:
                for r in t.r:
                    if r.eng != eng:
                        deps.append((r, "rar"))
        for t in writes:
            if t.w is not None:
                deps.append((t.w, "waw"))
            for r in t.r:
                deps.append((r, "war"))
        seen = set()
        for d, kind in deps:
            if d is op or id(d) in seen:
                continue
            if not d.dma and not op.dma and d.eng == eng:
                if eng == "pe":
                    continue
                if kind == "war" and eng != "pool":
                    continue
            seen.add(id(d))
            op.deps.append(d)
            d.signal = True
        for t in writes:
            t.w = op
            t.r = []
        for t in reads:
            if t.w is not op:
                t.r.append(op)
        self.ops.append(op)
        return op

    def emit(self):
        ctx = self.ctx
        nc = ctx.nc
        fin = Op("sp", None, False)
        lastd = {}
        for o in self.ops:
            if o.dma:
                o.signal = True
        self.ops.append(fin)
        for o in self.ops:
            if o.dma:
                q = o.eng
                k = ctx.dk[q]
                ctx.dk[q] += 1
                o.sem = ctx.dsem[q][k % NPOOL]
                o.val = 16 * (k // NPOOL + 1)
                if o.val > 16:
                    o.prewait = (o.sem, o.val - 16)
            elif o.signal:
                ctx.count[o.eng] += 1
                o.sem = ctx.esem[o.eng]
                o.val = ctx.count[o.eng]
            if o.dma:
                lastd[o.sem.name] = o
        fin.deps = list(lastd.values())
        per = {e: [o for o in self.ops if o.eng == e] for e in ENGS}
        bname = {"pe": "tensor", "act": "scalar", "dve": "vector", "pool": "gpsimd", "sp": "sync"}

        def run(e, engine):
            seen = ctx.seen[e]
            for o in per[e]:
                waits = [(d.sem, d.val) for d in o.deps]
                if o.prewait is not None:
                    waits.append(o.prewait)
                for sem, val in waits:
                    if seen.get(sem.name, 0) < val:
                        engine.wait_ge(sem, val)
                        seen[sem.name] = val
                        ctx.ninst += 1
                if o.fn is None:
                    continue
                inst = o.fn(engine)
                ctx.ninst += 1
                if o.signal:
                    inst.then_inc(o.sem, 16 if o.dma else 1)

        with nc.Block() as block:
            for e in ENGS:
                if per[e]:
                    getattr(block, bname[e])(lambda engine, e=e: run(e, engine))
        for t in self.tiles:
            t.w = None
            t.r = []


D = 1024
LN_EPS = 1e-5


class KB:
    def __init__(self, S):
        self.S = S
        self.NT = S // 128
        self.nc = bass.Bass("TRN2", target_bir_lowering=False)
        self.stack = ExitStack()
        self.ctx = Ctx(self.nc, self.stack)
        self.dram = {}
        self.psum_h = [self.stack.enter_context(self.nc.psum_tensor("ps%d" % i, [128, 512], F32))
                       for i in range(8)]

    def din(self, name, shape, dt=F32):
        t = self.nc.dram_tensor(name, list(shape), dt, kind="ExternalInput")
        self.dram[name] = t
        return t

    def dout(self, name, shape, dt=F32):
        t = self.nc.dram_tensor(name, list(shape), dt, kind="ExternalOutput")
        self.dram[name] = t
        return t

    def dscr(self, name, shape, dt=F32, debug=False):
        if name in getattr(self, "ext_in", ()):
            return self.din(name, shape, dt)
        t = self.nc.dram_tensor(name, list(shape), dt, kind="ExternalOutput" if debug else "Internal")
        self.dram[name] = t
        return t

    def close(self):
        self.stack.close()


class PH(Phase):
    def __init__(self, kb, name):
        super().__init__(kb.ctx, name)
        self.kb = kb
        self.nc = kb.nc
        self.st = ExitStack()
        self.ps = [self.tile(h, "ps%d" % i) for i, h in enumerate(kb.psum_h)]
        for t in self.ps:
            t.psum = True
        self.nsb = 0

    def sb(self, shape, dt=F32, name=None):
        self.nsb += 1
        h = self.st.enter_context(self.nc.sbuf_tensor("%s_%s%d" % (self.name, name or "t", self.nsb), list(shape), dt))
        return self.tile(h, name or "t")

    def dt_(self, name):
        key = "_dt_" + name
        if not hasattr(self, key):
            setattr(self, key, self.tile(self.kb.dram[name], name))
        return getattr(self, key)

    def finish(self):
        self.emit()
        self.st.close()

    def dma(self, q, out_ap, in_ap, reads, writes):
        return self.add(q, lambda e: e.dma_start(out=out_ap, in_=in_ap), reads, writes, dma=True)

    def mm(self, out_ap, lhsT, rhs, start, stop, reads, writes, skip=False):
        if skip:
            return self.add("pe", lambda e: e.matmul(out_ap, lhsT, rhs, start=start, stop=stop, skip_group_check=True),
                            reads, writes)
        return self.add("pe", lambda e: e.matmul(out_ap, lhsT, rhs, start=start, stop=stop), reads, writes)

    def tr(self, out_ap, in_ap, ident_ap, reads, writes):
        return self.add("pe", lambda e: e.transpose(out_ap, in_ap, ident_ap), reads, writes)

    def act(self, out_ap, in_ap, func, reads, writes, bias=0.0, scale=1.0, accum_out=None, eng="act"):
        kw = {}
        if accum_out is not None:
            kw["accum_out"] = accum_out
        return self.add(eng, lambda e: e.activation(out_ap, in_ap, func, bias=bias, scale=scale, **kw), reads, writes)

    def tt(self, eng, out_ap, in0, in1, op, reads, writes):
        return self.add(eng, lambda e: e.tensor_tensor(out_ap, in0, in1, op), reads, writes)

    def ts(self, eng, out_ap, in0, s1, s2, op0, op1, reads, writes, accum_out=None):
        if s2 is None:
            return self.add(eng, lambda e: e.tensor_scalar(out_ap, in0, s1, None, op0), reads, writes)
        if accum_out is not None:
            return self.add(eng, lambda e: e.tensor_scalar(out_ap, in0, s1, s2, op0, op1, accum_out=accum_out), reads, writes)
        return self.add(eng, lambda e: e.tensor_scalar(out_ap, in0, s1, s2, op0, op1), reads, writes)

    def stt(self, eng, out_ap, in0, scalar, in1, op0, op1, reads, writes):
        return self.add(eng, lambda e: e.scalar_tensor_tensor(out_ap, in0, scalar, in1, op0, op1), reads, writes)

    def cp(self, eng, out_ap, in_ap, reads, writes):
        if eng == "act":
            return self.add(eng, lambda e: e.copy(out_ap, in_ap), reads, writes)
        return self.add(eng, lambda e: e.tensor_copy(out_ap, in_ap), reads, writes)


def bcast_row(dram_ap_1d, n):
    return dram_ap_1d.partition_broadcast(128)


def ln_rows(ph, src, i, gB, bB, ident_bf, xres_name, xT_name, bufs, psb):
    nc = ph.nc
    st, mv, rs, nm, xn, ybf, xTs = (bufs[k] for k in ("st", "mv", "rs", "nm", "xn", "ybf", "xTs"))
    xh = bufs["xh"]
    for h in range(2):
        ph.add("dve", lambda e, h=h: e.bn_stats(st[:, h * 6:(h + 1) * 6], src[:, h * 512:(h + 1) * 512]), [src], [st])
    ph.add("dve", lambda e: e.bn_aggr(mv[:, 0:2], st[:, 0:12]), [st], [mv])
    ph.act(rs[:, 0:1], mv[:, 1:2], AF.Ln, [mv], [rs], bias=bufs["eps"][:, 0:1])
    ph.act(rs[:, 0:1], rs[:, 0:1], AF.Exp, [rs], [rs], scale=-0.5)
    ph.ts("dve", nm[:, 0:1], mv[:, 0:1], rs[:, 0:1], -1.0, ALU.mult, ALU.mult, [mv, rs], [nm])
    xres = ph.kb.dram[xres_name]
    for h in range(2):
        hs = slice(h * 512, (h + 1) * 512)
        ph.act(xn[:, hs], src[:, hs], AF.Identity, [src, rs, nm], [xh[h]], bias=nm[:, 0:1], scale=rs[:, 0:1])
        ph.tt("dve", xn[:, hs], xn[:, hs], gB[:, hs], ALU.mult, [xh[h], gB], [xh[h]])
        ph.tt(("dve", "pool")[h], xn[:, hs], xn[:, hs], bB[:, hs], ALU.add, [xh[h], bB], [xh[h]])
        ph.dma("sp", xres[i * 128:(i + 1) * 128, hs], xn[:, hs], [xh[h]], [])
    if xT_name is None:
        return
    pb = psb.ap[:, :].bitcast(BF16)
    for h in range(2):
        hs = slice(h * 512, (h + 1) * 512)
        ph.cp("act", ybf[:, hs], xn[:, hs], [xh[h]], [ybf])
    for c in range(8):
        ph.tr(pb[:, c * 128:(c + 1) * 128], ybf[:, c * 128:(c + 1) * 128], ident_bf[:, :], [ybf, ident_bf], [psb])
    ph.cp("dve", xTs[:, :], pb[:, :], [psb], [xTs])
    xT = ph.kb.dram[xT_name]
    dst = xT[:, i * 128:(i + 1) * 128].rearrange("(c p) t -> p c t", p=128)
    ph.dma("pool", dst, xTs[:, :].rearrange("p (c t) -> p c t", c=8), [xTs], [])


def ln_bufs(ph, k):
    eps = ph.sb([128, 1], F32, "eps%d" % k)
    ph.add("pool", lambda e: e.memset(eps[:, :], LN_EPS), [], [eps])
    xn = ph.sb([128, 1024], F32, "xn%d" % k)
    xh = [ph.tile(xn.ap[:, 0:512], "xnA"), ph.tile(xn.ap[:, 512:1024], "xnB")]
    return dict(eps=eps, st=ph.sb([128, 12], F32, "st%d" % k), mv=ph.sb([128, 2], F32, "mv%d" % k),
                rs=ph.sb([128, 1], F32, "rs%d" % k), nm=ph.sb([128, 1], F32, "nm%d" % k),
                xn=xn, xh=xh, ybf=ph.sb([128, 1024], BF16, "ybf%d" % k),
                xTs=ph.sb([128, 1024], BF16, "xTs%d" % k))


def load_consts(ph):
    c = {}
    c["ident_bf"] = ph.sb([128, 128], BF16, "identbf")
    c["ident_f"] = ph.sb([128, 128], F32, "identf")
    ph.dma("sp", c["ident_f"][:, :], ph.kb.dram["c_ident"][:, :], [], [c["ident_f"]])
    ph.cp("dve", c["ident_bf"][:, :], c["ident_f"][:, :], [c["ident_f"]], [c["ident_bf"]])
    return c


def phase_ln0(kb):
    ph = PH(kb, "ln0")
    S, NT = kb.S, kb.NT
    c = load_consts(ph)
    gB = ph.sb([128, 1024], F32, "gB")
    bB = ph.sb([128, 1024], F32, "bB")
    ph.dma("sp", gB[:, :], kb.dram["ln_in_g"][:].partition_broadcast(128), [], [gB])
    ph.dma("sp", bB[:, :], kb.dram["ln_in_b"][:].partition_broadcast(128), [], [bB])
    NB = 3
    xin = [ph.sb([128, 1024], F32, "xin%d" % k) for k in range(NB)]
    bufs = [ln_bufs(ph, k) for k in range(4)]
    for i in range(NT):
        src = xin[i % NB]
        ph.dma("sp", src[:, :], kb.dram["x"][i * 128:(i + 1) * 128, :], [], [src])
        ln_rows(ph, src, i, gB, bB, c["ident_bf"], "xres", "xT", bufs[i % 4], ph.ps[i % 4])
    ph.finish()


N_DIFF = 1536
NRW = [1824, 1856]


def phase_inproj(kb, l):
    ph = PH(kb, "ip%d" % l)
    S, NT = kb.S, kb.NT
    NG = S // 512
    ncol = N_DIFF + NRW[l]
    nrw_tiles = 15
    w_dram = kb.dram["w_in%d" % l]
    xT_sb = ph.sb([128, 8, S], BF16, "xT")
    xT = kb.dram["xT"]
    for c in range(8):
        ph.dma("sp" if c % 2 == 0 else "pool", xT_sb[:, c, :], xT[c * 128:(c + 1) * 128, :], [], [xT_sb])
    wbf = [ph.sb([128, ncol], BF16, "wbf%d" % c) for c in range(8)]
    hc = ncol // 4
    wst = [ph.sb([128, hc], F32, "wst%d" % k) for k in range(4)]
    kk_ = 0
    for hh in range(4):
        for c in range(8):
            s = wst[kk_ % 4]
            ph.dma(("act", "sp")[kk_ % 2], s[:, :], w_dram[c * 128:(c + 1) * 128, hh * hc:(hh + 1) * hc], [], [s])
            ph.cp(("dve", "pool", "act")[kk_ % 3], wbf[c][:, hh * hc:(hh + 1) * hc], s[:, :], [s], [wbf[c]])
            kk_ += 1
    mu_sb = ph.sb([128, 15], F32, "mu")
    ph.dma("sp", mu_sb[:, :], kb.dram["mu%d" % l][:, :], [], [mu_sb])
    bank = [0]

    def nextbank():
        b = ph.ps[bank[0] % 8]
        bank[0] += 1
        return b

    ob = [ph.sb([128, 512], BF16, "ob%d" % k) for k in range(4)]
    oi = 0
    for which, name, scale in ((0, "qT", 0.125), (1, "kT", 1.0)):
        for h in range(4):
            col0 = which * 512 + h * 128
            for g in range(NG):
                pb = nextbank()
                for c in range(8):
                    ph.mm(pb[:, :], wbf[c][:, col0:col0 + 128], xT_sb[:, c, g * 512:(g + 1) * 512],
                          c == 0, c == 7, [wbf[c], xT_sb], [pb])
                o = ob[oi % 4]
                oi += 1
                if oi % 2 == 0:
                    ph.act(o[:, :], pb[:, :], AF.Copy, [pb], [o], scale=scale)
                else:
                    ph.ts("dve", o[:, :], pb[:, :], scale, None, ALU.mult, None, [pb], [o])
                ph.dma("sp", kb.dram[name][h, :, g * 512:(g + 1) * 512], o[:, :], [o], [])
    for i in range(NT):
        pb = nextbank()
        for c in range(8):
            ph.mm(pb[:, :], xT_sb[:, c, i * 128:(i + 1) * 128], wbf[c][:, 1024:1536], c == 0, c == 7,
                  [wbf[c], xT_sb], [pb])
        o = ob[oi % 4]
        oi += 1
        if oi % 2 == 0:
            ph.cp("act", o[:, :], pb[:, :], [pb], [o])
        else:
            ph.cp("dve", o[:, :], pb[:, :], [pb], [o])
        ph.dma("sp", kb.dram["vd"][i * 128:(i + 1) * 128, :], o[:, :], [o], [])
    pfull = [ph.sb([128, S + 1], F32, "pfull%d" % k) for k in range(2)]
    tmp = [ph.sb([128, S], F32, "tmp%d" % k) for k in range(1)]
    for k in range(2):
        ph.add("pool", lambda e, k=k: e.memset(pfull[k][:, 0:1], 0.0), [], [pfull[k]])
    for t in range(nrw_tiles):
        col0 = N_DIFF + t * 128
        rows = min(128, ncol - col0)
        pf = pfull[t % 2]
        tm = tmp[0]
        for g in range(NG):
            pb = nextbank()
            for c in range(8):
                ph.mm(pb[0:rows, :], wbf[c][:, col0:col0 + rows], xT_sb[:, c, g * 512:(g + 1) * 512],
                      c == 0, c == 7, [wbf[c], xT_sb], [pb])
            ph.cp("act", pf[0:rows, 1 + g * 512:1 + (g + 1) * 512], pb[0:rows, :], [pb], [pf])
        ph.tt("pool", tm[0:rows, :], pf[0:rows, 0:S], pf[0:rows, 1:S + 1], ALU.subtract, [pf], [tm])
        ph.stt("dve", tm[0:rows, :], tm[0:rows, :], mu_sb[0:rows, t:t + 1], pf[0:rows, 1:S + 1], ALU.mult, ALU.add,
               [tm, mu_sb, pf], [tm])
        ph.dma("sp", kb.dram["prT"][t * 128:t * 128 + rows, :], tm[0:rows, :], [tm], [])
        if l == 0 and 8 <= t < 12:
            ph.dma("pool", kb.dram["vfT"][(t - 8) * 128:(t - 7) * 128, :], tm[0:rows, :], [tm], [])
    ph.finish()


LAM_INIT = [0.8 - 0.6 * math.exp(-0.3 * l) for l in range(2)]
SUBLN_EPS = 1e-5


def phase_pre(kb):
    ph = PH(kb, "pre")
    nc = kb.nc
    dr = kb.dram
    lt = ph.sb([1, 8, 64], F32, "lt")
    for i, nm in enumerate(("lambda_q1", "lambda_k1", "lambda_q2", "lambda_k2")):
        ph.dma("sp", lt[0:1, 2 * i:2 * i + 2, :], dr[nm][:, :].rearrange("(o l) d -> o l d", o=1), [], [lt])
    pr = ph.sb([1, 4, 64], F32, "pr")
    ph.tt("dve", pr[0:1, 0:2, :], lt[0:1, 0:2, :], lt[0:1, 2:4, :], ALU.mult, [lt], [pr])
    ph.tt("dve", pr[0:1, 2:4, :], lt[0:1, 4:6, :], lt[0:1, 6:8, :], ALU.mult, [lt], [pr])
    sm = ph.sb([1, 4], F32, "sm")
    ph.add("dve", lambda e: e.reduce_sum(sm[0:1, 0:4], pr[0:1, :, :], AX.X), [pr], [sm])
    ex = ph.sb([1, 4], F32, "ex")
    ph.act(ex[0:1, :], sm[0:1, :], AF.Exp, [sm], [ex])
    lam = ph.sb([1, 2], F32, "lam")
    ph.tt("dve", lam[0:1, :], ex[0:1, 2:4], ex[0:1, 0:2], ALU.subtract, [ex], [lam])
    for l in range(2):
        ph.ts("dve", lam[0:1, l:l + 1], lam[0:1, l:l + 1], -LAM_INIT[l], None, ALU.add, None, [lam], [lam])
    ph.dma("sp", dr["lamneg"][0:1, :], lam[0:1, :], [lam], [])
    rb = ph.sb([32, 4], F32, "rb")
    oh = ph.sb([32, 385], F32, "oh")
    ph.dma("sp", rb[:, :], dr["rel_bias"][:, :], [], [rb])
    ph.dma("sp", oh[:, :], dr["c_oh"][:, :], [], [oh])
    J = ph.sb([128, 128], F32, "J")
    Mc = ph.sb([128, 128], F32, "Mc")
    ph.dma("pool", J[:, :], dr["c_J"][:, :], [], [J])
    ph.dma("pool", Mc[:, :], dr["c_causal"][:, :], [], [Mc])
    pb = ph.ps[0]
    ph.mm(pb[0:4, 0:385], rb[:, :], oh[:, :], True, True, [rb, oh], [pb])
    G = ph.sb([4, 385], F32, "G")
    ph.cp("dve", G[:, :], pb[0:4, 0:385], [pb], [G])
    EG = ph.sb([4, 384], F32, "EG")
    ph.ts("dve", EG[:, :], G[:, 0:384], G[:, 384:385], None, ALU.subtract, None, [G], [EG])
    ph.act(EG[:, :], EG[:, :], AF.Exp, [EG], [EG])
    egd = ph.dt_("EGd")
    ph.dma("sp", dr["EGd"][:, :], EG[:, :], [EG], [egd])
    for h in range(4):
        for ty in range(2):
            tr = ph.sb([128, 128], F32, "trev")
            src = bass.AP(dr["EGd"], h * 384 + 128 * ty, [[1, 128], [1, 128]])
            ph.dma("sp", tr[:, :], src, [egd], [tr])
            p2 = ph.ps[1 + (h * 2 + ty) % 4]
            ph.mm(p2[:, 0:128], J[:, :], tr[:, :], True, True, [J, tr], [p2])
            eb = ph.sb([128, 128], F32, "eb")
            if ty == 0:
                ph.tt("dve", eb[:, :], p2[:, 0:128], Mc[:, :], ALU.mult, [p2, Mc], [eb])
            else:
                ph.cp("dve", eb[:, :], p2[:, 0:128], [p2], [eb])
            ph.dma("sp", dr["Eb"][h, ty, :, :], eb[:, :], [eb], [])
    ph.finish()


def phase_attn(kb, l):
    ph = PH(kb, "at%d" % l)
    S, NT = kb.S, kb.NT
    NG = S // 512
    dr = kb.dram
    c = load_consts(ph)
    lamneg = ph.sb([128, 2], F32, "lamneg")
    ph.dma("sp", lamneg[:, :], bass.AP(dr["lamneg"], 0, [[0, 128], [1, 2]]), [], [lamneg])
    sg = ph.sb([128, 1], F32, "sg")
    ph.dma("sp", sg[:, :], dr["subln_g"][l:l + 1, :].rearrange("o d -> d o"), [], [sg])
    epsS = ph.sb([128, 1], F32, "epsS")
    ph.add("pool", lambda e: e.memset(epsS[:, :], SUBLN_EPS), [], [epsS])
    Eb = [[ph.sb([128, 128], F32, "Eb") for ty in range(2)] for h in range(2)]
    qT = [ph.sb([128, S], BF16, "qT") for _ in range(2)]
    kT = [ph.sb([128, S], BF16, "kT") for _ in range(2)]
    Vx = [ph.sb([128, NT, 129], BF16, "Vx") for _ in range(2)]
    for k in range(2):
        ph.add("pool", lambda e, k=k: e.memset(Vx[k][:, :, 128:129], 1.0), [], [Vx[k]])
    Pt = [ph.sb([128, 512], BF16, "Pt") for _ in range(4)]
    Pf = [ph.sb([128, 256], F32, "Pf") for _ in range(2)]
    Osb = [ph.sb([128, 480], F32, "Osb") for _ in range(3)]
    small = {k: ph.sb([128, 4], F32, k) for k in ("r1", "r2", "ss", "rstd")}
    t1 = [ph.sb([128, 128], F32, "t1") for _ in range(2)]
    ybf = [ph.sb([128, 128], BF16, "ybf") for _ in range(2)]
    outT = [ph.sb([128, 512], BF16, "outT") for _ in range(2)]
    Obank = [ph.ps[0], ph.ps[1], ph.ps[2]]
    Sbank = [ph.ps[3], ph.ps[4], ph.ps[5], ph.ps[6]]
    Tbank = ph.ps[7]
    pi = [0]
    pfi = [0]
    for h in range(4):
        hb = h % 2
        q_sb, k_sb, v_sb, E = qT[hb], kT[hb], Vx[hb], Eb[hb]
        ph.dma("sp", q_sb[:, :], dr["qT"][h, :, :], [], [q_sb])
        ph.dma("pool", k_sb[:, :], dr["kT"][h, :, :], [], [k_sb])
        ph.dma("sp", v_sb[:, :, 0:128], dr["vd"][:, h * 128:(h + 1) * 128].rearrange("(t p) d -> p t d", p=128),
               [], [v_sb])
        for ty in range(2):
            ph.dma("pool", E[ty][:, :], dr["Eb"][h, ty, :, :], [], [E[ty]])
        for g in range(NG):
            steps = [(kbk, m) for kbk in range(4 * g + 4) for m in range(2)]
            sb_of = {}

            def emit_qk(si):
                kbk, m = steps[si]
                jlo = max(0, kbk - 4 * g)
                n = (4 - jlo) * 128
                sbk = Sbank[(kbk % 2) * 2 + m]
                sb_of[si] = sbk
                rows = slice(m * 64, (m + 1) * 64)
                ph.mm(sbk[:, 0:n], k_sb[rows, kbk * 128:(kbk + 1) * 128],
                      q_sb[rows, (4 * g + jlo) * 128:(4 * g + 4) * 128], True, True, [k_sb, q_sb], [sbk])

            def emit_rest(si):
                kbk, m = steps[si]
                jlo = max(0, kbk - 4 * g)
                n = (4 - jlo) * 128
                sbk = sb_of[si]
                P = Pt[pi[0] % 4]
                pi[0] += 1
                near = [(j, 4 * g + j - kbk) for j in range(jlo, 4) if 0 <= 4 * g + j - kbk <= 1]
                nn = len(near)
                if nn:
                    j0 = near[0][0]
                    c0 = (j0 - jlo) * 128
                    pf = Pf[pfi[0] % 2]
                    pfi[0] += 1
                    ph.act(pf[:, 0:nn * 128], sbk[:, c0:c0 + nn * 128], AF.Exp, [sbk], [pf])
                    for ii, (j, ty) in enumerate(near):
                        ph.tt("dve", P[:, c0 + ii * 128:c0 + (ii + 1) * 128], pf[:, ii * 128:(ii + 1) * 128],
                              E[ty][:, :], ALU.mult, [pf, E[ty]], [P])
                    far0 = c0 + nn * 128
                    if far0 < n:
                        ph.act(P[:, far0:n], sbk[:, far0:n], AF.Exp, [sbk], [P])
                else:
                    ph.act(P[:, 0:n], sbk[:, 0:n], AF.Exp, [sbk], [P])
                for j in range(jlo, 4):
                    idx = m * 4 + j
                    ob = Obank[idx // 3]
                    cc = (idx % 3) * 160
                    ph.mm(ob[:, cc:cc + 129], P[:, (j - jlo) * 128:(j - jlo + 1) * 128], v_sb[:, kbk, :],
                          kbk == 0 and idx % 3 == 0, kbk == 4 * g + j, [P, v_sb], [ob], skip=True)

            ns = len(steps)
            emit_qk(0)
            emit_qk(1)
            for si in range(0, ns, 2):
                if si + 2 < ns:
                    emit_qk(si + 2)
                    emit_qk(si + 3)
                emit_rest(si)
                emit_rest(si + 1)
            for b4 in range(3):
                if b4 % 2 == 0:
                    ph.cp("act", Osb[b4][:, :], Obank[b4][:, 0:480], [Obank[b4]], [Osb[b4]])
                else:
                    ph.cp("dve", Osb[b4][:, :], Obank[b4][:, 0:480], [Obank[b4]], [Osb[b4]])
            oT = outT[g % 2]
            tb = Tbank.ap[:, :].bitcast(BF16)
            for j in range(4):
                o0 = Osb[j // 3]
                o1 = Osb[(4 + j) // 3]
                cc = (j % 3) * 160
                cc1 = ((4 + j) % 3) * 160
                r1, r2, ss, rstd = (small[k] for k in ("r1", "r2", "ss", "rstd"))
                ph.add("dve", lambda e, o0=o0, cc=cc, j=j: e.reciprocal(r1[:, j:j + 1], o0[:, cc + 128:cc + 129]), [o0], [r1])
                ph.add("dve", lambda e, o1=o1, cc1=cc1, j=j: e.reciprocal(r2[:, j:j + 1], o1[:, cc1 + 128:cc1 + 129]), [o1], [r2])
                ph.ts("dve", r2[:, j:j + 1], r2[:, j:j + 1], lamneg[:, l:l + 1], None, ALU.mult, None, [r2, lamneg], [r2])
                tt1 = t1[j % 2]
                ph.ts("dve", tt1[:, :], o0[:, cc:cc + 128], r1[:, j:j + 1], None, ALU.mult, None, [o0, r1], [tt1])
                ph.stt("dve", tt1[:, :], o1[:, cc1:cc1 + 128], r2[:, j:j + 1], tt1[:, :], ALU.mult, ALU.add, [o1, r2, tt1], [tt1])
                yb = ybf[j % 2]
                ph.act(yb[:, :], tt1[:, :], AF.Square, [tt1], [yb, ss], accum_out=ss[:, j:j + 1])
                ph.act(rstd[:, j:j + 1], ss[:, j:j + 1], AF.Ln, [ss], [rstd], bias=epsS[:, 0:1], scale=1.0 / 128)
                ph.act(rstd[:, j:j + 1], rstd[:, j:j + 1], AF.Exp, [rstd], [rstd], scale=-0.5)
                ph.act(yb[:, :], tt1[:, :], AF.Copy, [tt1, rstd], [yb], scale=rstd[:, j:j + 1])
                ph.tr(tb[:, j * 128:(j + 1) * 128], yb[:, :], c["ident_bf"][:, :], [yb, c["ident_bf"]], [Tbank])
            ph.ts("dve", oT[:, :], tb[:, 0:512], sg[:, 0:1], 1.0 - LAM_INIT[l], ALU.mult, ALU.mult, [Tbank, sg], [oT])
            ph.dma("sp", dr["mixT"][h * 128:(h + 1) * 128, g * 512:(g + 1) * 512], oT[:, :], [oT], [])
    ph.finish()


def host_consts():
    c = {}
    c["c_ident"] = np.eye(128, dtype=np.float32)
    c["c_J"] = np.ascontiguousarray(np.eye(128, dtype=np.float32)[::-1])
    kk, qq = np.meshgrid(np.arange(128), np.arange(128), indexing="ij")
    c["c_causal"] = (qq >= kk).astype(np.float32)
    n = np.maximum(np.arange(384) - 127, 0)
    nf = np.maximum(n, 1).astype(np.float32)
    large = 16 + (np.log(nf / np.float32(16)) / np.float32(math.log(128 / 16)) * np.float32(16)).astype(np.int32)
    large = np.minimum(large, 31)
    bucket = np.where(n < 16, n, large)
    oh = np.zeros((32, 385), np.float32)
    oh[bucket, np.arange(384)] = 1.0
    oh[31, 384] = 1.0
    c["c_oh"] = oh
    pp, ff = np.meshgrid(np.arange(128), np.arange(128), indexing="ij")
    SU = (ff > pp).astype(np.float32)
    IU = (ff >= pp).astype(np.float32)
    c["c_mask4"] = np.ascontiguousarray(np.concatenate([SU, IU, SU, IU], axis=1))
    c["c_maskSL"] = (ff < pp).astype(np.float32)
    c["c_bones"] = ((pp // 64) == (ff // 64)).astype(np.float32)
    cm = np.ones((128, 512), np.float32)
    cm[:, ::128] = 0.0
    c["c_cm01"] = cm
    return c


def host_rwp(inp, l):
    vs = [inp["rw_w0"][l], inp["rw_a0"][l], inp["rw_k_k"][l], inp["rw_k_a"][l], inp["rw_r_k"][l].reshape(512),
          inp["rw_gn_g"][l], inp["rw_gn_b"][l], inp["rw_v0"][l - 1] if l > 0 else np.zeros(512, np.float32)]
    a = np.stack([np.asarray(v, np.float32).reshape(4, 128).T for v in vs], axis=1)
    return np.ascontiguousarray(a)


GN_EPS = 64e-5
DECAY_C = -math.exp(-0.5)


def phase_rwkv(kb, l):
    ph = PH(kb, "rw%d" % l)
    S, NT = kb.S, kb.NT
    NTB = S // 512
    dr = kb.dram
    cst = load_consts(ph)
    identbf = cst["ident_bf"]
    mask4 = ph.sb([128, 512], F32, "mask4")
    maskSL = ph.sb([128, 128], F32, "maskSL")
    bones = ph.sb([128, 128], F32, "bones")
    cm01 = ph.sb([128, 512], F32, "cm01")
    ph.dma("sp", mask4[:, :], dr["c_mask4"][:, :], [], [mask4])
    ph.dma("sp", maskSL[:, :], dr["c_maskSL"][:, :], [], [maskSL])
    ph.dma("sp", bones[:, :], dr["c_bones"][:, :], [], [bones])
    ph.dma("sp", cm01[:, :], dr["c_cm01"][:, :], [], [cm01])
    prm = ph.sb([128, 8, 4], F32, "prm")
    ph.dma("sp", prm[:, :, :], dr["rwp%d" % l][:, :, :], [], [prm])
    W0, A0, KK_, KA_, RK_, GG_, GB_, V0_ = range(8)
    WA = ph.sb([128, 512], F32, "WA")
    WG1 = ph.sb([128, 512], F32, "WG1")
    WGV = ph.sb([64, 512], F32, "WGV")
    ph.dma("sp", WA[0:64, :], dr["rw_w_up"][l, :, :], [], [WA])
    ph.dma("sp", WA[64:128, :], dr["rw_a_up"][l, :, :], [], [WA])
    ph.dma("sp", WG1[:, :], dr["rw_g_up"][l, 0:128, :], [], [WG1])
    ph.dma("sp", WGV[0:32, :], dr["rw_g_up"][l, 128:160, :], [], [WGV])
    if l > 0:
        ph.dma("sp", WGV[32:64, :], dr["rw_v_up"][l - 1, :, :], [], [WGV])
    omk = ph.sb([128, 4], F32, "omk")
    ph.ts("dve", omk[:, :], prm[:, KA_, :], -1.0, 1.0, ALU.mult, ALU.add, [prm], [omk])
    epsG = ph.sb([128, 1], F32, "epsG")
    ph.add("pool", lambda e: e.memset(epsG[:, :], GN_EPS), [], [epsG])
    H = ph.sb([128, 4, 64], F32, "H")
    Hb = [ph.sb([128, 64], BF16, "Hb%d" % hp) for hp in range(4)]
    Hs = [ph.sb([128, 64], F32, "Hs%d" % hp) for hp in range(4)]
    for hp in range(4):
        ph.add("pool", lambda e, hp=hp: e.memset(Hs[hp][:, :], 0.0), [], [Hs[hp]])
        ph.add("pool", lambda e, hp=hp: e.memset(Hb[hp][:, :], 0.0), [], [Hb[hp]])
    XWA = ph.sb([128, 512], F32, "XWA")
    XG1 = ph.sb([128, 512], F32, "XG1")
    XG2 = ph.sb([64, 512], F32, "XG2")
    TH = ph.sb([64, 512], F32, "TH")
    SG1 = ph.sb([128, 512], F32, "SG1")
    SG2 = ph.sb([32, 512], F32, "SG2")
    def f32t(n):
        return ph.sb([128, 512], F32, n)

    def bft(n):
        return ph.sb([128, 512], BF16, n)
    NSET = 2
    sets = []
    for k in range(NSET):
        sets.append(dict(
            R=f32t("R"), Kt=f32t("Kt"), Vt=f32t("Vt"), VF=f32t("VF") if l > 0 else None,
            cum=f32t("cum"), epos=f32t("epos"), AR=ph.sb([128, 4, 2, 128], BF16, "AR"),
            BT=bft("BT"), KT=bft("KT"), BH=bft("BH"), KH=bft("KH"), VB=bft("VB"),
            tok=[ph.sb([128, 4, 128], BF16, "tok") for _ in range(4)],
            gsb=f32t("gsb"), bon=f32t("bon"), outT=bft("outT")))
    sgw, lw, tmpc, eprev, eneg, edec, a_, kk, kk2, ssm, rn, kkn, tq, kp, bbt, vp, gate, pr_ = (
        f32t(n) for n in ("sgw", "lw", "tmpc", "eprev", "eneg", "edec", "a", "kk", "kk2", "ssm", "rn", "kkn",
                          "tq", "kp", "bbt", "vp", "gate", "pr"))
    M4 = [ph.sb([128, 512], BF16, "M4") for _ in range(4)]
    PQ = [[ph.sb([128, 256], BF16, "PQ") for _ in range(2)] for _ in range(2)]
    P0 = [ph.sb([128, 128], BF16, "P0") for _ in range(2)]
    TT = [[ph.sb([128, 128], BF16, "TT") for _ in range(2)] for _ in range(2)]
    AkV = [ph.sb([128, 64], BF16, "AkV") for _ in range(2)]
    WTsb = [ph.sb([128, 128], BF16, "WTsb") for _ in range(2)]
    Uloc = [ph.sb([128, 64], F32, "Uloc") for _ in range(2)]
    Usb = [ph.sb([128, 64], BF16, "Usb") for _ in range(2)]
    ynorm = [ph.sb([128, 128], BF16, "ynorm") for _ in range(2)]
    gst = [ph.sb([128, 6], F32, "gst") for _ in range(2)]
    gmv = [ph.sb([128, 2], F32, "gmv") for _ in range(2)]
    grs = [ph.sb([128, 1], F32, "grs") for _ in range(2)]
    gnm = [ph.sb([128, 1], F32, "gnm") for _ in range(2)]
    fin = [ph.sb([128, 128], F32, "fin") for _ in range(2)]
    PA = [ph.ps[0], ph.ps[1]]
    TRB = [ph.ps[2], ph.ps[3]]
    kb_h = kb.psum_h

    def subtiles(bi, bounds):
        return [ph.ps[bi] for a, b in bounds]
    W1 = [subtiles(4 + e, [(0, 64), (64, 128), (128, 192), (192, 256), (256, 320), (320, 384), (384, 512)])
          for e in range(2)]
    W2 = [subtiles(6 + e, [(0, 128), (128, 384), (384, 512)]) for e in range(2)]
    pai = [0]

    def nextpa():
        b = PA[pai[0] % 2]
        pai[0] += 1
        return b

    def c3(ap):
        return ap.rearrange("p (c t) -> p c t", c=4)

    ui = [0]
    for tb in range(NTB):
        tcols = slice(tb * 512, (tb + 1) * 512)
        ph.dma("sp", XWA[:, :], dr["prT"][1536:1664, tcols], [], [XWA])
        ph.dma("pool", XG1[:, :], dr["prT"][1664:1792, tcols], [], [XG1])
        ph.dma("pool", XG2[:, :], dr["prT"][1792:1856, tcols], [], [XG2])
        ph.act(TH[:, :], XWA[0:64, :], AF.Tanh, [XWA], [TH])
        ph.act(SG1[:, :], XG1[:, :], AF.Sigmoid, [XG1], [SG1])
        ph.act(SG2[:, :], XG2[0:32, :], AF.Sigmoid, [XG2], [SG2])
        for hp in range(4):
            s = sets[(tb * 4 + hp) % NSET]
            R, Kt, Vt, VF, cum, epos, AR, BT, KT, BH, KH, VB, tok, gsb, bon, outT = (
                s[k] for k in ("R", "Kt", "Vt", "VF", "cum", "epos", "AR", "BT", "KT", "BH", "KH", "VB", "tok",
                               "gsb", "bon", "outT"))
            hc = slice(hp * 128, (hp + 1) * 128)
            P = lambda i: prm[:, i, hp:hp + 1]
            ph.dma("sp", R[:, :], dr["prT"][hp * 128:(hp + 1) * 128, tcols], [], [R])
            ph.dma("sp", Kt[:, :], dr["prT"][512 + hp * 128:512 + (hp + 1) * 128, tcols], [], [Kt])
            ph.dma("pool", Vt[:, :], dr["prT"][1024 + hp * 128:1024 + (hp + 1) * 128, tcols], [], [Vt])
            if l > 0:
                ph.dma("pool", VF[:, :], dr["vfT"][hp * 128:(hp + 1) * 128, tcols], [], [VF])
            pa = nextpa()
            ph.mm(pa[:, :], WA[0:64, hc], TH[:, :], True, True, [WA, TH], [pa])
            ph.act(sgw[:, :], pa[:, :], AF.Sigmoid, [pa, prm], [sgw], bias=P(W0))
            ph.ts("dve", lw[:, :], sgw[:, :], DECAY_C, None, ALU.mult, None, [sgw], [lw])
            ph.add("dve", lambda e, cum=cum: e.tensor_tensor_scan(cum[:, :], cm01[:, :], lw[:, :], 0.0, ALU.mult, ALU.add),
                   [cm01, lw], [cum])
            ph.act(epos[:, :], cum[:, :], AF.Exp, [cum], [epos])
            ph.tt("pool", tmpc[:, :], cum[:, :], lw[:, :], ALU.subtract, [cum, lw], [tmpc])
            ph.act(eprev[:, :], tmpc[:, :], AF.Exp, [tmpc], [eprev])
            ph.act(eneg[:, :], cum[:, :], AF.Exp, [cum], [eneg], scale=-1.0)
            for c in range(4):
                ph.act(edec[:, c * 128:(c + 1) * 128], cum[:, c * 128:(c + 1) * 128], AF.Exp, [cum], [edec],
                       scale=-1.0, bias=cum[:, c * 128 + 127:c * 128 + 128])
            pa = nextpa()
            ph.mm(pa[:, :], WA[64:128, hc], XWA[64:128, :], True, True, [WA, XWA], [pa])
            ph.act(a_[:, :], pa[:, :], AF.Sigmoid, [pa, prm], [a_], bias=P(A0))
            ph.ts("dve", kk[:, :], Kt[:, :], P(KK_), None, ALU.mult, None, [Kt, prm], [kk])
            ph.tt("pool", kk2[:, :], kk[:, :], kk[:, :], ALU.mult, [kk], [kk2])
            pa = nextpa()
            ph.mm(pa[:, :], bones[:, :], kk2[:, :], True, True, [bones, kk2], [pa])
            ph.ts("dve", ssm[:, :], pa[:, :], 1e-24, None, ALU.max, None, [pa], [ssm])
            ph.act(rn[:, :], ssm[:, :], AF.Ln, [ssm], [rn])
            ph.act(rn[:, :], rn[:, :], AF.Exp, [rn], [rn], scale=-0.5)
            ph.tt("dve", kkn[:, :], kk[:, :], rn[:, :], ALU.mult, [kk, rn], [kkn])
            ph.ts("pool", tq[:, :], a_[:, :], P(KA_), omk[:, hp:hp + 1], ALU.mult, ALU.add, [a_, prm, omk], [tq])
            ph.tt("pool", kp[:, :], tq[:, :], Kt[:, :], ALU.mult, [tq, Kt], [kp])
            ph.stt("dve", AR[:, :, 0, :], c3(kkn[:, :]), -1.0, c3(eprev[:, :]), ALU.mult, ALU.mult, [kkn, eprev], [AR])
            ph.tt("pool", AR[:, :, 1, :], c3(R[:, :]), c3(epos[:, :]), ALU.mult, [R, epos], [AR])
            ph.tt("pool", bbt[:, :], kkn[:, :], a_[:, :], ALU.mult, [kkn, a_], [bbt])
            ph.tt("dve", BT[:, :], bbt[:, :], eneg[:, :], ALU.mult, [bbt, eneg], [BT])
            ph.tt("pool", BH[:, :], bbt[:, :], edec[:, :], ALU.mult, [bbt, edec], [BH])
            ph.tt("dve", KT[:, :], kp[:, :], eneg[:, :], ALU.mult, [kp, eneg], [KT])
            ph.tt("pool", KH[:, :], kp[:, :], edec[:, :], ALU.mult, [kp, edec], [KH])
            if l > 0:
                pa = nextpa()
                ph.mm(pa[:, :], WGV[32:64, hc], XG2[32:64, :], True, True, [WGV, XG2], [pa])
                ph.act(gate[:, :], pa[:, :], AF.Sigmoid, [pa, prm], [gate], bias=P(V0_))
                ph.tt("pool", vp[:, :], VF[:, :], Vt[:, :], ALU.subtract, [VF, Vt], [vp])
                ph.tt("dve", vp[:, :], vp[:, :], gate[:, :], ALU.mult, [vp, gate], [vp])
                ph.tt("pool", vp[:, :], vp[:, :], Vt[:, :], ALU.add, [vp, Vt], [vp])
                vsrc = vp
            else:
                vsrc = Vt
            ph.cp("act", VB[:, :], vsrc[:, :], [vsrc], [VB])
            pa = nextpa()
            ph.mm(pa[:, :], WG1[:, hc], SG1[:, :], True, False, [WG1, SG1], [pa])
            ph.mm(pa[:, :], WGV[0:32, hc], SG2[:, :], False, True, [WGV, SG2], [pa])
            ph.cp("act", gsb[:, :], pa[:, :], [pa], [gsb])
            ph.ts("pool", pr_[:, :], R[:, :], P(RK_), None, ALU.mult, None, [R, prm], [pr_])
            ph.tt("pool", pr_[:, :], pr_[:, :], kp[:, :], ALU.mult, [pr_, kp], [pr_])
            pa = nextpa()
            ph.mm(pa[:, :], bones[:, :], pr_[:, :], True, True, [bones, pr_], [pa])
            ph.tt("dve", bon[:, :], pa[:, :], vsrc[:, :], ALU.mult, [pa, vsrc], [bon])
            for c in range(4):
                trb = TRB[c // 2]
                tv = trb.ap[:, :].bitcast(BF16)
                base = (c % 2) * 512
                cs = slice(c * 128, (c + 1) * 128)
                for qi, (src, sap) in enumerate(((AR, AR[:, c, 0, :]), (BH, BH[:, cs]), (KH, KH[:, cs]), (VB, VB[:, cs]))):
                    ph.tr(tv[:, base + qi * 128:base + (qi + 1) * 128], sap, identbf[:, :], [src, identbf], [trb])
                ev = tv[:, base:base + 512].rearrange("p (q t) -> p q t", q=4)
                if c % 2 == 0:
                    ph.cp("dve", tok[c][:, :, :], ev, [trb], [tok[c]])
                else:
                    ph.cp("act", tok[c][:, :, :], ev, [trb], [tok[c]])
            for c in range(4):
                cs = slice(c * 128, (c + 1) * 128)
                gl = c * 128 + 127
                for e in range(2):
                    rows = slice(e * 64, (e + 1) * 64)
                    w1, w2 = W1[e], W2[e]
                    w1h, w2h = kb_h[4 + e], kb_h[6 + e]
                    m4 = M4[ui[0] % 4]
                    ui[0] += 1
                    ph.mm(w1h[:, 0:256], BT[rows, cs], AR[rows, c, :, :].rearrange("p a t -> p (a t)"), True, True,
                          [BT, AR], [w1[0]])
                    ph.mm(w1h[:, 256:512], KT[rows, cs], AR[rows, c, :, :].rearrange("p a t -> p (a t)"), True, True,
                          [KT, AR], [w1[0]])
                    ph.mm(w2h[:, 0:128], AR[rows, c, 0, :], BT[rows, cs], True, True, [AR, BT], [w2[0]])
                    ph.tt("dve", m4[:, :], w1h[:, 0:512], mask4[:, :], ALU.mult, [w1[0], mask4], [m4])
                    ph.tt("dve", P0[e][:, :], w2h[:, 0:128], maskSL[:, :], ALU.mult, [w2[0], maskSL], [P0[e]])
                    ph.tt("pool", TT[e][0][:, :], m4[:, 0:128], identbf[:, :], ALU.add, [m4, identbf], [TT[e][0]])
                for k in range(6):
                    for e in range(2):
                        w2, w2h = W2[e], kb_h[6 + e]
                        m4 = M4[(ui[0] - 2 + e) % 4]
                        if k == 0:
                            Pk, Qk, Pt_, Qt_ = P0[e][:, :], m4[:, 0:128], P0[e], m4
                        else:
                            pq = PQ[e][k % 2]
                            Pk, Qk, Pt_, Qt_ = pq[:, 0:128], pq[:, 128:256], pq, pq
                        pqn = PQ[e][(k + 1) % 2]
                        ph.mm(w2h[:, 128:256], Qk, Pk, True, True, [Pt_, Qt_], [w2[1]])
                        if k < 5:
                            ph.mm(w2h[:, 256:384], Pk, Qk, True, True, [Pt_, Qt_], [w2[1]])
                            ph.cp("act", pqn[:, 0:256], w2h[:, 128:384], [w2[1]], [pqn])
                        else:
                            ph.cp("act", pqn[:, 0:128], w2h[:, 128:256], [w2[1]], [pqn])
                        ph.mm(w2h[:, 384:512], pqn[:, 0:128], TT[e][k % 2][:, :], True, True, [pqn, TT[e][k % 2]], [w2[2]])
                        ph.tt("dve", TT[e][(k + 1) % 2][:, :], TT[e][k % 2][:, :], w2h[:, 384:512], ALU.add,
                              [TT[e][k % 2], w2[2]], [TT[e][(k + 1) % 2]])
                for e in range(2):
                    rows = slice(e * 64, (e + 1) * 64)
                    w1, w1h = W1[e], kb_h[4 + e]
                    m4 = M4[(ui[0] - 2 + e) % 4]
                    TTf = TT[e][0]
                    Vtok = tok[c][:, 3, rows]
                    ph.mm(w1h[:, 0:64], m4[:, 256:384], Vtok, True, True, [m4, tok[c]], [w1[0]])
                    ph.cp("act", AkV[e][:, :], w1h[:, 0:64], [w1[0]], [AkV[e]])
                    ph.mm(w1h[:, 384:512], tok[c][:, 0, :], TTf[:, :], True, True, [tok[c], TTf], [w1[6]])
                    ph.cp("act", WTsb[e][rows, :], w1h[rows, 384:512], [w1[6]], [WTsb[e]])
                    ph.mm(w1h[:, 64:128], TTf[:, :], AkV[e][:, :], True, True, [TTf, AkV[e]], [w1[1]])
                    ph.cp("act", Uloc[e][:, :], w1h[:, 64:128], [w1[1]], [Uloc[e]])
                    ph.mm(w1h[:, 128:192], WTsb[e][rows, :], Hb[hp][rows, :], True, True, [WTsb[e], Hb[hp]], [w1[2]])
                    ph.tt("dve", Usb[e][:, :], w1h[:, 128:192], Uloc[e][:, :], ALU.add, [w1[2], Uloc[e]], [Usb[e]])
                    ph.mm(w1h[:, 256:320], AR[rows, c, 1, :], Hb[hp][rows, :], True, False, [AR, Hb[hp]], [w1[4]])
                    ph.mm(w1h[:, 256:320], m4[:, 128:256], Usb[e][:, :], False, False, [m4, Usb[e]], [w1[4]])
                    ph.mm(w1h[:, 256:320], m4[:, 384:512], Vtok, False, True, [m4, tok[c]], [w1[4]])
                    ph.mm(w1h[:, 192:256], tok[c][:, 1, :], Usb[e][:, :], True, False, [tok[c], Usb[e]], [w1[3]])
                    ph.mm(w1h[:, 192:256], tok[c][:, 2, :], Vtok, False, True, [tok[c]], [w1[3]])
                    ph.stt("dve", Hs[hp][rows, :], Hs[hp][rows, :], epos[rows, gl:gl + 1], w1h[rows, 192:256],
                           ALU.mult, ALU.add, [Hs[hp], epos, w1[3]], [Hs[hp]])
                    ph.cp("act", Hb[hp][rows, :], Hs[hp][rows, :], [Hs[hp]], [Hb[hp]])
                    yn = ynorm[c % 2]
                    ph.add("dve", lambda en, e=e, w1h=w1h: en.bn_stats(gst[e][:, 0:6], w1h[:, 256:320]), [w1[4]], [gst[e]])
                    ph.add("dve", lambda en, e=e: en.bn_aggr(gmv[e][:, 0:2], gst[e][:, 0:6]), [gst[e]], [gmv[e]])
                    ph.act(grs[e][:, 0:1], gmv[e][:, 1:2], AF.Ln, [gmv[e]], [grs[e]], bias=epsG[:, 0:1])
                    ph.act(grs[e][:, 0:1], grs[e][:, 0:1], AF.Exp, [grs[e]], [grs[e]], scale=-0.5)
                    ph.ts("dve", gnm[e][:, 0:1], gmv[e][:, 0:1], grs[e][:, 0:1], -1.0, ALU.mult, ALU.mult,
                          [gmv[e], grs[e]], [gnm[e]])
                    ph.act(yn[:, e * 64:(e + 1) * 64], w1h[:, 256:320], AF.Identity, [w1[4], grs[e], gnm[e]], [yn],
                           bias=gnm[e][:, 0:1], scale=grs[e][:, 0:1])
                yn = ynorm[c % 2]
                trb = TRB[1]
                tv = trb.ap[:, :].bitcast(BF16)
                ph.tr(tv[:, 0:128], yn[:, :], identbf[:, :], [yn, identbf], [trb])
                fn = fin[c % 2]
                ph.ts("dve", fn[:, :], tv[:, 0:128], P(GG_), P(GB_), ALU.mult, ALU.add, [trb, prm], [fn])
                ph.tt("pool", fn[:, :], fn[:, :], bon[:, cs], ALU.add, [fn, bon], [fn])
                ph.tt("pool", outT[:, cs], fn[:, :], gsb[:, cs], ALU.mult, [fn, gsb], [outT])
            ph.dma("sp", dr["mixT"][512 + hp * 128:512 + (hp + 1) * 128, tcols], outT[:, :], [outT], [])
    ph.finish()


ALPHA = (2 * 2) ** 0.25


def load_cast_weight(ph, w_rows_fn, nrows_tiles, ncols, wbf_fn, stage, q="act"):
    pc = stage[0].ap.shape[1]
    k = 0
    for c0 in range(0, ncols, pc):
        for r in range(nrows_tiles):
            n = min(pc, ncols - c0)
            s = stage[k % len(stage)]
            ph.dma(q if k % 2 == 0 else "sp", s[:, 0:n], w_rows_fn(r)[:, c0:c0 + n], [], [s])
            ph.cp(("dve", "pool", "act")[k % 3], wbf_fn(r)[:, c0:c0 + n], s[:, 0:n], [s], [wbf_fn(r)])
            k += 1


def phase_outproj(kb, l):
    ph = PH(kb, "op%d" % l)
    S, NT = kb.S, kb.NT
    dr = kb.dram
    c = load_consts(ph)
    gB = ph.sb([128, 1024], F32, "gB")
    bB = ph.sb([128, 1024], F32, "bB")
    ph.dma("sp", gB[:, :], dr["ln_mix_g"][l, :].partition_broadcast(128), [], [gB])
    ph.dma("sp", bB[:, :], dr["ln_mix_b"][l, :].partition_broadcast(128), [], [bB])
    wbf = [ph.sb([128, 1024], BF16, "wo%d" % r) for r in range(8)]
    stage = [ph.sb([128, 1024], F32, "stg%d" % k) for k in range(2)]
    load_cast_weight(ph, lambda r: dr["w_out"][l, r * 128:(r + 1) * 128, :], 8, 1024, lambda r: wbf[r], stage)
    mx = [ph.sb([128, 8, 512], BF16, "mx%d" % k) for k in range(2)]
    xr = [ph.sb([128, 1024], F32, "xr%d" % k) for k in range(3)]
    srcs = [ph.sb([128, 1024], F32, "src%d" % k) for k in range(3)]
    bufs = [ln_bufs(ph, k) for k in range(4)]
    for g in range(S // 512):
        m = mx[g % 2]
        ph.dma("pool", m[:, :, :], dr["mixT"][:, g * 512:(g + 1) * 512].rearrange("(c p) t -> p c t", p=128), [], [m])
        for j in range(4):
            i = g * 4 + j
            x_ = xr[i % 3]
            ph.dma("sp", x_[:, :], dr["xres"][i * 128:(i + 1) * 128, :], [], [x_])
            src = srcs[i % 3]
            for hh in range(2):
                pb = ph.ps[(i % 2) * 2 + hh]
                for cc in range(8):
                    ph.mm(pb[:, :], m[:, cc, j * 128:(j + 1) * 128], wbf[cc][:, hh * 512:(hh + 1) * 512],
                          cc == 0, cc == 7, [m, wbf[cc]], [pb])
                ph.stt("dve", src[:, hh * 512:(hh + 1) * 512], x_[:, hh * 512:(hh + 1) * 512], ALPHA, pb[:, :],
                       ALU.mult, ALU.add, [x_, pb], [src])
            ln_rows(ph, src, i, gB, bB, c["ident_bf"], "xres", "xT", bufs[i % 4], ph.ps[4 + i % 4])
    ph.finish()


def phase_ffn_up(kb, l):
    ph = PH(kb, "fu%d" % l)
    S, NT = kb.S, kb.NT
    dr = kb.dram
    wbf = [ph.sb([128, 4096], BF16, "wu%d" % r) for r in range(8)]
    stage = [ph.sb([128, 1024], F32, "stg%d" % k) for k in range(4)]
    load_cast_weight(ph, lambda r: dr["w_up"][l, r * 128:(r + 1) * 128, :], 8, 4096, lambda r: wbf[r], stage)
    xg = [ph.sb([128, 8, 512], BF16, "xg%d" % k) for k in range(2)]
    tmp = [ph.sb([128, 512], F32, "tmp%d" % k) for k in range(3)]
    ho = [ph.sb([128, 512], BF16, "ho%d" % k) for k in range(4)]
    k = 0
    for g in range(S // 512):
        x_ = xg[g % 2]
        ph.dma("pool", x_[:, :, :], dr["xT"][:, g * 512:(g + 1) * 512].rearrange("(c p) t -> p c t", p=128), [], [x_])
        for f in range(32):
            pb = ph.ps[k % 6]
            for cc in range(8):
                ph.mm(pb[:, :], wbf[cc][:, f * 128:(f + 1) * 128], x_[:, cc, :], cc == 0, cc == 7, [wbf[cc], x_], [pb])
            t = tmp[k % 3]
            h = ho[k % 4]
            ph.act(t[:, :], pb[:, :], AF.Relu, [pb], [t])
            ph.tt(("pool", "dve")[k % 2], h[:, :], t[:, :], t[:, :], ALU.mult, [t], [h])
            ph.dma("sp", dr["hT"][f * 128:(f + 1) * 128, g * 512:(g + 1) * 512], h[:, :], [h], [])
            k += 1
    ph.finish()


def phase_ffn_down(kb, l, last):
    ph = PH(kb, "fd%d" % l)
    S, NT = kb.S, kb.NT
    dr = kb.dram
    c = load_consts(ph)
    gB = ph.sb([128, 1024], F32, "gB")
    bB = ph.sb([128, 1024], F32, "bB")
    ph.dma("sp", gB[:, :], dr["ln_ffn_g"][l, :].partition_broadcast(128), [], [gB])
    ph.dma("sp", bB[:, :], dr["ln_ffn_b"][l, :].partition_broadcast(128), [], [bB])
    wbf = [ph.sb([128, 1024], BF16, "wd%d" % r) for r in range(32)]
    stage = [ph.sb([128, 512], F32, "stg%d" % k) for k in range(6)]
    load_cast_weight(ph, lambda r: dr["w_down"][l, r * 128:(r + 1) * 128, :], 32, 1024, lambda r: wbf[r], stage)
    hg = [ph.sb([128, 32, 512], BF16, "hg%d" % k) for k in range(2)]
    xr = [ph.sb([128, 1024], F32, "xr%d" % k) for k in range(3)]
    srcs = [ph.sb([128, 1024], F32, "src%d" % k) for k in range(3)]
    bufs = [ln_bufs(ph, k) for k in range(4)]
    for g in range(S // 512):
        h_ = hg[g % 2]
        for q4 in range(4):
            ph.dma(("pool", "sp")[q4 % 2], h_[:, q4 * 8:(q4 + 1) * 8, :],
                   dr["hT"][q4 * 1024:(q4 + 1) * 1024, g * 512:(g + 1) * 512].rearrange("(f p) t -> p f t", p=128),
                   [], [h_])
        for j in range(4):
            i = g * 4 + j
            x_ = xr[i % 3]
            ph.dma("sp", x_[:, :], dr["xres"][i * 128:(i + 1) * 128, :], [], [x_])
            src = srcs[i % 3]
            for hh in range(2):
                pb = ph.ps[(i % 2) * 2 + hh]
                for f in range(32):
                    ph.mm(pb[:, :], h_[:, f, j * 128:(j + 1) * 128], wbf[f][:, hh * 512:(hh + 1) * 512],
                          f == 0, f == 31, [h_, wbf[f]], [pb])
                ph.stt("dve", src[:, hh * 512:(hh + 1) * 512], x_[:, hh * 512:(hh + 1) * 512], ALPHA, pb[:, :],
                       ALU.mult, ALU.add, [x_, pb], [src])
            if last:
                ln_rows(ph, src, i, gB, bB, c["ident_bf"], "out", None, bufs[i % 4], ph.ps[4 + i % 4])
            else:
                ln_rows(ph, src, i, gB, bB, c["ident_bf"], "xres", "xT", bufs[i % 4], ph.ps[4 + i % 4])
    ph.finish()


CONST_SHAPES = {"c_ident": [128, 128], "c_J": [128, 128], "c_causal": [128, 128], "c_oh": [32, 385],
                "c_mask4": [128, 512], "c_maskSL": [128, 128], "c_bones": [128, 128], "c_cm01": [128, 512]}


def build_program(S, debug=(), only=None, ext_in=()):
    kb = KB(S)
    kb.ext_in = tuple(ext_in)
    kb.din("x", [S, 1024])
    kb.din("ln_in_g", [1024])
    kb.din("ln_in_b", [1024])
    kb.din("w_in0", [1024, 3360])
    kb.din("w_in1", [1024, 3392])
    kb.din("mu0", [128, 15])
    kb.din("mu1", [128, 15])
    kb.din("rel_bias", [32, 4])
    for nm in ("lambda_q1", "lambda_k1", "lambda_q2", "lambda_k2"):
        kb.din(nm, [2, 64])
    kb.din("subln_g", [2, 128])
    kb.din("rwp0", [128, 8, 4])
    kb.din("rwp1", [128, 8, 4])
    kb.din("rw_w_up", [2, 64, 512])
    kb.din("rw_a_up", [2, 64, 512])
    kb.din("rw_g_up", [2, 160, 512])
    kb.din("rw_v_up", [1, 32, 512])
    kb.din("w_out", [2, 1024, 1024])
    kb.din("ln_mix_g", [2, 1024])
    kb.din("ln_mix_b", [2, 1024])
    kb.din("w_up", [2, 1024, 4096])
    kb.din("w_down", [2, 4096, 1024])
    kb.din("ln_ffn_g", [2, 1024])
    kb.din("ln_ffn_b", [2, 1024])
    for nm, shp in CONST_SHAPES.items():
        kb.din(nm, shp)
    dbg = lambda n: n in debug
    kb.dscr("xres", [S, 1024], F32, dbg("xres"))
    kb.dscr("xT", [1024, S], BF16, dbg("xT"))
    kb.dscr("qT", [4, 128, S], BF16, dbg("qT"))
    kb.dscr("kT", [4, 128, S], BF16, dbg("kT"))
    kb.dscr("vd", [S, 512], BF16, dbg("vd"))
    kb.dscr("prT", [1920, S], F32, dbg("prT"))
    kb.dscr("vfT", [512, S], F32, dbg("vfT"))
    kb.dscr("lamneg", [1, 2], F32, dbg("lamneg"))
    kb.dscr("EGd", [4, 384], F32, dbg("EGd"))
    kb.dscr("Eb", [4, 2, 128, 128], F32, dbg("Eb"))
    kb.dscr("mixT", [1024, S], BF16, dbg("mixT"))
    kb.dscr("hT", [4096, S], BF16, dbg("hT"))
    kb.dout("out", [S, 1024], F32)
    on = lambda n: only is None or n in only
    if on("pre"):
        phase_pre(kb)
    if on("ln0"):
        phase_ln0(kb)
    for l in range(2):
        if on("ip%d" % l):
            phase_inproj(kb, l)
        if on("at%d" % l):
            phase_attn(kb, l)
        if on("rw%d" % l):
            phase_rwkv2(kb, l)
        if on("of%d" % l):
            phase_opfu(kb, l)
        if on("fd%d" % l):
            phase_ffn_down(kb, l, last=(l == 1))
    kb.close()
    return kb


def host_inputs(inp, S):
    f = lambda a: np.ascontiguousarray(np.asarray(a, dtype=np.float32))
    m = {}
    m["ln_in_g"] = f(inp["ln_in_g"])
    m["ln_in_b"] = f(inp["ln_in_b"])
    m["w_in0"] = f(inp["w_in_first"])
    m["w_in1"] = f(inp["w_in_rest"][0])
    for l, mu in enumerate((inp["mu_first"], inp["mu_rest"][0])):
        mp = np.zeros(1920, np.float32)
        mp[:mu.shape[0]] = mu
        m["mu%d" % l] = np.ascontiguousarray(mp.reshape(15, 128).T)
    for nm in ("rel_bias", "lambda_q1", "lambda_k1", "lambda_q2", "lambda_k2", "subln_g", "rw_w_up", "rw_a_up",
               "rw_g_up", "rw_v_up", "w_out", "ln_mix_g", "ln_mix_b", "w_up", "w_down", "ln_ffn_g", "ln_ffn_b"):
        m[nm] = f(inp[nm])
    m["rwp0"] = host_rwp(inp, 0)
    m["rwp1"] = host_rwp(inp, 1)
    m.update(host_consts())
    return m


_PROG = {}


def kernel(**inputs):
    x = np.asarray(inputs["x"], dtype=np.float32)
    B, S, _ = x.shape
    if S not in _PROG:
        _PROG[S] = build_program(S)
    kb = _PROG[S]
    shared = host_inputs(inputs, S)
    in_maps = []
    for b in range(B):
        m = dict(shared)
        m["x"] = np.ascontiguousarray(x[b])
        in_maps.append(m)
    res = run_bass_kernel_spmd(kb.nc, in_maps, core_ids=list(range(B)))
    return np.stack([np.asarray(r["out"], dtype=np.float32) for r in res.results], axis=0)


def phase_rwkv2(kb, l):
    ph = PH(kb, "rw%d" % l)
    S, NT = kb.S, kb.NT
    NTB = S // 512
    dr = kb.dram
    cst = load_consts(ph)
    identbf = cst["ident_bf"]
    mask4 = ph.sb([128, 512], F32, "mask4")
    maskSL = ph.sb([128, 128], F32, "maskSL")
    bones = ph.sb([128, 128], F32, "bones")
    cm01 = ph.sb([128, 512], F32, "cm01")
    ph.dma("sp", mask4[:, :], dr["c_mask4"][:, :], [], [mask4])
    ph.dma("sp", maskSL[:, :], dr["c_maskSL"][:, :], [], [maskSL])
    ph.dma("sp", bones[:, :], dr["c_bones"][:, :], [], [bones])
    ph.dma("sp", cm01[:, :], dr["c_cm01"][:, :], [], [cm01])
    prm = ph.sb([128, 8, 4], F32, "prm")
    ph.dma("sp", prm[:, :, :], dr["rwp%d" % l][:, :, :], [], [prm])
    W0, A0, KK_, KA_, RK_, GG_, GB_, V0_ = range(8)
    WA = ph.sb([128, 512], F32, "WA")
    WG1 = ph.sb([128, 512], F32, "WG1")
    WGV = ph.sb([64, 512], F32, "WGV")
    ph.dma("sp", WA[0:64, :], dr["rw_w_up"][l, :, :], [], [WA])
    ph.dma("sp", WA[64:128, :], dr["rw_a_up"][l, :, :], [], [WA])
    ph.dma("sp", WG1[:, :], dr["rw_g_up"][l, 0:128, :], [], [WG1])
    ph.dma("sp", WGV[0:32, :], dr["rw_g_up"][l, 128:160, :], [], [WGV])
    if l > 0:
        ph.dma("sp", WGV[32:64, :], dr["rw_v_up"][l - 1, :, :], [], [WGV])
    omk = ph.sb([128, 4], F32, "omk")
    ph.ts("dve", omk[:, :], prm[:, KA_, :], -1.0, 1.0, ALU.mult, ALU.add, [prm], [omk])
    epsG = ph.sb([128, 1], F32, "epsG")
    ph.add("pool", lambda e: e.memset(epsG[:, :], GN_EPS), [], [epsG])
    Hb = [ph.sb([128, 64], BF16, "Hb%d" % hp) for hp in range(4)]
    Hs = [ph.sb([128, 64], F32, "Hs%d" % hp) for hp in range(4)]
    for hp in range(4):
        ph.add("pool", lambda e, hp=hp: e.memset(Hs[hp][:, :], 0.0), [], [Hs[hp]])
        ph.add("pool", lambda e, hp=hp: e.memset(Hb[hp][:, :], 0.0), [], [Hb[hp]])
    XWA = ph.sb([128, 512], F32, "XWA")
    XG1 = ph.sb([128, 512], F32, "XG1")
    XG2 = ph.sb([64, 512], F32, "XG2")
    TH = ph.sb([64, 512], F32, "TH")
    SG1 = ph.sb([128, 512], F32, "SG1")
    SG2 = ph.sb([32, 512], F32, "SG2")

    def f32t(n):
        return ph.sb([128, 512], F32, n)

    def bft(n):
        return ph.sb([128, 512], BF16, n)
    pers = []
    for hp in range(4):
        pers.append(dict(gsb=f32t("gsb"), bon=f32t("bon"), AR=ph.sb([128, 4, 2, 128], BF16, "AR"),
                         BT=bft("BT"), KT=bft("KT"), tok=[ph.sb([128, 4, 128], BF16, "tok") for _ in range(4)],
                         outT=bft("outT"), gam=ph.sb([128, 4], F32, "gam"),
                         keepM=[ph.sb([128, 256], BF16, "keepM") for _ in range(8)],
                         Uloc=[ph.sb([128, 64], F32, "Uloc") for _ in range(8)],
                         WTsb=[ph.sb([128, 128], BF16, "WTsb") for _ in range(4)]))
    RIN = [dict(R=f32t("R"), Kt=f32t("Kt"), Vt=f32t("Vt"), VF=f32t("VF") if l > 0 else None) for _ in range(2)]
    TMP = [[f32t("tmp%d" % i) for i in range(10)] for _ in range(2)]
    TB16 = [(bft("BH"), bft("KH"), bft("VB")) for _ in range(2)]
    M4 = [ph.sb([128, 512], BF16, "M4") for _ in range(8)]
    P0 = [ph.sb([128, 128], BF16, "P0") for _ in range(8)]
    XK = [[ph.sb([128, 384], BF16, "XK") for _ in range(2)] for _ in range(8)]
    AkV = [ph.sb([128, 64], BF16, "AkV") for _ in range(8)]
    Usb = [ph.sb([128, 64], BF16, "Usb") for _ in range(8)]
    ynorm = [ph.sb([128, 128], BF16, "ynorm") for _ in range(4)]
    gst = [ph.sb([128, 6], F32, "gst") for _ in range(8)]
    gmv = [ph.sb([128, 2], F32, "gmv") for _ in range(8)]
    grs = [ph.sb([128, 1], F32, "grs") for _ in range(8)]
    gnm = [ph.sb([128, 1], F32, "gnm") for _ in range(8)]
    fin = [ph.sb([128, 128], F32, "fin") for _ in range(4)]
    kb_h = kb.psum_h
    PA = [ph.ps[0], ph.ps[1], ph.ps[2], ph.ps[3]]
    pai = [0]

    def nextpa():
        b = PA[pai[0] % 4]
        pai[0] += 1
        return b

    def c3(ap):
        return ap.rearrange("p (c t) -> p c t", c=4)
    evi = [0]

    import os
    EV = os.environ.get("RW2_EV", "")

    def evac_copy(out_ap, in_ap, reads, writes):
        evi[0] += 1
        eng = ("act", "dve")[evi[0] % 2]
        if EV:
            eng = EV
        ph.cp(eng, out_ap, in_ap, reads, writes)

    for tb in range(NTB):
        tcols = slice(tb * 512, (tb + 1) * 512)
        ph.dma("sp", XWA[:, :], dr["prT"][1536:1664, tcols], [], [XWA])
        ph.dma("pool", XG1[:, :], dr["prT"][1664:1792, tcols], [], [XG1])
        ph.dma("pool", XG2[:, :], dr["prT"][1792:1856, tcols], [], [XG2])
        ph.act(TH[:, :], XWA[0:64, :], AF.Tanh, [XWA], [TH])
        ph.act(SG1[:, :], XG1[:, :], AF.Sigmoid, [XG1], [SG1])
        ph.act(SG2[:, :], XG2[0:32, :], AF.Sigmoid, [XG2], [SG2])
        def prep_gen(hp):
                s = pers[hp]
                rin = RIN[hp % 2]
                R, Kt, Vt, VF = rin["R"], rin["Kt"], rin["Vt"], rin["VF"]
                gsb, bon, AR, BT, KT, tok, gam = (s[k] for k in ("gsb", "bon", "AR", "BT", "KT", "tok", "gam"))
                hc = slice(hp * 128, (hp + 1) * 128)
                P = lambda i: prm[:, i, hp:hp + 1]
                ph.dma("sp", R[:, :], dr["prT"][hp * 128:(hp + 1) * 128, tcols], [], [R])
                yield
                ph.dma("sp", Kt[:, :], dr["prT"][512 + hp * 128:512 + (hp + 1) * 128, tcols], [], [Kt])
                yield
                ph.dma("pool", Vt[:, :], dr["prT"][1024 + hp * 128:1024 + (hp + 1) * 128, tcols], [], [Vt])
                yield
                if l > 0:
                    ph.dma("pool", VF[:, :], dr["vfT"][hp * 128:(hp + 1) * 128, tcols], [], [VF])
                    yield
                sgw, cum, epos, eneg, edec, a_, kk, kkn, kp, bbt = TMP[hp % 2]
                BH, KH, VB = TB16[hp % 2]
                pa = nextpa()
                ph.mm(pa[:, :], WA[0:64, hc], TH[:, :], True, True, [WA, TH], [pa])
                yield
                ph.act(sgw[:, :], pa[:, :], AF.Sigmoid, [pa, prm], [sgw], bias=P(W0))
                yield
                ph.ts("dve", sgw[:, :], sgw[:, :], DECAY_C, None, ALU.mult, None, [sgw], [sgw])
                yield
                ph.add("dve", lambda e, cum=cum, sgw=sgw: e.tensor_tensor_scan(cum[:, :], cm01[:, :], sgw[:, :], 0.0,
                                                                              ALU.mult, ALU.add), [cm01, sgw], [cum])
                yield
                ph.act(epos[:, :], cum[:, :], AF.Exp, [cum], [epos])
                yield
                ph.act(eneg[:, :], cum[:, :], AF.Exp, [cum], [eneg], scale=-1.0)
                yield
                for c in range(4):
                    ph.act(edec[:, c * 128:(c + 1) * 128], cum[:, c * 128:(c + 1) * 128], AF.Exp, [cum], [edec],
                           scale=-1.0, bias=cum[:, c * 128 + 127:c * 128 + 128])
                    yield
                ph.cp("pool", gam[:, :], c3(epos[:, :])[:, :, 127], [epos], [gam])
                yield
                pa = nextpa()
                ph.mm(pa[:, :], WA[64:128, hc], XWA[64:128, :], True, True, [WA, XWA], [pa])
                yield
                ph.act(a_[:, :], pa[:, :], AF.Sigmoid, [pa, prm], [a_], bias=P(A0))
                yield
                ph.act(kk[:, :], Kt[:, :], AF.Square, [Kt, prm], [kk], scale=P(KK_))
                yield
                pa = nextpa()
                ph.mm(pa[:, :], bones[:, :], kk[:, :], True, True, [bones, kk], [pa])
                yield
                ph.ts("dve", kk[:, :], pa[:, :], 1e-24, None, ALU.max, None, [pa], [kk])
                yield
                ph.act(kk[:, :], kk[:, :], AF.Ln, [kk], [kk])
                yield
                ph.act(kk[:, :], kk[:, :], AF.Exp, [kk], [kk], scale=-0.5)
                yield
                ph.stt("dve", kkn[:, :], Kt[:, :], P(KK_), kk[:, :], ALU.mult, ALU.mult, [Kt, prm, kk], [kkn])
                yield
                ph.act(kp[:, :], a_[:, :], AF.Identity, [a_, prm, omk], [kp], scale=P(KA_), bias=omk[:, hp:hp + 1])
                yield
                ph.tt("pool", kp[:, :], kp[:, :], Kt[:, :], ALU.mult, [kp, Kt], [kp])
                yield
                ph.stt("dve", AR[:, :, 0, 1:128], c3(kkn[:, :])[:, :, 1:128], -1.0, c3(epos[:, :])[:, :, 0:127],
                       ALU.mult, ALU.mult, [kkn, epos], [AR])
                yield
                ph.ts("dve", AR[:, :, 0, 0:1], c3(kkn[:, :])[:, :, 0:1], -1.0, None, ALU.mult, None, [kkn], [AR])
                yield
                ph.tt("pool", AR[:, :, 1, :], c3(R[:, :]), c3(epos[:, :]), ALU.mult, [R, epos], [AR])
                yield
                ph.tt("pool", bbt[:, :], kkn[:, :], a_[:, :], ALU.mult, [kkn, a_], [bbt])
                yield
                ph.tt("dve", BT[:, :], bbt[:, :], eneg[:, :], ALU.mult, [bbt, eneg], [BT])
                yield
                ph.tt("pool", BH[:, :], bbt[:, :], edec[:, :], ALU.mult, [bbt, edec], [BH])
                yield
                ph.tt("dve", KT[:, :], kp[:, :], eneg[:, :], ALU.mult, [kp, eneg], [KT])
                yield
                ph.tt("pool", KH[:, :], kp[:, :], edec[:, :], ALU.mult, [kp, edec], [KH])
                yield
                if l > 0:
                    pa = nextpa()
                    ph.mm(pa[:, :], WGV[32:64, hc], XG2[32:64, :], True, True, [WGV, XG2], [pa])
                    yield
                    ph.act(a_[:, :], pa[:, :], AF.Sigmoid, [pa, prm], [a_], bias=P(V0_))
                    yield
                    ph.tt("pool", sgw[:, :], VF[:, :], Vt[:, :], ALU.subtract, [VF, Vt], [sgw])
                    yield
                    ph.tt("dve", sgw[:, :], sgw[:, :], a_[:, :], ALU.mult, [sgw, a_], [sgw])
                    yield
                    ph.tt("pool", sgw[:, :], sgw[:, :], Vt[:, :], ALU.add, [sgw, Vt], [sgw])
                    yield
                    vsrc = sgw
                else:
                    vsrc = Vt
                ph.cp("act", VB[:, :], vsrc[:, :], [vsrc], [VB])
                yield
                pa = nextpa()
                ph.mm(pa[:, :], WG1[:, hc], SG1[:, :], True, False, [WG1, SG1], [pa])
                yield
                ph.mm(pa[:, :], WGV[0:32, hc], SG2[:, :], False, True, [WGV, SG2], [pa])
                yield
                ph.cp("act", gsb[:, :], pa[:, :], [pa], [gsb])
                yield
                ph.stt("dve", bbt[:, :], R[:, :], P(RK_), kp[:, :], ALU.mult, ALU.mult, [R, prm, kp], [bbt])
                yield
                pa = nextpa()
                ph.mm(pa[:, :], bones[:, :], bbt[:, :], True, True, [bones, bbt], [pa])
                yield
                ph.tt("dve", bon[:, :], pa[:, :], vsrc[:, :], ALU.mult, [pa, vsrc], [bon])
                yield
                for c in range(4):
                    trb = ph.ps[6 + c // 2]
                    tv = trb.ap[:, :].bitcast(BF16)
                    base = (c % 2) * 512
                    cs = slice(c * 128, (c + 1) * 128)
                    for qi, (src, sap) in enumerate(((AR, AR[:, c, 0, :]), (BH, BH[:, cs]), (KH, KH[:, cs]), (VB, VB[:, cs]))):
                        ph.tr(tv[:, base + qi * 128:base + (qi + 1) * 128], sap, identbf[:, :], [src, identbf], [trb])
                    ev = tv[:, base:base + 512].rearrange("p (q t) -> p q t", q=4)
                    evac_copy(tok[c][:, :, :], ev, [trb], [tok[c]])
                    yield
        for pair in ((0, 1), (2, 3)):
            gens = [prep_gen(hp) for hp in pair]
            while gens:
                for g_ in list(gens):
                    try:
                        next(g_)
                    except StopIteration:
                        gens.remove(g_)
        import os
        STOP = int(os.environ.get("RW2_STOP", "99"))
        if STOP <= 1:
            continue
        for hp in range(4):
            s = pers[hp]
            AR, BT, KT, tok, keepM, Uloc, WTsb = (s[k] for k in ("AR", "BT", "KT", "tok", "keepM", "Uloc", "WTsb"))
            units = [(c, e) for c in range(4) for e in range(2)]
            for sw in range(4):
                for ui2 in range(2):
                    u = sw * 2 + ui2
                    c, e = units[u]
                    rows = slice(e * 64, (e + 1) * 64)
                    cs = slice(c * 128, (c + 1) * 128)
                    gb = ph.ps[4 + ui2]
                    g3 = ph.ps[2 + ui2]
                    arr = AR[rows, c, :, :].rearrange("p a t -> p (a t)")
                    ph.mm(gb[:, 0:256], BT[rows, cs], arr, True, True, [BT, AR], [gb])
                    ph.mm(gb[:, 256:512], KT[rows, cs], arr, True, True, [KT, AR], [gb])
                    ph.mm(g3[:, 0:128], AR[rows, c, 0, :], BT[rows, cs], True, True, [AR, BT], [g3])
                for ui2 in range(2):
                    u = sw * 2 + ui2
                    gb = ph.ps[4 + ui2]
                    g3 = ph.ps[2 + ui2]
                    ph.tt("dve", M4[u][:, :], gb[:, 0:512], mask4[:, :], ALU.mult, [gb, mask4], [M4[u]])
                    ph.tt("dve", P0[u][:, :], g3[:, 0:128], maskSL[:, :], ALU.mult, [g3, maskSL], [P0[u]])
                    ph.tt("pool", XK[u][1][:, 256:384], M4[u][:, 0:128], identbf[:, :], ALU.add, [M4[u], identbf], [XK[u][1]])
                    ph.cp("pool", keepM[u][:, 0:128], M4[u][:, 128:256], [M4[u]], [keepM[u]])
                    ph.cp("pool", keepM[u][:, 128:256], M4[u][:, 384:512], [M4[u]], [keepM[u]])
            if STOP <= 2:
                continue
            for k in range(7):
                for rnd in range(1):
                    us_ = range(8)
                    for u in us_:
                        cb = ph.ps[u]
                        if k == 0:
                            Pk, Qk, Pt_, Qt_ = P0[u][:, :], M4[u][:, 0:128], P0[u], M4[u]
                        else:
                            xk = XK[u][k % 2]
                            Pk, Qk, Pt_, Qt_ = xk[:, 0:128], xk[:, 128:256], xk, xk
                        if k < 6:
                            ph.mm(cb[:, 0:128], Qk, Pk, True, True, [Pt_, Qt_], [cb])
                        if k == 0:
                            ph.mm(cb[:, 128:256], Pk, Qk, True, True, [Pt_, Qt_], [cb])
                        elif k < 5:
                            ph.mm(cb[:, 128:384], Pk, xk[:, 128:384], True, True, [xk], [cb])
                        else:
                            ph.mm(cb[:, 256:384], Pk, xk[:, 256:384], True, True, [xk], [cb])
                    for u in us_:
                        cb = ph.ps[u]
                        xn = XK[u][(k + 1) % 2]
                        if k < 5:
                            ph.cp("act", xn[:, 0:256], cb[:, 0:256], [cb], [xn])
                        elif k == 5:
                            ph.cp("act", xn[:, 0:128], cb[:, 0:128], [cb], [xn])
                        if k >= 1:
                            xk = XK[u][k % 2]
                            ph.tt("dve", xn[:, 256:384], cb[:, 256:384], xk[:, 256:384], ALU.add, [cb, xk], [xn])
            if STOP <= 3:
                continue
            for u in range(8):
                c, e = units[u]
                rows = slice(e * 64, (e + 1) * 64)
                TTf = XK[u][1]
                Vtok = tok[c][:, 3, rows]
                ph.mm(ph.ps[2][:, u * 64:(u + 1) * 64], M4[u][:, 256:384], Vtok, True, True, [M4[u], tok[c]], [ph.ps[2]])
                wb = ph.ps[3 + u // 4]
                ph.mm(wb[:, (u % 4) * 128:(u % 4 + 1) * 128], tok[c][:, 0, :], TTf[:, 256:384], True, True, [tok[c], TTf], [wb])
            for u in range(8):
                c, e = units[u]
                rows = slice(e * 64, (e + 1) * 64)
                evac_copy(AkV[u][:, :], ph.ps[2][:, u * 64:(u + 1) * 64], [ph.ps[2]], [AkV[u]])
                wb = ph.ps[3 + u // 4]
                evac_copy(WTsb[c][rows, :], wb[rows, (u % 4) * 128:(u % 4 + 1) * 128], [wb], [WTsb[c]])
            for u in range(8):
                ph.mm(ph.ps[5][:, u * 64:(u + 1) * 64], XK[u][1][:, 256:384], AkV[u][:, :], True, True, [XK[u][1], AkV[u]], [ph.ps[5]])
            for u in range(8):
                evac_copy(Uloc[u][:, :], ph.ps[5][:, u * 64:(u + 1) * 64], [ph.ps[5]], [Uloc[u]])
        for c in range(4 if STOP > 4 else 0):
            cs = slice(c * 128, (c + 1) * 128)
            HE = [(hp, e) for hp in range(4) for e in range(2)]

            def reg(hp, e, j):
                return ph.ps[2 + e * 2 + (hp % 2)], (hp // 2) * 192 + j * 64
            for hp, e in HE:
                s = pers[hp]
                rows = slice(e * 64, (e + 1) * 64)
                b, o = reg(hp, e, 0)
                ph.mm(b[:, o:o + 64], s["WTsb"][c][rows, :], Hb[hp][rows, :], True, True, [s["WTsb"][c], Hb[hp]], [b])
            for hp, e in HE:
                s = pers[hp]
                b, o = reg(hp, e, 0)
                us = Usb[hp * 2 + e]
                ph.tt("dve", us[:, :], b[:, o:o + 64], s["Uloc"][c * 2 + e][:, :], ALU.add, [b, s["Uloc"][c * 2 + e]], [us])
            if STOP <= 5:
                continue
            for hp, e in HE:
                s = pers[hp]
                rows = slice(e * 64, (e + 1) * 64)
                us = Usb[hp * 2 + e]
                km = s["keepM"][c * 2 + e]
                tokc = s["tok"][c]
                Vtok = tokc[:, 3, rows]
                b, o = reg(hp, e, 1)
                ph.mm(b[:, o:o + 64], s["AR"][rows, c, 1, :], Hb[hp][rows, :], True, False, [s["AR"], Hb[hp]], [b])
                ph.mm(b[:, o:o + 64], km[:, 0:128], us[:, :], False, False, [km, us], [b])
                ph.mm(b[:, o:o + 64], km[:, 128:256], Vtok, False, True, [km, tokc], [b])
                b, o = reg(hp, e, 2)
                ph.mm(b[:, o:o + 64], tokc[:, 1, :], us[:, :], True, False, [tokc, us], [b])
                ph.mm(b[:, o:o + 64], tokc[:, 2, :], Vtok, False, True, [tokc], [b])
            if STOP <= 6:
                continue
            for hp, e in HE:
                s = pers[hp]
                rows = slice(e * 64, (e + 1) * 64)
                b, o = reg(hp, e, 2)
                ph.stt("dve", Hs[hp][rows, :], Hs[hp][rows, :], s["gam"][rows, c:c + 1], b[rows, o:o + 64],
                       ALU.mult, ALU.add, [Hs[hp], s["gam"], b], [Hs[hp]])
                ph.cp("act", Hb[hp][rows, :], Hs[hp][rows, :], [Hs[hp]], [Hb[hp]])
            if STOP <= 7:
                continue
            for hp, e in HE:
                i8 = hp * 2 + e
                b, o = reg(hp, e, 1)
                ph.add("dve", lambda en, i8=i8, b=b, o=o: en.bn_stats(gst[i8][:, 0:6], b[:, o:o + 64]), [b], [gst[i8]])
                ph.add("dve", lambda en, i8=i8: en.bn_aggr(gmv[i8][:, 0:2], gst[i8][:, 0:6]), [gst[i8]], [gmv[i8]])
            for hp, e in HE:
                i8 = hp * 2 + e
                ph.act(grs[i8][:, 0:1], gmv[i8][:, 1:2], AF.Ln, [gmv[i8]], [grs[i8]], bias=epsG[:, 0:1])
            for hp, e in HE:
                i8 = hp * 2 + e
                ph.act(grs[i8][:, 0:1], grs[i8][:, 0:1], AF.Exp, [grs[i8]], [grs[i8]], scale=-0.5)
            for hp, e in HE:
                i8 = hp * 2 + e
                ph.ts("dve", gnm[i8][:, 0:1], gmv[i8][:, 0:1], grs[i8][:, 0:1], -1.0, ALU.mult, ALU.mult,
                      [gmv[i8], grs[i8]], [gnm[i8]])
            for hp, e in HE:
                i8 = hp * 2 + e
                b, o = reg(hp, e, 1)
                ph.act(ynorm[hp][:, e * 64:(e + 1) * 64], b[:, o:o + 64], AF.Identity, [b, grs[i8], gnm[i8]], [ynorm[hp]],
                       bias=gnm[i8][:, 0:1], scale=grs[i8][:, 0:1])
            if STOP <= 8:
                continue
            tv = ph.ps[6].ap[:, :].bitcast(BF16)
            for hp in range(4):
                ph.tr(tv[:, hp * 128:(hp + 1) * 128], ynorm[hp][:, :], identbf[:, :], [ynorm[hp], identbf], [ph.ps[6]])
            for hp in range(4):
                s = pers[hp]
                P = lambda i: prm[:, i, hp:hp + 1]
                ph.ts("dve", fin[hp][:, :], tv[:, hp * 128:(hp + 1) * 128], P(GG_), P(GB_), ALU.mult, ALU.add,
                      [ph.ps[6], prm], [fin[hp]])
                ph.tt("pool", fin[hp][:, :], fin[hp][:, :], s["bon"][:, cs], ALU.add, [fin[hp], s["bon"]], [fin[hp]])
                ph.tt("pool", s["outT"][:, cs], fin[hp][:, :], s["gsb"][:, cs], ALU.mult, [fin[hp], s["gsb"]], [s["outT"]])
        for hp in range(4):
            ph.dma("sp", dr["mixT"][512 + hp * 128:512 + (hp + 1) * 128, tcols], pers[hp]["outT"][:, :], [pers[hp]["outT"]], [])
    ph.finish()


def phase_opfu(kb, l):
    ph = PH(kb, "of%d" % l)
    S, NT = kb.S, kb.NT
    NG = S // 512
    dr = kb.dram
    c = load_consts(ph)
    ident_bf = c["ident_bf"]
    gB = ph.sb([128, 1024], F32, "gB")
    bB = ph.sb([128, 1024], F32, "bB")
    ph.dma("sp", gB[:, :], dr["ln_mix_g"][l, :].partition_broadcast(128), [], [gB])
    ph.dma("sp", bB[:, :], dr["ln_mix_b"][l, :].partition_broadcast(128), [], [bB])
    wo = [ph.sb([128, 1024], BF16, "wo%d" % r) for r in range(8)]
    wu = [ph.sb([128, 4096], BF16, "wu%d" % r) for r in range(8)]
    stage = [ph.sb([128, 1024], F32, "stg%d" % k) for k in range(4)]
    load_cast_weight(ph, lambda r: dr["w_out"][l, r * 128:(r + 1) * 128, :], 8, 1024, lambda r: wo[r], stage)
    load_cast_weight(ph, lambda r: dr["w_up"][l, r * 128:(r + 1) * 128, :], 8, 4096, lambda r: wu[r], stage)
    mx = [ph.sb([128, 8, 512], BF16, "mx%d" % k) for k in range(2)]
    xg = [ph.sb([128, 8, 512], BF16, "xg%d" % k) for k in range(2)]
    xr = [ph.sb([128, 1024], F32, "xr%d" % k) for k in range(2)]
    srcs = [ph.sb([128, 1024], F32, "src%d" % k) for k in range(2)]
    bufs = [ln_bufs(ph, k) for k in range(3)]
    tmp = [ph.sb([128, 512], F32, "tmp%d" % k) for k in range(3)]
    ho = [ph.sb([128, 512], BF16, "ho%d" % k) for k in range(4)]
    kcnt = [0]

    def part_a(g):
        m = mx[g % 2]
        x_g = xg[g % 2]
        ph.dma("pool", m[:, :, :], dr["mixT"][:, g * 512:(g + 1) * 512].rearrange("(c p) t -> p c t", p=128), [], [m])
        for j in range(4):
            i = g * 4 + j
            x_ = xr[i % 2]
            ph.dma("sp", x_[:, :], dr["xres"][i * 128:(i + 1) * 128, :], [], [x_])
            src = srcs[i % 2]
            for hh in range(2):
                pb = ph.ps[hh]
                for cc in range(8):
                    ph.mm(pb[:, :], m[:, cc, j * 128:(j + 1) * 128], wo[cc][:, hh * 512:(hh + 1) * 512],
                          cc == 0, cc == 7, [m, wo[cc]], [pb])
                ph.stt("dve", src[:, hh * 512:(hh + 1) * 512], x_[:, hh * 512:(hh + 1) * 512], ALPHA, pb[:, :],
                       ALU.mult, ALU.add, [x_, pb], [src])
            b = bufs[i % 3]
            st, mv, rs, nm, xn, ybf, xh = (b[k] for k in ("st", "mv", "rs", "nm", "xn", "ybf", "xh"))
            for h in range(2):
                ph.add("dve", lambda e, h=h, st=st, src=src: e.bn_stats(st[:, h * 6:(h + 1) * 6], src[:, h * 512:(h + 1) * 512]),
                       [src], [st])
            ph.add("dve", lambda e, mv=mv, st=st: e.bn_aggr(mv[:, 0:2], st[:, 0:12]), [st], [mv])
            ph.act(rs[:, 0:1], mv[:, 1:2], AF.Ln, [mv], [rs], bias=b["eps"][:, 0:1])
            ph.act(rs[:, 0:1], rs[:, 0:1], AF.Exp, [rs], [rs], scale=-0.5)
            ph.ts("dve", nm[:, 0:1], mv[:, 0:1], rs[:, 0:1], -1.0, ALU.mult, ALU.mult, [mv, rs], [nm])
            for h in range(2):
                hs = slice(h * 512, (h + 1) * 512)
                ph.act(xn[:, hs], src[:, hs], AF.Identity, [src, rs, nm], [xh[h]], bias=nm[:, 0:1], scale=rs[:, 0:1])
                ph.tt("dve", xn[:, hs], xn[:, hs], gB[:, hs], ALU.mult, [xh[h], gB], [xh[h]])
                ph.tt(("dve", "pool")[h], xn[:, hs], xn[:, hs], bB[:, hs], ALU.add, [xh[h], bB], [xh[h]])
                ph.dma("sp", dr["xres"][i * 128:(i + 1) * 128, hs], xn[:, hs], [xh[h]], [])
                ph.cp("act", ybf[:, hs], xn[:, hs], [xh[h]], [ybf])
            psb = ph.ps[2 + i % 2]
            pbv = psb.ap[:, :].bitcast(BF16)
            for cc in range(8):
                ph.tr(pbv[:, cc * 128:(cc + 1) * 128], ybf[:, cc * 128:(cc + 1) * 128], ident_bf[:, :], [ybf, ident_bf], [psb])
            ph.cp("dve", x_g[:, :, j * 128:(j + 1) * 128], pbv[:, :].rearrange("p (c t) -> p c t", c=8), [psb], [x_g])

    def part_b(g):
        x_g = xg[g % 2]
        for f in range(32):
            k = kcnt[0]
            kcnt[0] += 1
            pb = ph.ps[4 + k % 4]
            for cc in range(8):
                ph.mm(pb[:, :], wu[cc][:, f * 128:(f + 1) * 128], x_g[:, cc, :], cc == 0, cc == 7, [wu[cc], x_g], [pb])
            t = tmp[k % 3]
            h = ho[k % 4]
            ph.act(t[:, :], pb[:, :], AF.Relu, [pb], [t])
            ph.tt("pool", h[:, :], t[:, :], t[:, :], ALU.mult, [t], [h])
            ph.dma("sp", dr["hT"][f * 128:(f + 1) * 128, g * 512:(g + 1) * 512], h[:, :], [h], [])

    part_a(0)
    for g in range(NG):
        if g + 1 < NG:
            part_a(g + 1)
        part_b(g)
    ph.finish()
```

```python
from contextlib import ExitStack
import math
import numpy as np
import concourse.bass as bass
import concourse.mybir as mybir
from concourse.bass_utils import run_bass_kernel_spmd

F32 = mybir.dt.float32
BF16 = mybir.dt.bfloat16
AF = mybir.ActivationFunctionType
ALU = mybir.AluOpType
AX = mybir.AxisListType

ENGS = ("pe", "act", "dve", "pool", "sp")
NPOOL = 14


class T:
    __slots__ = ("ap", "name", "w", "r", "psum")

    def __init__(self, ap, name=""):
        self.ap = ap
        self.name = name
        self.w = None
        self.r = []
        self.psum = False

    def __getitem__(self, k):
        return self.ap[k]


class Op:
    __slots__ = ("eng", "fn", "dma", "deps", "signal", "sem", "val", "prewait")

    def __init__(self, eng, fn, dma):
        self.eng = eng
        self.fn = fn
        self.dma = dma
        self.deps = []
        self.signal = False
        self.sem = None
        self.val = 0
        self.prewait = None


class Ctx:
    def __init__(self, nc, stack):
        self.nc = nc
        self.esem = {e: stack.enter_context(nc.semaphore("s_" + e)) for e in ENGS}
        self.dsem = {q: [stack.enter_context(nc.semaphore("d_%s%d" % (q, i))) for i in range(NPOOL)]
                     for q in ("sp", "pool", "act")}
        self.count = {e: 0 for e in ENGS}
        self.dk = {q: 0 for q in ("sp", "pool", "act")}
        self.seen = {e: {} for e in ENGS}
        self.ninst = 0


class Phase:
    def __init__(self, ctx, name):
        self.ctx = ctx
        self.name = name
        self.ops = []
        self.tiles = []

    def tile(self, ap, name=""):
        t = T(ap, name)
        self.tiles.append(t)
        return t

    def add(self, eng, fn, reads=(), writes=(), dma=False):
        op = Op(eng, fn, dma)
        deps = []
        for t in reads:
            if t.w is not None:
                deps.append((t.w, "raw"))
            if t.psum:
                for r in t.r:
                    if r.eng != eng:
                        deps.append((r, "rar"))
        for t in writes:
            if t.w is not None:
                deps.append((t.w, "waw"))
            for r in t.r:
                deps.append((r, "war"))
        seen = set()
        for d, kind in deps:
            if d is op or id(d) in seen:
                continue
            if not d.dma and not op.dma and d.eng == eng:
                if eng == "pe":
                    continue
                if kind == "war" and eng != "pool":
                    continue
            seen.add(id(d))
            op.deps.append(d)
            d.signal = True
        for t in writes:
            t.w = op
            t.r = []
        for t in reads:
            if t.w is not op:
                t.r.append(op)
        self.ops.append(op)
        return op

    def emit(self):
        ctx = self.ctx
        nc = ctx.nc
        fin = Op("sp", None, False)
        lastd = {}
        for o in self.ops:
            if o.dma:
                o.signal = True
        self.ops.append(fin)
        for o in self.ops:
            if o.dma:
                q = o.eng
                k = ctx.dk[q]
                ctx.dk[q] += 1
                o.sem = ctx.dsem[q][k % NPOOL]
                o.val = 16 * (k // NPOOL + 1)
                if o.val > 16:
                    o.prewait = (o.sem, o.val - 16)
            elif o.signal:
                ctx.count[o.eng] += 1
                o.sem = ctx.esem[o.eng]
                o.val = ctx.count[o.eng]
            if o.dma:
                lastd[o.sem.name] = o
        fin.deps = list(lastd.values())
        per = {e: [o for o in self.ops if o.eng == e] for e in ENGS}
        bname = {"pe": "tensor", "act": "scalar", "dve": "vector", "pool": "gpsimd", "sp": "sync"}

        def run(e, engine):
            seen = ctx.seen[e]
            for o in per[e]:
                waits = [(d.sem, d.val) for d in o.deps]
                if o.prewait is not None:
                    waits.append(o.prewait)
                for sem, val in waits:
                    if seen.get(sem.name, 0) < val:
                        engine.wait_ge(sem, val)
                        seen[sem.name] = val
                        ctx.ninst += 1
                if o.fn is None:
                    continue
                inst = o.fn(engine)
                ctx.ninst += 1
                if o.signal:
                    inst.then_inc(o.sem, 16 if o.dma else 1)

        with nc.Block() as block:
            for e in ENGS:
                if per[e]:
                    getattr(block, bname[e])(lambda engine, e=e: run(e, engine))
        for t in self.tiles:
            t.w = None
            t.r = []


D = 1024
LN_EPS = 1e-5


class KB:
    def __init__(self, S):
        self.S = S
        self.NT = S // 128
        self.nc = bass.Bass("TRN2", target_bir_lowering=False)
        self.stack = ExitStack()
        self.ctx = Ctx(self.nc, self.stack)
        self.dram = {}
        self.psum_h = [self.stack.enter_context(self.nc.psum_tensor("ps%d" % i, [128, 512], F32))
                       for i in range(8)]

    def din(self, name, shape, dt=F32):
        t = self.nc.dram_tensor(name, list(shape), dt, kind="ExternalInput")
        self.dram[name] = t
        return t

    def dout(self, name, shape, dt=F32):
        t = self.nc.dram_tensor(name, list(shape), dt, kind="ExternalOutput")
        self.dram[name] = t
        return t

    def dscr(self, name, shape, dt=F32, debug=False):
        if name in getattr(self, "ext_in", ()):
            return self.din(name, shape, dt)
        t = self.nc.dram_tensor(name, list(shape), dt, kind="ExternalOutput" if debug else "Internal")
        self.dram[name] = t
        return t

    def close(self):
        self.stack.close()


class PH(Phase):
    def __init__(self, kb, name):
        super().__init__(kb.ctx, name)
        self.kb = kb
        self.nc = kb.nc
        self.st = ExitStack()
        self.ps = [self.tile(h, "ps%d" % i) for i, h in enumerate(kb.psum_h)]
        for t in self.ps:
            t.psum = True
        self.nsb = 0

    def sb(self, shape, dt=F32, name=None):
        self.nsb += 1
        h = self.st.enter_context(self.nc.sbuf_tensor("%s_%s%d" % (self.name, name or "t", self.nsb), list(shape), dt))
        return self.tile(h, name or "t")

    def dt_(self, name):
        key = "_dt_" + name
        if not hasattr(self, key):
            setattr(self, key, self.tile(self.kb.dram[name], name))
        return getattr(self, key)

    def finish(self):
        self.emit()
        self.st.close()

    def dma(self, q, out_ap, in_ap, reads, writes):
        return self.add(q, lambda e: e.dma_start(out=out_ap, in_=in_ap), reads, writes, dma=True)

    def mm(self, out_ap, lhsT, rhs, start, stop, reads, writes, skip=False):
        if skip:
            return self.add("pe", lambda e: e.matmul(out_ap, lhsT, rhs, start=start, stop=stop, skip_group_check=True),
                            reads, writes)
        return self.add("pe", lambda e: e.matmul(out_ap, lhsT, rhs, start=start, stop=stop), reads, writes)

    def tr(self, out_ap, in_ap, ident_ap, reads, writes):
        return self.add("pe", lambda e: e.transpose(out_ap, in_ap, ident_ap), reads, writes)

    def act(self, out_ap, in_ap, func, reads, writes, bias=0.0, scale=1.0, accum_out=None, eng="act"):
        kw = {}
        if accum_out is not None:
            kw["accum_out"] = accum_out
        return self.add(eng, lambda e: e.activation(out_ap, in_ap, func, bias=bias, scale=scale, **kw), reads, writes)

    def tt(self, eng, out_ap, in0, in1, op, reads, writes):
        return self.add(eng, lambda e: e.tensor_tensor(out_ap, in0, in1, op), reads, writes)

    def ts(self, eng, out_ap, in0, s1, s2, op0, op1, reads, writes, accum_out=None):
        if s2 is None:
            return self.add(eng, lambda e: e.tensor_scalar(out_ap, in0, s1, None, op0), reads, writes)
        if accum_out is not None:
            return self.add(eng, lambda e: e.tensor_scalar(out_ap, in0, s1, s2, op0, op1, accum_out=accum_out), reads, writes)
        return self.add(eng, lambda e: e.tensor_scalar(out_ap, in0, s1, s2, op0, op1), reads, writes)

    def stt(self, eng, out_ap, in0, scalar, in1, op0, op1, reads, writes):
        return self.add(eng, lambda e: e.scalar_tensor_tensor(out_ap, in0, scalar, in1, op0, op1), reads, writes)

    def cp(self, eng, out_ap, in_ap, reads, writes):
        if eng == "act":
            return self.add(eng, lambda e: e.copy(out_ap, in_ap), reads, writes)
        return self.add(eng, lambda e: e.tensor_copy(out_ap, in_ap), reads, writes)


def bcast_row(dram_ap_1d, n):
    return dram_ap_1d.partition_broadcast(128)


def ln_rows(ph, src, i, gB, bB, ident_bf, xres_name, xT_name, bufs, psb):
    nc = ph.nc
    st, mv, rs, nm, xn, ybf, xTs = (bufs[k] for k in ("st", "mv", "rs", "nm", "xn", "ybf", "xTs"))
    xh = bufs["xh"]
    for h in range(2):
        ph.add("dve", lambda e, h=h: e.bn_stats(st[:, h * 6:(h + 1) * 6], src[:, h * 512:(h + 1) * 512]), [src], [st])
    ph.add("dve", lambda e: e.bn_aggr(mv[:, 0:2], st[:, 0:12]), [st], [mv])
    ph.act(rs[:, 0:1], mv[:, 1:2], AF.Ln, [mv], [rs], bias=bufs["eps"][:, 0:1])
    ph.act(rs[:, 0:1], rs[:, 0:1], AF.Exp, [rs], [rs], scale=-0.5)
    ph.ts("dve", nm[:, 0:1], mv[:, 0:1], rs[:, 0:1], -1.0, ALU.mult, ALU.mult, [mv, rs], [nm])
    xres = ph.kb.dram[xres_name]
    for h in range(2):
        hs = slice(h * 512, (h + 1) * 512)
        ph.act(xn[:, hs], src[:, hs], AF.Identity, [src, rs, nm], [xh[h]], bias=nm[:, 0:1], scale=rs[:, 0:1])
        ph.tt("dve", xn[:, hs], xn[:, hs], gB[:, hs], ALU.mult, [xh[h], gB], [xh[h]])
        ph.tt(("dve", "pool")[h], xn[:, hs], xn[:, hs], bB[:, hs], ALU.add, [xh[h], bB], [xh[h]])
        ph.dma("sp", xres[i * 128:(i + 1) * 128, hs], xn[:, hs], [xh[h]], [])
    if xT_name is None:
        return
    pb = psb.ap[:, :].bitcast(BF16)
    for h in range(2):
        hs = slice(h * 512, (h + 1) * 512)
        ph.cp("act", ybf[:, hs], xn[:, hs], [xh[h]], [ybf])
    for c in range(8):
        ph.tr(pb[:, c * 128:(c + 1) * 128], ybf[:, c * 128:(c + 1) * 128], ident_bf[:, :], [ybf, ident_bf], [psb])
    ph.cp("dve", xTs[:, :], pb[:, :], [psb], [xTs])
    xT = ph.kb.dram[xT_name]
    dst = xT[:, i * 128:(i + 1) * 128].rearrange("(c p) t -> p c t", p=128)
    ph.dma("pool", dst, xTs[:, :].rearrange("p (c t) -> p c t", c=8), [xTs], [])


def ln_bufs(ph, k):
    eps = ph.sb([128, 1], F32, "eps%d" % k)
    ph.add("pool", lambda e: e.memset(eps[:, :], LN_EPS), [], [eps])
    xn = ph.sb([128, 1024], F32, "xn%d" % k)
    xh = [ph.tile(xn.ap[:, 0:512], "xnA"), ph.tile(xn.ap[:, 512:1024], "xnB")]
    return dict(eps=eps, st=ph.sb([128, 12], F32, "st%d" % k), mv=ph.sb([128, 2], F32, "mv%d" % k),
                rs=ph.sb([128, 1], F32, "rs%d" % k), nm=ph.sb([128, 1], F32, "nm%d" % k),
                xn=xn, xh=xh, ybf=ph.sb([128, 1024], BF16, "ybf%d" % k),
                xTs=ph.sb([128, 1024], BF16, "xTs%d" % k))


def load_consts(ph):
    c = {}
    c["ident_bf"] = ph.sb([128, 128], BF16, "identbf")
    c["ident_f"] = ph.sb([128, 128], F32, "identf")
    ph.dma("sp", c["ident_f"][:, :], ph.kb.dram["c_ident"][:, :], [], [c["ident_f"]])
    ph.cp("dve", c["ident_bf"][:, :], c["ident_f"][:, :], [c["ident_f"]], [c["ident_bf"]])
    return c


def phase_ln0(kb):
    ph = PH(kb, "ln0")
    S, NT = kb.S, kb.NT
    c = load_consts(ph)
    gB = ph.sb([128, 1024], F32, "gB")
    bB = ph.sb([128, 1024], F32, "bB")
    ph.dma("sp", gB[:, :], kb.dram["ln_in_g"][:].partition_broadcast(128), [], [gB])
    ph.dma("sp", bB[:, :], kb.dram["ln_in_b"][:].partition_broadcast(128), [], [bB])
    NB = 3
    xin = [ph.sb([128, 1024], F32, "xin%d" % k) for k in range(NB)]
    bufs = [ln_bufs(ph, k) for k in range(4)]
    for i in range(NT):
        src = xin[i % NB]
        ph.dma("sp", src[:, :], kb.dram["x"][i * 128:(i + 1) * 128, :], [], [src])
        ln_rows(ph, src, i, gB, bB, c["ident_bf"], "xres", "xT", bufs[i % 4], ph.ps[i % 4])
    ph.finish()


N_DIFF = 1536
NRW = [1824, 1856]


def phase_inproj(kb, l):
    ph = PH(kb, "ip%d" % l)
    S, NT = kb.S, kb.NT
    NG = S // 512
    ncol = N_DIFF + NRW[l]
    nrw_tiles = 15
    w_dram = kb.dram["w_in%d" % l]
    xT_sb = ph.sb([128, 8, S], BF16, "xT")
    xT = kb.dram["xT"]
    for c in range(8):
        ph.dma("sp" if c % 2 == 0 else "pool", xT_sb[:, c, :], xT[c * 128:(c + 1) * 128, :], [], [xT_sb])
    WI = WPieces(ph, lambda r: w_dram[r * 128:(r + 1) * 128, :], 8, ncol, 512, nstage=3, name="wi")
    mu_sb = ph.sb([128, 15], F32, "mu")
    ph.dma("sp", mu_sb[:, :], kb.dram["mu%d" % l][:, :], [], [mu_sb])
    bank = [0]

    def nextbank():
        b = ph.ps[bank[0] % 8]
        bank[0] += 1
        return b

    ob = [ph.sb([128, 512], BF16, "ob%d" % k) for k in range(4)]
    oi = 0
    for which, name, scale in ((0, "qT", 0.125), (1, "kT", 1.0)):
        for h in range(4):
            col0 = which * 512 + h * 128
            for g in range(NG):
                pb = nextbank()
                for c in range(8):
                    wt, wap = WI.get(c, col0, 128)
                    ph.mm(pb[:, :], wap, xT_sb[:, c, g * 512:(g + 1) * 512], c == 0, c == 7, [wt, xT_sb], [pb])
                o = ob[oi % 4]
                oi += 1
                if oi % 2 == 0:
                    ph.act(o[:, :], pb[:, :], AF.Copy, [pb], [o], scale=scale)
                else:
                    ph.ts("dve", o[:, :], pb[:, :], scale, None, ALU.mult, None, [pb], [o])
                ph.dma("sp", kb.dram[name][h, :, g * 512:(g + 1) * 512], o[:, :], [o], [])
    for i in range(NT):
        pb = nextbank()
        for c in range(8):
            wt, wap = WI.get(c, 1024, 512)
            ph.mm(pb[:, :], xT_sb[:, c, i * 128:(i + 1) * 128], wap, c == 0, c == 7, [wt, xT_sb], [pb])
        o = ob[oi % 4]
        oi += 1
        if oi % 2 == 0:
            ph.cp("act", o[:, :], pb[:, :], [pb], [o])
        else:
            ph.cp("dve", o[:, :], pb[:, :], [pb], [o])
        ph.dma("sp", kb.dram["vd"][i * 128:(i + 1) * 128, :], o[:, :], [o], [])
    pfull = [ph.sb([128, S + 1], F32, "pfull%d" % k) for k in range(2)]
    tmp = [ph.sb([128, S], F32, "tmp%d" % k) for k in range(1)]
    for k in range(2):
        ph.add("pool", lambda e, k=k: e.memset(pfull[k][:, 0:1], 0.0), [], [pfull[k]])
    for t in range(nrw_tiles):
        col0 = N_DIFF + t * 128
        rows = min(128, ncol - col0)
        pf = pfull[t % 2]
        tm = tmp[0]
        for g in range(NG):
            pb = nextbank()
            for c in range(8):
                wt, wap = WI.get(c, col0, rows)
                ph.mm(pb[0:rows, :], wap, xT_sb[:, c, g * 512:(g + 1) * 512], c == 0, c == 7, [wt, xT_sb], [pb])
            ph.cp("act", pf[0:rows, 1 + g * 512:1 + (g + 1) * 512], pb[0:rows, :], [pb], [pf])
        ph.tt("pool", tm[0:rows, :], pf[0:rows, 0:S], pf[0:rows, 1:S + 1], ALU.subtract, [pf], [tm])
        ph.stt("dve", tm[0:rows, :], tm[0:rows, :], mu_sb[0:rows, t:t + 1], pf[0:rows, 1:S + 1], ALU.mult, ALU.add,
               [tm, mu_sb, pf], [tm])
        ph.dma("sp", kb.dram["prT"][t * 128:t * 128 + rows, :], tm[0:rows, :], [tm], [])
        if l == 0 and 8 <= t < 12:
            ph.dma("pool", kb.dram["vfT"][(t - 8) * 128:(t - 7) * 128, :], tm[0:rows, :], [tm], [])
    ph.finish()


LAM_INIT = [0.8 - 0.6 * math.exp(-0.3 * l) for l in range(2)]
SUBLN_EPS = 1e-5


def phase_pre(kb):
    ph = PH(kb, "pre")
    nc = kb.nc
    dr = kb.dram
    lt = ph.sb([1, 8, 64], F32, "lt")
    for i, nm in enumerate(("lambda_q1", "lambda_k1", "lambda_q2", "lambda_k2")):
        ph.dma("sp", lt[0:1, 2 * i:2 * i + 2, :], dr[nm][:, :].rearrange("(o l) d -> o l d", o=1), [], [lt])
    pr = ph.sb([1, 4, 64], F32, "pr")
    ph.tt("dve", pr[0:1, 0:2, :], lt[0:1, 0:2, :], lt[0:1, 2:4, :], ALU.mult, [lt], [pr])
    ph.tt("dve", pr[0:1, 2:4, :], lt[0:1, 4:6, :], lt[0:1, 6:8, :], ALU.mult, [lt], [pr])
    sm = ph.sb([1, 4], F32, "sm")
    ph.add("dve", lambda e: e.reduce_sum(sm[0:1, 0:4], pr[0:1, :, :], AX.X), [pr], [sm])
    ex = ph.sb([1, 4], F32, "ex")
    ph.act(ex[0:1, :], sm[0:1, :], AF.Exp, [sm], [ex])
    lam = ph.sb([1, 2], F32, "lam")
    ph.tt("dve", lam[0:1, :], ex[0:1, 2:4], ex[0:1, 0:2], ALU.subtract, [ex], [lam])
    for l in range(2):
        ph.ts("dve", lam[0:1, l:l + 1], lam[0:1, l:l + 1], -LAM_INIT[l], None, ALU.add, None, [lam], [lam])
    ph.dma("sp", dr["lamneg"][0:1, :], lam[0:1, :], [lam], [])
    rb = ph.sb([32, 4], F32, "rb")
    oh = ph.sb([32, 385], F32, "oh")
    ph.dma("sp", rb[:, :], dr["rel_bias"][:, :], [], [rb])
    ph.dma("sp", oh[:, :], dr["c_oh"][:, :], [], [oh])
    J = ph.sb([128, 128], F32, "J")
    Mc = ph.sb([128, 128], F32, "Mc")
    ph.dma("pool", J[:, :], dr["c_J"][:, :], [], [J])
    ph.dma("pool", Mc[:, :], dr["c_causal"][:, :], [], [Mc])
    pb = ph.ps[0]
    ph.mm(pb[0:4, 0:385], rb[:, :], oh[:, :], True, True, [rb, oh], [pb])
    G = ph.sb([4, 385], F32, "G")
    ph.cp("dve", G[:, :], pb[0:4, 0:385], [pb], [G])
    EG = ph.sb([4, 384], F32, "EG")
    ph.ts("dve", EG[:, :], G[:, 0:384], G[:, 384:385], None, ALU.subtract, None, [G], [EG])
    ph.act(EG[:, :], EG[:, :], AF.Exp, [EG], [EG])
    egd = ph.dt_("EGd")
    ph.dma("sp", dr["EGd"][:, :], EG[:, :], [EG], [egd])
    for h in range(4):
        for ty in range(2):
            tr = ph.sb([128, 128], F32, "trev")
            src = bass.AP(dr["EGd"], h * 384 + 128 * ty, [[1, 128], [1, 128]])
            ph.dma("sp", tr[:, :], src, [egd], [tr])
            p2 = ph.ps[1 + (h * 2 + ty) % 4]
            ph.mm(p2[:, 0:128], J[:, :], tr[:, :], True, True, [J, tr], [p2])
            eb = ph.sb([128, 128], F32, "eb")
            if ty == 0:
                ph.tt("dve", eb[:, :], p2[:, 0:128], Mc[:, :], ALU.mult, [p2, Mc], [eb])
            else:
                ph.cp("dve", eb[:, :], p2[:, 0:128], [p2], [eb])
            ph.dma("sp", dr["Eb"][h, ty, :, :], eb[:, :], [eb], [])
    ph.finish()


def phase_attn(kb, l):
    ph = PH(kb, "at%d" % l)
    S, NT = kb.S, kb.NT
    NG = S // 512
    dr = kb.dram
    c = load_consts(ph)
    lamneg = ph.sb([128, 2], F32, "lamneg")
    ph.dma("sp", lamneg[:, :], bass.AP(dr["lamneg"], 0, [[0, 128], [1, 2]]), [], [lamneg])
    sg = ph.sb([128, 1], F32, "sg")
    ph.dma("sp", sg[:, :], dr["subln_g"][l:l + 1, :].rearrange("o d -> d o"), [], [sg])
    epsS = ph.sb([128, 1], F32, "epsS")
    ph.add("pool", lambda e: e.memset(epsS[:, :], SUBLN_EPS), [], [epsS])
    Eb = [[ph.sb([128, 128], F32, "Eb") for ty in range(2)] for h in range(2)]
    qT = [ph.sb([128, S], BF16, "qT") for _ in range(2)]
    kT = [ph.sb([128, S], BF16, "kT") for _ in range(2)]
    Vx = [ph.sb([128, NT, 129], BF16, "Vx") for _ in range(2)]
    for k in range(2):
        ph.add("pool", lambda e, k=k: e.memset(Vx[k][:, :, 128:129], 1.0), [], [Vx[k]])
    Pt = [ph.sb([128, 512], BF16, "Pt") for _ in range(4)]
    Pf = [ph.sb([128, 256], F32, "Pf") for _ in range(2)]
    Osb = [ph.sb([128, 480], F32, "Osb") for _ in range(3)]
    small = {k: ph.sb([128, 4], F32, k) for k in ("r1", "r2", "ss", "rstd")}
    t1 = [ph.sb([128, 128], F32, "t1") for _ in range(2)]
    ybf = [ph.sb([128, 128], BF16, "ybf") for _ in range(2)]
    outT = [ph.sb([128, 512], BF16, "outT") for _ in range(2)]
    Obank = [ph.ps[0], ph.ps[1], ph.ps[2]]
    Sbank = [ph.ps[3], ph.ps[4], ph.ps[5], ph.ps[6]]
    Tbank = ph.ps[7]
    pi = [0]
    pfi = [0]
    def load_head(h):
        hb = h % 2
        q_sb, k_sb, v_sb, E = qT[hb], kT[hb], Vx[hb], Eb[hb]
        ph.dma("sp", q_sb[:, :], dr["qT"][h, :, :], [], [q_sb])
        ph.dma("pool", k_sb[:, :], dr["kT"][h, :, :], [], [k_sb])
        ph.dma("sp", v_sb[:, :, 0:128], dr["vd"][:, h * 128:(h + 1) * 128].rearrange("(t p) d -> p t d", p=128),
               [], [v_sb])
        for ty in range(2):
            ph.dma("pool", E[ty][:, :], dr["Eb"][h, ty, :, :], [], [E[ty]])
    load_head(0)
    for h in range(4):
        hb = h % 2
        q_sb, k_sb, v_sb, E = qT[hb], kT[hb], Vx[hb], Eb[hb]
        if h + 1 < 4:
            load_head(h + 1)
        for g in range(NG):
            steps = [(kbk, m) for kbk in range(4 * g + 4) for m in range(2)]
            sb_of = {}

            def emit_qk(si):
                kbk, m = steps[si]
                jlo = max(0, kbk - 4 * g)
                n = (4 - jlo) * 128
                sbk = Sbank[(kbk % 2) * 2 + m]
                sb_of[si] = sbk
                rows = slice(m * 64, (m + 1) * 64)
                ph.mm(sbk[:, 0:n], k_sb[rows, kbk * 128:(kbk + 1) * 128],
                      q_sb[rows, (4 * g + jlo) * 128:(4 * g + 4) * 128], True, True, [k_sb, q_sb], [sbk])

            def emit_rest(si):
                kbk, m = steps[si]
                jlo = max(0, kbk - 4 * g)
                n = (4 - jlo) * 128
                sbk = sb_of[si]
                P = Pt[pi[0] % 4]
                pi[0] += 1
                near = [(j, 4 * g + j - kbk) for j in range(jlo, 4) if 0 <= 4 * g + j - kbk <= 1]
                nn = len(near)
                if nn:
                    j0 = near[0][0]
                    c0 = (j0 - jlo) * 128
                    pf = Pf[pfi[0] % 2]
                    pfi[0] += 1
                    ph.act(pf[:, 0:nn * 128], sbk[:, c0:c0 + nn * 128], AF.Exp, [sbk], [pf])
                    for ii, (j, ty) in enumerate(near):
                        ph.tt("dve", P[:, c0 + ii * 128:c0 + (ii + 1) * 128], pf[:, ii * 128:(ii + 1) * 128],
                              E[ty][:, :], ALU.mult, [pf, E[ty]], [P])
                    far0 = c0 + nn * 128
                    if far0 < n:
                        ph.act(P[:, far0:n], sbk[:, far0:n], AF.Exp, [sbk], [P])
                else:
                    ph.act(P[:, 0:n], sbk[:, 0:n], AF.Exp, [sbk], [P])
                for j in range(jlo, 4):
                    idx = m * 4 + j
                    ob = Obank[idx // 3]
                    cc = (idx % 3) * 160
                    ph.mm(ob[:, cc:cc + 129], P[:, (j - jlo) * 128:(j - jlo + 1) * 128], v_sb[:, kbk, :],
                          kbk == 0 and idx % 3 == 0, kbk == 4 * g + j, [P, v_sb], [ob], skip=True)

            ns = len(steps)
            emit_qk(0)
            emit_qk(1)
            for si in range(0, ns, 2):
                if si + 2 < ns:
                    emit_qk(si + 2)
                    emit_qk(si + 3)
                emit_rest(si)
                emit_rest(si + 1)
            for b4 in range(3):
                if b4 % 2 == 0:
                    ph.cp("act", Osb[b4][:, :], Obank[b4][:, 0:480], [Obank[b4]], [Osb[b4]])
                else:
                    ph.cp("dve", Osb[b4][:, :], Obank[b4][:, 0:480], [Obank[b4]], [Osb[b4]])
            oT = outT[g % 2]
            tb = Tbank.ap[:, :].bitcast(BF16)
            for j in range(4):
                o0 = Osb[j // 3]
                o1 = Osb[(4 + j) // 3]
                cc = (j % 3) * 160
                cc1 = ((4 + j) % 3) * 160
                r1, r2, ss, rstd = (small[k] for k in ("r1", "r2", "ss", "rstd"))
                ph.add("dve", lambda e, o0=o0, cc=cc, j=j: e.reciprocal(r1[:, j:j + 1], o0[:, cc + 128:cc + 129]), [o0], [r1])
                ph.add("dve", lambda e, o1=o1, cc1=cc1, j=j: e.reciprocal(r2[:, j:j + 1], o1[:, cc1 + 128:cc1 + 129]), [o1], [r2])
                ph.ts("dve", r2[:, j:j + 1], r2[:, j:j + 1], lamneg[:, l:l + 1], None, ALU.mult, None, [r2, lamneg], [r2])
                tt1 = t1[j % 2]
                ph.ts("dve", tt1[:, :], o0[:, cc:cc + 128], r1[:, j:j + 1], None, ALU.mult, None, [o0, r1], [tt1])
                ph.stt("dve", tt1[:, :], o1[:, cc1:cc1 + 128], r2[:, j:j + 1], tt1[:, :], ALU.mult, ALU.add, [o1, r2, tt1], [tt1])
                yb = ybf[j % 2]
                ph.act(yb[:, :], tt1[:, :], AF.Square, [tt1], [yb, ss], accum_out=ss[:, j:j + 1])
                ph.act(rstd[:, j:j + 1], ss[:, j:j + 1], AF.Ln, [ss], [rstd], bias=epsS[:, 0:1], scale=1.0 / 128)
                ph.act(rstd[:, j:j + 1], rstd[:, j:j + 1], AF.Exp, [rstd], [rstd], scale=-0.5)
                ph.act(yb[:, :], tt1[:, :], AF.Copy, [tt1, rstd], [yb], scale=rstd[:, j:j + 1])
                ph.tr(tb[:, j * 128:(j + 1) * 128], yb[:, :], c["ident_bf"][:, :], [yb, c["ident_bf"]], [Tbank])
            ph.ts("dve", oT[:, :], tb[:, 0:512], sg[:, 0:1], 1.0 - LAM_INIT[l], ALU.mult, ALU.mult, [Tbank, sg], [oT])
            ph.dma("sp", dr["mixT"][h * 128:(h + 1) * 128, g * 512:(g + 1) * 512], oT[:, :], [oT], [])
    ph.finish()


def host_consts():
    c = {}
    c["c_ident"] = np.eye(128, dtype=np.float32)
    c["c_J"] = np.ascontiguousarray(np.eye(128, dtype=np.float32)[::-1])
    kk, qq = np.meshgrid(np.arange(128), np.arange(128), indexing="ij")
    c["c_causal"] = (qq >= kk).astype(np.float32)
    n = np.maximum(np.arange(384) - 127, 0)
    nf = np.maximum(n, 1).astype(np.float32)
    large = 16 + (np.log(nf / np.float32(16)) / np.float32(math.log(128 / 16)) * np.float32(16)).astype(np.int32)
    large = np.minimum(large, 31)
    bucket = np.where(n < 16, n, large)
    oh = np.zeros((32, 385), np.float32)
    oh[bucket, np.arange(384)] = 1.0
    oh[31, 384] = 1.0
    c["c_oh"] = oh
    pp, ff = np.meshgrid(np.arange(128), np.arange(128), indexing="ij")
    SU = (ff > pp).astype(np.float32)
    IU = (ff >= pp).astype(np.float32)
    c["c_mask4"] = np.ascontiguousarray(np.concatenate([SU, IU, SU, IU], axis=1))
    c["c_maskSL"] = (ff < pp).astype(np.float32)
    c["c_bones"] = ((pp // 64) == (ff // 64)).astype(np.float32)
    cm = np.ones((128, 512), np.float32)
    cm[:, ::128] = 0.0
    c["c_cm01"] = cm
    return c


def host_rwp(inp, l):
    vs = [inp["rw_w0"][l], inp["rw_a0"][l], inp["rw_k_k"][l], inp["rw_k_a"][l], inp["rw_r_k"][l].reshape(512),
          inp["rw_gn_g"][l], inp["rw_gn_b"][l], inp["rw_v0"][l - 1] if l > 0 else np.zeros(512, np.float32)]
    a = np.stack([np.asarray(v, np.float32).reshape(4, 128).T for v in vs], axis=1)
    return np.ascontiguousarray(a)


GN_EPS = 64e-5
DECAY_C = -math.exp(-0.5)


def phase_rwkv(kb, l):
    ph = PH(kb, "rw%d" % l)
    S, NT = kb.S, kb.NT
    NTB = S // 512
    dr = kb.dram
    cst = load_consts(ph)
    identbf = cst["ident_bf"]
    mask4 = ph.sb([128, 512], F32, "mask4")
    maskSL = ph.sb([128, 128], F32, "maskSL")
    bones = ph.sb([128, 128], F32, "bones")
    cm01 = ph.sb([128, 512], F32, "cm01")
    ph.dma("sp", mask4[:, :], dr["c_mask4"][:, :], [], [mask4])
    ph.dma("sp", maskSL[:, :], dr["c_maskSL"][:, :], [], [maskSL])
    ph.dma("sp", bones[:, :], dr["c_bones"][:, :], [], [bones])
    ph.dma("sp", cm01[:, :], dr["c_cm01"][:, :], [], [cm01])
    prm = ph.sb([128, 8, 4], F32, "prm")
    ph.dma("sp", prm[:, :, :], dr["rwp%d" % l][:, :, :], [], [prm])
    W0, A0, KK_, KA_, RK_, GG_, GB_, V0_ = range(8)
    WA = ph.sb([128, 512], F32, "WA")
    WG1 = ph.sb([128, 512], F32, "WG1")
    WGV = ph.sb([64, 512], F32, "WGV")
    ph.dma("sp", WA[0:64, :], dr["rw_w_up"][l, :, :], [], [WA])
    ph.dma("sp", WA[64:128, :], dr["rw_a_up"][l, :, :], [], [WA])
    ph.dma("sp", WG1[:, :], dr["rw_g_up"][l, 0:128, :], [], [WG1])
    ph.dma("sp", WGV[0:32, :], dr["rw_g_up"][l, 128:160, :], [], [WGV])
    if l > 0:
        ph.dma("sp", WGV[32:64, :], dr["rw_v_up"][l - 1, :, :], [], [WGV])
    omk = ph.sb([128, 4], F32, "omk")
    ph.ts("dve", omk[:, :], prm[:, KA_, :], -1.0, 1.0, ALU.mult, ALU.add, [prm], [omk])
    epsG = ph.sb([128, 1], F32, "epsG")
    ph.add("pool", lambda e: e.memset(epsG[:, :], GN_EPS), [], [epsG])
    H = ph.sb([128, 4, 64], F32, "H")
    Hb = [ph.sb([128, 64], BF16, "Hb%d" % hp) for hp in range(4)]
    Hs = [ph.sb([128, 64], F32, "Hs%d" % hp) for hp in range(4)]
    for hp in range(4):
        ph.add("pool", lambda e, hp=hp: e.memset(Hs[hp][:, :], 0.0), [], [Hs[hp]])
        ph.add("pool", lambda e, hp=hp: e.memset(Hb[hp][:, :], 0.0), [], [Hb[hp]])
    XWA = ph.sb([128, 512], F32, "XWA")
    XG1 = ph.sb([128, 512], F32, "XG1")
    XG2 = ph.sb([64, 512], F32, "XG2")
    TH = ph.sb([64, 512], F32, "TH")
    SG1 = ph.sb([128, 512], F32, "SG1")
    SG2 = ph.sb([32, 512], F32, "SG2")
    def f32t(n):
        return ph.sb([128, 512], F32, n)

    def bft(n):
        return ph.sb([128, 512], BF16, n)
    NSET = 2
    sets = []
    for k in range(NSET):
        sets.append(dict(
            R=f32t("R"), Kt=f32t("Kt"), Vt=f32t("Vt"), VF=f32t("VF") if l > 0 else None,
            cum=f32t("cum"), epos=f32t("epos"), AR=ph.sb([128, 4, 2, 128], BF16, "AR"),
            BT=bft("BT"), KT=bft("KT"), BH=bft("BH"), KH=bft("KH"), VB=bft("VB"),
            tok=[ph.sb([128, 4, 128], BF16, "tok") for _ in range(4)],
            gsb=f32t("gsb"), bon=f32t("bon"), outT=bft("outT")))
    sgw, lw, tmpc, eprev, eneg, edec, a_, kk, kk2, ssm, rn, kkn, tq, kp, bbt, vp, gate, pr_ = (
        f32t(n) for n in ("sgw", "lw", "tmpc", "eprev", "eneg", "edec", "a", "kk", "kk2", "ssm", "rn", "kkn",
                          "tq", "kp", "bbt", "vp", "gate", "pr"))
    M4 = [ph.sb([128, 512], BF16, "M4") for _ in range(4)]
    PQ = [[ph.sb([128, 256], BF16, "PQ") for _ in range(2)] for _ in range(2)]
    P0 = [ph.sb([128, 128], BF16, "P0") for _ in range(2)]
    TT = [[ph.sb([128, 128], BF16, "TT") for _ in range(2)] for _ in range(2)]
    AkV = [ph.sb([128, 64], BF16, "AkV") for _ in range(2)]
    WTsb = [ph.sb([128, 128], BF16, "WTsb") for _ in range(2)]
    Uloc = [ph.sb([128, 64], F32, "Uloc") for _ in range(2)]
    Usb = [ph.sb([128, 64], BF16, "Usb") for _ in range(2)]
    ynorm = [ph.sb([128, 128], BF16, "ynorm") for _ in range(2)]
    gst = [ph.sb([128, 6], F32, "gst") for _ in range(2)]
    gmv = [ph.sb([128, 2], F32, "gmv") for _ in range(2)]
    grs = [ph.sb([128, 1], F32, "grs") for _ in range(2)]
    gnm = [ph.sb([128, 1], F32, "gnm") for _ in range(2)]
    fin = [ph.sb([128, 128], F32, "fin") for _ in range(2)]
    PA = [ph.ps[0], ph.ps[1]]
    TRB = [ph.ps[2], ph.ps[3]]
    kb_h = kb.psum_h

    def subtiles(bi, bounds):
        return [ph.ps[bi] for a, b in bounds]
    W1 = [subtiles(4 + e, [(0, 64), (64, 128), (128, 192), (192, 256), (256, 320), (320, 384), (384, 512)])
          for e in range(2)]
    W2 = [subtiles(6 + e, [(0, 128), (128, 384), (384, 512)]) for e in range(2)]
    pai = [0]

    def nextpa():
        b = PA[pai[0] % 2]
        pai[0] += 1
        return b

    def c3(ap):
        return ap.rearrange("p (c t) -> p c t", c=4)

    ui = [0]
    for tb in range(NTB):
        tcols = slice(tb * 512, (tb + 1) * 512)
        ph.dma("sp", XWA[:, :], dr["prT"][1536:1664, tcols], [], [XWA])
        ph.dma("pool", XG1[:, :], dr["prT"][1664:1792, tcols], [], [XG1])
        ph.dma("pool", XG2[:, :], dr["prT"][1792:1856, tcols], [], [XG2])
        ph.act(TH[:, :], XWA[0:64, :], AF.Tanh, [XWA], [TH])
        ph.act(SG1[:, :], XG1[:, :], AF.Sigmoid, [XG1], [SG1])
        ph.act(SG2[:, :], XG2[0:32, :], AF.Sigmoid, [XG2], [SG2])
        for hp in range(4):
            s = sets[(tb * 4 + hp) % NSET]
            R, Kt, Vt, VF, cum, epos, AR, BT, KT, BH, KH, VB, tok, gsb, bon, outT = (
                s[k] for k in ("R", "Kt", "Vt", "VF", "cum", "epos", "AR", "BT", "KT", "BH", "KH", "VB", "tok",
                               "gsb", "bon", "outT"))
            hc = slice(hp * 128, (hp + 1) * 128)
            P = lambda i: prm[:, i, hp:hp + 1]
            ph.dma("sp", R[:, :], dr["prT"][hp * 128:(hp + 1) * 128, tcols], [], [R])
            ph.dma("sp", Kt[:, :], dr["prT"][512 + hp * 128:512 + (hp + 1) * 128, tcols], [], [Kt])
            ph.dma("pool", Vt[:, :], dr["prT"][1024 + hp * 128:1024 + (hp + 1) * 128, tcols], [], [Vt])
            if l > 0:
                ph.dma("pool", VF[:, :], dr["vfT"][hp * 128:(hp + 1) * 128, tcols], [], [VF])
            pa = nextpa()
            ph.mm(pa[:, :], WA[0:64, hc], TH[:, :], True, True, [WA, TH], [pa])
            ph.act(sgw[:, :], pa[:, :], AF.Sigmoid, [pa, prm], [sgw], bias=P(W0))
            ph.ts("dve", lw[:, :], sgw[:, :], DECAY_C, None, ALU.mult, None, [sgw], [lw])
            ph.add("dve", lambda e, cum=cum: e.tensor_tensor_scan(cum[:, :], cm01[:, :], lw[:, :], 0.0, ALU.mult, ALU.add),
                   [cm01, lw], [cum])
            ph.act(epos[:, :], cum[:, :], AF.Exp, [cum], [epos])
            ph.tt("pool", tmpc[:, :], cum[:, :], lw[:, :], ALU.subtract, [cum, lw], [tmpc])
            ph.act(eprev[:, :], tmpc[:, :], AF.Exp, [tmpc], [eprev])
            ph.act(eneg[:, :], cum[:, :], AF.Exp, [cum], [eneg], scale=-1.0)
            for c in range(4):
                ph.act(edec[:, c * 128:(c + 1) * 128], cum[:, c * 128:(c + 1) * 128], AF.Exp, [cum], [edec],
                       scale=-1.0, bias=cum[:, c * 128 + 127:c * 128 + 128])
            pa = nextpa()
            ph.mm(pa[:, :], WA[64:128, hc], XWA[64:128, :], True, True, [WA, XWA], [pa])
            ph.act(a_[:, :], pa[:, :], AF.Sigmoid, [pa, prm], [a_], bias=P(A0))
            ph.ts("dve", kk[:, :], Kt[:, :], P(KK_), None, ALU.mult, None, [Kt, prm], [kk])
            ph.tt("pool", kk2[:, :], kk[:, :], kk[:, :], ALU.mult, [kk], [kk2])
            pa = nextpa()
            ph.mm(pa[:, :], bones[:, :], kk2[:, :], True, True, [bones, kk2], [pa])
            ph.ts("dve", ssm[:, :], pa[:, :], 1e-24, None, ALU.max, None, [pa], [ssm])
            ph.act(rn[:, :], ssm[:, :], AF.Ln, [ssm], [rn])
            ph.act(rn[:, :], rn[:, :], AF.Exp, [rn], [rn], scale=-0.5)
            ph.tt("dve", kkn[:, :], kk[:, :], rn[:, :], ALU.mult, [kk, rn], [kkn])
            ph.ts("pool", tq[:, :], a_[:, :], P(KA_), omk[:, hp:hp + 1], ALU.mult, ALU.add, [a_, prm, omk], [tq])
            ph.tt("pool", kp[:, :], tq[:, :], Kt[:, :], ALU.mult, [tq, Kt], [kp])
            ph.stt("dve", AR[:, :, 0, :], c3(kkn[:, :]), -1.0, c3(eprev[:, :]), ALU.mult, ALU.mult, [kkn, eprev], [AR])
            ph.tt("pool", AR[:, :, 1, :], c3(R[:, :]), c3(epos[:, :]), ALU.mult, [R, epos], [AR])
            ph.tt("pool", bbt[:, :], kkn[:, :], a_[:, :], ALU.mult, [kkn, a_], [bbt])
            ph.tt("dve", BT[:, :], bbt[:, :], eneg[:, :], ALU.mult, [bbt, eneg], [BT])
            ph.tt("pool", BH[:, :], bbt[:, :], edec[:, :], ALU.mult, [bbt, edec], [BH])
            ph.tt("dve", KT[:, :], kp[:, :], eneg[:, :], ALU.mult, [kp, eneg], [KT])
            ph.tt("pool", KH[:, :], kp[:, :], edec[:, :], ALU.mult, [kp, edec], [KH])
            if l > 0:
                pa = nextpa()
                ph.mm(pa[:, :], WGV[32:64, hc], XG2[32:64, :], True, True, [WGV, XG2], [pa])
                ph.act(gate[:, :], pa[:, :], AF.Sigmoid, [pa, prm], [gate], bias=P(V0_))
                ph.tt("pool", vp[:, :], VF[:, :], Vt[:, :], ALU.subtract, [VF, Vt], [vp])
                ph.tt("dve", vp[:, :], vp[:, :], gate[:, :], ALU.mult, [vp, gate], [vp])
                ph.tt("pool", vp[:, :], vp[:, :], Vt[:, :], ALU.add, [vp, Vt], [vp])
                vsrc = vp
            else:
                vsrc = Vt
            ph.cp("act", VB[:, :], vsrc[:, :], [vsrc], [VB])
            pa = nextpa()
            ph.mm(pa[:, :], WG1[:, hc], SG1[:, :], True, False, [WG1, SG1], [pa])
            ph.mm(pa[:, :], WGV[0:32, hc], SG2[:, :], False, True, [WGV, SG2], [pa])
            ph.cp("act", gsb[:, :], pa[:, :], [pa], [gsb])
            ph.ts("pool", pr_[:, :], R[:, :], P(RK_), None, ALU.mult, None, [R, prm], [pr_])
            ph.tt("pool", pr_[:, :], pr_[:, :], kp[:, :], ALU.mult, [pr_, kp], [pr_])
            pa = nextpa()
            ph.mm(pa[:, :], bones[:, :], pr_[:, :], True, True, [bones, pr_], [pa])
            ph.tt("dve", bon[:, :], pa[:, :], vsrc[:, :], ALU.mult, [pa, vsrc], [bon])
            for c in range(4):
                trb = TRB[c // 2]
                tv = trb.ap[:, :].bitcast(BF16)
                base = (c % 2) * 512
                cs = slice(c * 128, (c + 1) * 128)
                for qi, (src, sap) in enumerate(((AR, AR[:, c, 0, :]), (BH, BH[:, cs]), (KH, KH[:, cs]), (VB, VB[:, cs]))):
                    ph.tr(tv[:, base + qi * 128:base + (qi + 1) * 128], sap, identbf[:, :], [src, identbf], [trb])
                ev = tv[:, base:base + 512].rearrange("p (q t) -> p q t", q=4)
                if c % 2 == 0:
                    ph.cp("dve", tok[c][:, :, :], ev, [trb], [tok[c]])
                else:
                    ph.cp("act", tok[c][:, :, :], ev, [trb], [tok[c]])
            for c in range(4):
                cs = slice(c * 128, (c + 1) * 128)
                gl = c * 128 + 127
                for e in range(2):
                    rows = slice(e * 64, (e + 1) * 64)
                    w1, w2 = W1[e], W2[e]
                    w1h, w2h = kb_h[4 + e], kb_h[6 + e]
                    m4 = M4[ui[0] % 4]
                    ui[0] += 1
                    ph.mm(w1h[:, 0:256], BT[rows, cs], AR[rows, c, :, :].rearrange("p a t -> p (a t)"), True, True,
                          [BT, AR], [w1[0]])
                    ph.mm(w1h[:, 256:512], KT[rows, cs], AR[rows, c, :, :].rearrange("p a t -> p (a t)"), True, True,
                          [KT, AR], [w1[0]])
                    ph.mm(w2h[:, 0:128], AR[rows, c, 0, :], BT[rows, cs], True, True, [AR, BT], [w2[0]])
                    ph.tt("dve", m4[:, :], w1h[:, 0:512], mask4[:, :], ALU.mult, [w1[0], mask4], [m4])
                    ph.tt("dve", P0[e][:, :], w2h[:, 0:128], maskSL[:, :], ALU.mult, [w2[0], maskSL], [P0[e]])
                    ph.tt("pool", TT[e][0][:, :], m4[:, 0:128], identbf[:, :], ALU.add, [m4, identbf], [TT[e][0]])
                for k in range(6):
                    for e in range(2):
                        w2, w2h = W2[e], kb_h[6 + e]
                        m4 = M4[(ui[0] - 2 + e) % 4]
                        if k == 0:
                            Pk, Qk, Pt_, Qt_ = P0[e][:, :], m4[:, 0:128], P0[e], m4
                        else:
                            pq = PQ[e][k % 2]
                            Pk, Qk, Pt_, Qt_ = pq[:, 0:128], pq[:, 128:256], pq, pq
                        pqn = PQ[e][(k + 1) % 2]
                        ph.mm(w2h[:, 128:256], Qk, Pk, True, True, [Pt_, Qt_], [w2[1]])
                        if k < 5:
                            ph.mm(w2h[:, 256:384], Pk, Qk, True, True, [Pt_, Qt_], [w2[1]])
                            ph.cp("act", pqn[:, 0:256], w2h[:, 128:384], [w2[1]], [pqn])
                        else:
                            ph.cp("act", pqn[:, 0:128], w2h[:, 128:256], [w2[1]], [pqn])
                        ph.mm(w2h[:, 384:512], pqn[:, 0:128], TT[e][k % 2][:, :], True, True, [pqn, TT[e][k % 2]], [w2[2]])
                        ph.tt("dve", TT[e][(k + 1) % 2][:, :], TT[e][k % 2][:, :], w2h[:, 384:512], ALU.add,
                              [TT[e][k % 2], w2[2]], [TT[e][(k + 1) % 2]])
                for e in range(2):
                    rows = slice(e * 64, (e + 1) * 64)
                    w1, w1h = W1[e], kb_h[4 + e]
                    m4 = M4[(ui[0] - 2 + e) % 4]
                    TTf = TT[e][0]
                    Vtok = tok[c][:, 3, rows]
                    ph.mm(w1h[:, 0:64], m4[:, 256:384], Vtok, True, True, [m4, tok[c]], [w1[0]])
                    ph.cp("act", AkV[e][:, :], w1h[:, 0:64], [w1[0]], [AkV[e]])
                    ph.mm(w1h[:, 384:512], tok[c][:, 0, :], TTf[:, :], True, True, [tok[c], TTf], [w1[6]])
                    ph.cp("act", WTsb[e][rows, :], w1h[rows, 384:512], [w1[6]], [WTsb[e]])
                    ph.mm(w1h[:, 64:128], TTf[:, :], AkV[e][:, :], True, True, [TTf, AkV[e]], [w1[1]])
                    ph.cp("act", Uloc[e][:, :], w1h[:, 64:128], [w1[1]], [Uloc[e]])
                    ph.mm(w1h[:, 128:192], WTsb[e][rows, :], Hb[hp][rows, :], True, True, [WTsb[e], Hb[hp]], [w1[2]])
                    ph.tt("dve", Usb[e][:, :], w1h[:, 128:192], Uloc[e][:, :], ALU.add, [w1[2], Uloc[e]], [Usb[e]])
                    ph.mm(w1h[:, 256:320], AR[rows, c, 1, :], Hb[hp][rows, :], True, False, [AR, Hb[hp]], [w1[4]])
                    ph.mm(w1h[:, 256:320], m4[:, 128:256], Usb[e][:, :], False, False, [m4, Usb[e]], [w1[4]])
                    ph.mm(w1h[:, 256:320], m4[:, 384:512], Vtok, False, True, [m4, tok[c]], [w1[4]])
                    ph.mm(w1h[:, 192:256], tok[c][:, 1, :], Usb[e][:, :], True, False, [tok[c], Usb[e]], [w1[3]])
                    ph.mm(w1h[:, 192:256], tok[c][:, 2, :], Vtok, False, True, [tok[c]], [w1[3]])
                    ph.stt("dve", Hs[hp][rows, :], Hs[hp][rows, :], epos[rows, gl:gl + 1], w1h[rows, 192:256],
                           ALU.mult, ALU.add, [Hs[hp], epos, w1[3]], [Hs[hp]])
                    ph.cp("act", Hb[hp][rows, :], Hs[hp][rows, :], [Hs[hp]], [Hb[hp]])
                    yn = ynorm[c % 2]
                    ph.add("dve", lambda en, e=e, w1h=w1h: en.bn_stats(gst[e][:, 0:6], w1h[:, 256:320]), [w1[4]], [gst[e]])
                    ph.add("dve", lambda en, e=e: en.bn_aggr(gmv[e][:, 0:2], gst[e][:, 0:6]), [gst[e]], [gmv[e]])
                    ph.act(grs[e][:, 0:1], gmv[e][:, 1:2], AF.Ln, [gmv[e]], [grs[e]], bias=epsG[:, 0:1])
                    ph.act(grs[e][:, 0:1], grs[e][:, 0:1], AF.Exp, [grs[e]], [grs[e]], scale=-0.5)
                    ph.ts("dve", gnm[e][:, 0:1], gmv[e][:, 0:1], grs[e][:, 0:1], -1.0, ALU.mult, ALU.mult,
                          [gmv[e], grs[e]], [gnm[e]])
                    ph.act(yn[:, e * 64:(e + 1) * 64], w1h[:, 256:320], AF.Identity, [w1[4], grs[e], gnm[e]], [yn],
                           bias=gnm[e][:, 0:1], scale=grs[e][:, 0:1])
                yn = ynorm[c % 2]
                trb = TRB[1]
                tv = trb.ap[:, :].bitcast(BF16)
                ph.tr(tv[:, 0:128], yn[:, :], identbf[:, :], [yn, identbf], [trb])
                fn = fin[c % 2]
                ph.ts("dve", fn[:, :], tv[:, 0:128], P(GG_), P(GB_), ALU.mult, ALU.add, [trb, prm], [fn])
                ph.tt("pool", fn[:, :], fn[:, :], bon[:, cs], ALU.add, [fn, bon], [fn])
                ph.tt("pool", outT[:, cs], fn[:, :], gsb[:, cs], ALU.mult, [fn, gsb], [outT])
            ph.dma("sp", dr["mixT"][512 + hp * 128:512 + (hp + 1) * 128, tcols], outT[:, :], [outT], [])
    ph.finish()


ALPHA = (2 * 2) ** 0.25


def load_cast_weight(ph, w_rows_fn, nrows_tiles, ncols, wbf_fn, stage, q="act"):
    pc = stage[0].ap.shape[1]
    k = 0
    for c0 in range(0, ncols, pc):
        for r in range(nrows_tiles):
            n = min(pc, ncols - c0)
            s = stage[k % len(stage)]
            ph.dma(q if k % 2 == 0 else "sp", s[:, 0:n], w_rows_fn(r)[:, c0:c0 + n], [], [s])
            ph.cp(("dve", "pool", "act")[k % 3], wbf_fn(r)[:, c0:c0 + n], s[:, 0:n], [s], [wbf_fn(r)])
            k += 1


class WPieces:
    def __init__(self, ph, w_rows_fn, nrows_tiles, ncols, piece, nstage=4, name="w"):
        self.piece = piece
        self.t = {}
        stage = [ph.sb([128, piece], F32, "%sstg%d" % (name, k)) for k in range(nstage)]
        npc = (ncols + piece - 1) // piece
        for r in range(nrows_tiles):
            for pi in range(npc):
                n = min(piece, ncols - pi * piece)
                self.t[(r, pi)] = ph.sb([128, n], BF16, "%s%d_%d" % (name, r, pi))
        k = 0
        for pi in range(npc):
            n = min(piece, ncols - pi * piece)
            for r in range(nrows_tiles):
                s = stage[k % nstage]
                ph.dma(("act", "sp")[k % 2], s[:, 0:n], w_rows_fn(r)[:, pi * piece:pi * piece + n], [], [s])
                ph.cp(("dve", "pool", "act")[k % 3], self.t[(r, pi)][:, 0:n], s[:, 0:n], [s], [self.t[(r, pi)]])
                k += 1

    def get(self, r, c0, n):
        pi = c0 // self.piece
        o = c0 - pi * self.piece
        assert o + n <= self.piece or (pi, o) == (pi, 0), (c0, n)
        t = self.t[(r, pi)]
        return t, t[:, o:o + n]


def phase_outproj(kb, l):
    ph = PH(kb, "op%d" % l)
    S, NT = kb.S, kb.NT
    dr = kb.dram
    c = load_consts(ph)
    gB = ph.sb([128, 1024], F32, "gB")
    bB = ph.sb([128, 1024], F32, "bB")
    ph.dma("sp", gB[:, :], dr["ln_mix_g"][l, :].partition_broadcast(128), [], [gB])
    ph.dma("sp", bB[:, :], dr["ln_mix_b"][l, :].partition_broadcast(128), [], [bB])
    wbf = [ph.sb([128, 1024], BF16, "wo%d" % r) for r in range(8)]
    stage = [ph.sb([128, 1024], F32, "stg%d" % k) for k in range(2)]
    load_cast_weight(ph, lambda r: dr["w_out"][l, r * 128:(r + 1) * 128, :], 8, 1024, lambda r: wbf[r], stage)
    mx = [ph.sb([128, 8, 512], BF16, "mx%d" % k) for k in range(2)]
    xr = [ph.sb([128, 1024], F32, "xr%d" % k) for k in range(3)]
    srcs = [ph.sb([128, 1024], F32, "src%d" % k) for k in range(3)]
    bufs = [ln_bufs(ph, k) for k in range(4)]
    for g in range(S // 512):
        m = mx[g % 2]
        ph.dma("pool", m[:, :, :], dr["mixT"][:, g * 512:(g + 1) * 512].rearrange("(c p) t -> p c t", p=128), [], [m])
        for j in range(4):
            i = g * 4 + j
            x_ = xr[i % 3]
            ph.dma("sp", x_[:, :], dr["xres"][i * 128:(i + 1) * 128, :], [], [x_])
            src = srcs[i % 3]
            for hh in range(2):
                pb = ph.ps[(i % 2) * 2 + hh]
                for cc in range(8):
                    ph.mm(pb[:, :], m[:, cc, j * 128:(j + 1) * 128], wbf[cc][:, hh * 512:(hh + 1) * 512],
                          cc == 0, cc == 7, [m, wbf[cc]], [pb])
                ph.stt("dve", src[:, hh * 512:(hh + 1) * 512], x_[:, hh * 512:(hh + 1) * 512], ALPHA, pb[:, :],
                       ALU.mult, ALU.add, [x_, pb], [src])
            ln_rows(ph, src, i, gB, bB, c["ident_bf"], "xres", "xT", bufs[i % 4], ph.ps[4 + i % 4])
    ph.finish()


def phase_ffn_up(kb, l):
    ph = PH(kb, "fu%d" % l)
    S, NT = kb.S, kb.NT
    dr = kb.dram
    wbf = [ph.sb([128, 4096], BF16, "wu%d" % r) for r in range(8)]
    stage = [ph.sb([128, 1024], F32, "stg%d" % k) for k in range(4)]
    load_cast_weight(ph, lambda r: dr["w_up"][l, r * 128:(r + 1) * 128, :], 8, 4096, lambda r: wbf[r], stage)
    xg = [ph.sb([128, 8, 512], BF16, "xg%d" % k) for k in range(2)]
    tmp = [ph.sb([128, 512], F32, "tmp%d" % k) for k in range(3)]
    ho = [ph.sb([128, 512], BF16, "ho%d" % k) for k in range(4)]
    k = 0
    for g in range(S // 512):
        x_ = xg[g % 2]
        ph.dma("pool", x_[:, :, :], dr["xT"][:, g * 512:(g + 1) * 512].rearrange("(c p) t -> p c t", p=128), [], [x_])
        for f in range(32):
            pb = ph.ps[k % 6]
            for cc in range(8):
                ph.mm(pb[:, :], wbf[cc][:, f * 128:(f + 1) * 128], x_[:, cc, :], cc == 0, cc == 7, [wbf[cc], x_], [pb])
            t = tmp[k % 3]
            h = ho[k % 4]
            ph.act(t[:, :], pb[:, :], AF.Relu, [pb], [t])
            ph.tt(("pool", "dve")[k % 2], h[:, :], t[:, :], t[:, :], ALU.mult, [t], [h])
            ph.dma("sp", dr["hT"][f * 128:(f + 1) * 128, g * 512:(g + 1) * 512], h[:, :], [h], [])
            k += 1
    ph.finish()


def phase_ffn_down(kb, l, last):
    ph = PH(kb, "fd%d" % l)
    S, NT = kb.S, kb.NT
    dr = kb.dram
    c = load_consts(ph)
    gB = ph.sb([128, 1024], F32, "gB")
    bB = ph.sb([128, 1024], F32, "bB")
    ph.dma("sp", gB[:, :], dr["ln_ffn_g"][l, :].partition_broadcast(128), [], [gB])
    ph.dma("sp", bB[:, :], dr["ln_ffn_b"][l, :].partition_broadcast(128), [], [bB])
    WD = WPieces(ph, lambda r: dr["w_down"][l, r * 128:(r + 1) * 128, :], 32, 1024, 512, nstage=6, name="wd")
    hg = [ph.sb([128, 32, 512], BF16, "hg%d" % k) for k in range(2)]
    xr = [ph.sb([128, 1024], F32, "xr%d" % k) for k in range(3)]
    srcs = [ph.sb([128, 1024], F32, "src%d" % k) for k in range(3)]
    bufs = [ln_bufs(ph, k) for k in range(4)]
    def load_h(g):
        h_ = hg[g % 2]
        for q4 in range(4):
            ph.dma(("pool", "act")[q4 % 2], h_[:, q4 * 8:(q4 + 1) * 8, :],
                   dr["hT"][q4 * 1024:(q4 + 1) * 1024, g * 512:(g + 1) * 512].rearrange("(f p) t -> p f t", p=128),
                   [], [h_])
    load_h(0)
    for g in range(S // 512):
        h_ = hg[g % 2]
        if g + 1 < S // 512:
            load_h(g + 1)
        for j in range(4):
            i = g * 4 + j
            x_ = xr[i % 3]
            ph.dma("sp", x_[:, :], dr["xres"][i * 128:(i + 1) * 128, :], [], [x_])
            src = srcs[i % 3]
            for hh in range(2):
                pb = ph.ps[(i % 2) * 2 + hh]
                for f in range(32):
                    wt, wap = WD.get(f, hh * 512, 512)
                    ph.mm(pb[:, :], h_[:, f, j * 128:(j + 1) * 128], wap, f == 0, f == 31, [h_, wt], [pb])
                ph.stt("dve", src[:, hh * 512:(hh + 1) * 512], x_[:, hh * 512:(hh + 1) * 512], ALPHA, pb[:, :],
                       ALU.mult, ALU.add, [x_, pb], [src])
            if last:
                ln_rows(ph, src, i, gB, bB, c["ident_bf"], "out", None, bufs[i % 4], ph.ps[4 + i % 4])
            else:
                ln_rows(ph, src, i, gB, bB, c["ident_bf"], "xres", "xT", bufs[i % 4], ph.ps[4 + i % 4])
    ph.finish()


CONST_SHAPES = {"c_ident": [128, 128], "c_J": [128, 128], "c_causal": [128, 128], "c_oh": [32, 385],
                "c_mask4": [128, 512], "c_maskSL": [128, 128], "c_bones": [128, 128], "c_cm01": [128, 512]}


def build_program(S, debug=(), only=None, ext_in=()):
    kb = KB(S)
    kb.ext_in = tuple(ext_in)
    kb.din("x", [S, 1024])
    kb.din("ln_in_g", [1024])
    kb.din("ln_in_b", [1024])
    kb.din("w_in0", [1024, 3360])
    kb.din("w_in1", [1024, 3392])
    kb.din("mu0", [128, 15])
    kb.din("mu1", [128, 15])
    kb.din("rel_bias", [32, 4])
    for nm in ("lambda_q1", "lambda_k1", "lambda_q2", "lambda_k2"):
        kb.din(nm, [2, 64])
    kb.din("subln_g", [2, 128])
    kb.din("rwp0", [128, 8, 4])
    kb.din("rwp1", [128, 8, 4])
    kb.din("rw_w_up", [2, 64, 512])
    kb.din("rw_a_up", [2, 64, 512])
    kb.din("rw_g_up", [2, 160, 512])
    kb.din("rw_v_up", [1, 32, 512])
    kb.din("w_out", [2, 1024, 1024])
    kb.din("ln_mix_g", [2, 1024])
    kb.din("ln_mix_b", [2, 1024])
    kb.din("w_up", [2, 1024, 4096])
    kb.din("w_down", [2, 4096, 1024])
    kb.din("ln_ffn_g", [2, 1024])
    kb.din("ln_ffn_b", [2, 1024])
    for nm, shp in CONST_SHAPES.items():
        kb.din(nm, shp)
    dbg = lambda n: n in debug
    kb.dscr("xres", [S, 1024], F32, dbg("xres"))
    kb.dscr("xT", [1024, S], BF16, dbg("xT"))
    kb.dscr("qT", [4, 128, S], BF16, dbg("qT"))
    kb.dscr("kT", [4, 128, S], BF16, dbg("kT"))
    kb.dscr("vd", [S, 512], BF16, dbg("vd"))
    kb.dscr("prT", [1920, S], F32, dbg("prT"))
    kb.dscr("vfT", [512, S], F32, dbg("vfT"))
    kb.dscr("lamneg", [1, 2], F32, dbg("lamneg"))
    kb.dscr("EGd", [4, 384], F32, dbg("EGd"))
    kb.dscr("Eb", [4, 2, 128, 128], F32, dbg("Eb"))
    kb.dscr("mixT", [1024, S], BF16, dbg("mixT"))
    kb.dscr("hT", [4096, S], BF16, dbg("hT"))
    kb.dout("out", [S, 1024], F32)
    on = lambda n: only is None or n in only
    if on("pre"):
        phase_pre(kb)
    if on("ln0"):
        phase_ln0(kb)
    for l in range(2):
        if on("ip%d" % l):
            phase_inproj(kb, l)
        if on("at%d" % l):
            phase_attn(kb, l)
        if on("rw%d" % l):
            phase_rwkv2(kb, l)
        if on("of%d" % l):
            phase_opfu(kb, l)
        if on("fd%d" % l):
            phase_ffn_down(kb, l, last=(l == 1))
    kb.close()
    return kb


def host_inputs(inp, S):
    f = lambda a: np.ascontiguousarray(np.asarray(a, dtype=np.float32))
    m = {}
    m["ln_in_g"] = f(inp["ln_in_g"])
    m["ln_in_b"] = f(inp["ln_in_b"])
    m["w_in0"] = f(inp["w_in_first"])
    m["w_in1"] = f(inp["w_in_rest"][0])
    for l, mu in enumerate((inp["mu_first"], inp["mu_rest"][0])):
        mp = np.zeros(1920, np.float32)
        mp[:mu.shape[0]] = mu
        m["mu%d" % l] = np.ascontiguousarray(mp.reshape(15, 128).T)
    for nm in ("rel_bias", "lambda_q1", "lambda_k1", "lambda_q2", "lambda_k2", "subln_g", "rw_w_up", "rw_a_up",
               "rw_g_up", "rw_v_up", "w_out", "ln_mix_g", "ln_mix_b", "w_up", "w_down", "ln_ffn_g", "ln_ffn_b"):
        m[nm] = f(inp[nm])
    m["rwp0"] = host_rwp(inp, 0)
    m["rwp1"] = host_rwp(inp, 1)
    m.update(host_consts())
    return m


_PROG = {}


def kernel(**inputs):
    x = np.asarray(inputs["x"], dtype=np.float32)
    B, S, _ = x.shape
    if S not in _PROG:
        _PROG[S] = build_program(S)
    kb = _PROG[S]
    shared = host_inputs(inputs, S)
    in_maps = []
    for b in range(B):
        m = dict(shared)
        m["x"] = np.ascontiguousarray(x[b])
        in_maps.append(m)
    res = run_bass_kernel_spmd(kb.nc, in_maps, core_ids=list(range(B)))
    return np.stack([np.asarray(r["out"], dtype=np.float32) for r in res.results], axis=0)


def phase_rwkv2(kb, l):
    ph = PH(kb, "rw%d" % l)
    S, NT = kb.S, kb.NT
    NTB = S // 512
    dr = kb.dram
    cst = load_consts(ph)
    identbf = cst["ident_bf"]
    mask4 = ph.sb([128, 512], F32, "mask4")
    maskSL = ph.sb([128, 128], F32, "maskSL")
    bones = ph.sb([128, 128], F32, "bones")
    cm01 = ph.sb([128, 512], F32, "cm01")
    ph.dma("sp", mask4[:, :], dr["c_mask4"][:, :], [], [mask4])
    ph.dma("sp", maskSL[:, :], dr["c_maskSL"][:, :], [], [maskSL])
    ph.dma("sp", bones[:, :], dr["c_bones"][:, :], [], [bones])
    ph.dma("sp", cm01[:, :], dr["c_cm01"][:, :], [], [cm01])
    prm = ph.sb([128, 8, 4], F32, "prm")
    ph.dma("sp", prm[:, :, :], dr["rwp%d" % l][:, :, :], [], [prm])
    W0, A0, KK_, KA_, RK_, GG_, GB_, V0_ = range(8)
    WA = ph.sb([128, 512], F32, "WA")
    WG1 = ph.sb([128, 512], F32, "WG1")
    WGV = ph.sb([64, 512], F32, "WGV")
    ph.dma("sp", WA[0:64, :], dr["rw_w_up"][l, :, :], [], [WA])
    ph.dma("sp", WA[64:128, :], dr["rw_a_up"][l, :, :], [], [WA])
    ph.dma("sp", WG1[:, :], dr["rw_g_up"][l, 0:128, :], [], [WG1])
    ph.dma("sp", WGV[0:32, :], dr["rw_g_up"][l, 128:160, :], [], [WGV])
    if l > 0:
        ph.dma("sp", WGV[32:64, :], dr["rw_v_up"][l - 1, :, :], [], [WGV])
    omk = ph.sb([128, 4], F32, "omk")
    ph.ts("dve", omk[:, :], prm[:, KA_, :], -1.0, 1.0, ALU.mult, ALU.add, [prm], [omk])
    epsG = ph.sb([128, 1], F32, "epsG")
    ph.add("pool", lambda e: e.memset(epsG[:, :], GN_EPS), [], [epsG])
    Hb = [ph.sb([128, 64], BF16, "Hb%d" % hp) for hp in range(4)]
    Hs = [ph.sb([128, 64], F32, "Hs%d" % hp) for hp in range(4)]
    for hp in range(4):
        ph.add("pool", lambda e, hp=hp: e.memset(Hs[hp][:, :], 0.0), [], [Hs[hp]])
        ph.add("pool", lambda e, hp=hp: e.memset(Hb[hp][:, :], 0.0), [], [Hb[hp]])
    XWA = ph.sb([128, 512], F32, "XWA")
    XG1 = ph.sb([128, 512], F32, "XG1")
    XG2 = ph.sb([64, 512], F32, "XG2")
    TH = ph.sb([64, 512], F32, "TH")
    SG1 = ph.sb([128, 512], F32, "SG1")
    SG2 = ph.sb([32, 512], F32, "SG2")

    def f32t(n):
        return ph.sb([128, 512], F32, n)

    def bft(n):
        return ph.sb([128, 512], BF16, n)
    pers = []
    for hp in range(4):
        pers.append(dict(gsb=f32t("gsb"), bon=f32t("bon"), AR=ph.sb([128, 4, 2, 128], BF16, "AR"),
                         BT=bft("BT"), KT=bft("KT"), tok=[ph.sb([128, 4, 128], BF16, "tok") for _ in range(4)],
                         outT=bft("outT"), gam=ph.sb([128, 4], F32, "gam"),
                         keepM=[ph.sb([128, 256], BF16, "keepM") for _ in range(8)],
                         Uloc=[ph.sb([128, 64], F32, "Uloc") for _ in range(8)],
                         WTsb=[ph.sb([128, 128], BF16, "WTsb") for _ in range(4)]))
    RIN = [dict(R=f32t("R"), Kt=f32t("Kt"), Vt=f32t("Vt"), VF=f32t("VF") if l > 0 else None) for _ in range(2)]
    TMP = [[f32t("tmp%d" % i) for i in range(10)] for _ in range(2)]
    TB16 = [(bft("BH"), bft("KH"), bft("VB")) for _ in range(2)]
    M4 = [ph.sb([128, 512], BF16, "M4") for _ in range(8)]
    P0 = [ph.sb([128, 128], BF16, "P0") for _ in range(8)]
    XK = [[ph.sb([128, 384], BF16, "XK") for _ in range(2)] for _ in range(8)]
    AkV = [ph.sb([128, 64], BF16, "AkV") for _ in range(8)]
    Usb = [ph.sb([128, 64], BF16, "Usb") for _ in range(8)]
    ynorm = [ph.sb([128, 128], BF16, "ynorm") for _ in range(4)]
    ynall = ph.sb([128, 512], BF16, "ynall")
    gsq = ph.sb([128, 512], F32, "gsq")
    gs1 = ph.sb([128, 8], F32, "gs1")
    gs2 = ph.sb([128, 8], F32, "gs2")
    gm2 = ph.sb([128, 8], F32, "gm2")
    gst = [ph.sb([128, 6], F32, "gst") for _ in range(8)]
    gmv = [ph.sb([128, 2], F32, "gmv") for _ in range(8)]
    grs = [ph.sb([128, 1], F32, "grs") for _ in range(8)]
    gnm = [ph.sb([128, 1], F32, "gnm") for _ in range(8)]
    fin = [ph.sb([128, 128], F32, "fin") for _ in range(4)]
    kb_h = kb.psum_h
    PA = [ph.ps[0], ph.ps[1], ph.ps[2], ph.ps[3]]
    pai = [0]

    def nextpa():
        b = PA[pai[0] % 4]
        pai[0] += 1
        return b

    def c3(ap):
        return ap.rearrange("p (c t) -> p c t", c=4)
    evi = [0]

    import os
    EV = os.environ.get("RW2_EV", "")

    def evac_copy(out_ap, in_ap, reads, writes):
        evi[0] += 1
        eng = ("act", "dve")[evi[0] % 2]
        if EV:
            eng = EV
        ph.cp(eng, out_ap, in_ap, reads, writes)

    def load_shared(tb):
        tcols = slice(tb * 512, (tb + 1) * 512)
        ph.dma("sp", XWA[:, :], dr["prT"][1536:1664, tcols], [], [XWA])
        ph.dma("pool", XG1[:, :], dr["prT"][1664:1792, tcols], [], [XG1])
        ph.dma("pool", XG2[:, :], dr["prT"][1792:1856, tcols], [], [XG2])
        ph.act(TH[:, :], XWA[0:64, :], AF.Tanh, [XWA], [TH])
        ph.act(SG1[:, :], XG1[:, :], AF.Sigmoid, [XG1], [SG1])
        ph.act(SG2[:, :], XG2[0:32, :], AF.Sigmoid, [XG2], [SG2])

    def load_rin(tb, hp):
        tcols = slice(tb * 512, (tb + 1) * 512)
        rin = RIN[hp % 2]
        R, Kt, Vt, VF = rin["R"], rin["Kt"], rin["Vt"], rin["VF"]
        ph.dma("sp", R[:, :], dr["prT"][hp * 128:(hp + 1) * 128, tcols], [], [R])
        ph.dma("sp", Kt[:, :], dr["prT"][512 + hp * 128:512 + (hp + 1) * 128, tcols], [], [Kt])
        ph.dma("pool", Vt[:, :], dr["prT"][1024 + hp * 128:1024 + (hp + 1) * 128, tcols], [], [Vt])
        if l > 0:
            ph.dma("pool", VF[:, :], dr["vfT"][hp * 128:(hp + 1) * 128, tcols], [], [VF])

    load_shared(0)
    load_rin(0, 0)
    load_rin(0, 1)
    for tb in range(NTB):
        tcols = slice(tb * 512, (tb + 1) * 512)
        def prep_gen(hp):
                s = pers[hp]
                rin = RIN[hp % 2]
                R, Kt, Vt, VF = rin["R"], rin["Kt"], rin["Vt"], rin["VF"]
                gsb, bon, AR, BT, KT, tok, gam = (s[k] for k in ("gsb", "bon", "AR", "BT", "KT", "tok", "gam"))
                hc = slice(hp * 128, (hp + 1) * 128)
                P = lambda i: prm[:, i, hp:hp + 1]
                if hp >= 2:
                    load_rin(tb, hp)
                yield
                sgw, cum, epos, eneg, edec, a_, kk, kkn, kp, bbt = TMP[hp % 2]
                BH, KH, VB = TB16[hp % 2]
                pa = nextpa()
                ph.mm(pa[:, :], WA[0:64, hc], TH[:, :], True, True, [WA, TH], [pa])
                yield
                ph.act(sgw[:, :], pa[:, :], AF.Sigmoid, [pa, prm], [sgw], bias=P(W0))
                yield
                ph.ts("dve", sgw[:, :], sgw[:, :], DECAY_C, None, ALU.mult, None, [sgw], [sgw])
                yield
                ph.add("dve", lambda e, cum=cum, sgw=sgw: e.tensor_tensor_scan(cum[:, :], cm01[:, :], sgw[:, :], 0.0,
                                                                              ALU.mult, ALU.add), [cm01, sgw], [cum])
                yield
                ph.act(epos[:, :], cum[:, :], AF.Exp, [cum], [epos])
                yield
                ph.act(eneg[:, :], cum[:, :], AF.Exp, [cum], [eneg], scale=-1.0)
                yield
                for c in range(4):
                    ph.act(edec[:, c * 128:(c + 1) * 128], cum[:, c * 128:(c + 1) * 128], AF.Exp, [cum], [edec],
                           scale=-1.0, bias=cum[:, c * 128 + 127:c * 128 + 128])
                    yield
                ph.cp("pool", gam[:, :], c3(epos[:, :])[:, :, 127], [epos], [gam])
                yield
                pa = nextpa()
                ph.mm(pa[:, :], WA[64:128, hc], XWA[64:128, :], True, True, [WA, XWA], [pa])
                yield
                ph.act(a_[:, :], pa[:, :], AF.Sigmoid, [pa, prm], [a_], bias=P(A0))
                yield
                ph.act(kk[:, :], Kt[:, :], AF.Square, [Kt, prm], [kk], scale=P(KK_))
                yield
                pa = nextpa()
                ph.mm(pa[:, :], bones[:, :], kk[:, :], True, True, [bones, kk], [pa])
                yield
                ph.ts("dve", kk[:, :], pa[:, :], 1e-24, None, ALU.max, None, [pa], [kk])
                yield
                ph.act(kk[:, :], kk[:, :], AF.Ln, [kk], [kk])
                yield
                ph.act(kk[:, :], kk[:, :], AF.Exp, [kk], [kk], scale=-0.5)
                yield
                ph.stt("dve", kkn[:, :], Kt[:, :], P(KK_), kk[:, :], ALU.mult, ALU.mult, [Kt, prm, kk], [kkn])
                yield
                ph.act(kp[:, :], a_[:, :], AF.Identity, [a_, prm, omk], [kp], scale=P(KA_), bias=omk[:, hp:hp + 1])
                yield
                ph.tt("pool", kp[:, :], kp[:, :], Kt[:, :], ALU.mult, [kp, Kt], [kp])
                yield
                ph.stt("dve", AR[:, :, 0, 1:128], c3(kkn[:, :])[:, :, 1:128], -1.0, c3(epos[:, :])[:, :, 0:127],
                       ALU.mult, ALU.mult, [kkn, epos], [AR])
                yield
                ph.ts("dve", AR[:, :, 0, 0:1], c3(kkn[:, :])[:, :, 0:1], -1.0, None, ALU.mult, None, [kkn], [AR])
                yield
                ph.tt("pool", AR[:, :, 1, :], c3(R[:, :]), c3(epos[:, :]), ALU.mult, [R, epos], [AR])
                yield
                ph.tt("pool", bbt[:, :], kkn[:, :], a_[:, :], ALU.mult, [kkn, a_], [bbt])
                yield
                ph.tt("dve", BT[:, :], bbt[:, :], eneg[:, :], ALU.mult, [bbt, eneg], [BT])
                yield
                ph.tt("pool", BH[:, :], bbt[:, :], edec[:, :], ALU.mult, [bbt, edec], [BH])
                yield
                ph.tt("dve", KT[:, :], kp[:, :], eneg[:, :], ALU.mult, [kp, eneg], [KT])
                yield
                ph.tt("pool", KH[:, :], kp[:, :], edec[:, :], ALU.mult, [kp, edec], [KH])
                yield
                if l > 0:
                    pa = nextpa()
                    ph.mm(pa[:, :], WGV[32:64, hc], XG2[32:64, :], True, True, [WGV, XG2], [pa])
                    yield
                    ph.act(a_[:, :], pa[:, :], AF.Sigmoid, [pa, prm], [a_], bias=P(V0_))
                    yield
                    ph.tt("pool", sgw[:, :], VF[:, :], Vt[:, :], ALU.subtract, [VF, Vt], [sgw])
                    yield
                    ph.tt("dve", sgw[:, :], sgw[:, :], a_[:, :], ALU.mult, [sgw, a_], [sgw])
                    yield
                    ph.tt("pool", sgw[:, :], sgw[:, :], Vt[:, :], ALU.add, [sgw, Vt], [sgw])
                    yield
                    vsrc = sgw
                else:
                    vsrc = Vt
                ph.cp("act", VB[:, :], vsrc[:, :], [vsrc], [VB])
                yield
                pa = nextpa()
                ph.mm(pa[:, :], WG1[:, hc], SG1[:, :], True, False, [WG1, SG1], [pa])
                yield
                ph.mm(pa[:, :], WGV[0:32, hc], SG2[:, :], False, True, [WGV, SG2], [pa])
                yield
                ph.cp("act", gsb[:, :], pa[:, :], [pa], [gsb])
                yield
                ph.stt("dve", bbt[:, :], R[:, :], P(RK_), kp[:, :], ALU.mult, ALU.mult, [R, prm, kp], [bbt])
                yield
                pa = nextpa()
                ph.mm(pa[:, :], bones[:, :], bbt[:, :], True, True, [bones, bbt], [pa])
                yield
                ph.tt("dve", bon[:, :], pa[:, :], vsrc[:, :], ALU.mult, [pa, vsrc], [bon])
                yield
                for c in range(4):
                    trb = ph.ps[6 + c // 2]
                    tv = trb.ap[:, :].bitcast(BF16)
                    base = (c % 2) * 512
                    cs = slice(c * 128, (c + 1) * 128)
                    for qi, (src, sap) in enumerate(((AR, AR[:, c, 0, :]), (BH, BH[:, cs]), (KH, KH[:, cs]), (VB, VB[:, cs]))):
                        ph.tr(tv[:, base + qi * 128:base + (qi + 1) * 128], sap, identbf[:, :], [src, identbf], [trb])
                    ev = tv[:, base:base + 512].rearrange("p (q t) -> p q t", q=4)
                    evac_copy(tok[c][:, :, :], ev, [trb], [tok[c]])
                    yield
        for pair in ((0, 1), (2, 3)):
            gens = [prep_gen(hp) for hp in pair]
            while gens:
                for g_ in list(gens):
                    try:
                        next(g_)
                    except StopIteration:
                        gens.remove(g_)
        if tb + 1 < NTB:
            load_shared(tb + 1)
            load_rin(tb + 1, 0)
            load_rin(tb + 1, 1)
        import os
        STOP = int(os.environ.get("RW2_STOP", "99"))
        if STOP <= 1:
            continue
        for hp in range(4):
            s = pers[hp]
            AR, BT, KT, tok, keepM, Uloc, WTsb = (s[k] for k in ("AR", "BT", "KT", "tok", "keepM", "Uloc", "WTsb"))
            units = [(c, e) for c in range(4) for e in range(2)]
            for sw in range(2):
                for ui4 in range(4):
                    u = sw * 4 + ui4
                    c, e = units[u]
                    rows = slice(e * 64, (e + 1) * 64)
                    cs = slice(c * 128, (c + 1) * 128)
                    gb = ph.ps[4 + ui4]
                    g3 = ph.ps[2 + e]
                    go = (ui4 // 2) * 128
                    arr = AR[rows, c, :, :].rearrange("p a t -> p (a t)")
                    ph.mm(gb[:, 0:256], BT[rows, cs], arr, True, True, [BT, AR], [gb])
                    ph.mm(gb[:, 256:512], KT[rows, cs], arr, True, True, [KT, AR], [gb])
                    ph.mm(g3[:, go:go + 128], AR[rows, c, 0, :], BT[rows, cs], True, True, [AR, BT], [g3])
                for ui4 in range(4):
                    u = sw * 4 + ui4
                    c, e = units[u]
                    gb = ph.ps[4 + ui4]
                    g3 = ph.ps[2 + e]
                    go = (ui4 // 2) * 128
                    ph.tt("dve", M4[u][:, :], gb[:, 0:512], mask4[:, :], ALU.mult, [gb, mask4], [M4[u]])
                    ph.tt("dve", P0[u][:, :], g3[:, go:go + 128], maskSL[:, :], ALU.mult, [g3, maskSL], [P0[u]])
                    ph.tt("pool", XK[u][1][:, 256:384], M4[u][:, 0:128], identbf[:, :], ALU.add, [M4[u], identbf], [XK[u][1]])
                    ph.cp("pool", keepM[u][:, 0:128], M4[u][:, 128:256], [M4[u]], [keepM[u]])
                    ph.cp("pool", keepM[u][:, 128:256], M4[u][:, 384:512], [M4[u]], [keepM[u]])
            if STOP <= 2:
                continue
            for k in range(7):
                for rnd in range(1):
                    us_ = range(8)
                    for u in us_:
                        cb = ph.ps[u]
                        if k == 0:
                            Pk, Qk, Pt_, Qt_ = P0[u][:, :], M4[u][:, 0:128], P0[u], M4[u]
                        else:
                            xk = XK[u][k % 2]
                            Pk, Qk, Pt_, Qt_ = xk[:, 0:128], xk[:, 128:256], xk, xk
                        if k < 6:
                            ph.mm(cb[:, 0:128], Qk, Pk, True, True, [Pt_, Qt_], [cb])
                        if k == 0:
                            ph.mm(cb[:, 128:256], Pk, Qk, True, True, [Pt_, Qt_], [cb])
                        elif k < 5:
                            ph.mm(cb[:, 128:384], Pk, xk[:, 128:384], True, True, [xk], [cb])
                        else:
                            ph.mm(cb[:, 256:384], Pk, xk[:, 256:384], True, True, [xk], [cb])
                    for u in us_:
                        cb = ph.ps[u]
                        xn = XK[u][(k + 1) % 2]
                        if k < 5:
                            ph.cp("act", xn[:, 0:256], cb[:, 0:256], [cb], [xn])
                        elif k == 5:
                            ph.cp("act", xn[:, 0:128], cb[:, 0:128], [cb], [xn])
                        if k >= 1:
                            xk = XK[u][k % 2]
                            ph.tt("dve", xn[:, 256:384], cb[:, 256:384], xk[:, 256:384], ALU.add, [cb, xk], [xn])
            if STOP <= 3:
                continue
            for u in range(8):
                c, e = units[u]
                rows = slice(e * 64, (e + 1) * 64)
                TTf = XK[u][1]
                Vtok = tok[c][:, 3, rows]
                ph.mm(ph.ps[2][:, u * 64:(u + 1) * 64], M4[u][:, 256:384], Vtok, True, True, [M4[u], tok[c]], [ph.ps[2]])
                wb = ph.ps[3 + u // 4]
                ph.mm(wb[:, (u % 4) * 128:(u % 4 + 1) * 128], tok[c][:, 0, :], TTf[:, 256:384], True, True, [tok[c], TTf], [wb])
            for u in range(8):
                c, e = units[u]
                rows = slice(e * 64, (e + 1) * 64)
                evac_copy(AkV[u][:, :], ph.ps[2][:, u * 64:(u + 1) * 64], [ph.ps[2]], [AkV[u]])
                wb = ph.ps[3 + u // 4]
                evac_copy(WTsb[c][rows, :], wb[rows, (u % 4) * 128:(u % 4 + 1) * 128], [wb], [WTsb[c]])
            for u in range(8):
                ph.mm(ph.ps[5][:, u * 64:(u + 1) * 64], XK[u][1][:, 256:384], AkV[u][:, :], True, True, [XK[u][1], AkV[u]], [ph.ps[5]])
            for u in range(8):
                evac_copy(Uloc[u][:, :], ph.ps[5][:, u * 64:(u + 1) * 64], [ph.ps[5]], [Uloc[u]])
        HE = [(hp, e) for hp in range(4) for e in range(2)]

        def reg(hp, e, j):
            return ph.ps[2 + e * 2 + (hp % 2)], (hp // 2) * 128 + j * 64

        def yreg(c, hp, e):
            return ph.ps[c % 2], (hp * 2 + e) * 64

        def critical(c):
            for hp, e in HE:
                s = pers[hp]
                rows = slice(e * 64, (e + 1) * 64)
                b, o = reg(hp, e, 0)
                ph.mm(b[:, o:o + 64], s["WTsb"][c][rows, :], Hb[hp][rows, :], True, True, [s["WTsb"][c], Hb[hp]], [b])
            for hp, e in HE:
                s = pers[hp]
                b, o = reg(hp, e, 0)
                us = Usb[hp * 2 + e]
                ph.tt("dve", us[:, :], b[:, o:o + 64], s["Uloc"][c * 2 + e][:, :], ALU.add, [b, s["Uloc"][c * 2 + e]], [us])
            for hp, e in HE:
                s = pers[hp]
                rows = slice(e * 64, (e + 1) * 64)
                us = Usb[hp * 2 + e]
                km = s["keepM"][c * 2 + e]
                tokc = s["tok"][c]
                Vtok = tokc[:, 3, rows]
                b, o = reg(hp, e, 1)
                ph.mm(b[:, o:o + 64], tokc[:, 1, :], us[:, :], True, False, [tokc, us], [b])
                ph.mm(b[:, o:o + 64], tokc[:, 2, :], Vtok, False, True, [tokc], [b])
                yb, yo = yreg(c, hp, e)
                ph.mm(yb[:, yo:yo + 64], s["AR"][rows, c, 1, :], Hb[hp][rows, :], True, False, [s["AR"], Hb[hp]], [yb])
                ph.mm(yb[:, yo:yo + 64], km[:, 0:128], us[:, :], False, False, [km, us], [yb])
                ph.mm(yb[:, yo:yo + 64], km[:, 128:256], Vtok, False, True, [km, tokc], [yb])
            for hp, e in HE:
                s = pers[hp]
                rows = slice(e * 64, (e + 1) * 64)
                b, o = reg(hp, e, 1)
                ph.stt("dve", Hs[hp][rows, :], Hs[hp][rows, :], s["gam"][rows, c:c + 1], b[rows, o:o + 64],
                       ALU.mult, ALU.add, [Hs[hp], s["gam"], b], [Hs[hp]])
                ph.cp("act", Hb[hp][rows, :], Hs[hp][rows, :], [Hs[hp]], [Hb[hp]])

        def epilogue(c):
            cs = slice(c * 128, (c + 1) * 128)
            yb = ph.ps[c % 2]
            y3 = yb[:, :].rearrange("p (h v) -> p h v", h=8)
            ph.act(gsq[:, :], yb[:, :], AF.Square, [yb], [gsq])
            ph.add("dve", lambda en, y3=y3: en.reduce_sum(gs1[:, 0:8], y3, AX.X), [yb], [gs1])
            ph.add("dve", lambda en: en.reduce_sum(gs2[:, 0:8], gsq[:, :].rearrange("p (h v) -> p h v", h=8), AX.X), [gsq], [gs2])
            ph.ts("dve", gs1[:, :], gs1[:, :], 1.0 / 64, None, ALU.mult, None, [gs1], [gs1])
            ph.tt("dve", gm2[:, :], gs1[:, :], gs1[:, :], ALU.mult, [gs1], [gm2])
            ph.stt("dve", gs2[:, :], gs2[:, :], 1.0 / 64, gm2[:, :], ALU.mult, ALU.subtract, [gs2, gm2], [gs2])
            ph.act(gs2[:, :], gs2[:, :], AF.Ln, [gs2], [gs2], bias=epsG[:, 0:1])
            ph.act(gs2[:, :], gs2[:, :], AF.Exp, [gs2], [gs2], scale=-0.5)
            mean_bc = gs1[:, 0:8].unsqueeze(2).to_broadcast([128, 8, 64])
            rstd_bc = gs2[:, 0:8].unsqueeze(2).to_broadcast([128, 8, 64])
            gq3 = gsq[:, :].rearrange("p (h v) -> p h v", h=8)
            ph.tt("dve", gq3, y3, mean_bc, ALU.subtract, [yb, gs1], [gsq])
            ph.tt("pool", ynall[:, :].rearrange("p (h v) -> p h v", h=8), gq3, rstd_bc, ALU.mult, [gsq, gs2], [ynall])
            tv = ph.ps[6].ap[:, :].bitcast(BF16)
            for hp in range(4):
                ph.tr(tv[:, hp * 128:(hp + 1) * 128], ynall[:, hp * 128:(hp + 1) * 128], identbf[:, :], [ynall, identbf], [ph.ps[6]])
            for hp in range(4):
                s = pers[hp]
                P = lambda i: prm[:, i, hp:hp + 1]
                ph.ts("dve", fin[hp][:, :], tv[:, hp * 128:(hp + 1) * 128], P(GG_), P(GB_), ALU.mult, ALU.add,
                      [ph.ps[6], prm], [fin[hp]])
                ph.tt("pool", fin[hp][:, :], fin[hp][:, :], s["bon"][:, cs], ALU.add, [fin[hp], s["bon"]], [fin[hp]])
                ph.tt("pool", s["outT"][:, cs], fin[hp][:, :], s["gsb"][:, cs], ALU.mult, [fin[hp], s["gsb"]], [s["outT"]])

        critical(0)
        for c in range(1, 4):
            critical(c)
            epilogue(c - 1)
        epilogue(3)
        for hp in range(4):
            ph.dma("sp", dr["mixT"][512 + hp * 128:512 + (hp + 1) * 128, tcols], pers[hp]["outT"][:, :], [pers[hp]["outT"]], [])
    ph.finish()


def phase_opfu(kb, l):
    ph = PH(kb, "of%d" % l)
    S, NT = kb.S, kb.NT
    NG = S // 512
    dr = kb.dram
    c = load_consts(ph)
    ident_bf = c["ident_bf"]
    gB = ph.sb([128, 1024], F32, "gB")
    bB = ph.sb([128, 1024], F32, "bB")
    ph.dma("sp", gB[:, :], dr["ln_mix_g"][l, :].partition_broadcast(128), [], [gB])
    ph.dma("sp", bB[:, :], dr["ln_mix_b"][l, :].partition_broadcast(128), [], [bB])
    WO = WPieces(ph, lambda r: dr["w_out"][l, r * 128:(r + 1) * 128, :], 8, 1024, 512, nstage=2, name="wo")
    WU = WPieces(ph, lambda r: dr["w_up"][l, r * 128:(r + 1) * 128, :], 8, 4096, 1024, nstage=3, name="wu")
    mx = [ph.sb([128, 8, 512], BF16, "mx%d" % k) for k in range(2)]
    xg = [ph.sb([128, 8, 512], BF16, "xg%d" % k) for k in range(2)]
    xr = [ph.sb([128, 1024], F32, "xr%d" % k) for k in range(2)]
    srcs = [ph.sb([128, 1024], F32, "src%d" % k) for k in range(2)]
    bufs = [ln_bufs(ph, k) for k in range(3)]
    tmp = [ph.sb([128, 512], F32, "tmp%d" % k) for k in range(3)]
    ho = [ph.sb([128, 512], BF16, "ho%d" % k) for k in range(4)]
    kcnt = [0]

    def part_a(g):
        m = mx[g % 2]
        x_g = xg[g % 2]
        ph.dma("pool", m[:, :, :], dr["mixT"][:, g * 512:(g + 1) * 512].rearrange("(c p) t -> p c t", p=128), [], [m])
        for j in range(4):
            i = g * 4 + j
            x_ = xr[i % 2]
            ph.dma("sp", x_[:, :], dr["xres"][i * 128:(i + 1) * 128, :], [], [x_])
            src = srcs[i % 2]
            for hh in range(2):
                pb = ph.ps[hh]
                for cc in range(8):
                    wt, wap = WO.get(cc, hh * 512, 512)
                    ph.mm(pb[:, :], m[:, cc, j * 128:(j + 1) * 128], wap, cc == 0, cc == 7, [m, wt], [pb])
                ph.stt("dve", src[:, hh * 512:(hh + 1) * 512], x_[:, hh * 512:(hh + 1) * 512], ALPHA, pb[:, :],
                       ALU.mult, ALU.add, [x_, pb], [src])
            b = bufs[i % 3]
            st, mv, rs, nm, xn, ybf, xh = (b[k] for k in ("st", "mv", "rs", "nm", "xn", "ybf", "xh"))
            for h in range(2):
                ph.add("dve", lambda e, h=h, st=st, src=src: e.bn_stats(st[:, h * 6:(h + 1) * 6], src[:, h * 512:(h + 1) * 512]),
                       [src], [st])
            ph.add("dve", lambda e, mv=mv, st=st: e.bn_aggr(mv[:, 0:2], st[:, 0:12]), [st], [mv])
            ph.act(rs[:, 0:1], mv[:, 1:2], AF.Ln, [mv], [rs], bias=b["eps"][:, 0:1])
            ph.act(rs[:, 0:1], rs[:, 0:1], AF.Exp, [rs], [rs], scale=-0.5)
            ph.ts("dve", nm[:, 0:1], mv[:, 0:1], rs[:, 0:1], -1.0, ALU.mult, ALU.mult, [mv, rs], [nm])
            for h in range(2):
                hs = slice(h * 512, (h + 1) * 512)
                ph.act(xn[:, hs], src[:, hs], AF.Identity, [src, rs, nm], [xh[h]], bias=nm[:, 0:1], scale=rs[:, 0:1])
                ph.tt("dve", xn[:, hs], xn[:, hs], gB[:, hs], ALU.mult, [xh[h], gB], [xh[h]])
                ph.tt(("dve", "pool")[h], xn[:, hs], xn[:, hs], bB[:, hs], ALU.add, [xh[h], bB], [xh[h]])
                ph.dma("sp", dr["xres"][i * 128:(i + 1) * 128, hs], xn[:, hs], [xh[h]], [])
                ph.cp("act", ybf[:, hs], xn[:, hs], [xh[h]], [ybf])
            psb = ph.ps[2 + i % 2]
            pbv = psb.ap[:, :].bitcast(BF16)
            for cc in range(8):
                ph.tr(pbv[:, cc * 128:(cc + 1) * 128], ybf[:, cc * 128:(cc + 1) * 128], ident_bf[:, :], [ybf, ident_bf], [psb])
            ph.cp("dve", x_g[:, :, j * 128:(j + 1) * 128], pbv[:, :].rearrange("p (c t) -> p c t", c=8), [psb], [x_g])

    def part_b(g):
        x_g = xg[g % 2]
        for f in range(32):
            k = kcnt[0]
            kcnt[0] += 1
            pb = ph.ps[4 + k % 4]
            for cc in range(8):
                wt, wap = WU.get(cc, f * 128, 128)
                ph.mm(pb[:, :], wap, x_g[:, cc, :], cc == 0, cc == 7, [wt, x_g], [pb])
            t = tmp[k % 3]
            h = ho[k % 4]
            ph.act(t[:, :], pb[:, :], AF.Relu, [pb], [t])
            ph.tt("pool", h[:, :], t[:, :], t[:, :], ALU.mult, [t], [h])
            ph.dma("sp", dr["hT"][f * 128:(f + 1) * 128, g * 512:(g + 1) * 512], h[:, :], [h], [])

    part_a(0)
    for g in range(NG):
        if g + 1 < NG:
            part_a(g + 1)
        part_b(g)
    ph.finish()
```

```python
from contextlib import ExitStack
import math
import numpy as np
import concourse.bass as bass
import concourse.mybir as mybir
from concourse.bass_utils import run_bass_kernel_spmd

F32 = mybir.dt.float32
BF16 = mybir.dt.bfloat16
AF = mybir.ActivationFunctionType
ALU = mybir.AluOpType
AX = mybir.AxisListType

ENGS = ("pe", "act", "dve", "pool", "sp")
NPOOL = 14


class T:
    __slots__ = ("ap", "name", "w", "r", "psum")

    def __init__(self, ap, name=""):
        self.ap = ap
        self.name = name
        self.w = None
        self.r = []
        self.psum = False

    def __getitem__(self, k):
        return self.ap[k]


class Op:
    __slots__ = ("eng", "fn", "dma", "deps", "signal", "sem", "val", "prewait")

    def __init__(self, eng, fn, dma):
        self.eng = eng
        self.fn = fn
        self.dma = dma
        self.deps = []
        self.signal = False
        self.sem = None
        self.val = 0
        self.prewait = None


class Ctx:
    def __init__(self, nc, stack):
        self.nc = nc
        self.esem = {e: stack.enter_context(nc.semaphore("s_" + e)) for e in ENGS}
        self.dsem = {q: [stack.enter_context(nc.semaphore("d_%s%d" % (q, i))) for i in range(NPOOL)]
                     for q in ("sp", "pool", "act")}
        self.count = {e: 0 for e in ENGS}
        self.dk = {q: 0 for q in ("sp", "pool", "act")}
        self.seen = {e: {} for e in ENGS}
        self.ninst = 0


class Phase:
    def __init__(self, ctx, name):
        self.ctx = ctx
        self.name = name
        self.ops = []
        self.tiles = []

    def tile(self, ap, name=""):
        t = T(ap, name)
        self.tiles.append(t)
        return t

    def add(self, eng, fn, reads=(), writes=(), dma=False):
        op = Op(eng, fn, dma)
        deps = []
        for t in reads:
            if t.w is not None:
                deps.append((t.w, "raw"))
            if t.psum:
                for r in t.r:
                    if r.eng != eng:
                        deps.append((r, "rar"))
        for t in writes:
            if t.w is not None:
                deps.append((t.w, "waw"))
            for r in t.r:
                deps.append((r, "war"))
        seen = set()
        for d, kind in deps:
            if d is op or id(d) in seen:
                continue
            if not d.dma and not op.dma and d.eng == eng:
                if eng == "pe":
                    continue
                if kind == "war" and eng != "pool":
                    continue
            seen.add(id(d))
            op.deps.append(d)
            d.signal = True
        for t in writes:
            t.w = op
            t.r = []
        for t in reads:
            if t.w is not op:
                t.r.append(op)
        self.ops.append(op)
        return op

    def emit(self):
        ctx = self.ctx
        nc = ctx.nc
        fin = Op("sp", None, False)
        lastd = {}
        for o in self.ops:
            if o.dma:
                o.signal = True
        self.ops.append(fin)
        for o in self.ops:
            if o.dma:
                q = o.eng
                k = ctx.dk[q]
                ctx.dk[q] += 1
                o.sem = ctx.dsem[q][k % NPOOL]
                o.val = 16 * (k // NPOOL + 1)
                if o.val > 16:
                    o.prewait = (o.sem, o.val - 16)
            elif o.signal:
                ctx.count[o.eng] += 1
                o.sem = ctx.esem[o.eng]
                o.val = ctx.count[o.eng]
            if o.dma:
                lastd[o.sem.name] = o
        fin.deps = list(lastd.values())
        per = {e: [o for o in self.ops if o.eng == e] for e in ENGS}
        bname = {"pe": "tensor", "act": "scalar", "dve": "vector", "pool": "gpsimd", "sp": "sync"}

        def run(e, engine):
            seen = ctx.seen[e]
            for o in per[e]:
                waits = [(d.sem, d.val) for d in o.deps]
                if o.prewait is not None:
                    waits.append(o.prewait)
                for sem, val in waits:
                    if seen.get(sem.name, 0) < val:
                        engine.wait_ge(sem, val)
                        seen[sem.name] = val
                        ctx.ninst += 1
                if o.fn is None:
                    continue
                inst = o.fn(engine)
                ctx.ninst += 1
                if o.signal:
                    inst.then_inc(o.sem, 16 if o.dma else 1)

        with nc.Block() as block:
            for e in ENGS:
                if per[e]:
                    getattr(block, bname[e])(lambda engine, e=e: run(e, engine))
        for t in self.tiles:
            t.w = None
            t.r = []


D = 1024
LN_EPS = 1e-5


class KB:
    def __init__(self, S):
        self.S = S
        self.NT = S // 128
        self.nc = bass.Bass("TRN2", target_bir_lowering=False)
        self.stack = ExitStack()
        self.ctx = Ctx(self.nc, self.stack)
        self.dram = {}
        self.psum_h = [self.stack.enter_context(self.nc.psum_tensor("ps%d" % i, [128, 512], F32))
                       for i in range(8)]

    def din(self, name, shape, dt=F32):
        t = self.nc.dram_tensor(name, list(shape), dt, kind="ExternalInput")
        self.dram[name] = t
        return t

    def dout(self, name, shape, dt=F32):
        t = self.nc.dram_tensor(name, list(shape), dt, kind="ExternalOutput")
        self.dram[name] = t
        return t

    def dscr(self, name, shape, dt=F32, debug=False):
        if name in getattr(self, "ext_in", ()):
            return self.din(name, shape, dt)
        t = self.nc.dram_tensor(name, list(shape), dt, kind="ExternalOutput" if debug else "Internal")
        self.dram[name] = t
        return t

    def close(self):
        self.stack.close()


class PH(Phase):
    def __init__(self, kb, name):
        super().__init__(kb.ctx, name)
        self.kb = kb
        self.nc = kb.nc
        self.st = ExitStack()
        self.ps = [self.tile(h, "ps%d" % i) for i, h in enumerate(kb.psum_h)]
        for t in self.ps:
            t.psum = True
        self.nsb = 0

    def sb(self, shape, dt=F32, name=None):
        self.nsb += 1
        h = self.st.enter_context(self.nc.sbuf_tensor("%s_%s%d" % (self.name, name or "t", self.nsb), list(shape), dt))
        return self.tile(h, name or "t")

    def dt_(self, name):
        key = "_dt_" + name
        if not hasattr(self, key):
            setattr(self, key, self.tile(self.kb.dram[name], name))
        return getattr(self, key)

    def finish(self):
        self.emit()
        self.st.close()

    def dma(self, q, out_ap, in_ap, reads, writes):
        return self.add(q, lambda e: e.dma_start(out=out_ap, in_=in_ap), reads, writes, dma=True)

    def mm(self, out_ap, lhsT, rhs, start, stop, reads, writes, skip=False):
        if skip:
            return self.add("pe", lambda e: e.matmul(out_ap, lhsT, rhs, start=start, stop=stop, skip_group_check=True),
                            reads, writes)
        return self.add("pe", lambda e: e.matmul(out_ap, lhsT, rhs, start=start, stop=stop), reads, writes)

    def tr(self, out_ap, in_ap, ident_ap, reads, writes):
        return self.add("pe", lambda e: e.transpose(out_ap, in_ap, ident_ap), reads, writes)

    def act(self, out_ap, in_ap, func, reads, writes, bias=0.0, scale=1.0, accum_out=None, eng="act"):
        kw = {}
        if accum_out is not None:
            kw["accum_out"] = accum_out
        return self.add(eng, lambda e: e.activation(out_ap, in_ap, func, bias=bias, scale=scale, **kw), reads, writes)

    def tt(self, eng, out_ap, in0, in1, op, reads, writes):
        return self.add(eng, lambda e: e.tensor_tensor(out_ap, in0, in1, op), reads, writes)

    def ts(self, eng, out_ap, in0, s1, s2, op0, op1, reads, writes, accum_out=None):
        if s2 is None:
            return self.add(eng, lambda e: e.tensor_scalar(out_ap, in0, s1, None, op0), reads, writes)
        if accum_out is not None:
            return self.add(eng, lambda e: e.tensor_scalar(out_ap, in0, s1, s2, op0, op1, accum_out=accum_out), reads, writes)
        return self.add(eng, lambda e: e.tensor_scalar(out_ap, in0, s1, s2, op0, op1), reads, writes)

    def stt(self, eng, out_ap, in0, scalar, in1, op0, op1, reads, writes):
        return self.add(eng, lambda e: e.scalar_tensor_tensor(out_ap, in0, scalar, in1, op0, op1), reads, writes)

    def cp(self, eng, out_ap, in_ap, reads, writes):
        if eng == "act":
            return self.add(eng, lambda e: e.copy(out_ap, in_ap), reads, writes)
        return self.add(eng, lambda e: e.tensor_copy(out_ap, in_ap), reads, writes)


def bcast_row(dram_ap_1d, n):
    return dram_ap_1d.partition_broadcast(128)


def ln_rows(ph, src, i, gB, bB, ident_bf, xres_name, xT_name, bufs, psb):
    nc = ph.nc
    st, mv, rs, nm, xn, ybf, xTs = (bufs[k] for k in ("st", "mv", "rs", "nm", "xn", "ybf", "xTs"))
    xh = bufs["xh"]
    for h in range(2):
        ph.add("dve", lambda e, h=h: e.bn_stats(st[:, h * 6:(h + 1) * 6], src[:, h * 512:(h + 1) * 512]), [src], [st])
    ph.add("dve", lambda e: e.bn_aggr(mv[:, 0:2], st[:, 0:12]), [st], [mv])
    ph.act(rs[:, 0:1], mv[:, 1:2], AF.Ln, [mv], [rs], bias=bufs["eps"][:, 0:1])
    ph.act(rs[:, 0:1], rs[:, 0:1], AF.Exp, [rs], [rs], scale=-0.5)
    ph.ts("dve", nm[:, 0:1], mv[:, 0:1], rs[:, 0:1], -1.0, ALU.mult, ALU.mult, [mv, rs], [nm])
    xres = ph.kb.dram[xres_name]
    for h in range(2):
        hs = slice(h * 512, (h + 1) * 512)
        ph.act(xn[:, hs], src[:, hs], AF.Identity, [src, rs, nm], [xh[h]], bias=nm[:, 0:1], scale=rs[:, 0:1])
        ph.tt("dve", xn[:, hs], xn[:, hs], gB[:, hs], ALU.mult, [xh[h], gB], [xh[h]])
        ph.tt(("dve", "pool")[h], xn[:, hs], xn[:, hs], bB[:, hs], ALU.add, [xh[h], bB], [xh[h]])
        ph.dma("sp", xres[i * 128:(i + 1) * 128, hs], xn[:, hs], [xh[h]], [])
    if xT_name is None:
        return
    pb = psb.ap[:, :].bitcast(BF16)
    for h in range(2):
        hs = slice(h * 512, (h + 1) * 512)
        ph.cp("act", ybf[:, hs], xn[:, hs], [xh[h]], [ybf])
    for c in range(8):
        ph.tr(pb[:, c * 128:(c + 1) * 128], ybf[:, c * 128:(c + 1) * 128], ident_bf[:, :], [ybf, ident_bf], [psb])
    ph.cp("dve", xTs[:, :], pb[:, :], [psb], [xTs])
    xT = ph.kb.dram[xT_name]
    dst = xT[:, i * 128:(i + 1) * 128].rearrange("(c p) t -> p c t", p=128)
    ph.dma("pool", dst, xTs[:, :].rearrange("p (c t) -> p c t", c=8), [xTs], [])


def ln_bufs(ph, k):
    eps = ph.sb([128, 1], F32, "eps%d" % k)
    ph.add("pool", lambda e: e.memset(eps[:, :], LN_EPS), [], [eps])
    xn = ph.sb([128, 1024], F32, "xn%d" % k)
    xh = [ph.tile(xn.ap[:, 0:512], "xnA"), ph.tile(xn.ap[:, 512:1024], "xnB")]
    return dict(eps=eps, st=ph.sb([128, 12], F32, "st%d" % k), mv=ph.sb([128, 2], F32, "mv%d" % k),
                rs=ph.sb([128, 1], F32, "rs%d" % k), nm=ph.sb([128, 1], F32, "nm%d" % k),
                xn=xn, xh=xh, ybf=ph.sb([128, 1024], BF16, "ybf%d" % k),
                xTs=ph.sb([128, 1024], BF16, "xTs%d" % k))


def load_consts(ph):
    c = {}
    c["ident_bf"] = ph.sb([128, 128], BF16, "identbf")
    c["ident_f"] = ph.sb([128, 128], F32, "identf")
    ph.dma("sp", c["ident_f"][:, :], ph.kb.dram["c_ident"][:, :], [], [c["ident_f"]])
    ph.cp("dve", c["ident_bf"][:, :], c["ident_f"][:, :], [c["ident_f"]], [c["ident_bf"]])
    return c


def phase_ln0(kb):
    ph = PH(kb, "ln0")
    S, NT = kb.S, kb.NT
    c = load_consts(ph)
    gB = ph.sb([128, 1024], F32, "gB")
    bB = ph.sb([128, 1024], F32, "bB")
    ph.dma("sp", gB[:, :], kb.dram["ln_in_g"][:].partition_broadcast(128), [], [gB])
    ph.dma("sp", bB[:, :], kb.dram["ln_in_b"][:].partition_broadcast(128), [], [bB])
    NB = 3
    xin = [ph.sb([128, 1024], F32, "xin%d" % k) for k in range(NB)]
    bufs = [ln_bufs(ph, k) for k in range(4)]
    def load_x(i):
        s_ = xin[i % NB]
        ph.dma("act", s_[:, :], kb.dram["x"][i * 128:(i + 1) * 128, :], [], [s_])
    load_x(0)
    if NT > 1:
        load_x(1)
    for i in range(NT):
        src = xin[i % NB]
        if i + 2 < NT:
            load_x(i + 2)
        ln_rows(ph, src, i, gB, bB, c["ident_bf"], "xres", "xT", bufs[i % 4], ph.ps[i % 4])
    ph.finish()


N_DIFF = 1536
NRW = [1824, 1856]


def phase_inproj(kb, l):
    ph = PH(kb, "ip%d" % l)
    S, NT = kb.S, kb.NT
    NG = S // 512
    ncol = N_DIFF + NRW[l]
    nrw_tiles = 15
    w_dram = kb.dram["w_in%d" % l]
    xT_sb = ph.sb([128, 8, S], BF16, "xT")
    xT = kb.dram["xT"]
    for c in range(8):
        ph.dma("sp" if c % 2 == 0 else "pool", xT_sb[:, c, :], xT[c * 128:(c + 1) * 128, :], [], [xT_sb])
    WI = WPieces(ph, lambda r: w_dram[r * 128:(r + 1) * 128, :], 8, ncol, 512, nstage=3, name="wi")
    mu_sb = ph.sb([128, 15], F32, "mu")
    ph.dma("sp", mu_sb[:, :], kb.dram["mu%d" % l][:, :], [], [mu_sb])
    bank = [0]

    def nextbank():
        b = ph.ps[bank[0] % 8]
        bank[0] += 1
        return b

    ob = [ph.sb([128, 512], BF16, "ob%d" % k) for k in range(4)]
    oi = 0
    for which, name, scale in ((0, "qT", 0.125), (1, "kT", 1.0)):
        for h in range(4):
            col0 = which * 512 + h * 128
            for g in range(NG):
                pb = nextbank()
                for c in range(8):
                    wt, wap = WI.get(c, col0, 128)
                    ph.mm(pb[:, :], wap, xT_sb[:, c, g * 512:(g + 1) * 512], c == 0, c == 7, [wt, xT_sb], [pb])
                o = ob[oi % 4]
                oi += 1
                if oi % 2 == 0:
                    ph.act(o[:, :], pb[:, :], AF.Copy, [pb], [o], scale=scale)
                else:
                    ph.ts("dve", o[:, :], pb[:, :], scale, None, ALU.mult, None, [pb], [o])
                ph.dma("sp", kb.dram[name][h, :, g * 512:(g + 1) * 512], o[:, :], [o], [])
    for i in range(NT):
        pb = nextbank()
        for c in range(8):
            wt, wap = WI.get(c, 1024, 512)
            ph.mm(pb[:, :], xT_sb[:, c, i * 128:(i + 1) * 128], wap, c == 0, c == 7, [wt, xT_sb], [pb])
        o = ob[oi % 4]
        oi += 1
        if oi % 2 == 0:
            ph.cp("act", o[:, :], pb[:, :], [pb], [o])
        else:
            ph.cp("dve", o[:, :], pb[:, :], [pb], [o])
        ph.dma("sp", kb.dram["vd"][i * 128:(i + 1) * 128, :], o[:, :], [o], [])
    pfull = [ph.sb([128, S + 1], F32, "pfull%d" % k) for k in range(2)]
    tmp = [ph.sb([128, S], F32, "tmp%d" % k) for k in range(1)]
    for k in range(2):
        ph.add("pool", lambda e, k=k: e.memset(pfull[k][:, 0:1], 0.0), [], [pfull[k]])
    for t in range(nrw_tiles):
        col0 = N_DIFF + t * 128
        rows = min(128, ncol - col0)
        pf = pfull[t % 2]
        tm = tmp[0]
        for g in range(NG):
            pb = nextbank()
            for c in range(8):
                wt, wap = WI.get(c, col0, rows)
                ph.mm(pb[0:rows, :], wap, xT_sb[:, c, g * 512:(g + 1) * 512], c == 0, c == 7, [wt, xT_sb], [pb])
            ph.cp("act", pf[0:rows, 1 + g * 512:1 + (g + 1) * 512], pb[0:rows, :], [pb], [pf])
        ph.tt("pool", tm[0:rows, :], pf[0:rows, 0:S], pf[0:rows, 1:S + 1], ALU.subtract, [pf], [tm])
        ph.stt("dve", tm[0:rows, :], tm[0:rows, :], mu_sb[0:rows, t:t + 1], pf[0:rows, 1:S + 1], ALU.mult, ALU.add,
               [tm, mu_sb, pf], [tm])
        ph.dma("sp", kb.dram["prT"][t * 128:t * 128 + rows, :], tm[0:rows, :], [tm], [])
        if l == 0 and 8 <= t < 12:
            ph.dma("pool", kb.dram["vfT"][(t - 8) * 128:(t - 7) * 128, :], tm[0:rows, :], [tm], [])
    ph.finish()


LAM_INIT = [0.8 - 0.6 * math.exp(-0.3 * l) for l in range(2)]
SUBLN_EPS = 1e-5


def phase_pre(kb):
    ph = PH(kb, "pre")
    nc = kb.nc
    dr = kb.dram
    lt = ph.sb([1, 8, 64], F32, "lt")
    for i, nm in enumerate(("lambda_q1", "lambda_k1", "lambda_q2", "lambda_k2")):
        ph.dma("sp", lt[0:1, 2 * i:2 * i + 2, :], dr[nm][:, :].rearrange("(o l) d -> o l d", o=1), [], [lt])
    pr = ph.sb([1, 4, 64], F32, "pr")
    ph.tt("dve", pr[0:1, 0:2, :], lt[0:1, 0:2, :], lt[0:1, 2:4, :], ALU.mult, [lt], [pr])
    ph.tt("dve", pr[0:1, 2:4, :], lt[0:1, 4:6, :], lt[0:1, 6:8, :], ALU.mult, [lt], [pr])
    sm = ph.sb([1, 4], F32, "sm")
    ph.add("dve", lambda e: e.reduce_sum(sm[0:1, 0:4], pr[0:1, :, :], AX.X), [pr], [sm])
    ex = ph.sb([1, 4], F32, "ex")
    ph.act(ex[0:1, :], sm[0:1, :], AF.Exp, [sm], [ex])
    lam = ph.sb([1, 2], F32, "lam")
    ph.tt("dve", lam[0:1, :], ex[0:1, 2:4], ex[0:1, 0:2], ALU.subtract, [ex], [lam])
    for l in range(2):
        ph.ts("dve", lam[0:1, l:l + 1], lam[0:1, l:l + 1], -LAM_INIT[l], None, ALU.add, None, [lam], [lam])
    ph.dma("sp", dr["lamneg"][0:1, :], lam[0:1, :], [lam], [])
    rb = ph.sb([32, 4], F32, "rb")
    oh = ph.sb([32, 385], F32, "oh")
    ph.dma("sp", rb[:, :], dr["rel_bias"][:, :], [], [rb])
    ph.dma("sp", oh[:, :], dr["c_oh"][:, :], [], [oh])
    J = ph.sb([128, 128], F32, "J")
    Mc = ph.sb([128, 128], F32, "Mc")
    ph.dma("pool", J[:, :], dr["c_J"][:, :], [], [J])
    ph.dma("pool", Mc[:, :], dr["c_causal"][:, :], [], [Mc])
    pb = ph.ps[0]
    ph.mm(pb[0:4, 0:385], rb[:, :], oh[:, :], True, True, [rb, oh], [pb])
    G = ph.sb([4, 385], F32, "G")
    ph.cp("dve", G[:, :], pb[0:4, 0:385], [pb], [G])
    EG = ph.sb([4, 384], F32, "EG")
    ph.ts("dve", EG[:, :], G[:, 0:384], G[:, 384:385], None, ALU.subtract, None, [G], [EG])
    ph.act(EG[:, :], EG[:, :], AF.Exp, [EG], [EG])
    egd = ph.dt_("EGd")
    ph.dma("sp", dr["EGd"][:, :], EG[:, :], [EG], [egd])
    for h in range(4):
        for ty in range(2):
            tr = ph.sb([128, 128], F32, "trev")
            src = bass.AP(dr["EGd"], h * 384 + 128 * ty, [[1, 128], [1, 128]])
            ph.dma("sp", tr[:, :], src, [egd], [tr])
            p2 = ph.ps[1 + (h * 2 + ty) % 4]
            ph.mm(p2[:, 0:128], J[:, :], tr[:, :], True, True, [J, tr], [p2])
            eb = ph.sb([128, 128], F32, "eb")
            if ty == 0:
                ph.tt("dve", eb[:, :], p2[:, 0:128], Mc[:, :], ALU.mult, [p2, Mc], [eb])
            else:
                ph.cp("dve", eb[:, :], p2[:, 0:128], [p2], [eb])
            ph.dma("sp", dr["Eb"][h, ty, :, :], eb[:, :], [eb], [])
    ph.finish()


def phase_attn(kb, l):
    ph = PH(kb, "at%d" % l)
    S, NT = kb.S, kb.NT
    NG = S // 512
    dr = kb.dram
    c = load_consts(ph)
    lamneg = ph.sb([128, 2], F32, "lamneg")
    ph.dma("sp", lamneg[:, :], bass.AP(dr["lamneg"], 0, [[0, 128], [1, 2]]), [], [lamneg])
    sg = ph.sb([128, 1], F32, "sg")
    ph.dma("sp", sg[:, :], dr["subln_g"][l:l + 1, :].rearrange("o d -> d o"), [], [sg])
    epsS = ph.sb([128, 1], F32, "epsS")
    ph.add("pool", lambda e: e.memset(epsS[:, :], SUBLN_EPS), [], [epsS])
    Eb = [[ph.sb([128, 128], F32, "Eb") for ty in range(2)] for h in range(2)]
    qT = [ph.sb([128, S], BF16, "qT") for _ in range(2)]
    kT = [ph.sb([128, S], BF16, "kT") for _ in range(2)]
    Vx = [ph.sb([128, NT, 129], BF16, "Vx") for _ in range(2)]
    for k in range(2):
        ph.add("pool", lambda e, k=k: e.memset(Vx[k][:, :, 128:129], 1.0), [], [Vx[k]])
    Pt = [ph.sb([128, 512], BF16, "Pt") for _ in range(4)]
    Pf = [ph.sb([128, 256], F32, "Pf") for _ in range(2)]
    Osb = [ph.sb([128, 480], F32, "Osb") for _ in range(3)]
    small = {k: ph.sb([128, 4], F32, k) for k in ("r1", "r2", "ss", "rstd")}
    t1 = [ph.sb([128, 128], F32, "t1") for _ in range(2)]
    ybf = [ph.sb([128, 128], BF16, "ybf") for _ in range(2)]
    outT = [ph.sb([128, 512], BF16, "outT") for _ in range(2)]
    Obank = [ph.ps[0], ph.ps[1], ph.ps[2]]
    Sbank = [ph.ps[3], ph.ps[4], ph.ps[5], ph.ps[6]]
    Tbank = ph.ps[7]
    pi = [0]
    pfi = [0]
    def load_head(h):
        hb = h % 2
        q_sb, k_sb, v_sb, E = qT[hb], kT[hb], Vx[hb], Eb[hb]
        ph.dma("sp", q_sb[:, :], dr["qT"][h, :, :], [], [q_sb])
        ph.dma("pool", k_sb[:, :], dr["kT"][h, :, :], [], [k_sb])
        ph.dma("sp", v_sb[:, :, 0:128], dr["vd"][:, h * 128:(h + 1) * 128].rearrange("(t p) d -> p t d", p=128),
               [], [v_sb])
        for ty in range(2):
            ph.dma("pool", E[ty][:, :], dr["Eb"][h, ty, :, :], [], [E[ty]])
    load_head(0)
    for h in range(4):
        hb = h % 2
        q_sb, k_sb, v_sb, E = qT[hb], kT[hb], Vx[hb], Eb[hb]
        if h + 1 < 4:
            load_head(h + 1)
        for g in range(NG):
            steps = [(kbk, m) for kbk in range(4 * g + 4) for m in range(2)]
            sb_of = {}

            def emit_qk(si):
                kbk, m = steps[si]
                jlo = max(0, kbk - 4 * g)
                n = (4 - jlo) * 128
                sbk = Sbank[(kbk % 2) * 2 + m]
                sb_of[si] = sbk
                rows = slice(m * 64, (m + 1) * 64)
                ph.mm(sbk[:, 0:n], k_sb[rows, kbk * 128:(kbk + 1) * 128],
                      q_sb[rows, (4 * g + jlo) * 128:(4 * g + 4) * 128], True, True, [k_sb, q_sb], [sbk])

            def emit_rest(si):
                kbk, m = steps[si]
                jlo = max(0, kbk - 4 * g)
                n = (4 - jlo) * 128
                sbk = sb_of[si]
                P = Pt[pi[0] % 4]
                pi[0] += 1
                near = [(j, 4 * g + j - kbk) for j in range(jlo, 4) if 0 <= 4 * g + j - kbk <= 1]
                nn = len(near)
                if nn:
                    j0 = near[0][0]
                    c0 = (j0 - jlo) * 128
                    pf = Pf[pfi[0] % 2]
                    pfi[0] += 1
                    ph.act(pf[:, 0:nn * 128], sbk[:, c0:c0 + nn * 128], AF.Exp, [sbk], [pf])
                    for ii, (j, ty) in enumerate(near):
                        ph.tt("dve", P[:, c0 + ii * 128:c0 + (ii + 1) * 128], pf[:, ii * 128:(ii + 1) * 128],
                              E[ty][:, :], ALU.mult, [pf, E[ty]], [P])
                    far0 = c0 + nn * 128
                    if far0 < n:
                        ph.act(P[:, far0:n], sbk[:, far0:n], AF.Exp, [sbk], [P])
                else:
                    ph.act(P[:, 0:n], sbk[:, 0:n], AF.Exp, [sbk], [P])
                for j in range(jlo, 4):
                    idx = m * 4 + j
                    ob = Obank[idx // 3]
                    cc = (idx % 3) * 160
                    ph.mm(ob[:, cc:cc + 129], P[:, (j - jlo) * 128:(j - jlo + 1) * 128], v_sb[:, kbk, :],
                          kbk == 0 and idx % 3 == 0, kbk == 4 * g + j, [P, v_sb], [ob], skip=True)

            ns = len(steps)
            emit_qk(0)
            emit_qk(1)
            for si in range(0, ns, 2):
                if si + 2 < ns:
                    emit_qk(si + 2)
                    emit_qk(si + 3)
                emit_rest(si)
                emit_rest(si + 1)
            for b4 in range(3):
                if b4 % 2 == 0:
                    ph.cp("act", Osb[b4][:, :], Obank[b4][:, 0:480], [Obank[b4]], [Osb[b4]])
                else:
                    ph.cp("dve", Osb[b4][:, :], Obank[b4][:, 0:480], [Obank[b4]], [Osb[b4]])
            oT = outT[g % 2]
            tb = Tbank.ap[:, :].bitcast(BF16)
            for j in range(4):
                o0 = Osb[j // 3]
                o1 = Osb[(4 + j) // 3]
                cc = (j % 3) * 160
                cc1 = ((4 + j) % 3) * 160
                r1, r2, ss, rstd = (small[k] for k in ("r1", "r2", "ss", "rstd"))
                ph.add("dve", lambda e, o0=o0, cc=cc, j=j: e.reciprocal(r1[:, j:j + 1], o0[:, cc + 128:cc + 129]), [o0], [r1])
                ph.add("dve", lambda e, o1=o1, cc1=cc1, j=j: e.reciprocal(r2[:, j:j + 1], o1[:, cc1 + 128:cc1 + 129]), [o1], [r2])
                ph.ts("dve", r2[:, j:j + 1], r2[:, j:j + 1], lamneg[:, l:l + 1], None, ALU.mult, None, [r2, lamneg], [r2])
                tt1 = t1[j % 2]
                ph.ts("dve", tt1[:, :], o0[:, cc:cc + 128], r1[:, j:j + 1], None, ALU.mult, None, [o0, r1], [tt1])
                ph.stt("dve", tt1[:, :], o1[:, cc1:cc1 + 128], r2[:, j:j + 1], tt1[:, :], ALU.mult, ALU.add, [o1, r2, tt1], [tt1])
                yb = ybf[j % 2]
                ph.act(yb[:, :], tt1[:, :], AF.Square, [tt1], [yb, ss], accum_out=ss[:, j:j + 1])
                ph.act(rstd[:, j:j + 1], ss[:, j:j + 1], AF.Ln, [ss], [rstd], bias=epsS[:, 0:1], scale=1.0 / 128)
                ph.act(rstd[:, j:j + 1], rstd[:, j:j + 1], AF.Exp, [rstd], [rstd], scale=-0.5)
                ph.act(yb[:, :], tt1[:, :], AF.Copy, [tt1, rstd], [yb], scale=rstd[:, j:j + 1])
                ph.tr(tb[:, j * 128:(j + 1) * 128], yb[:, :], c["ident_bf"][:, :], [yb, c["ident_bf"]], [Tbank])
            ph.ts("dve", oT[:, :], tb[:, 0:512], sg[:, 0:1], 1.0 - LAM_INIT[l], ALU.mult, ALU.mult, [Tbank, sg], [oT])
            ph.dma("sp", dr["mixT"][h * 128:(h + 1) * 128, g * 512:(g + 1) * 512], oT[:, :], [oT], [])
    ph.finish()


def host_consts():
    c = {}
    c["c_ident"] = np.eye(128, dtype=np.float32)
    c["c_J"] = np.ascontiguousarray(np.eye(128, dtype=np.float32)[::-1])
    kk, qq = np.meshgrid(np.arange(128), np.arange(128), indexing="ij")
    c["c_causal"] = (qq >= kk).astype(np.float32)
    n = np.maximum(np.arange(384) - 127, 0)
    nf = np.maximum(n, 1).astype(np.float32)
    large = 16 + (np.log(nf / np.float32(16)) / np.float32(math.log(128 / 16)) * np.float32(16)).astype(np.int32)
    large = np.minimum(large, 31)
    bucket = np.where(n < 16, n, large)
    oh = np.zeros((32, 385), np.float32)
    oh[bucket, np.arange(384)] = 1.0
    oh[31, 384] = 1.0
    c["c_oh"] = oh
    pp, ff = np.meshgrid(np.arange(128), np.arange(128), indexing="ij")
    SU = (ff > pp).astype(np.float32)
    IU = (ff >= pp).astype(np.float32)
    c["c_mask4"] = np.ascontiguousarray(np.concatenate([SU, IU, SU, IU], axis=1))
    c["c_maskSL"] = (ff < pp).astype(np.float32)
    c["c_bones"] = ((pp // 64) == (ff // 64)).astype(np.float32)
    cm = np.ones((128, 512), np.float32)
    cm[:, ::128] = 0.0
    c["c_cm01"] = cm
    return c


def host_rwp(inp, l):
    vs = [inp["rw_w0"][l], inp["rw_a0"][l], inp["rw_k_k"][l], inp["rw_k_a"][l], inp["rw_r_k"][l].reshape(512),
          inp["rw_gn_g"][l], inp["rw_gn_b"][l], inp["rw_v0"][l - 1] if l > 0 else np.zeros(512, np.float32)]
    a = np.stack([np.asarray(v, np.float32).reshape(4, 128).T for v in vs], axis=1)
    return np.ascontiguousarray(a)


GN_EPS = 64e-5
DECAY_C = -math.exp(-0.5)


def phase_rwkv(kb, l):
    ph = PH(kb, "rw%d" % l)
    S, NT = kb.S, kb.NT
    NTB = S // 512
    dr = kb.dram
    cst = load_consts(ph)
    identbf = cst["ident_bf"]
    mask4 = ph.sb([128, 512], F32, "mask4")
    maskSL = ph.sb([128, 128], F32, "maskSL")
    bones = ph.sb([128, 128], F32, "bones")
    cm01 = ph.sb([128, 512], F32, "cm01")
    ph.dma("sp", mask4[:, :], dr["c_mask4"][:, :], [], [mask4])
    ph.dma("sp", maskSL[:, :], dr["c_maskSL"][:, :], [], [maskSL])
    ph.dma("sp", bones[:, :], dr["c_bones"][:, :], [], [bones])
    ph.dma("sp", cm01[:, :], dr["c_cm01"][:, :], [], [cm01])
    prm = ph.sb([128, 8, 4], F32, "prm")
    ph.dma("sp", prm[:, :, :], dr["rwp%d" % l][:, :, :], [], [prm])
    W0, A0, KK_, KA_, RK_, GG_, GB_, V0_ = range(8)
    WA = ph.sb([128, 512], F32, "WA")
    WG1 = ph.sb([128, 512], F32, "WG1")
    WGV = ph.sb([64, 512], F32, "WGV")
    ph.dma("sp", WA[0:64, :], dr["rw_w_up"][l, :, :], [], [WA])
    ph.dma("sp", WA[64:128, :], dr["rw_a_up"][l, :, :], [], [WA])
    ph.dma("sp", WG1[:, :], dr["rw_g_up"][l, 0:128, :], [], [WG1])
    ph.dma("sp", WGV[0:32, :], dr["rw_g_up"][l, 128:160, :], [], [WGV])
    if l > 0:
        ph.dma("sp", WGV[32:64, :], dr["rw_v_up"][l - 1, :, :], [], [WGV])
    omk = ph.sb([128, 4], F32, "omk")
    ph.ts("dve", omk[:, :], prm[:, KA_, :], -1.0, 1.0, ALU.mult, ALU.add, [prm], [omk])
    epsG = ph.sb([128, 1], F32, "epsG")
    ph.add("pool", lambda e: e.memset(epsG[:, :], GN_EPS), [], [epsG])
    H = ph.sb([128, 4, 64], F32, "H")
    Hb = [ph.sb([128, 64], BF16, "Hb%d" % hp) for hp in range(4)]
    Hs = [ph.sb([128, 64], F32, "Hs%d" % hp) for hp in range(4)]
    for hp in range(4):
        ph.add("pool", lambda e, hp=hp: e.memset(Hs[hp][:, :], 0.0), [], [Hs[hp]])
        ph.add("pool", lambda e, hp=hp: e.memset(Hb[hp][:, :], 0.0), [], [Hb[hp]])
    XWA = ph.sb([128, 512], F32, "XWA")
    XG1 = ph.sb([128, 512], F32, "XG1")
    XG2 = ph.sb([64, 512], F32, "XG2")
    TH = ph.sb([64, 512], F32, "TH")
    SG1 = ph.sb([128, 512], F32, "SG1")
    SG2 = ph.sb([32, 512], F32, "SG2")
    def f32t(n):
        return ph.sb([128, 512], F32, n)

    def bft(n):
        return ph.sb([128, 512], BF16, n)
    NSET = 2
    sets = []
    for k in range(NSET):
        sets.append(dict(
            R=f32t("R"), Kt=f32t("Kt"), Vt=f32t("Vt"), VF=f32t("VF") if l > 0 else None,
            cum=f32t("cum"), epos=f32t("epos"), AR=ph.sb([128, 4, 2, 128], BF16, "AR"),
            BT=bft("BT"), KT=bft("KT"), BH=bft("BH"), KH=bft("KH"), VB=bft("VB"),
            tok=[ph.sb([128, 4, 128], BF16, "tok") for _ in range(4)],
            gsb=f32t("gsb"), bon=f32t("bon"), outT=bft("outT")))
    sgw, lw, tmpc, eprev, eneg, edec, a_, kk, kk2, ssm, rn, kkn, tq, kp, bbt, vp, gate, pr_ = (
        f32t(n) for n in ("sgw", "lw", "tmpc", "eprev", "eneg", "edec", "a", "kk", "kk2", "ssm", "rn", "kkn",
                          "tq", "kp", "bbt", "vp", "gate", "pr"))
    M4 = [ph.sb([128, 512], BF16, "M4") for _ in range(4)]
    PQ = [[ph.sb([128, 256], BF16, "PQ") for _ in range(2)] for _ in range(2)]
    P0 = [ph.sb([128, 128], BF16, "P0") for _ in range(2)]
    TT = [[ph.sb([128, 128], BF16, "TT") for _ in range(2)] for _ in range(2)]
    AkV = [ph.sb([128, 64], BF16, "AkV") for _ in range(2)]
    WTsb = [ph.sb([128, 128], BF16, "WTsb") for _ in range(2)]
    Uloc = [ph.sb([128, 64], F32, "Uloc") for _ in range(2)]
    Usb = [ph.sb([128, 64], BF16, "Usb") for _ in range(2)]
    ynorm = [ph.sb([128, 128], BF16, "ynorm") for _ in range(2)]
    gst = [ph.sb([128, 6], F32, "gst") for _ in range(2)]
    gmv = [ph.sb([128, 2], F32, "gmv") for _ in range(2)]
    grs = [ph.sb([128, 1], F32, "grs") for _ in range(2)]
    gnm = [ph.sb([128, 1], F32, "gnm") for _ in range(2)]
    fin = [ph.sb([128, 128], F32, "fin") for _ in range(2)]
    PA = [ph.ps[0], ph.ps[1]]
    TRB = [ph.ps[2], ph.ps[3]]
    kb_h = kb.psum_h

    def subtiles(bi, bounds):
        return [ph.ps[bi] for a, b in bounds]
    W1 = [subtiles(4 + e, [(0, 64), (64, 128), (128, 192), (192, 256), (256, 320), (320, 384), (384, 512)])
          for e in range(2)]
    W2 = [subtiles(6 + e, [(0, 128), (128, 384), (384, 512)]) for e in range(2)]
    pai = [0]

    def nextpa():
        b = PA[pai[0] % 2]
        pai[0] += 1
        return b

    def c3(ap):
        return ap.rearrange("p (c t) -> p c t", c=4)

    ui = [0]
    for tb in range(NTB):
        tcols = slice(tb * 512, (tb + 1) * 512)
        ph.dma("sp", XWA[:, :], dr["prT"][1536:1664, tcols], [], [XWA])
        ph.dma("pool", XG1[:, :], dr["prT"][1664:1792, tcols], [], [XG1])
        ph.dma("pool", XG2[:, :], dr["prT"][1792:1856, tcols], [], [XG2])
        ph.act(TH[:, :], XWA[0:64, :], AF.Tanh, [XWA], [TH])
        ph.act(SG1[:, :], XG1[:, :], AF.Sigmoid, [XG1], [SG1])
        ph.act(SG2[:, :], XG2[0:32, :], AF.Sigmoid, [XG2], [SG2])
        for hp in range(4):
            s = sets[(tb * 4 + hp) % NSET]
            R, Kt, Vt, VF, cum, epos, AR, BT, KT, BH, KH, VB, tok, gsb, bon, outT = (
                s[k] for k in ("R", "Kt", "Vt", "VF", "cum", "epos", "AR", "BT", "KT", "BH", "KH", "VB", "tok",
                               "gsb", "bon", "outT"))
            hc = slice(hp * 128, (hp + 1) * 128)
            P = lambda i: prm[:, i, hp:hp + 1]
            ph.dma("sp", R[:, :], dr["prT"][hp * 128:(hp + 1) * 128, tcols], [], [R])
            ph.dma("sp", Kt[:, :], dr["prT"][512 + hp * 128:512 + (hp + 1) * 128, tcols], [], [Kt])
            ph.dma("pool", Vt[:, :], dr["prT"][1024 + hp * 128:1024 + (hp + 1) * 128, tcols], [], [Vt])
            if l > 0:
                ph.dma("pool", VF[:, :], dr["vfT"][hp * 128:(hp + 1) * 128, tcols], [], [VF])
            pa = nextpa()
            ph.mm(pa[:, :], WA[0:64, hc], TH[:, :], True, True, [WA, TH], [pa])
            ph.act(sgw[:, :], pa[:, :], AF.Sigmoid, [pa, prm], [sgw], bias=P(W0))
            ph.ts("dve", lw[:, :], sgw[:, :], DECAY_C, None, ALU.mult, None, [sgw], [lw])
            ph.add("dve", lambda e, cum=cum: e.tensor_tensor_scan(cum[:, :], cm01[:, :], lw[:, :], 0.0, ALU.mult, ALU.add),
                   [cm01, lw], [cum])
            ph.act(epos[:, :], cum[:, :], AF.Exp, [cum], [epos])
            ph.tt("pool", tmpc[:, :], cum[:, :], lw[:, :], ALU.subtract, [cum, lw], [tmpc])
            ph.act(eprev[:, :], tmpc[:, :], AF.Exp, [tmpc], [eprev])
            ph.act(eneg[:, :], cum[:, :], AF.Exp, [cum], [eneg], scale=-1.0)
            for c in range(4):
                ph.act(edec[:, c * 128:(c + 1) * 128], cum[:, c * 128:(c + 1) * 128], AF.Exp, [cum], [edec],
                       scale=-1.0, bias=cum[:, c * 128 + 127:c * 128 + 128])
            pa = nextpa()
            ph.mm(pa[:, :], WA[64:128, hc], XWA[64:128, :], True, True, [WA, XWA], [pa])
            ph.act(a_[:, :], pa[:, :], AF.Sigmoid, [pa, prm], [a_], bias=P(A0))
            ph.ts("dve", kk[:, :], Kt[:, :], P(KK_), None, ALU.mult, None, [Kt, prm], [kk])
            ph.tt("pool", kk2[:, :], kk[:, :], kk[:, :], ALU.mult, [kk], [kk2])
            pa = nextpa()
            ph.mm(pa[:, :], bones[:, :], kk2[:, :], True, True, [bones, kk2], [pa])
            ph.ts("dve", ssm[:, :], pa[:, :], 1e-24, None, ALU.max, None, [pa], [ssm])
            ph.act(rn[:, :], ssm[:, :], AF.Ln, [ssm], [rn])
            ph.act(rn[:, :], rn[:, :], AF.Exp, [rn], [rn], scale=-0.5)
            ph.tt("dve", kkn[:, :], kk[:, :], rn[:, :], ALU.mult, [kk, rn], [kkn])
            ph.ts("pool", tq[:, :], a_[:, :], P(KA_), omk[:, hp:hp + 1], ALU.mult, ALU.add, [a_, prm, omk], [tq])
            ph.tt("pool", kp[:, :], tq[:, :], Kt[:, :], ALU.mult, [tq, Kt], [kp])
            ph.stt("dve", AR[:, :, 0, :], c3(kkn[:, :]), -1.0, c3(eprev[:, :]), ALU.mult, ALU.mult, [kkn, eprev], [AR])
            ph.tt("pool", AR[:, :, 1, :], c3(R[:, :]), c3(epos[:, :]), ALU.mult, [R, epos], [AR])
            ph.tt("pool", bbt[:, :], kkn[:, :], a_[:, :], ALU.mult, [kkn, a_], [bbt])
            ph.tt("dve", BT[:, :], bbt[:, :], eneg[:, :], ALU.mult, [bbt, eneg], [BT])
            ph.tt("pool", BH[:, :], bbt[:, :], edec[:, :], ALU.mult, [bbt, edec], [BH])
            ph.tt("dve", KT[:, :], kp[:, :], eneg[:, :], ALU.mult, [kp, eneg], [KT])
            ph.tt("pool", KH[:, :], kp[:, :], edec[:, :], ALU.mult, [kp, edec], [KH])
            if l > 0:
                pa = nextpa()
                ph.mm(pa[:, :], WGV[32:64, hc], XG2[32:64, :], True, True, [WGV, XG2], [pa])
                ph.act(gate[:, :], pa[:, :], AF.Sigmoid, [pa, prm], [gate], bias=P(V0_))
                ph.tt("pool", vp[:, :], VF[:, :], Vt[:, :], ALU.subtract, [VF, Vt], [vp])
                ph.tt("dve", vp[:, :], vp[:, :], gate[:, :], ALU.mult, [vp, gate], [vp])
                ph.tt("pool", vp[:, :], vp[:, :], Vt[:, :], ALU.add, [vp, Vt], [vp])
                vsrc = vp
            else:
                vsrc = Vt
            ph.cp("act", VB[:, :], vsrc[:, :], [vsrc], [VB])
            pa = nextpa()
            ph.mm(pa[:, :], WG1[:, hc], SG1[:, :], True, False, [WG1, SG1], [pa])
            ph.mm(pa[:, :], WGV[0:32, hc], SG2[:, :], False, True, [WGV, SG2], [pa])
            ph.cp("act", gsb[:, :], pa[:, :], [pa], [gsb])
            ph.ts("pool", pr_[:, :], R[:, :], P(RK_), None, ALU.mult, None, [R, prm], [pr_])
            ph.tt("pool", pr_[:, :], pr_[:, :], kp[:, :], ALU.mult, [pr_, kp], [pr_])
            pa = nextpa()
            ph.mm(pa[:, :], bones[:, :], pr_[:, :], True, True, [bones, pr_], [pa])
            ph.tt("dve", bon[:, :], pa[:, :], vsrc[:, :], ALU.mult, [pa, vsrc], [bon])
            for c in range(4):
                trb = TRB[c // 2]
                tv = trb.ap[:, :].bitcast(BF16)
                base = (c % 2) * 512
                cs = slice(c * 128, (c + 1) * 128)
                for qi, (src, sap) in enumerate(((AR, AR[:, c, 0, :]), (BH, BH[:, cs]), (KH, KH[:, cs]), (VB, VB[:, cs]))):
                    ph.tr(tv[:, base + qi * 128:base + (qi + 1) * 128], sap, identbf[:, :], [src, identbf], [trb])
                ev = tv[:, base:base + 512].rearrange("p (q t) -> p q t", q=4)
                if c % 2 == 0:
                    ph.cp("dve", tok[c][:, :, :], ev, [trb], [tok[c]])
                else:
                    ph.cp("act", tok[c][:, :, :], ev, [trb], [tok[c]])
            for c in range(4):
                cs = slice(c * 128, (c + 1) * 128)
                gl = c * 128 + 127
                for e in range(2):
                    rows = slice(e * 64, (e + 1) * 64)
                    w1, w2 = W1[e], W2[e]
                    w1h, w2h = kb_h[4 + e], kb_h[6 + e]
                    m4 = M4[ui[0] % 4]
                    ui[0] += 1
                    ph.mm(w1h[:, 0:256], BT[rows, cs], AR[rows, c, :, :].rearrange("p a t -> p (a t)"), True, True,
                          [BT, AR], [w1[0]])
                    ph.mm(w1h[:, 256:512], KT[rows, cs], AR[rows, c, :, :].rearrange("p a t -> p (a t)"), True, True,
                          [KT, AR], [w1[0]])
                    ph.mm(w2h[:, 0:128], AR[rows, c, 0, :], BT[rows, cs], True, True, [AR, BT], [w2[0]])
                    ph.tt("dve", m4[:, :], w1h[:, 0:512], mask4[:, :], ALU.mult, [w1[0], mask4], [m4])
                    ph.tt("dve", P0[e][:, :], w2h[:, 0:128], maskSL[:, :], ALU.mult, [w2[0], maskSL], [P0[e]])
                    ph.tt("pool", TT[e][0][:, :], m4[:, 0:128], identbf[:, :], ALU.add, [m4, identbf], [TT[e][0]])
                for k in range(6):
                    for e in range(2):
                        w2, w2h = W2[e], kb_h[6 + e]
                        m4 = M4[(ui[0] - 2 + e) % 4]
                        if k == 0:
                            Pk, Qk, Pt_, Qt_ = P0[e][:, :], m4[:, 0:128], P0[e], m4
                        else:
                            pq = PQ[e][k % 2]
                            Pk, Qk, Pt_, Qt_ = pq[:, 0:128], pq[:, 128:256], pq, pq
                        pqn = PQ[e][(k + 1) % 2]
                        ph.mm(w2h[:, 128:256], Qk, Pk, True, True, [Pt_, Qt_], [w2[1]])
                        if k < 5:
                            ph.mm(w2h[:, 256:384], Pk, Qk, True, True, [Pt_, Qt_], [w2[1]])
                            ph.cp("act", pqn[:, 0:256], w2h[:, 128:384], [w2[1]], [pqn])
                        else:
                            ph.cp("act", pqn[:, 0:128], w2h[:, 128:256], [w2[1]], [pqn])
                        ph.mm(w2h[:, 384:512], pqn[:, 0:128], TT[e][k % 2][:, :], True, True, [pqn, TT[e][k % 2]], [w2[2]])
                        ph.tt("dve", TT[e][(k + 1) % 2][:, :], TT[e][k % 2][:, :], w2h[:, 384:512], ALU.add,
                              [TT[e][k % 2], w2[2]], [TT[e][(k + 1) % 2]])
                for e in range(2):
                    rows = slice(e * 64, (e + 1) * 64)
                    w1, w1h = W1[e], kb_h[4 + e]
                    m4 = M4[(ui[0] - 2 + e) % 4]
                    TTf = TT[e][0]
                    Vtok = tok[c][:, 3, rows]
                    ph.mm(w1h[:, 0:64], m4[:, 256:384], Vtok, True, True, [m4, tok[c]], [w1[0]])
                    ph.cp("act", AkV[e][:, :], w1h[:, 0:64], [w1[0]], [AkV[e]])
                    ph.mm(w1h[:, 384:512], tok[c][:, 0, :], TTf[:, :], True, True, [tok[c], TTf], [w1[6]])
                    ph.cp("act", WTsb[e][rows, :], w1h[rows, 384:512], [w1[6]], [WTsb[e]])
                    ph.mm(w1h[:, 64:128], TTf[:, :], AkV[e][:, :], True, True, [TTf, AkV[e]], [w1[1]])
                    ph.cp("act", Uloc[e][:, :], w1h[:, 64:128], [w1[1]], [Uloc[e]])
                    ph.mm(w1h[:, 128:192], WTsb[e][rows, :], Hb[hp][rows, :], True, True, [WTsb[e], Hb[hp]], [w1[2]])
                    ph.tt("dve", Usb[e][:, :], w1h[:, 128:192], Uloc[e][:, :], ALU.add, [w1[2], Uloc[e]], [Usb[e]])
                    ph.mm(w1h[:, 256:320], AR[rows, c, 1, :], Hb[hp][rows, :], True, False, [AR, Hb[hp]], [w1[4]])
                    ph.mm(w1h[:, 256:320], m4[:, 128:256], Usb[e][:, :], False, False, [m4, Usb[e]], [w1[4]])
                    ph.mm(w1h[:, 256:320], m4[:, 384:512], Vtok, False, True, [m4, tok[c]], [w1[4]])
                    ph.mm(w1h[:, 192:256], tok[c][:, 1, :], Usb[e][:, :], True, False, [tok[c], Usb[e]], [w1[3]])
                    ph.mm(w1h[:, 192:256], tok[c][:, 2, :], Vtok, False, True, [tok[c]], [w1[3]])
                    ph.stt("dve", Hs[hp][rows, :], Hs[hp][rows, :], epos[rows, gl:gl + 1], w1h[rows, 192:256],
                           ALU.mult, ALU.add, [Hs[hp], epos, w1[3]], [Hs[hp]])
                    ph.cp("act", Hb[hp][rows, :], Hs[hp][rows, :], [Hs[hp]], [Hb[hp]])
                    yn = ynorm[c % 2]
                    ph.add("dve", lambda en, e=e, w1h=w1h: en.bn_stats(gst[e][:, 0:6], w1h[:, 256:320]), [w1[4]], [gst[e]])
                    ph.add("dve", lambda en, e=e: en.bn_aggr(gmv[e][:, 0:2], gst[e][:, 0:6]), [gst[e]], [gmv[e]])
                    ph.act(grs[e][:, 0:1], gmv[e][:, 1:2], AF.Ln, [gmv[e]], [grs[e]], bias=epsG[:, 0:1])
                    ph.act(grs[e][:, 0:1], grs[e][:, 0:1], AF.Exp, [grs[e]], [grs[e]], scale=-0.5)
                    ph.ts("dve", gnm[e][:, 0:1], gmv[e][:, 0:1], grs[e][:, 0:1], -1.0, ALU.mult, ALU.mult,
                          [gmv[e], grs[e]], [gnm[e]])
                    ph.act(yn[:, e * 64:(e + 1) * 64], w1h[:, 256:320], AF.Identity, [w1[4], grs[e], gnm[e]], [yn],
                           bias=gnm[e][:, 0:1], scale=grs[e][:, 0:1])
                yn = ynorm[c % 2]
                trb = TRB[1]
                tv = trb.ap[:, :].bitcast(BF16)
                ph.tr(tv[:, 0:128], yn[:, :], identbf[:, :], [yn, identbf], [trb])
                fn = fin[c % 2]
                ph.ts("dve", fn[:, :], tv[:, 0:128], P(GG_), P(GB_), ALU.mult, ALU.add, [trb, prm], [fn])
                ph.tt("pool", fn[:, :], fn[:, :], bon[:, cs], ALU.add, [fn, bon], [fn])
                ph.tt("pool", outT[:, cs], fn[:, :], gsb[:, cs], ALU.mult, [fn, gsb], [outT])
            ph.dma("sp", dr["mixT"][512 + hp * 128:512 + (hp + 1) * 128, tcols], outT[:, :], [outT], [])
    ph.finish()


ALPHA = (2 * 2) ** 0.25


def load_cast_weight(ph, w_rows_fn, nrows_tiles, ncols, wbf_fn, stage, q="act"):
    pc = stage[0].ap.shape[1]
    k = 0
    for c0 in range(0, ncols, pc):
        for r in range(nrows_tiles):
            n = min(pc, ncols - c0)
            s = stage[k % len(stage)]
            ph.dma(q if k % 2 == 0 else "sp", s[:, 0:n], w_rows_fn(r)[:, c0:c0 + n], [], [s])
            ph.cp(("dve", "pool", "act")[k % 3], wbf_fn(r)[:, c0:c0 + n], s[:, 0:n], [s], [wbf_fn(r)])
            k += 1


class WPieces:
    def __init__(self, ph, w_rows_fn, nrows_tiles, ncols, piece, nstage=4, name="w"):
        self.piece = piece
        self.t = {}
        stage = [ph.sb([128, piece], F32, "%sstg%d" % (name, k)) for k in range(nstage)]
        npc = (ncols + piece - 1) // piece
        for r in range(nrows_tiles):
            for pi in range(npc):
                n = min(piece, ncols - pi * piece)
                self.t[(r, pi)] = ph.sb([128, n], BF16, "%s%d_%d" % (name, r, pi))
        k = 0
        for pi in range(npc):
            n = min(piece, ncols - pi * piece)
            for r in range(nrows_tiles):
                s = stage[k % nstage]
                ph.dma(("act", "sp")[k % 2], s[:, 0:n], w_rows_fn(r)[:, pi * piece:pi * piece + n], [], [s])
                ph.cp(("dve", "pool", "act")[k % 3], self.t[(r, pi)][:, 0:n], s[:, 0:n], [s], [self.t[(r, pi)]])
                k += 1

    def get(self, r, c0, n):
        pi = c0 // self.piece
        o = c0 - pi * self.piece
        assert o + n <= self.piece or (pi, o) == (pi, 0), (c0, n)
        t = self.t[(r, pi)]
        return t, t[:, o:o + n]


def phase_outproj(kb, l):
    ph = PH(kb, "op%d" % l)
    S, NT = kb.S, kb.NT
    dr = kb.dram
    c = load_consts(ph)
    gB = ph.sb([128, 1024], F32, "gB")
    bB = ph.sb([128, 1024], F32, "bB")
    ph.dma("sp", gB[:, :], dr["ln_mix_g"][l, :].partition_broadcast(128), [], [gB])
    ph.dma("sp", bB[:, :], dr["ln_mix_b"][l, :].partition_broadcast(128), [], [bB])
    wbf = [ph.sb([128, 1024], BF16, "wo%d" % r) for r in range(8)]
    stage = [ph.sb([128, 1024], F32, "stg%d" % k) for k in range(2)]
    load_cast_weight(ph, lambda r: dr["w_out"][l, r * 128:(r + 1) * 128, :], 8, 1024, lambda r: wbf[r], stage)
    mx = [ph.sb([128, 8, 512], BF16, "mx%d" % k) for k in range(2)]
    xr = [ph.sb([128, 1024], F32, "xr%d" % k) for k in range(3)]
    srcs = [ph.sb([128, 1024], F32, "src%d" % k) for k in range(3)]
    bufs = [ln_bufs(ph, k) for k in range(4)]
    for g in range(S // 512):
        m = mx[g % 2]
        ph.dma("pool", m[:, :, :], dr["mixT"][:, g * 512:(g + 1) * 512].rearrange("(c p) t -> p c t", p=128), [], [m])
        for j in range(4):
            i = g * 4 + j
            x_ = xr[i % 3]
            ph.dma("sp", x_[:, :], dr["xres"][i * 128:(i + 1) * 128, :], [], [x_])
            src = srcs[i % 3]
            for hh in range(2):
                pb = ph.ps[(i % 2) * 2 + hh]
                for cc in range(8):
                    ph.mm(pb[:, :], m[:, cc, j * 128:(j + 1) * 128], wbf[cc][:, hh * 512:(hh + 1) * 512],
                          cc == 0, cc == 7, [m, wbf[cc]], [pb])
                ph.stt("dve", src[:, hh * 512:(hh + 1) * 512], x_[:, hh * 512:(hh + 1) * 512], ALPHA, pb[:, :],
                       ALU.mult, ALU.add, [x_, pb], [src])
            ln_rows(ph, src, i, gB, bB, c["ident_bf"], "xres", "xT", bufs[i % 4], ph.ps[4 + i % 4])
    ph.finish()


def phase_ffn_up(kb, l):
    ph = PH(kb, "fu%d" % l)
    S, NT = kb.S, kb.NT
    dr = kb.dram
    wbf = [ph.sb([128, 4096], BF16, "wu%d" % r) for r in range(8)]
    stage = [ph.sb([128, 1024], F32, "stg%d" % k) for k in range(4)]
    load_cast_weight(ph, lambda r: dr["w_up"][l, r * 128:(r + 1) * 128, :], 8, 4096, lambda r: wbf[r], stage)
    xg = [ph.sb([128, 8, 512], BF16, "xg%d" % k) for k in range(2)]
    tmp = [ph.sb([128, 512], F32, "tmp%d" % k) for k in range(3)]
    ho = [ph.sb([128, 512], BF16, "ho%d" % k) for k in range(4)]
    k = 0
    for g in range(S // 512):
        x_ = xg[g % 2]
        ph.dma("pool", x_[:, :, :], dr["xT"][:, g * 512:(g + 1) * 512].rearrange("(c p) t -> p c t", p=128), [], [x_])
        for f in range(32):
            pb = ph.ps[k % 6]
            for cc in range(8):
                ph.mm(pb[:, :], wbf[cc][:, f * 128:(f + 1) * 128], x_[:, cc, :], cc == 0, cc == 7, [wbf[cc], x_], [pb])
            t = tmp[k % 3]
            h = ho[k % 4]
            ph.act(t[:, :], pb[:, :], AF.Relu, [pb], [t])
            ph.tt(("pool", "dve")[k % 2], h[:, :], t[:, :], t[:, :], ALU.mult, [t], [h])
            ph.dma("sp", dr["hT"][f * 128:(f + 1) * 128, g * 512:(g + 1) * 512], h[:, :], [h], [])
            k += 1
    ph.finish()


def phase_ffn_down(kb, l, last):
    ph = PH(kb, "fd%d" % l)
    S, NT = kb.S, kb.NT
    dr = kb.dram
    c = load_consts(ph)
    gB = ph.sb([128, 1024], F32, "gB")
    bB = ph.sb([128, 1024], F32, "bB")
    ph.dma("sp", gB[:, :], dr["ln_ffn_g"][l, :].partition_broadcast(128), [], [gB])
    ph.dma("sp", bB[:, :], dr["ln_ffn_b"][l, :].partition_broadcast(128), [], [bB])
    WD = WPieces(ph, lambda r: dr["w_down"][l, r * 128:(r + 1) * 128, :], 32, 1024, 512, nstage=6, name="wd")
    hg = [ph.sb([128, 32, 512], BF16, "hg%d" % k) for k in range(2)]
    xr = [ph.sb([128, 1024], F32, "xr%d" % k) for k in range(3)]
    srcs = [ph.sb([128, 1024], F32, "src%d" % k) for k in range(3)]
    bufs = [ln_bufs(ph, k) for k in range(4)]
    def load_h(g):
        h_ = hg[g % 2]
        for q4 in range(4):
            ph.dma(("pool", "act")[q4 % 2], h_[:, q4 * 8:(q4 + 1) * 8, :],
                   dr["hT"][q4 * 1024:(q4 + 1) * 1024, g * 512:(g + 1) * 512].rearrange("(f p) t -> p f t", p=128),
                   [], [h_])
    load_h(0)
    for g in range(S // 512):
        h_ = hg[g % 2]
        if g + 1 < S // 512:
            load_h(g + 1)
        for j in range(4):
            i = g * 4 + j
            x_ = xr[i % 3]
            ph.dma("sp", x_[:, :], dr["xres"][i * 128:(i + 1) * 128, :], [], [x_])
            src = srcs[i % 3]
            for hh in range(2):
                pb = ph.ps[(i % 2) * 2 + hh]
                for f in range(32):
                    wt, wap = WD.get(f, hh * 512, 512)
                    ph.mm(pb[:, :], h_[:, f, j * 128:(j + 1) * 128], wap, f == 0, f == 31, [h_, wt], [pb])
                ph.stt("dve", src[:, hh * 512:(hh + 1) * 512], x_[:, hh * 512:(hh + 1) * 512], ALPHA, pb[:, :],
                       ALU.mult, ALU.add, [x_, pb], [src])
            if last:
                ln_rows(ph, src, i, gB, bB, c["ident_bf"], "out", None, bufs[i % 4], ph.ps[4 + i % 4])
            else:
                ln_rows(ph, src, i, gB, bB, c["ident_bf"], "xres", "xT", bufs[i % 4], ph.ps[4 + i % 4])
    ph.finish()


CONST_SHAPES = {"c_ident": [128, 128], "c_J": [128, 128], "c_causal": [128, 128], "c_oh": [32, 385],
                "c_mask4": [128, 512], "c_maskSL": [128, 128], "c_bones": [128, 128], "c_cm01": [128, 512]}


def build_program(S, debug=(), only=None, ext_in=()):
    kb = KB(S)
    kb.ext_in = tuple(ext_in)
    kb.din("x", [S, 1024])
    kb.din("ln_in_g", [1024])
    kb.din("ln_in_b", [1024])
    kb.din("w_in0", [1024, 3360])
    kb.din("w_in1", [1024, 3392])
    kb.din("mu0", [128, 15])
    kb.din("mu1", [128, 15])
    kb.din("rel_bias", [32, 4])
    for nm in ("lambda_q1", "lambda_k1", "lambda_q2", "lambda_k2"):
        kb.din(nm, [2, 64])
    kb.din("subln_g", [2, 128])
    kb.din("rwp0", [128, 8, 4])
    kb.din("rwp1", [128, 8, 4])
    kb.din("rw_w_up", [2, 64, 512])
    kb.din("rw_a_up", [2, 64, 512])
    kb.din("rw_g_up", [2, 160, 512])
    kb.din("rw_v_up", [1, 32, 512])
    kb.din("w_out", [2, 1024, 1024])
    kb.din("ln_mix_g", [2, 1024])
    kb.din("ln_mix_b", [2, 1024])
    kb.din("w_up", [2, 1024, 4096])
    kb.din("w_down", [2, 4096, 1024])
    kb.din("ln_ffn_g", [2, 1024])
    kb.din("ln_ffn_b", [2, 1024])
    for nm, shp in CONST_SHAPES.items():
        kb.din(nm, shp)
    dbg = lambda n: n in debug
    kb.dscr("xres", [S, 1024], F32, dbg("xres"))
    kb.dscr("xT", [1024, S], BF16, dbg("xT"))
    kb.dscr("qT", [4, 128, S], BF16, dbg("qT"))
    kb.dscr("kT", [4, 128, S], BF16, dbg("kT"))
    kb.dscr("vd", [S, 512], BF16, dbg("vd"))
    kb.dscr("prT", [1920, S], F32, dbg("prT"))
    kb.dscr("vfT", [512, S], F32, dbg("vfT"))
    kb.dscr("lamneg", [1, 2], F32, dbg("lamneg"))
    kb.dscr("EGd", [4, 384], F32, dbg("EGd"))
    kb.dscr("Eb", [4, 2, 128, 128], F32, dbg("Eb"))
    kb.dscr("mixT", [1024, S], BF16, dbg("mixT"))
    kb.dscr("hT", [4096, S], BF16, dbg("hT"))
    kb.dout("out", [S, 1024], F32)
    on = lambda n: only is None or n in only
    if on("pre"):
        phase_pre(kb)
    if on("ln0"):
        phase_ln0(kb)
    for l in range(2):
        if on("ip%d" % l):
            phase_inproj(kb, l)
        if on("at%d" % l):
            phase_attn(kb, l)
        if on("rw%d" % l):
            phase_rwkv2(kb, l)
        if on("of%d" % l):
            phase_opfu(kb, l)
        if on("fd%d" % l):
            phase_ffn_down(kb, l, last=(l == 1))
    kb.close()
    return kb


def host_inputs(inp, S):
    f = lambda a: np.ascontiguousarray(np.asarray(a, dtype=np.float32))
    m = {}
    m["ln_in_g"] = f(inp["ln_in_g"])
    m["ln_in_b"] = f(inp["ln_in_b"])
    m["w_in0"] = f(inp["w_in_first"])
    m["w_in1"] = f(inp["w_in_rest"][0])
    for l, mu in enumerate((inp["mu_first"], inp["mu_rest"][0])):
        mp = np.zeros(1920, np.float32)
        mp[:mu.shape[0]] = mu
        m["mu%d" % l] = np.ascontiguousarray(mp.reshape(15, 128).T)
    for nm in ("rel_bias", "lambda_q1", "lambda_k1", "lambda_q2", "lambda_k2", "subln_g", "rw_w_up", "rw_a_up",
               "rw_g_up", "rw_v_up", "w_out", "ln_mix_g", "ln_mix_b", "w_up", "w_down", "ln_ffn_g", "ln_ffn_b"):
        m[nm] = f(inp[nm])
    m["rwp0"] = host_rwp(inp, 0)
    m["rwp1"] = host_rwp(inp, 1)
    m.update(host_consts())
    return m


_PROG = {}


def kernel(**inputs):
    x = np.asarray(inputs["x"], dtype=np.float32)
    B, S, _ = x.shape
    if S not in _PROG:
        _PROG[S] = build_program(S)
    kb = _PROG[S]
    shared = host_inputs(inputs, S)
    in_maps = []
    for b in range(B):
        m = dict(shared)
        m["x"] = np.ascontiguousarray(x[b])
        in_maps.append(m)
    res = run_bass_kernel_spmd(kb.nc, in_maps, core_ids=list(range(B)))
    return np.stack([np.asarray(r["out"], dtype=np.float32) for r in res.results], axis=0)


def phase_rwkv2(kb, l):
    ph = PH(kb, "rw%d" % l)
    S, NT = kb.S, kb.NT
    NTB = S // 512
    dr = kb.dram
    cst = load_consts(ph)
    identbf = cst["ident_bf"]
    mask4 = ph.sb([128, 512], F32, "mask4")
    maskSL = ph.sb([128, 128], F32, "maskSL")
    bones = ph.sb([128, 128], F32, "bones")
    cm01 = ph.sb([128, 512], F32, "cm01")
    ph.dma("sp", mask4[:, :], dr["c_mask4"][:, :], [], [mask4])
    ph.dma("sp", maskSL[:, :], dr["c_maskSL"][:, :], [], [maskSL])
    ph.dma("sp", bones[:, :], dr["c_bones"][:, :], [], [bones])
    ph.dma("sp", cm01[:, :], dr["c_cm01"][:, :], [], [cm01])
    prm = ph.sb([128, 8, 4], F32, "prm")
    ph.dma("sp", prm[:, :, :], dr["rwp%d" % l][:, :, :], [], [prm])
    W0, A0, KK_, KA_, RK_, GG_, GB_, V0_ = range(8)
    WA = ph.sb([128, 512], F32, "WA")
    WG1 = ph.sb([128, 512], F32, "WG1")
    WGV = ph.sb([64, 512], F32, "WGV")
    ph.dma("sp", WA[0:64, :], dr["rw_w_up"][l, :, :], [], [WA])
    ph.dma("sp", WA[64:128, :], dr["rw_a_up"][l, :, :], [], [WA])
    ph.dma("sp", WG1[:, :], dr["rw_g_up"][l, 0:128, :], [], [WG1])
    ph.dma("sp", WGV[0:32, :], dr["rw_g_up"][l, 128:160, :], [], [WGV])
    if l > 0:
        ph.dma("sp", WGV[32:64, :], dr["rw_v_up"][l - 1, :, :], [], [WGV])
    omk = ph.sb([128, 4], F32, "omk")
    ph.ts("dve", omk[:, :], prm[:, KA_, :], -1.0, 1.0, ALU.mult, ALU.add, [prm], [omk])
    epsG = ph.sb([128, 1], F32, "epsG")
    ph.add("pool", lambda e: e.memset(epsG[:, :], GN_EPS), [], [epsG])
    Hb = [ph.sb([128, 64], BF16, "Hb%d" % hp) for hp in range(4)]
    Hs = [ph.sb([128, 64], F32, "Hs%d" % hp) for hp in range(4)]
    for hp in range(4):
        ph.add("pool", lambda e, hp=hp: e.memset(Hs[hp][:, :], 0.0), [], [Hs[hp]])
        ph.add("pool", lambda e, hp=hp: e.memset(Hb[hp][:, :], 0.0), [], [Hb[hp]])
    XWA = ph.sb([128, 512], F32, "XWA")
    XG1 = ph.sb([128, 512], F32, "XG1")
    XG2 = ph.sb([64, 512], F32, "XG2")
    TH = ph.sb([64, 512], F32, "TH")
    SG1 = ph.sb([128, 512], F32, "SG1")
    SG2 = ph.sb([32, 512], F32, "SG2")

    def f32t(n):
        return ph.sb([128, 512], F32, n)

    def bft(n):
        return ph.sb([128, 512], BF16, n)
    pers = []
    for hp in range(4):
        pers.append(dict(gsb=f32t("gsb"), bon=f32t("bon"), AR=ph.sb([128, 4, 2, 128], BF16, "AR"),
                         BT=bft("BT"), KT=bft("KT"), tok=[ph.sb([128, 4, 128], BF16, "tok") for _ in range(4)],
                         outT=bft("outT"), gam=ph.sb([128, 4], F32, "gam"),
                         keepM=[ph.sb([128, 256], BF16, "keepM") for _ in range(8)],
                         Uloc=[ph.sb([128, 64], F32, "Uloc") for _ in range(8)],
                         WTsb=[ph.sb([128, 128], BF16, "WTsb") for _ in range(4)]))
    RIN = [dict(R=f32t("R"), Kt=f32t("Kt"), Vt=f32t("Vt"), VF=f32t("VF") if l > 0 else None) for _ in range(2)]
    TMP = [[f32t("tmp%d" % i) for i in range(10)] for _ in range(2)]
    TB16 = [(bft("BH"), bft("KH"), bft("VB")) for _ in range(2)]
    M4 = [ph.sb([128, 512], BF16, "M4") for _ in range(8)]
    P0 = [ph.sb([128, 128], BF16, "P0") for _ in range(8)]
    XK = [[ph.sb([128, 384], BF16, "XK") for _ in range(2)] for _ in range(8)]
    AkV = [ph.sb([128, 64], BF16, "AkV") for _ in range(8)]
    Usb = [ph.sb([128, 64], BF16, "Usb") for _ in range(8)]
    ynorm = [ph.sb([128, 128], BF16, "ynorm") for _ in range(4)]
    ynall = ph.sb([128, 512], BF16, "ynall")
    gsq = ph.sb([128, 512], F32, "gsq")
    gs1 = ph.sb([128, 8], F32, "gs1")
    gs2 = ph.sb([128, 8], F32, "gs2")
    gm2 = ph.sb([128, 8], F32, "gm2")
    gst = [ph.sb([128, 6], F32, "gst") for _ in range(8)]
    gmv = [ph.sb([128, 2], F32, "gmv") for _ in range(8)]
    grs = [ph.sb([128, 1], F32, "grs") for _ in range(8)]
    gnm = [ph.sb([128, 1], F32, "gnm") for _ in range(8)]
    fin = [ph.sb([128, 128], F32, "fin") for _ in range(4)]
    kb_h = kb.psum_h
    PA = [ph.ps[0], ph.ps[1], ph.ps[2], ph.ps[3]]
    pai = [0]

    def nextpa():
        b = PA[pai[0] % 4]
        pai[0] += 1
        return b

    def c3(ap):
        return ap.rearrange("p (c t) -> p c t", c=4)
    evi = [0]

    import os
    EV = os.environ.get("RW2_EV", "")

    def evac_copy(out_ap, in_ap, reads, writes):
        evi[0] += 1
        eng = ("act", "dve")[evi[0] % 2]
        if EV:
            eng = EV
        ph.cp(eng, out_ap, in_ap, reads, writes)

    def load_shared(tb):
        tcols = slice(tb * 512, (tb + 1) * 512)
        ph.dma("sp", XWA[:, :], dr["prT"][1536:1664, tcols], [], [XWA])
        ph.dma("pool", XG1[:, :], dr["prT"][1664:1792, tcols], [], [XG1])
        ph.dma("pool", XG2[:, :], dr["prT"][1792:1856, tcols], [], [XG2])
        ph.act(TH[:, :], XWA[0:64, :], AF.Tanh, [XWA], [TH])
        ph.act(SG1[:, :], XG1[:, :], AF.Sigmoid, [XG1], [SG1])
        ph.act(SG2[:, :], XG2[0:32, :], AF.Sigmoid, [XG2], [SG2])

    def load_rin(tb, hp):
        tcols = slice(tb * 512, (tb + 1) * 512)
        rin = RIN[hp % 2]
        R, Kt, Vt, VF = rin["R"], rin["Kt"], rin["Vt"], rin["VF"]
        ph.dma("sp", R[:, :], dr["prT"][hp * 128:(hp + 1) * 128, tcols], [], [R])
        ph.dma("sp", Kt[:, :], dr["prT"][512 + hp * 128:512 + (hp + 1) * 128, tcols], [], [Kt])
        ph.dma("pool", Vt[:, :], dr["prT"][1024 + hp * 128:1024 + (hp + 1) * 128, tcols], [], [Vt])
        if l > 0:
            ph.dma("pool", VF[:, :], dr["vfT"][hp * 128:(hp + 1) * 128, tcols], [], [VF])

    load_shared(0)
    load_rin(0, 0)
    load_rin(0, 1)
    for tb in range(NTB):
        tcols = slice(tb * 512, (tb + 1) * 512)
        def prep_gen(hp):
                s = pers[hp]
                rin = RIN[hp % 2]
                R, Kt, Vt, VF = rin["R"], rin["Kt"], rin["Vt"], rin["VF"]
                gsb, bon, AR, BT, KT, tok, gam = (s[k] for k in ("gsb", "bon", "AR", "BT", "KT", "tok", "gam"))
                hc = slice(hp * 128, (hp + 1) * 128)
                P = lambda i: prm[:, i, hp:hp + 1]
                if hp >= 2:
                    load_rin(tb, hp)
                yield
                sgw, cum, epos, eneg, edec, a_, kk, kkn, kp, bbt = TMP[hp % 2]
                BH, KH, VB = TB16[hp % 2]
                pa = nextpa()
                ph.mm(pa[:, :], WA[0:64, hc], TH[:, :], True, True, [WA, TH], [pa])
                yield
                ph.act(sgw[:, :], pa[:, :], AF.Sigmoid, [pa, prm], [sgw], bias=P(W0))
                yield
                ph.ts("dve", sgw[:, :], sgw[:, :], DECAY_C, None, ALU.mult, None, [sgw], [sgw])
                yield
                ph.add("dve", lambda e, cum=cum, sgw=sgw: e.tensor_tensor_scan(cum[:, :], cm01[:, :], sgw[:, :], 0.0,
                                                                              ALU.mult, ALU.add), [cm01, sgw], [cum])
                yield
                ph.act(epos[:, :], cum[:, :], AF.Exp, [cum], [epos])
                yield
                ph.act(eneg[:, :], cum[:, :], AF.Exp, [cum], [eneg], scale=-1.0)
                yield
                for c in range(4):
                    ph.act(edec[:, c * 128:(c + 1) * 128], cum[:, c * 128:(c + 1) * 128], AF.Exp, [cum], [edec],
                           scale=-1.0, bias=cum[:, c * 128 + 127:c * 128 + 128])
                    yield
                ph.cp("pool", gam[:, :], c3(epos[:, :])[:, :, 127], [epos], [gam])
                yield
                pa = nextpa()
                ph.mm(pa[:, :], WA[64:128, hc], XWA[64:128, :], True, True, [WA, XWA], [pa])
                yield
                ph.act(a_[:, :], pa[:, :], AF.Sigmoid, [pa, prm], [a_], bias=P(A0))
                yield
                ph.act(kk[:, :], Kt[:, :], AF.Square, [Kt, prm], [kk], scale=P(KK_))
                yield
                pa = nextpa()
                ph.mm(pa[:, :], bones[:, :], kk[:, :], True, True, [bones, kk], [pa])
                yield
                ph.ts("dve", kk[:, :], pa[:, :], 1e-24, None, ALU.max, None, [pa], [kk])
                yield
                ph.act(kk[:, :], kk[:, :], AF.Ln, [kk], [kk])
                yield
                ph.act(kk[:, :], kk[:, :], AF.Exp, [kk], [kk], scale=-0.5)
                yield
                ph.stt("dve", kkn[:, :], Kt[:, :], P(KK_), kk[:, :], ALU.mult, ALU.mult, [Kt, prm, kk], [kkn])
                yield
                ph.act(kp[:, :], a_[:, :], AF.Identity, [a_, prm, omk], [kp], scale=P(KA_), bias=omk[:, hp:hp + 1])
                yield
                ph.tt("pool", kp[:, :], kp[:, :], Kt[:, :], ALU.mult, [kp, Kt], [kp])
                yield
                ph.stt("dve", AR[:, :, 0, 1:128], c3(kkn[:, :])[:, :, 1:128], -1.0, c3(epos[:, :])[:, :, 0:127],
                       ALU.mult, ALU.mult, [kkn, epos], [AR])
                yield
                ph.ts("dve", AR[:, :, 0, 0:1], c3(kkn[:, :])[:, :, 0:1], -1.0, None, ALU.mult, None, [kkn], [AR])
                yield
                ph.tt("pool", AR[:, :, 1, :], c3(R[:, :]), c3(epos[:, :]), ALU.mult, [R, epos], [AR])
                yield
                ph.tt("pool", bbt[:, :], kkn[:, :], a_[:, :], ALU.mult, [kkn, a_], [bbt])
                yield
                ph.tt("dve", BT[:, :], bbt[:, :], eneg[:, :], ALU.mult, [bbt, eneg], [BT])
                yield
                ph.tt("pool", BH[:, :], bbt[:, :], edec[:, :], ALU.mult, [bbt, edec], [BH])
                yield
                ph.tt("dve", KT[:, :], kp[:, :], eneg[:, :], ALU.mult, [kp, eneg], [KT])
                yield
                ph.tt("pool", KH[:, :], kp[:, :], edec[:, :], ALU.mult, [kp, edec], [KH])
                yield
                if l > 0:
                    pa = nextpa()
                    ph.mm(pa[:, :], WGV[32:64, hc], XG2[32:64, :], True, True, [WGV, XG2], [pa])
                    yield
                    ph.act(a_[:, :], pa[:, :], AF.Sigmoid, [pa, prm], [a_], bias=P(V0_))
                    yield
                    ph.tt("pool", sgw[:, :], VF[:, :], Vt[:, :], ALU.subtract, [VF, Vt], [sgw])
                    yield
                    ph.tt("dve", sgw[:, :], sgw[:, :], a_[:, :], ALU.mult, [sgw, a_], [sgw])
                    yield
                    ph.tt("pool", sgw[:, :], sgw[:, :], Vt[:, :], ALU.add, [sgw, Vt], [sgw])
                    yield
                    vsrc = sgw
                else:
                    vsrc = Vt
                ph.cp("act", VB[:, :], vsrc[:, :], [vsrc], [VB])
                yield
                pa = nextpa()
                ph.mm(pa[:, :], WG1[:, hc], SG1[:, :], True, False, [WG1, SG1], [pa])
                yield
                ph.mm(pa[:, :], WGV[0:32, hc], SG2[:, :], False, True, [WGV, SG2], [pa])
                yield
                ph.cp("act", gsb[:, :], pa[:, :], [pa], [gsb])
                yield
                ph.stt("dve", bbt[:, :], R[:, :], P(RK_), kp[:, :], ALU.mult, ALU.mult, [R, prm, kp], [bbt])
                yield
                pa = nextpa()
                ph.mm(pa[:, :], bones[:, :], bbt[:, :], True, True, [bones, bbt], [pa])
                yield
                ph.tt("dve", bon[:, :], pa[:, :], vsrc[:, :], ALU.mult, [pa, vsrc], [bon])
                yield
                for c in range(4):
                    trb = ph.ps[6 + c // 2]
                    tv = trb.ap[:, :].bitcast(BF16)
                    base = (c % 2) * 512
                    cs = slice(c * 128, (c + 1) * 128)
                    for qi, (src, sap) in enumerate(((AR, AR[:, c, 0, :]), (BH, BH[:, cs]), (KH, KH[:, cs]), (VB, VB[:, cs]))):
                        ph.tr(tv[:, base + qi * 128:base + (qi + 1) * 128], sap, identbf[:, :], [src, identbf], [trb])
                    ev = tv[:, base:base + 512].rearrange("p (q t) -> p q t", q=4)
                    evac_copy(tok[c][:, :, :], ev, [trb], [tok[c]])
                    yield
        for pair in ((0, 1), (2, 3)):
            gens = [prep_gen(hp) for hp in pair]
            while gens:
                for g_ in list(gens):
                    try:
                        next(g_)
                    except StopIteration:
                        gens.remove(g_)
        if tb + 1 < NTB:
            load_shared(tb + 1)
            load_rin(tb + 1, 0)
            load_rin(tb + 1, 1)
        import os
        STOP = int(os.environ.get("RW2_STOP", "99"))
        if STOP <= 1:
            continue
        for hp in range(4):
            s = pers[hp]
            AR, BT, KT, tok, keepM, Uloc, WTsb = (s[k] for k in ("AR", "BT", "KT", "tok", "keepM", "Uloc", "WTsb"))
            units = [(c, e) for c in range(4) for e in range(2)]
            for sw in range(2):
                for ui4 in range(4):
                    u = sw * 4 + ui4
                    c, e = units[u]
                    rows = slice(e * 64, (e + 1) * 64)
                    cs = slice(c * 128, (c + 1) * 128)
                    gb = ph.ps[4 + ui4]
                    g3 = ph.ps[2 + e]
                    go = (ui4 // 2) * 128
                    arr = AR[rows, c, :, :].rearrange("p a t -> p (a t)")
                    ph.mm(gb[:, 0:256], BT[rows, cs], arr, True, True, [BT, AR], [gb])
                    ph.mm(gb[:, 256:512], KT[rows, cs], arr, True, True, [KT, AR], [gb])
                    ph.mm(g3[:, go:go + 128], AR[rows, c, 0, :], BT[rows, cs], True, True, [AR, BT], [g3])
                for ui4 in range(4):
                    u = sw * 4 + ui4
                    c, e = units[u]
                    gb = ph.ps[4 + ui4]
                    g3 = ph.ps[2 + e]
                    go = (ui4 // 2) * 128
                    ph.tt("dve", M4[u][:, :], gb[:, 0:512], mask4[:, :], ALU.mult, [gb, mask4], [M4[u]])
                    ph.tt("dve", P0[u][:, :], g3[:, go:go + 128], maskSL[:, :], ALU.mult, [g3, maskSL], [P0[u]])
                    ph.tt("pool", XK[u][1][:, 256:384], M4[u][:, 0:128], identbf[:, :], ALU.add, [M4[u], identbf], [XK[u][1]])
                    ph.cp("pool", keepM[u][:, 0:128], M4[u][:, 128:256], [M4[u]], [keepM[u]])
                    ph.cp("pool", keepM[u][:, 128:256], M4[u][:, 384:512], [M4[u]], [keepM[u]])
            if STOP <= 2:
                continue
            for k in range(7):
                for rnd in range(1):
                    us_ = range(8)
                    for u in us_:
                        cb = ph.ps[u]
                        if k == 0:
                            Pk, Qk, Pt_, Qt_ = P0[u][:, :], M4[u][:, 0:128], P0[u], M4[u]
                        else:
                            xk = XK[u][k % 2]
                            Pk, Qk, Pt_, Qt_ = xk[:, 0:128], xk[:, 128:256], xk, xk
                        if k < 6:
                            ph.mm(cb[:, 0:128], Qk, Pk, True, True, [Pt_, Qt_], [cb])
                        if k == 0:
                            ph.mm(cb[:, 128:256], Pk, Qk, True, True, [Pt_, Qt_], [cb])
                        elif k < 5:
                            ph.mm(cb[:, 128:384], Pk, xk[:, 128:384], True, True, [xk], [cb])
                        else:
                            ph.mm(cb[:, 256:384], Pk, xk[:, 256:384], True, True, [xk], [cb])
                    for u in us_:
                        cb = ph.ps[u]
                        xn = XK[u][(k + 1) % 2]
                        if k < 5:
                            ph.cp("act", xn[:, 0:256], cb[:, 0:256], [cb], [xn])
                        elif k == 5:
                            ph.cp("act", xn[:, 0:128], cb[:, 0:128], [cb], [xn])
                        if k >= 1:
                            xk = XK[u][k % 2]
                            ph.tt("dve", xn[:, 256:384], cb[:, 256:384], xk[:, 256:384], ALU.add, [cb, xk], [xn])
            if STOP <= 3:
                continue
            for u in range(8):
                c, e = units[u]
                rows = slice(e * 64, (e + 1) * 64)
                TTf = XK[u][1]
                Vtok = tok[c][:, 3, rows]
                ph.mm(ph.ps[2][:, u * 64:(u + 1) * 64], M4[u][:, 256:384], Vtok, True, True, [M4[u], tok[c]], [ph.ps[2]])
                wb = ph.ps[3 + u // 4]
                ph.mm(wb[:, (u % 4) * 128:(u % 4 + 1) * 128], tok[c][:, 0, :], TTf[:, 256:384], True, True, [tok[c], TTf], [wb])
            for u in range(8):
                c, e = units[u]
                rows = slice(e * 64, (e + 1) * 64)
                evac_copy(AkV[u][:, :], ph.ps[2][:, u * 64:(u + 1) * 64], [ph.ps[2]], [AkV[u]])
                wb = ph.ps[3 + u // 4]
                evac_copy(WTsb[c][rows, :], wb[rows, (u % 4) * 128:(u % 4 + 1) * 128], [wb], [WTsb[c]])
            for u in range(8):
                ph.mm(ph.ps[5][:, u * 64:(u + 1) * 64], XK[u][1][:, 256:384], AkV[u][:, :], True, True, [XK[u][1], AkV[u]], [ph.ps[5]])
            for u in range(8):
                evac_copy(Uloc[u][:, :], ph.ps[5][:, u * 64:(u + 1) * 64], [ph.ps[5]], [Uloc[u]])
        HE = [(hp, e) for hp in range(4) for e in range(2)]

        def reg(hp, e, j):
            return ph.ps[2 + e * 2 + (hp % 2)], (hp // 2) * 128 + j * 64

        def yreg(c, hp, e):
            return ph.ps[c % 2], (hp * 2 + e) * 64

        def critical(c):
            for hp, e in HE:
                s = pers[hp]
                rows = slice(e * 64, (e + 1) * 64)
                b, o = reg(hp, e, 0)
                ph.mm(b[:, o:o + 64], s["WTsb"][c][rows, :], Hb[hp][rows, :], True, True, [s["WTsb"][c], Hb[hp]], [b])
            for hp, e in HE:
                s = pers[hp]
                b, o = reg(hp, e, 0)
                us = Usb[hp * 2 + e]
                ph.tt("dve", us[:, :], b[:, o:o + 64], s["Uloc"][c * 2 + e][:, :], ALU.add, [b, s["Uloc"][c * 2 + e]], [us])
            for hp, e in HE:
                s = pers[hp]
                rows = slice(e * 64, (e + 1) * 64)
                us = Usb[hp * 2 + e]
                km = s["keepM"][c * 2 + e]
                tokc = s["tok"][c]
                Vtok = tokc[:, 3, rows]
                b, o = reg(hp, e, 1)
                ph.mm(b[:, o:o + 64], tokc[:, 1, :], us[:, :], True, False, [tokc, us], [b])
                ph.mm(b[:, o:o + 64], tokc[:, 2, :], Vtok, False, True, [tokc], [b])
                yb, yo = yreg(c, hp, e)
                ph.mm(yb[:, yo:yo + 64], s["AR"][rows, c, 1, :], Hb[hp][rows, :], True, False, [s["AR"], Hb[hp]], [yb])
                ph.mm(yb[:, yo:yo + 64], km[:, 0:128], us[:, :], False, False, [km, us], [yb])
                ph.mm(yb[:, yo:yo + 64], km[:, 128:256], Vtok, False, True, [km, tokc], [yb])
            for hp, e in HE:
                s = pers[hp]
                rows = slice(e * 64, (e + 1) * 64)
                b, o = reg(hp, e, 1)
                ph.stt("dve", Hs[hp][rows, :], Hs[hp][rows, :], s["gam"][rows, c:c + 1], b[rows, o:o + 64],
                       ALU.mult, ALU.add, [Hs[hp], s["gam"], b], [Hs[hp]])
                ph.cp("act", Hb[hp][rows, :], Hs[hp][rows, :], [Hs[hp]], [Hb[hp]])

        def epilogue(c):
            cs = slice(c * 128, (c + 1) * 128)
            yb = ph.ps[c % 2]
            y3 = yb[:, :].rearrange("p (h v) -> p h v", h=8)
            ph.act(gsq[:, :], yb[:, :], AF.Square, [yb], [gsq])
            ph.add("dve", lambda en, y3=y3: en.reduce_sum(gs1[:, 0:8], y3, AX.X), [yb], [gs1])
            ph.add("dve", lambda en: en.reduce_sum(gs2[:, 0:8], gsq[:, :].rearrange("p (h v) -> p h v", h=8), AX.X), [gsq], [gs2])
            ph.ts("dve", gs1[:, :], gs1[:, :], 1.0 / 64, None, ALU.mult, None, [gs1], [gs1])
            ph.tt("dve", gm2[:, :], gs1[:, :], gs1[:, :], ALU.mult, [gs1], [gm2])
            ph.stt("dve", gs2[:, :], gs2[:, :], 1.0 / 64, gm2[:, :], ALU.mult, ALU.subtract, [gs2, gm2], [gs2])
            ph.act(gs2[:, :], gs2[:, :], AF.Ln, [gs2], [gs2], bias=epsG[:, 0:1])
            ph.act(gs2[:, :], gs2[:, :], AF.Exp, [gs2], [gs2], scale=-0.5)
            mean_bc = gs1[:, 0:8].unsqueeze(2).to_broadcast([128, 8, 64])
            rstd_bc = gs2[:, 0:8].unsqueeze(2).to_broadcast([128, 8, 64])
            gq3 = gsq[:, :].rearrange("p (h v) -> p h v", h=8)
            ph.tt("dve", gq3, y3, mean_bc, ALU.subtract, [yb, gs1], [gsq])
            ph.tt("pool", ynall[:, :].rearrange("p (h v) -> p h v", h=8), gq3, rstd_bc, ALU.mult, [gsq, gs2], [ynall])
            tv = ph.ps[6].ap[:, :].bitcast(BF16)
            for hp in range(4):
                ph.tr(tv[:, hp * 128:(hp + 1) * 128], ynall[:, hp * 128:(hp + 1) * 128], identbf[:, :], [ynall, identbf], [ph.ps[6]])
            for hp in range(4):
                s = pers[hp]
                P = lambda i: prm[:, i, hp:hp + 1]
                ph.ts("dve", fin[hp][:, :], tv[:, hp * 128:(hp + 1) * 128], P(GG_), P(GB_), ALU.mult, ALU.add,
                      [ph.ps[6], prm], [fin[hp]])
                ph.tt("pool", fin[hp][:, :], fin[hp][:, :], s["bon"][:, cs], ALU.add, [fin[hp], s["bon"]], [fin[hp]])
                ph.tt("pool", s["outT"][:, cs], fin[hp][:, :], s["gsb"][:, cs], ALU.mult, [fin[hp], s["gsb"]], [s["outT"]])

        critical(0)
        for c in range(1, 4):
            critical(c)
            epilogue(c - 1)
        epilogue(3)
        for hp in range(4):
            ph.dma("sp", dr["mixT"][512 + hp * 128:512 + (hp + 1) * 128, tcols], pers[hp]["outT"][:, :], [pers[hp]["outT"]], [])
    ph.finish()


def phase_opfu(kb, l):
    ph = PH(kb, "of%d" % l)
    S, NT = kb.S, kb.NT
    NG = S // 512
    dr = kb.dram
    c = load_consts(ph)
    ident_bf = c["ident_bf"]
    gB = ph.sb([128, 1024], F32, "gB")
    bB = ph.sb([128, 1024], F32, "bB")
    ph.dma("sp", gB[:, :], dr["ln_mix_g"][l, :].partition_broadcast(128), [], [gB])
    ph.dma("sp", bB[:, :], dr["ln_mix_b"][l, :].partition_broadcast(128), [], [bB])
    WO = WPieces(ph, lambda r: dr["w_out"][l, r * 128:(r + 1) * 128, :], 8, 1024, 512, nstage=2, name="wo")
    WU = WPieces(ph, lambda r: dr["w_up"][l, r * 128:(r + 1) * 128, :], 8, 4096, 1024, nstage=3, name="wu")
    mx = [ph.sb([128, 8, 512], BF16, "mx%d" % k) for k in range(2)]
    xg = [ph.sb([128, 8, 512], BF16, "xg%d" % k) for k in range(2)]
    xr = [ph.sb([128, 1024], F32, "xr%d" % k) for k in range(2)]
    srcs = [ph.sb([128, 1024], F32, "src%d" % k) for k in range(2)]
    bufs = [ln_bufs(ph, k) for k in range(3)]
    tmp = [ph.sb([128, 512], F32, "tmp%d" % k) for k in range(3)]
    ho = [ph.sb([128, 512], BF16, "ho%d" % k) for k in range(4)]
    kcnt = [0]

    def part_a(g):
        m = mx[g % 2]
        x_g = xg[g % 2]
        ph.dma("pool", m[:, :, :], dr["mixT"][:, g * 512:(g + 1) * 512].rearrange("(c p) t -> p c t", p=128), [], [m])
        for j in range(4):
            i = g * 4 + j
            x_ = xr[i % 2]
            ph.dma("sp", x_[:, :], dr["xres"][i * 128:(i + 1) * 128, :], [], [x_])
            src = srcs[i % 2]
            for hh in range(2):
                pb = ph.ps[hh]
                for cc in range(8):
                    wt, wap = WO.get(cc, hh * 512, 512)
                    ph.mm(pb[:, :], m[:, cc, j * 128:(j + 1) * 128], wap, cc == 0, cc == 7, [m, wt], [pb])
                ph.stt("dve", src[:, hh * 512:(hh + 1) * 512], x_[:, hh * 512:(hh + 1) * 512], ALPHA, pb[:, :],
                       ALU.mult, ALU.add, [x_, pb], [src])
            b = bufs[i % 3]
            st, mv, rs, nm, xn, ybf, xh = (b[k] for k in ("st", "mv", "rs", "nm", "xn", "ybf", "xh"))
            for h in range(2):
                ph.add("dve", lambda e, h=h, st=st, src=src: e.bn_stats(st[:, h * 6:(h + 1) * 6], src[:, h * 512:(h + 1) * 512]),
                       [src], [st])
            ph.add("dve", lambda e, mv=mv, st=st: e.bn_aggr(mv[:, 0:2], st[:, 0:12]), [st], [mv])
            ph.act(rs[:, 0:1], mv[:, 1:2], AF.Ln, [mv], [rs], bias=b["eps"][:, 0:1])
            ph.act(rs[:, 0:1], rs[:, 0:1], AF.Exp, [rs], [rs], scale=-0.5)
            ph.ts("dve", nm[:, 0:1], mv[:, 0:1], rs[:, 0:1], -1.0, ALU.mult, ALU.mult, [mv, rs], [nm])
            for h in range(2):
                hs = slice(h * 512, (h + 1) * 512)
                ph.act(xn[:, hs], src[:, hs], AF.Identity, [src, rs, nm], [xh[h]], bias=nm[:, 0:1], scale=rs[:, 0:1])
                ph.tt("dve", xn[:, hs], xn[:, hs], gB[:, hs], ALU.mult, [xh[h], gB], [xh[h]])
                ph.tt(("dve", "pool")[h], xn[:, hs], xn[:, hs], bB[:, hs], ALU.add, [xh[h], bB], [xh[h]])
                ph.dma("sp", dr["xres"][i * 128:(i + 1) * 128, hs], xn[:, hs], [xh[h]], [])
                ph.cp("act", ybf[:, hs], xn[:, hs], [xh[h]], [ybf])
            psb = ph.ps[2 + i % 2]
            pbv = psb.ap[:, :].bitcast(BF16)
            for cc in range(8):
                ph.tr(pbv[:, cc * 128:(cc + 1) * 128], ybf[:, cc * 128:(cc + 1) * 128], ident_bf[:, :], [ybf, ident_bf], [psb])
            ph.cp("dve", x_g[:, :, j * 128:(j + 1) * 128], pbv[:, :].rearrange("p (c t) -> p c t", c=8), [psb], [x_g])

    def part_b(g):
        x_g = xg[g % 2]
        for f in range(32):
            k = kcnt[0]
            kcnt[0] += 1
            pb = ph.ps[4 + k % 4]
            for cc in range(8):
                wt, wap = WU.get(cc, f * 128, 128)
                ph.mm(pb[:, :], wap, x_g[:, cc, :], cc == 0, cc == 7, [wt, x_g], [pb])
            t = tmp[k % 3]
            h = ho[k % 4]
            ph.act(t[:, :], pb[:, :], AF.Relu, [pb], [t])
            ph.tt("pool", h[:, :], t[:, :], t[:, :], ALU.mult, [t], [h])
            ph.dma("sp", dr["hT"][f * 128:(f + 1) * 128, g * 512:(g + 1) * 512], h[:, :], [h], [])

    part_a(0)
    for g in range(NG):
        if g + 1 < NG:
            part_a(g + 1)
        part_b(g)
    ph.finish()
```
